# Optimizing a Trainium2 kernel written in Bass

```python
import jax, jax.numpy as jnp
from jax import lax
import numpy as np

D_MODEL = 1024
BATCH = 2
SEQ = 8192
DEPTH = 2
DEC_BATCH = 128
DEC_SEQ = 4
PAST_LEN = 8192
PAGE_SIZE = 128

R_HEADS = 8
R_HEAD_DIM = 64
R_WIDTH = R_HEADS * R_HEAD_DIM
DECAY_LORA = 64
ICLR_LORA = 64
GATE_LORA = 128
SHIFT_W = 3 * R_WIDTH + DECAY_LORA + ICLR_LORA + GATE_LORA
A_HEADS = 8
KV_HEADS = 2
HEAD_DIM = 64
Q_WIDTH = A_HEADS * HEAD_DIM
KV_WIDTH = KV_HEADS * HEAD_DIM
GROUP = A_HEADS // KV_HEADS
WINDOW = 128
BLOCK = 128
ROPE_THETA = 10000.0
ATTN_SCALE = HEAD_DIM ** -0.5
IN_W = SHIFT_W + Q_WIDTH + 2 * KV_WIDTH + 2 * D_MODEL
D_FF = 4 * D_MODEL
ALPHA = (2 * DEPTH) ** 0.25
BETA = (8 * DEPTH) ** -0.25
LN_EPS = 1e-5
GN_EPS = 64e-5

kernel_name = 'rwkv7_swa_sink_gated_hybrid_step'


def layer_norm(x, g, b):
    xf = x.astype(jnp.float32)
    mu = jnp.mean(xf, -1, keepdims=True)
    var = jnp.mean(jnp.square(xf - mu), -1, keepdims=True)
    return ((xf - mu) * lax.rsqrt(var + LN_EPS)).astype(x.dtype) * g + b


def rope(x, pos):
    half = HEAD_DIM // 2
    inv_freq = ROPE_THETA ** (-jnp.arange(half, dtype=jnp.float32) / half)
    ang = pos.astype(jnp.float32)[:, None] * inv_freq[None, :]
    cos = jnp.cos(ang)[None, :, None, :]
    sin = jnp.sin(ang)[None, :, None, :]
    xf = x.astype(jnp.float32)
    x1, x2 = xf[..., :half], xf[..., half:]
    return jnp.concatenate([x1 * cos - x2 * sin, x2 * cos + x1 * sin], -1).astype(x.dtype)


def wkv7_scan(S0, r, logw, k, v, kk, a):
    def step(S, inp):
        r_t, lw_t, k_t, v_t, kk_t, a_t = inp
        sa = jnp.einsum('bhvk,bhk->bhv', S, -kk_t)
        S = (S * jnp.exp(lw_t)[:, :, None, :] + sa[..., None] * (kk_t * a_t)[:, :, None, :]
             + v_t[..., None] * k_t[:, :, None, :])
        return S, jnp.einsum('bhvk,bhk->bhv', S, r_t)
    xs = tuple(jnp.moveaxis(t, 1, 0) for t in (r, logw, k, v, kk, a))
    S, ys = lax.scan(step, S0, xs)
    return jnp.moveaxis(ys, 0, 1), S


def rwkv7_branch(z, z_prev, S0, P):
    B, T, _ = z.shape
    f32 = jnp.float32
    prev = jnp.concatenate([z_prev[:, None].astype(z.dtype), z[:, :-1]], axis=1)
    zs = (z + (prev - z) * P['mu_shift']).astype(f32)
    o1, o2, o3 = R_WIDTH, 2 * R_WIDTH, 3 * R_WIDTH
    o4, o5 = o3 + DECAY_LORA, o3 + DECAY_LORA + ICLR_LORA
    r, k, v = zs[..., :o1], zs[..., o1:o2], zs[..., o2:o3]
    wd, ad, gd = zs[..., o3:o4], zs[..., o4:o5], zs[..., o5:]
    pre_w = P['decay_base'].astype(f32) + jnp.tanh(wd) @ P['decay_up'].astype(f32)
    logw = -jnp.exp(-jax.nn.softplus(-pre_w) - 0.5)
    a = jax.nn.sigmoid(P['iclr_base'].astype(f32) + ad @ P['iclr_up'].astype(f32))
    g = jax.nn.sigmoid(gd) @ P['gate_up'].astype(f32)
    heads = lambda t: t.reshape(B, T, R_HEADS, R_HEAD_DIM)
    kk = heads(k * P['k_k'].astype(f32))
    kk = kk * lax.rsqrt(jnp.maximum(jnp.sum(kk * kk, -1, keepdims=True), 1e-24))
    k = k * (1.0 + (a - 1.0) * P['k_a'].astype(f32))
    r, logw, k, v, a = heads(r), heads(logw), heads(k), heads(v), heads(a)
    y, S = wkv7_scan(S0.astype(f32), r, logw, k, v, kk, a)
    mu = jnp.mean(y, -1, keepdims=True)
    var = jnp.mean(jnp.square(y - mu), -1, keepdims=True)
    y = ((y - mu) * lax.rsqrt(var + GN_EPS)).reshape(B, T, R_WIDTH)
    y = y * P['lnx_g'].astype(f32) + P['lnx_b'].astype(f32)
    bonus = jnp.sum(r * k * P['r_k'].astype(f32), -1, keepdims=True) * v
    y = (y + bonus.reshape(B, T, R_WIDTH)) * g
    return y.astype(z.dtype), S.astype(S0.dtype)


def sink_softmax(s, mask, sink):
    s = jnp.where(mask, s, -jnp.inf)
    sk = sink.astype(jnp.float32)[:, :, None, None]
    m = jnp.maximum(jnp.max(s, -1, keepdims=True), sk)
    p = jnp.exp(s - m)
    return p / (jnp.sum(p, -1, keepdims=True) + jnp.exp(sk - m))


def attn_prompt(q, k, v, sinks):
    B, T = q.shape[:2]
    nb = T // BLOCK
    f32 = jnp.float32
    qb = q.astype(f32).reshape(B, nb, BLOCK, KV_HEADS, GROUP, HEAD_DIM)
    def band(t):
        tb = t.astype(f32).reshape(B, nb, BLOCK, KV_HEADS, HEAD_DIM)
        prev = jnp.pad(tb[:, :-1], ((0, 0), (1, 0), (0, 0), (0, 0), (0, 0)))
        return jnp.concatenate([prev, tb], axis=2)
    kb, vb = band(k), band(v)
    s = jnp.einsum('bnqhgd,bnkhd->bnhgqk', qb, kb) * ATTN_SCALE
    qi = jnp.arange(BLOCK)[:, None] + BLOCK
    kj = jnp.arange(2 * BLOCK)[None, :]
    dist = qi - kj
    in_win = (dist >= 0) & (dist <= WINDOW)
    has_prev = (jnp.arange(nb) > 0)[:, None, None] | (kj >= BLOCK)[None]
    mask = (in_win[None] & has_prev)[:, None, None]
    p = sink_softmax(s, mask, sinks.reshape(KV_HEADS, GROUP))
    o = jnp.einsum('bnhgqk,bnkhd->bnqhgd', p, vb)
    return o.reshape(B, T, Q_WIDTH).astype(q.dtype)


def attn_sample(q, k, v, k_buf, v_buf, sinks):
    B, T = q.shape[:2]
    W = k_buf.shape[1]
    f32 = jnp.float32
    kc = jnp.concatenate([k_buf.astype(k.dtype), k], axis=1)
    vc = jnp.concatenate([v_buf.astype(v.dtype), v], axis=1)
    qpos = PAST_LEN + jnp.arange(T)
    kpos = PAST_LEN - W + jnp.arange(W + T)
    dist = qpos[:, None] - kpos[None, :]
    mask = (dist >= 0) & (dist <= WINDOW)
    qg = q.astype(f32).reshape(B, T, KV_HEADS, GROUP, HEAD_DIM)
    s = jnp.einsum('bqhgd,bkhd->bhgqk', qg, kc.astype(f32)) * ATTN_SCALE
    p = sink_softmax(s, mask, sinks.reshape(KV_HEADS, GROUP))
    o = jnp.einsum('bhgqk,bkhd->bqhgd', p, vc.astype(f32))
    return o.reshape(B, T, Q_WIDTH).astype(q.dtype), kc[:, -W:], vc[:, -W:]


def trunk_layer(x, pos, shift_prev, S0, k_buf, v_buf, wbuf, P):
    B, T, _ = x.shape
    z = jnp.einsum('btd,de->bte', x, P['w_in'])
    o_q = SHIFT_W
    o_k = o_q + Q_WIDTH
    o_v = o_k + KV_WIDTH
    o_ga = o_v + KV_WIDTH
    o_gb = o_ga + D_MODEL
    z_r = z[..., :o_q]
    q = rope(z[..., o_q:o_k].reshape(B, T, A_HEADS, HEAD_DIM), pos)
    k = rope(z[..., o_k:o_v].reshape(B, T, KV_HEADS, HEAD_DIM), pos)
    v = z[..., o_v:o_ga].reshape(B, T, KV_HEADS, HEAD_DIM)
    gate_r = jax.nn.sigmoid(z[..., o_ga:o_gb])
    gate_a = jax.nn.sigmoid(z[..., o_gb:])
    y_r, S_new = rwkv7_branch(z_r, shift_prev, S0, P)
    if k_buf is None:
        y_a = attn_prompt(q, k, v, P['sinks'])
        k_new, v_new = k[:, -wbuf:], v[:, -wbuf:]
    else:
        y_a, k_new, v_new = attn_sample(q, k, v, k_buf, v_buf, P['sinks'])
    mix = (gate_r * jnp.einsum('btc,cd->btd', y_r, P['w_br_rwkv'])
           + gate_a * jnp.einsum('btc,cd->btd', y_a, P['w_br_attn']))
    x = layer_norm(ALPHA * x + mix @ P['w_out'], P['ln1_g'], P['ln1_b'])
    h = jnp.square(jax.nn.relu(x @ P['w_ff_up']))
    x = layer_norm(ALPHA * x + h @ P['w_ff_down'], P['ln2_g'], P['ln2_b'])
    return x, z_r[:, -1], S_new, k_new, v_new


def setup_inputs(seed: int = 0) -> dict:
    key = jax.random.key(seed)
    ks = iter(jax.random.split(key, 32))
    f32 = jnp.float32
    nrm = lambda shape, s: jax.random.normal(next(ks), shape, f32) * s
    L = DEPTH
    wbuf = min(WINDOW, PAST_LEN)
    return {
        'x_prompt': nrm((BATCH, SEQ, D_MODEL), 1.0),
        'x_sample': nrm((DEC_BATCH, DEC_SEQ, D_MODEL), 1.0),
        'state_wkv': nrm((L, DEC_BATCH, R_HEADS, R_HEAD_DIM, R_HEAD_DIM), 0.5),
        'state_shift': nrm((L, DEC_BATCH, SHIFT_W), 1.0),
        'cache_k_win': nrm((L, DEC_BATCH, wbuf, KV_HEADS, HEAD_DIM), 1.0),
        'cache_v_win': nrm((L, DEC_BATCH, wbuf, KV_HEADS, HEAD_DIM), 1.0),
        'w_in': nrm((L, D_MODEL, IN_W), D_MODEL ** -0.5),
        'mu_shift': jax.random.uniform(next(ks), (L, SHIFT_W), f32),
        'decay_base': jax.random.uniform(next(ks), (L, R_WIDTH), f32, -6.0, -1.0),
        'decay_up': nrm((L, DECAY_LORA, R_WIDTH), 0.5 * DECAY_LORA ** -0.5),
        'iclr_base': nrm((L, R_WIDTH), 0.1),
        'iclr_up': nrm((L, ICLR_LORA, R_WIDTH), 0.5 * ICLR_LORA ** -0.5),
        'gate_up': nrm((L, GATE_LORA, R_WIDTH), GATE_LORA ** -0.5),
        'k_k': 0.85 + nrm((L, R_WIDTH), 0.05),
        'k_a': 1.0 + nrm((L, R_WIDTH), 0.05),
        'r_k': nrm((L, R_HEADS, R_HEAD_DIM), 0.1),
        'lnx_g': 1.0 + nrm((L, R_WIDTH), 0.05),
        'lnx_b': nrm((L, R_WIDTH), 0.02),
        'sinks': nrm((L, A_HEADS), 0.5),
        'w_br_rwkv': nrm((L, R_WIDTH, D_MODEL), BETA * R_WIDTH ** -0.5),
        'w_br_attn': nrm((L, Q_WIDTH, D_MODEL), BETA * Q_WIDTH ** -0.5),
        'w_out': nrm((L, D_MODEL, D_MODEL), BETA * D_MODEL ** -0.5),
        'ln1_g': 1.0 + nrm((L, D_MODEL), 0.05),
        'ln1_b': nrm((L, D_MODEL), 0.02),
        'w_ff_up': nrm((L, D_MODEL, D_FF), D_MODEL ** -0.5),
        'w_ff_down': nrm((L, D_FF, D_MODEL), BETA * D_FF ** -0.5),
        'ln2_g': 1.0 + nrm((L, D_MODEL), 0.05),
        'ln2_b': nrm((L, D_MODEL), 0.02),
    }


def reference(x_prompt, x_sample, state_wkv, state_shift, cache_k_win, cache_v_win,
              w_in, mu_shift, decay_base, decay_up, iclr_base, iclr_up, gate_up, k_k, k_a, r_k,
              lnx_g, lnx_b, sinks, w_br_rwkv, w_br_attn, w_out, ln1_g, ln1_b,
              w_ff_up, w_ff_down, ln2_g, ln2_b):
    wbuf = cache_k_win.shape[2]
    Bp, Tp, _ = x_prompt.shape
    Ts = x_sample.shape[1]
    pos_p = jnp.arange(Tp, dtype=jnp.int32)
    pos_s = PAST_LEN + jnp.arange(Ts, dtype=jnp.int32)
    hp, hs = x_prompt, x_sample
    p_wkv, p_shift, p_k, p_v = [], [], [], []
    s_wkv, s_shift, s_k, s_v = [], [], [], []
    for l in range(DEPTH):
        P = dict(w_in=w_in[l], mu_shift=mu_shift[l], decay_base=decay_base[l], decay_up=decay_up[l],
                 iclr_base=iclr_base[l], iclr_up=iclr_up[l], gate_up=gate_up[l], k_k=k_k[l], k_a=k_a[l],
                 r_k=r_k[l], lnx_g=lnx_g[l], lnx_b=lnx_b[l], sinks=sinks[l], w_br_rwkv=w_br_rwkv[l],
                 w_br_attn=w_br_attn[l], w_out=w_out[l], ln1_g=ln1_g[l], ln1_b=ln1_b[l],
                 w_ff_up=w_ff_up[l], w_ff_down=w_ff_down[l], ln2_g=ln2_g[l], ln2_b=ln2_b[l])
        shift0 = jnp.zeros((Bp, SHIFT_W), hp.dtype)
        S0 = jnp.zeros((Bp, R_HEADS, R_HEAD_DIM, R_HEAD_DIM), state_wkv.dtype)
        hp, sh, S, kb, vb = trunk_layer(hp, pos_p, shift0, S0, None, None, wbuf, P)
        p_wkv.append(S); p_shift.append(sh); p_k.append(kb); p_v.append(vb)
        hs, sh, S, kb, vb = trunk_layer(hs, pos_s, state_shift[l], state_wkv[l],
                                        cache_k_win[l], cache_v_win[l], wbuf, P)
        s_wkv.append(S); s_shift.append(sh); s_k.append(kb); s_v.append(vb)
    return (hp, hs,
            jnp.stack(p_wkv), jnp.stack(p_shift), jnp.stack(p_k), jnp.stack(p_v),
            jnp.stack(s_wkv), jnp.stack(s_shift), jnp.stack(s_k), jnp.stack(s_v))
```

```python
import numpy as np
from contextlib import ExitStack
import concourse.bass as bass
import concourse.mybir as mybir
from concourse.ap import AP
from concourse.bass_utils import run_bass_kernel_spmd

F32 = mybir.dt.float32
BF16 = mybir.dt.bfloat16
AF = mybir.ActivationFunctionType
ALU = mybir.AluOpType

D = 1024
NT = 256
TL = NT // 128
SHIFT_W = 1792
IN_W = 4608
DFF = 4096
ALPHA = 4 ** 0.25
LN_EPS = 1e-5
GN_EPS = 64e-5
DECAY_C = 0.6065306597126334
PVL = 78
QM = 'pool'
NCST = 1792


class Sched:
    COMPUTE = ('pe', 'act', 'dve', 'pool')

    def __init__(self, nc, es, n_dma_sems=24):
        self.nc = nc
        self.h = {'pe': nc.tensor, 'act': nc.scalar, 'dve': nc.vector, 'pool': nc.gpsimd, 'sp': nc.sync}
        self.ops = []
        self.last_w = {}
        self.readers = {}
        self.sem = {e: es.enter_context(nc.semaphore("s_" + e)) for e in self.COMPUTE}
        self.dsem = []
        self.dq = {}
        for q, n in (('sp', n_dma_sems), ('act', 8), ('pool', 12)):
            self.dq[q] = list(range(len(self.dsem), len(self.dsem) + n))
            self.dsem += [es.enter_context(nc.semaphore("d%s%d" % (q, i))) for i in range(n)]

    def _add(self, kind, eng, fn, r, w):
        isps = lambda k: isinstance(k, tuple) and k[0] in ('ps', 'pst')
        w = list(w) + [k for k in r if isps(k)]
        r = [k for k in r if not isps(k)]
        oid = len(self.ops)
        deps = set()
        for k in r:
            if k in self.last_w:
                deps.add(self.last_w[k])
        for k in w:
            if k in self.last_w:
                deps.add(self.last_w[k])
            deps |= self.readers.get(k, set())
        for k in r:
            self.readers.setdefault(k, set()).add(oid)
        for k in w:
            self.last_w[k] = oid
            self.readers[k] = set()
        deps.discard(oid)
        self.ops.append(dict(kind=kind, eng=eng, fn=fn, deps=deps))
        return oid

    disabled = False

    def op(self, eng, fn, r=(), w=()):
        if self.disabled:
            return None
        return self._add('c', eng, fn, r, w)

    def dma(self, eng, out, in_, r=(), w=(), **kw):
        if self.disabled:
            return None
        return self._add('d', eng, (out, in_, kw), r, w)

    def emit(self):
        ops = self.ops
        need = [False] * len(ops)
        for i, o in enumerate(ops):
            for d in o['deps']:
                p = ops[d]
                if p['kind'] == 'c':
                    if p['eng'] == o['eng'] and o['kind'] == 'c' and p['eng'] == 'pe':
                        continue
                    need[d] = True
        cnt = {e: 0 for e in self.COMPUTE}
        tok = [None] * len(ops)
        seen = {}
        dcount = [0] * len(self.dsem)
        dk = {q: 0 for q in self.dq}
        nwaits = 0
        acts = {e: [] for e in self.h}
        for i, o in enumerate(ops):
            e = o['eng']
            wl = {}
            for d in o['deps']:
                p = ops[d]
                if p['kind'] == 'c' and p['eng'] == e and o['kind'] == 'c' and e == 'pe':
                    continue
                t = tok[d]
                if t is None:
                    continue
                ts, tv = t
                if tv > wl.get(id(ts), (ts, 0))[1]:
                    wl[id(ts)] = (ts, tv)
            if o['kind'] == 'd':
                j = self.dq[e][dk[e] % len(self.dq[e])]
                dk[e] += 1
                dsj = self.dsem[j]
                if dcount[j] > 0 and dcount[j] > wl.get(id(dsj), (dsj, 0))[1]:
                    wl[id(dsj)] = (dsj, dcount[j])
            for ws, wv in wl.values():
                key = (e, id(ws))
                if seen.get(key, 0) >= wv:
                    continue
                acts[e].append((lambda s_, v_: (lambda h: h.wait_ge(s_, v_)))(ws, wv))
                nwaits += 1
                seen[key] = wv
            if o['kind'] == 'c':
                if need[i]:
                    cnt[e] += 1
                    acts[e].append((lambda fn_, sm_: (lambda h: fn_(h).then_inc(sm_, 1)))(o['fn'], self.sem[e]))
                    tok[i] = (self.sem[e], cnt[e])
                else:
                    acts[e].append(o['fn'])
            else:
                out, in_, kw = o['fn']
                dcount[j] += 16
                acts[e].append((lambda o_, i_, k_, s_: (lambda h: h.dma_start(out=o_, in_=i_, **k_).then_inc(s_, 16)))(out, in_, kw, dsj))
                tok[i] = (dsj, dcount[j])
        for j, fs in enumerate(self.dsem):
            if dcount[j] > 0:
                acts['sp'].append((lambda s_, v_: (lambda h: h.wait_ge(s_, v_)))(fs, dcount[j]))
        with self.nc.Block() as block:
            @block.sync
            def _(h):
                for a in acts['sp']:
                    a(h)

            @block.tensor
            def _(h):
                for a in acts['pe']:
                    a(h)

            @block.scalar
            def _(h):
                for a in acts['act']:
                    a(h)

            @block.vector
            def _(h):
                for a in acts['dve']:
                    a(h)

            @block.gpsimd
            def _(h):
                for a in acts['pool']:
                    a(h)
        return dict(n_ops=len(ops), n_waits=nwaits, signals=dict(cnt))


def make_consts():
    c = np.zeros((128, NCST), np.float32)
    idx = np.arange(128)
    c[:, 0:128] = np.eye(128)
    c[:, 128:256] = (idx[:, None] // 64 == idx[None, :] // 64)
    same = (idx[:, None] // 64) == (idx[None, :] // 64)
    mstrict = same & (idx[:, None] < idx[None, :])
    mincl = same & (idx[:, None] <= idx[None, :])
    c[:, 256:384] = mstrict.T
    c[:, 384:512] = mstrict
    c[:, 512:640] = mincl
    c[:, 640:768] = idx[:, None] >= idx[None, :]
    c[:, 768:896] = idx[:, None] <= idx[None, :]
    c[:, 896:1024] = 0.0
    c[:, 1024:1152] = idx[:, None] <= idx[None, :]
    prot = np.zeros((128, 128), np.float32)
    for hb in (0, 64):
        for dd in range(32):
            prot[hb + dd + 32, hb + dd] = -1.0
            prot[hb + dd, hb + dd + 32] = 1.0
    c[:, 1152:1280] = prot
    rm = np.ones((128, 256), np.float32)
    rm[:, 0::64] = 0.0
    c[:, 1280:1536] = rm
    c[:, 1536:1664] = 1.0
    c[:, 1664:1728] = (idx[:, None] % 64) == np.arange(64)[None, :]
    return c


def rope_tables(pos):
    half = 32
    inv = (10000.0 ** (-np.arange(half, dtype=np.float32) / half)).astype(np.float32)
    ang = pos.astype(np.float32)[None, :] * inv[:, None]
    cos = np.cos(ang).astype(np.float32)
    sin = np.sin(ang).astype(np.float32)
    return np.tile(cos, (4, 1)), np.tile(sin, (4, 1))


def build(SEQ, NSS=16, dbg=(), upto=99, noconv=False):
    NG = SEQ // NT
    NTS = NSS * 4
    nc = bass.Bass("TRN2", target_bir_lowering=False)
    din = lambda name, shape, dt=F32: nc.dram_tensor(name, list(shape), dt, kind="ExternalInput").ap()
    dout = lambda name, shape, dt=F32: nc.dram_tensor(name, list(shape), dt, kind="ExternalOutput").ap()
    dscr = lambda name, shape, dt=BF16: nc.dram_tensor(name, list(shape), dt).ap()

    xp = din("xp", [SEQ, D])
    w_in = din("w_in", [2, D, IN_W]); w_brr = din("w_br_rwkv", [2, 512, D]); w_bra = din("w_br_attn", [2, 512, D])
    w_out = din("w_out", [2, D, D]); w_up = din("w_ff_up", [2, D, DFF]); w_dn = din("w_ff_down", [2, DFF, D])
    d_up = din("decay_up", [2, 64, 512]); i_up = din("iclr_up", [2, 64, 512]); g_up = din("gate_up", [2, 128, 512])
    pv_d = din("pv_in", [128, 2 * PVL]); cst_d = din("cst_in", [128, NCST])
    cos_d = din("cos_in", [128, SEQ]); sin_d = din("sin_in", [128, SEQ])

    xs = din("xs", [NSS, 4, D]); swkv_i = din("swkv_i", [2, NSS, 8, 64, 64]); sshift_i = din("sshift_i", [2, NSS, SHIFT_W])
    sck_i = din("sck_i", [2, NSS, 128, 2, 64]); scv_i = din("scv_i", [2, NSS, 128, 2, 64])
    coss_d = din("coss_in", [128, NT]); sins_d = din("sins_in", [128, NT]); tmask_d = din("tmask_in", [128, NT])
    y_s = dout("y_s", [NSS, 4, D]); so_wkv = dout("s_wkv", [2, NSS, 8, 64, 64]); so_shift = dout("s_shift", [2, NSS, SHIFT_W])
    so_ck = dout("s_ck", [2, NSS, 128, 2, 64]); so_cv = dout("s_cv", [2, NSS, 128, 2, 64])
    y_p = dout("y_p", [SEQ, D]); o_wkv = dout("p_wkv", [2, 8, 64, 64]); o_shift = dout("p_shift", [2, SHIFT_W])
    o_ck = dout("p_ck", [2, 128, 2, 64]); o_cv = dout("p_cv", [2, 128, 2, 64])
    dbg_out = {}

    wi_b = dscr("wi_b", [2, D, IN_W]); wbr_b = dscr("wbr_b", [2, 512, D]); wba_b = dscr("wba_b", [2, 512, D])
    wo_b = dscr("wo_b", [2, D, D]); wu_b = dscr("wu_b", [2, D, DFF]); wd_b = dscr("wd_b", [2, DFF, D])

    with ExitStack() as es:
        S = Sched(nc, es)
        T = lambda name, shape, dt=F32: es.enter_context(nc.sbuf_tensor(name, list(shape), dt))
        def MM(out, lhsT, rhs, start=True, stop=True, r=(), w=()):
            S.op('pe', lambda e: e.matmul(out, lhsT=lhsT, rhs=rhs, start=start, stop=stop), r=r, w=w)

        def TR(out, in_, ident, r=(), w=()):
            S.op('pe', lambda e: e.transpose(out, in_, ident), r=r, w=w)

        def TT(eng, out, in0, in1, op, r=(), w=()):
            S.op(eng, lambda e: e.tensor_tensor(out=out, in0=in0, in1=in1, op=op), r=r, w=w)

        def TS(eng, out, in0, s1, op0, s2=None, op1=None, r=(), w=()):
            if op1 is None:
                S.op(eng, lambda e: e.tensor_scalar(out=out, in0=in0, scalar1=s1, scalar2=None, op0=op0), r=r, w=w)
            else:
                S.op(eng, lambda e: e.tensor_scalar(out=out, in0=in0, scalar1=s1, scalar2=s2, op0=op0, op1=op1), r=r, w=w)

        def STT(eng, out, in0, scalar, in1, op0, op1, r=(), w=()):
            S.op(eng, lambda e: e.scalar_tensor_tensor(out=out, in0=in0, scalar=scalar, in1=in1, op0=op0, op1=op1), r=r, w=w)

        def ACT(out, in_, func, bias=None, scale=1.0, r=(), w=()):
            if bias is None:
                S.op('act', lambda e: e.activation(out=out, in_=in_, func=func, scale=scale), r=r, w=w)
            else:
                S.op('act', lambda e: e.activation(out=out, in_=in_, func=func, bias=bias, scale=scale), r=r, w=w)

        def CP(eng, out, in_, r=(), w=()):
            if eng == 'act':
                S.op('act', lambda e: e.copy(out=out, in_=in_), r=r, w=w)
            else:
                S.op(eng, lambda e: e.tensor_copy(out=out, in_=in_), r=r, w=w)

        def RCP(out, in_, r=(), w=()):
            S.op('dve', lambda e: e.reciprocal(out=out, in_=in_), r=r, w=w)

        def bc_mid(ap2d, n):
            return ap2d.unsqueeze(1).broadcast_to([ap2d.shape[0], n, ap2d.shape[1]])

        def dump(name, ap, keys, shape=None):
            if name not in dbg or name in dbg_out:
                return
            dbg_out[name] = 1
            shp = list(ap.shape)
            o = dout("dbg_" + name, shp, ap.dtype)
            full = o if len(shp) == 2 else o
            S.dma(QM, o[tuple(slice(None) for _ in shp)], ap, r=keys)

        cst = T("cst", [128, NCST])
        S.dma(QM, cst[:], cst_d[:, :], w=['cst'])
        identf = cst[:, 0:128]; blockones = cst[:, 128:256]; maskT = cst[:, 256:384]; mask12 = cst[:, 384:640]
        mAtt = cst[:, 640:896]; mAtt0 = cst[:, 896:1152]; resetm = cst[:, 1280:1536]; I2 = cst[:, 1664:1728]
        identb = T("identb", [128, 128], BF16); protb = T("protb", [128, 128], BF16); onesb = T("onesb", [128, 128], BF16)
        CP('pool', identb[:], cst[:, 0:128], r=['cst'], w=['identb'])
        CP('pool', protb[:], cst[:, 1152:1280], r=['cst'], w=['protb'])
        CP('pool', onesb[:], cst[:, 1536:1664], r=['cst'], w=['onesb'])
        pv = T("pv", [128, 2 * PVL])
        S.dma(QM, pv[:], pv_d[:, :], w=['pv'])
        pd = T("pd", [128, 2, 24])
        for l in range(2):
            b0 = l * PVL
            TS('dve', pd[:, l, 0:14], pv[:, b0:b0 + 14], -1.0, ALU.mult, 1.0, ALU.add, r=['pv'], w=[('pd', l)])
            TS('dve', pd[:, l, 14:18], pv[:, b0 + 18:b0 + 22], -1.0, ALU.mult, 1.0, ALU.add, r=['pv'], w=[('pd', l)])
            ACT(pd[:, l, 18:22], pv[:, b0 + 74:b0 + 78], AF.Exp, r=['pv'], w=[('pd', l)])
        P = lambda l, a, b: pv[:, l * PVL + a: l * PVL + b]
        lora = T("lora", [128, 2, 2, 512], BF16)
        for l in range(2):
            S.dma('pool', lora[0:64, l, 0, :], d_up[l], w=[('lora', l)])
            S.dma('pool', lora[64:128, l, 0, :], i_up[l], w=[('lora', l)])
            S.dma('pool', lora[:, l, 1, :], g_up[l], w=[('lora', l)])

        def conv(dst, src, l, rows, key):
            for k in range(rows // 128):
                S.dma('pool', dst[l, k * 128:(k + 1) * 128, :], src[l, k * 128:(k + 1) * 128, :], w=[(key, l, k)])
        convspec = dict(wi=(wi_b, w_in, D), wbr=(wbr_b, w_brr, 512), wba=(wba_b, w_bra, 512), wo=(wo_b, w_out, D),
                        wu=(wu_b, w_up, D), wd=(wd_b, w_dn, DFF))
        converted = set()

        def ensure_conv(kind, l):
            if (kind, l) in converted:
                return
            converted.add((kind, l))
            dst, src, rows = convspec[kind]
            conv(dst, src, l, rows, kind)

        NSLOT = 3
        ring = [T("ring%d" % i, [128, 4096], BF16) for i in range(NSLOT)]
        wk2 = T("wk2", [128, 8, 2, 128], BF16)

        def wsrc(kind, l, i):
            if kind == 'wi':
                return wi_b[l].rearrange("(k p) n -> p k n", p=128)[:, :, i * 512:(i + 1) * 512], [('wi', l, k) for k in range(8)], [8, 512]
            if kind == 'wbr':
                return wbr_b[l].rearrange("(k p) n -> p k n", p=128), [('wbr', l, k) for k in range(4)], [4, 1024]
            if kind == 'wba':
                return wba_b[l].rearrange("(k p) n -> p k n", p=128), [('wba', l, k) for k in range(4)], [4, 1024]
            if kind == 'wo':
                return wo_b[l].rearrange("(k p) n -> p k n", p=128)[:, :, i * 512:(i + 1) * 512], [('wo', l, k) for k in range(8)], [8, 512]
            if kind == 'wu':
                return wu_b[l].rearrange("(k p) n -> p k n", p=128)[:, :, i * 512:(i + 1) * 512], [('wu', l, k) for k in range(8)], [8, 512]
            if kind == 'wd':
                return wd_b[l].rearrange("(f p) n -> p f n", p=128)[:, :, i * 128:(i + 1) * 128], [('wd', l, k) for k in range(32)], [32, 128]
        layer_loads = ([('wi', i) for i in range(5)] + [('wbr', 0), ('wi', 5), ('wi', 6), ('wba', 0), ('wi', 7), ('wi', 8),
                       ('wo', 0), ('wo', 1)] + [('wu', i) for i in range(8)] + [('wd', i) for i in range(8)])
        all_loads = [(kind, l, i) for g in range(NG + NSS) for l in range(2) for (kind, i) in layer_loads]
        wstate = dict(issued=0, used=0, done=set())

        def w_can_issue(n):
            return n < len(all_loads) and (n - NSLOT < 0 or (n - NSLOT) in wstate['done'])

        def w_issue():
            n = wstate['issued']
            kind, l, i = all_loads[n]
            ensure_conv(kind, l)
            src, keys, shp = wsrc(kind, l, i)
            slot = n % NSLOT
            dst = ring[slot][:].rearrange("p (a b) -> p a b", a=shp[0])
            S.dma('sp', dst, src, r=keys, w=[('ring', slot)])
            wstate['issued'] = n + 1

        def w_prefetch():
            while wstate['issued'] < min(wstate['used'] + NSLOT, len(all_loads)) and w_can_issue(wstate['issued']):
                w_issue()

        def w_get(kind, l, i):
            n = wstate['used']
            assert all_loads[n] == (kind, l, i), (all_loads[n], kind, l, i)
            wstate['used'] = n + 1
            while wstate['issued'] <= n:
                assert w_can_issue(wstate['issued']), ("ring slot still live", n)
                w_issue()
            w_prefetch()
            _, _, shp = wsrc(kind, l, i)
            slot = n % NSLOT
            return ring[slot][:].rearrange("p (a b) -> p a b", a=shp[0]), ('ring', slot), n

        def w_done(n):
            wstate['done'].add(n)
            w_prefetch()

        ps = es.enter_context(nc.psum_tensor("ps", [128, 3072], F32))
        pst = es.enter_context(nc.psum_tensor("pst", [128, 2048], BF16))

        def pk(c0, c1):
            return [('ps', b) for b in range(c0 // 512, (c1 - 1) // 512 + 1)]
        big = dict(i=0)

        def bigslot():
            i = big['i'] % 2
            big['i'] += 1
            c0 = 2048 + i * 512
            return ps[:, c0:c0 + 256], [('ps', 4 + i)]

        xT = T("xT", [128, 8, NT]); xTb = T("xTb", [128, 8, NT], BF16)
        xio = T("xio", [128, TL, D])
        Zc = [T("Zc%d" % i, [128, NT + 1]) for i in range(2)]
        CARRY = T("CARRY", [128, 2, 14])
        rT = T("rT", [128, 4, NT]); kraw = T("kraw", [128, 4, NT]); vT = T("vT", [128, 4, NT])
        tw = T("tw", [128, NT], BF16); sgd = T("sgd", [128, NT], BF16)
        ACRC = T("ACRC", [128, 4, TL, 2, 128], BF16)
        bcb = T("bcb", [128, 4, NT], BF16); kcb = T("kcb", [128, 4, NT], BF16); asb = T("asb", [128, 4, NT], BF16)
        vb = T("vb", [128, 4, NT], BF16); rs = T("rs", [128, 4, NT]); bonus = T("bonus", [128, 4, NT], BF16)
        gg = T("gg", [128, 4, NT], BF16); GC = T("GC", [128, 4, NT // 64])
        NTM = 9
        Tm = [T("Tm%d" % i, [128, NT]) for i in range(NTM)]
        X = T("X", [128, 8, 128], BF16); VV = T("VV", [128, 8, 128], BF16)
        BcT = T("BcT", [128, 8, 64], BF16); KcT = T("KcT", [128, 8, 64], BF16)
        SC1 = T("SC1", [128, 8, 256], BF16); SC2 = T("SC2", [128, 8, 256], BF16)
        Pm = [T("Pm%d" % i, [128, 8, 128], BF16) for i in range(2)]
        PTm = [T("PTm%d" % i, [128, 8, 128], BF16) for i in range(2)]
        RhatT = T("RhatT", [128, 4, 128]); McT = T("McT", [128, 4, 2, 64]); H = T("H", [128, 2, 4, 64]); Nc = T("Nc", [128, 4, 2, 64])
        YT = T("YT", [128, 4, NT])
        yf = T("yf", [128, 4, NT], BF16); qT = T("qT", [128, 4, NT], BF16)
        KT2 = T("KT2", [128, 2, 2, 128 + NT], BF16); Vtm = T("Vtm", [128, 2, 1 + TL, 128], BF16)
        pT = T("pT", [128, 8, 2, 128], BF16); YA = T("YA", [128, 4, NT], BF16)
        cosT = T("cosT", [128, NT]); sinT = T("sinT", [128, NT])
        qraw = [T("qraw%d" % i, [128, NT], BF16) for i in range(2)]
        kf = T("kf", [128, 2, 128]); vf = T("vf", [128, 128])
        Gtmp = [T("Gtmp%d" % i, [128, NT], BF16) for i in range(2)]
        mixR = T("mixR", [128, 8, NT], BF16); mix = T("mix", [128, 8, NT], BF16)
        x1 = T("x1", [128, 8, NT])
        x1b = [T("x1b%d" % i, [128, NT], BF16) for i in range(2)]
        x1q = [T("x1q%d" % i, [128, NT], BF16) for i in range(2)]
        hT = T("hT", [128, 32, NT], BF16)
        ostage = T("ostage", [128, 256])
        tmask = T("tmask", [128, NT]); SHF = T("SHF", [128, 2, 14]); Snat = T("Snat", [64, 512]); ckd = T("ckd", [128, 2, 2, 64])
        S.dma(QM, tmask[:], tmask_d[:, :], w=['tmask'])

        S.op('pool', lambda e: e.memset(H[:], 0.0), w=[('H', 0), ('H', 1)])
        S.op('pool', lambda e: e.memset(CARRY[:], 0.0), w=[('CARRY', 0), ('CARRY', 1)])
        S.op('pool', lambda e: e.memset(KT2[:], 0.0), w=[('KT2', 0), ('KT2', 1)])
        S.op('pool', lambda e: e.memset(Vtm[:], 0.0), w=[('Vtm', 0), ('Vtm', 1)])
        S.op('pool', lambda e: e.memset(VV[:], 0.0), w=['VV'])

        def layernorm(l, ga, gb_, tag):
            s1 = ps[:, 0:NT]; s2 = ps[:, 512:512 + NT]
            for k in range(8):
                j = k % 2
                CP('act', x1b[j][:], x1[:, k, :], r=[('x1', k)], w=[('x1b', j)])
                ACT(x1q[j][:], x1[:, k, :], AF.Square, r=[('x1', k)], w=[('x1q', j)])
                MM(s1, onesb[:], x1b[j][:], start=(k == 0), stop=(k == 7), r=['onesb', ('x1b', j)], w=pk(0, NT))
                MM(s2, onesb[:], x1q[j][:], start=(k == 0), stop=(k == 7), r=['onesb', ('x1q', j)], w=pk(512, 512 + NT))
            mean, msq, var, rstd = Tm[0], Tm[1], Tm[2], Tm[3]
            ACT(mean[:], s1, AF.Copy, scale=1.0 / D, r=pk(0, NT), w=[('Tm', 0)])
            TT('pool', msq[:], mean[:], mean[:], ALU.mult, r=[('Tm', 0)], w=[('Tm', 1)])
            STT('dve', var[:], s2, 1.0 / D, msq[:], ALU.mult, ALU.subtract, r=pk(512, 512 + NT) + [('Tm', 1)], w=[('Tm', 2)])
            ACT(var[:], var[:], AF.Sqrt, bias=epsln[:, 0:1], r=[('Tm', 2), 'eps'], w=[('Tm', 2)])
            RCP(rstd[:], var[:], r=[('Tm', 2)], w=[('Tm', 3)])
            for k in range(8):
                d = Tm[4 + (k % 2)]
                TT('pool', d[:], x1[:, k, :], mean[:], ALU.subtract, r=[('x1', k), ('Tm', 0)], w=[('Tm', 4 + k % 2)])
                TT('pool', d[:], d[:], rstd[:], ALU.mult, r=[('Tm', 4 + k % 2), ('Tm', 3)], w=[('Tm', 4 + k % 2)])
                TS('dve', xT[:, k, :], d[:], P(l, ga + k, ga + k + 1), ALU.mult, P(l, gb_ + k, gb_ + k + 1), ALU.add,
                   r=[('Tm', 4 + k % 2), 'pv'], w=[('xT', k)])
                CP('act', xTb[:, k, :], xT[:, k, :], r=[('xT', k)], w=[('xTb', k)])

        epsln = T("epsln", [128, 2])
        S.op('pool', lambda e: e.memset(epsln[:, 0:1], LN_EPS), w=['eps'])
        S.op('pool', lambda e: e.memset(epsln[:, 1:2], GN_EPS), w=['eps'])

        for gi in range(NG + NSS):
            samp = gi >= NG
            g = gi if not samp else -1
            q = gi - NG
            t0g = g * NT
            if not samp:
                S.dma('pool', cosT[:], cos_d[:, t0g:t0g + NT], w=['cosT'])
                S.dma('pool', sinT[:], sin_d[:, t0g:t0g + NT], w=['sinT'])
            else:
                S.dma('pool', cosT[:], coss_d[:, :], w=['cosT'])
                S.dma('pool', sinT[:], sins_d[:, :], w=['sinT'])
            if upto < 1:
                S.disabled = True
            if not samp:
                S.dma('pool', xio[:], xp[t0g:t0g + NT, :].rearrange("(t p) d -> p t d", p=128), w=['xio'])
            else:
                S.op('pool', lambda e: e.memset(xio[:], 0.0), w=['xio'])
                S.dma('pool', xio[0:4, 0, :], xs[q], w=['xio'])
            for kp in range(4):
                reg = ps[:, kp * 512:(kp + 1) * 512]
                for kk in range(2):
                    k = kp * 2 + kk
                    for t in range(TL):
                        TR(ps[:, kp * 512 + kk * 256 + t * 128: kp * 512 + kk * 256 + (t + 1) * 128],
                           xio[:, t, k * 128:(k + 1) * 128], identf, r=['xio', 'cst'], w=pk(kp * 512, kp * 512 + 512))
                CP('act', xT[:, 2 * kp:2 * kp + 2, :], reg.rearrange("p (a b) -> p a b", a=2), r=pk(kp * 512, kp * 512 + 512),
                   w=[('xT', 2 * kp), ('xT', 2 * kp + 1)])
                CP('dve', xTb[:, 2 * kp:2 * kp + 2, :], reg.rearrange("p (a b) -> p a b", a=2), r=pk(kp * 512, kp * 512 + 512),
                   w=[('xTb', 2 * kp), ('xTb', 2 * kp + 1)])
            for l in range(2):
                last = (g == NG - 1)
                if samp:
                    S.dma(QM, Snat[:].rearrange("v (h k) -> v h k", k=64), swkv_i[l, q].rearrange("h v k -> v h k"), w=['Snat'])
                    for c in range(4):
                        TR(ps[:, c * 64:(c + 1) * 64], Snat[:, c * 128:(c + 1) * 128], identf[0:64, 0:64], r=['Snat', 'cst'], w=pk(0, 256))
                    CP('dve', H[:, l, :, :], ps[:, 0:256].rearrange("p (c j) -> p c j", j=64), r=pk(0, 256), w=[('H', l)])
                    S.dma(QM, CARRY[:, l, :], sshift_i[l, q].rearrange("(c p) -> p c", p=128), w=[('CARRY', l)], allow_slow_non_contiguous=True)
                    for dup in range(2):
                        S.dma(QM, ckd[:, :, dup, :], sck_i[l, q], w=['ckd'])
                    for kvh in range(2):
                        TR(ps[:, 512 + kvh * 128:512 + (kvh + 1) * 128], ckd[:, kvh, :, :].rearrange("p a b -> p (a b)"), identf, r=['ckd', 'cst'], w=pk(512, 1024))
                    CP('act', KT2[:, l, :, 0:128], ps[:, 512:768].rearrange("p (a b) -> p a b", b=128), r=pk(512, 1024), w=[('KT2', l)])
                    S.dma('pool', Vtm[:, l, 0, :], scv_i[l, q].rearrange("t h d -> t (h d)"), w=[('Vtm', l)])
                allx = [('xTb', k) for k in range(8)]
                ensure_conv('wi', l)
                for kvh in range(2):
                    for dup in range(2):
                        S.dma('pool', wk2[:, :, kvh, dup * 64:(dup + 1) * 64],
                              wi_b[l].rearrange("(k p) n -> p k n", p=128)[:, :, 2304 + kvh * 64: 2304 + (kvh + 1) * 64],
                              r=[('wi', l, k) for k in range(8)], w=['wk2'])
                if upto < 2:
                    S.disabled = True
                for blk in range(5):
                    W, wkey, wn = w_get('wi', l, blk)
                    for j in range(4):
                        c = blk * 4 + j
                        if c in (18, 19):
                            continue
                        po, pkey = bigslot()
                        for k in range(8):
                            MM(po, W[:, k, j * 128:(j + 1) * 128], xTb[:, k, :], start=(k == 0), stop=(k == 7),
                               r=[wkey, ('xTb', k)], w=pkey)
                        if c < 14:
                            z = Zc[c % 2]; zk = ('Zc', c % 2)
                            CP('act', z[:, 1:NT + 1], po, r=pkey, w=[zk])
                            CP('pool', z[:, 0:1], CARRY[:, l, c:c + 1], r=[('CARRY', l)], w=[zk])
                            tmp = Tm[c % 2]
                            TS('dve', tmp[:], z[:, 1:NT + 1], pd[:, l, c:c + 1], ALU.mult, r=[zk, ('pd', l)], w=[('Tm', c % 2)])
                            if c < 4:
                                dst, dk_ = rT[:, c, :], ('rT', c)
                            elif c < 8:
                                dst, dk_ = kraw[:, c - 4, :], ('kraw', c - 4)
                            elif c < 12:
                                dst, dk_ = vT[:, c - 8, :], ('vT', c - 8)
                            else:
                                dst, dk_ = Tm[2 + c % 2][:], ('Tm', 2 + c % 2)
                            STT('dve', dst, z[:, 0:NT], P(l, c, c + 1), tmp[:], ALU.mult, ALU.add,
                                r=[zk, 'pv', ('Tm', c % 2)], w=[dk_])
                            CP('pool', CARRY[:, l, c:c + 1], z[:, NT:NT + 1], r=[zk], w=[('CARRY', l)])
                            if samp:
                                CP('pool', SHF[:, l, c:c + 1], z[:, 4:5], r=[zk], w=[('SHF', l)])
                            if c == 12:
                                ACT(tw[0:64, :], dst[0:64, :], AF.Tanh, r=[dk_], w=['tw'])
                                CP('act', tw[64:128, :], dst[64:128, :], r=[dk_], w=['tw'])
                            if c == 13:
                                ACT(sgd[:], dst, AF.Sigmoid, r=[dk_], w=['sgd'])
                        else:
                            qi = c - 14
                            qr = qraw[qi % 2]; qk = ('qraw', qi % 2)
                            CP('act', qr[:], po, r=pkey, w=[qk])
                            p2, p2k = bigslot()
                            MM(p2, protb[:], qr[:], r=['protb', qk], w=p2k)
                            ta = Tm[4 + qi % 2]; tb_ = Tm[6 + qi % 2]
                            TT('dve', ta[:], p2, sinT[:], ALU.mult, r=p2k + ['sinT'], w=[('Tm', 4 + qi % 2)])
                            TT('pool', tb_[:], qr[:], cosT[:], ALU.mult, r=[qk, 'cosT'], w=[('Tm', 6 + qi % 2)])
                            TT('pool', qT[:, qi, :], ta[:], tb_[:], ALU.add, r=[('Tm', 4 + qi % 2), ('Tm', 6 + qi % 2)], w=[('qT', qi)])
                    if blk == 4:
                        for t in range(TL):
                            po, pkey = bigslot()
                            for k in range(8):
                                MM(po[:, 0:128], xTb[:, k, t * 128:(t + 1) * 128], W[:, k, 384:512], start=(k == 0), stop=(k == 7),
                                   r=[wkey, ('xTb', k)], w=pkey)
                            CP('act', Vtm[:, l, 1 + t, :], po[:, 0:128], r=pkey, w=[('Vtm', l)])
                            if (t == TL - 1 and last) or (samp and t == 0):
                                CP('dve', vf[:], po[:, 0:128], r=pkey, w=['vf'])
                    w_done(wn)
                for kvh in range(2):
                    po, pkey = bigslot()
                    for k in range(8):
                        MM(po, wk2[:, k, kvh, :], xTb[:, k, :], start=(k == 0), stop=(k == 7), r=['wk2', ('xTb', k)], w=pkey)
                    qr = qraw[kvh]; qk = ('qraw', kvh)
                    CP('act', qr[:], po, r=pkey, w=[qk])
                    p2, p2k = bigslot()
                    MM(p2, protb[:], qr[:], r=['protb', qk], w=p2k)
                    ta = Tm[4 + kvh]; tb_ = Tm[6 + kvh]
                    TT('dve', ta[:], p2, sinT[:], ALU.mult, r=p2k + ['sinT'], w=[('Tm', 4 + kvh)])
                    TT('pool', tb_[:], qr[:], cosT[:], ALU.mult, r=[qk, 'cosT'], w=[('Tm', 6 + kvh)])
                    TT('pool', KT2[:, l, kvh, 128:128 + NT], ta[:], tb_[:], ALU.add, r=[('Tm', 4 + kvh), ('Tm', 6 + kvh)], w=[('KT2', l)])
                    if last:
                        TT('pool', kf[:, kvh, :], ta[:, NT - 128:NT], tb_[:, NT - 128:NT], ALU.add,
                           r=[('Tm', 4 + kvh), ('Tm', 6 + kvh)], w=['kf'])
                    if samp:
                        TT('pool', kf[:, kvh, :], ta[:, 0:128], tb_[:, 0:128], ALU.add,
                           r=[('Tm', 4 + kvh), ('Tm', 6 + kvh)], w=['kf'])
                if upto < 3:
                    S.disabled = True
                for c in range(4):
                    sg, ic, kk_, t4, bT_, gs, E3, E1, rkk = Tm[0], Tm[1], Tm[2], Tm[3], Tm[4], Tm[5], Tm[6], Tm[7], Tm[8]
                    K = lambda i: ('Tm', i)
                    cs = slice(c * 128, (c + 1) * 128)
                    p1, p1k = bigslot()
                    MM(p1, lora[0:64, l, 0, cs], tw[0:64, :], r=[('lora', l), 'tw'], w=p1k)
                    ACT(sg[:], p1, AF.Sigmoid, bias=P(l, 34 + c, 35 + c), r=p1k + ['pv'], w=[K(0)])
                    if samp:
                        TT('pool', sg[:], sg[:], tmask[:], ALU.mult, r=[K(0), 'tmask'], w=[K(0)])
                    p2, p2k = bigslot()
                    MM(p2, lora[64:128, l, 0, cs], tw[64:128, :], r=[('lora', l), 'tw'], w=p2k)
                    ACT(ic[:], p2, AF.Sigmoid, bias=P(l, 38 + c, 39 + c), r=p2k + ['pv'], w=[K(1)])
                    p3, p3k = bigslot()
                    MM(p3, lora[:, l, 1, cs], sgd[:], r=[('lora', l), 'sgd'], w=p3k)
                    CP('act', gg[:, c, :], p3, r=p3k, w=[('gg', c)])
                    TS('dve', kk_[:], kraw[:, c, :], P(l, 14 + c, 15 + c), ALU.mult, r=[('kraw', c), 'pv'], w=[K(2)])
                    TT('pool', t4[:], kk_[:], kk_[:], ALU.mult, r=[K(2)], w=[K(3)])
                    p4, p4k = bigslot()
                    MM(p4, blockones, t4[:], r=['cst', K(3)], w=p4k)
                    TS('dve', t4[:], p4, 1e-24, ALU.max, r=p4k, w=[K(3)])
                    ACT(t4[:], t4[:], AF.Sqrt, r=[K(3)], w=[K(3)])
                    RCP(t4[:], t4[:], r=[K(3)], w=[K(3)])
                    TT('pool', kk_[:], kk_[:], t4[:], ALU.mult, r=[K(2), K(3)], w=[K(2)])
                    if samp:
                        TT('pool', kk_[:], kk_[:], tmask[:], ALU.mult, r=[K(2), 'tmask'], w=[K(2)])
                    TT('pool', bT_[:], kk_[:], ic[:], ALU.mult, r=[K(2), K(1)], w=[K(4)])
                    TS('dve', ic[:], ic[:], P(l, 18 + c, 19 + c), ALU.mult, pd[:, l, 14 + c:15 + c], ALU.add,
                       r=[K(1), 'pv', ('pd', l)], w=[K(1)])
                    TT('pool', ic[:], kraw[:, c, :], ic[:], ALU.mult, r=[('kraw', c), K(1)], w=[K(1)])
                    if samp:
                        TT('pool', ic[:], ic[:], tmask[:], ALU.mult, r=[K(1), 'tmask'], w=[K(1)])
                    S.op('dve', (lambda o_, d0, d1: (lambda e: e.tensor_tensor_scan(out=o_, data0=d0, data1=d1, initial=0.0,
                                                                                    op0=ALU.mult, op1=ALU.add)))(gs[:], resetm, sg[:]),
                         r=['cst', K(0)], w=[K(5)])
                    ACT(E3[:], gs[:], AF.Exp, scale=-DECAY_C, r=[K(5)], w=[K(6)])
                    ACT(gs[:], gs[:], AF.Exp, scale=DECAY_C, r=[K(5)], w=[K(5)])
                    ACT(sg[:], sg[:], AF.Exp, scale=DECAY_C, r=[K(0)], w=[K(0)])
                    nch = NT // 64
                    e3v = E3[:].rearrange("p (a b) -> p a b", b=64)
                    i3v = gs[:].rearrange("p (a b) -> p a b", b=64)
                    CP('pool', GC[:, c, :], e3v[:, :, 63], r=[K(6)], w=[('GC', c)])
                    TT('dve', E1[:].rearrange("p (a b) -> p a b", b=64), e3v, i3v[:, :, 63:64].broadcast_to([128, nch, 64]),
                       ALU.mult, r=[K(6), K(5)], w=[K(7)])
                    TT('dve', i3v, i3v, e3v[:, :, 63:64].broadcast_to([128, nch, 64]), ALU.mult,
                       r=[K(5), K(6)], w=[K(5)])
                    acv = ACRC[:, c, :, 0, :]
                    rcv = ACRC[:, c, :, 1, :]
                    v3 = lambda ap: ap.rearrange("p (a b) -> p a b", b=128)
                    TT('pool', rcv, v3(rT[:, c, :]), v3(E1[:]), ALU.mult, r=[('rT', c), K(7)], w=[('ACRC', c)])
                    TT('pool', rs[:, c, :], rT[:, c, :], E3[:], ALU.mult, r=[('rT', c), K(6)], w=[('rs', c)])
                    STT('dve', sg[:], kk_[:], -1.0, sg[:], ALU.mult, ALU.mult, r=[K(2), K(0)], w=[K(0)])
                    TT('dve', acv, v3(sg[:]), v3(E1[:]), ALU.mult, r=[K(0), K(7)], w=[('ACRC', c)])
                    TT('pool', asb[:, c, :], sg[:], E3[:], ALU.mult, r=[K(0), K(6)], w=[('asb', c)])
                    TT('pool', bcb[:, c, :], bT_[:], gs[:], ALU.mult, r=[K(4), K(5)], w=[('bcb', c)])
                    TT('dve', kcb[:, c, :], ic[:], gs[:], ALU.mult, r=[K(1), K(5)], w=[('kcb', c)])
                    CP('act', vb[:, c, :], vT[:, c, :], r=[('vT', c)], w=[('vb', c)])
                    TT('pool', rkk[:], rT[:, c, :], ic[:], ALU.mult, r=[('rT', c), K(1)], w=[K(8)])
                    TS('dve', rkk[:], rkk[:], P(l, 22 + c, 23 + c), ALU.mult, r=[K(8), 'pv'], w=[K(8)])
                    p5, p5k = bigslot()
                    MM(p5, blockones, rkk[:], r=['cst', K(8)], w=p5k)
                    TT('dve', bonus[:, c, :], p5, vT[:, c, :], ALU.mult, r=p5k + [('vT', c)], w=[('bonus', c)])
                dump("rT", rT[:], [('rT', c) for c in range(4)])
                dump("rs", rs[:], [('rs', c) for c in range(4)])
                dump("asb", asb[:], [('asb', c) for c in range(4)])
                dump("bcb", bcb[:], [('bcb', c) for c in range(4)])
                dump("kcb", kcb[:], [('kcb', c) for c in range(4)])
                dump("ACRC", ACRC[:].rearrange("p a b c d -> p (a b c d)"), [('ACRC', c) for c in range(4)])
                if upto < 3.05:
                    S.disabled = True
                allp = [('asb', c) for c in range(4)] + [('bcb', c) for c in range(4)] + [('kcb', c) for c in range(4)] + [('vb', c) for c in range(4)]
                for t in range(TL):
                    tsl = slice(t * 128, (t + 1) * 128)
                    for c in range(4):
                        MM(ps[:, c * 128:(c + 1) * 128], asb[:, c, tsl], identb[:], r=[('asb', c), 'identb'], w=pk(0, 512))
                        MM(ps[:, 512 + c * 128:512 + (c + 1) * 128], bcb[:, c, tsl], identb[:], r=[('bcb', c), 'identb'], w=pk(512, 1024))
                        MM(ps[:, 1024 + c * 128:1024 + (c + 1) * 128], kcb[:, c, tsl], identb[:], r=[('kcb', c), 'identb'], w=pk(1024, 1536))
                        MM(ps[:, 1536 + c * 128:1536 + (c + 1) * 128], vb[:, c, tsl], identb[:], r=[('vb', c), 'identb'], w=pk(1536, 2048))
                    h64 = lambda ap: ap.rearrange("p (h j) -> p h j", j=64)
                    CP('act', X[:, :, 0:64], h64(ps[:, 0:512]), r=pk(0, 512), w=['X'])
                    CP('dve', BcT[:], h64(ps[:, 512:1024]), r=pk(512, 1024), w=['BcT'])
                    CP('act', KcT[:], h64(ps[:, 1024:1536]), r=pk(1024, 1536), w=['KcT'])
                    CP('dve', VV[:, :, 64:128], h64(ps[:, 1536:2048]), r=pk(1536, 2048), w=['VV'])
                    if upto < 3.1:
                        S.disabled = True
                    m12 = bc_mid(mask12, 4)
                    for hg in range(2):
                        for hi in (0, 2, 1, 3):
                            h = hg * 4 + hi; c = h // 2; pb = 64 * (h % 2)
                            rhs2 = ACRC[pb:pb + 64, c, t, :, :].rearrange("p a b -> p (a b)")
                            MM(ps[:, hi * 256:(hi + 1) * 256], bcb[pb:pb + 64, c, tsl], rhs2, r=[('bcb', c), ('ACRC', c)], w=pk(hi * 256, hi * 256 + 256))
                            MM(ps[:, 1024 + hi * 256:1024 + (hi + 1) * 256], kcb[pb:pb + 64, c, tsl], rhs2, r=[('kcb', c), ('ACRC', c)],
                               w=pk(1024 + hi * 256, 1024 + hi * 256 + 256))
                        m12h = bc_mid(mask12, 2)
                        for bq in range(2):
                            TT('dve', SC1[:, hg * 4 + 2 * bq:hg * 4 + 2 * bq + 2, :], ps[:, bq * 512:(bq + 1) * 512].rearrange("p (h j) -> p h j", j=256), m12h, ALU.mult,
                               r=pk(bq * 512, bq * 512 + 512) + ['cst'], w=[('SC1', hg)])
                            TT('dve', SC2[:, hg * 4 + 2 * bq:hg * 4 + 2 * bq + 2, :], ps[:, 1024 + bq * 512:1024 + (bq + 1) * 512].rearrange("p (h j) -> p h j", j=256), m12h, ALU.mult,
                               r=pk(1024 + bq * 512, 1024 + bq * 512 + 512) + ['cst'], w=[('SC2', hg)])
                    if upto < 3.2:
                        S.disabled = True
                    sck = [('SC1', 0), ('SC1', 1)]; sck2 = [('SC2', 0), ('SC2', 1)]
                    for h in (0, 2, 4, 6, 1, 3, 5, 7):
                        c = h // 2; pb = 64 * (h % 2)
                        MM(ps[:, h * 128:(h + 1) * 128], ACRC[pb:pb + 64, c, t, 0, :], bcb[pb:pb + 64, c, tsl], r=[('ACRC', c), ('bcb', c)],
                           w=pk(h * 128, h * 128 + 128))
                    for bq in range(2):
                        TT('dve', Pm[0][:, 4 * bq:4 * bq + 4, :], ps[:, bq * 512:(bq + 1) * 512].rearrange("p (h j) -> p h j", j=128), bc_mid(maskT, 4), ALU.mult,
                           r=pk(bq * 512, bq * 512 + 512) + ['cst'], w=[('Pm', 0)])
                    CP('pool', PTm[0][:], SC1[:, :, 0:128], r=sck, w=[('PTm', 0)])
                    for h in range(8):
                        MM(ps[:, 1024 + h * 64:1024 + (h + 1) * 64], SC2[:, h, 0:128], VV[:, h, 64:128], r=sck2 + ['VV'],
                           w=pk(1024 + h * 64, 1024 + h * 64 + 64))
                    CP('act', X[:, :, 64:128], h64(ps[:, 1024:1536]), r=pk(1024, 1536), w=['X'])
                    if upto < 3.3:
                        S.disabled = True
                    for j in range(6):
                        a = j % 2; b = 1 - a
                        for h in range(8):
                            MM(ps[:, h * 128:(h + 1) * 128], PTm[a][:, h, :], X[:, h, :], r=[('PTm', a), 'X'], w=pk(h * 128, h * 128 + 128))
                        if j < 5:
                            for h in range(8):
                                MM(ps[:, 1024 + h * 128:1024 + (h + 1) * 128], PTm[a][:, h, :], Pm[a][:, h, :], r=[('PTm', a), ('Pm', a)],
                                   w=pk(1024 + h * 128, 1024 + h * 128 + 128))
                            for h in range(8):
                                MM(ps[:, 2048 + h * 128:2048 + (h + 1) * 128], Pm[a][:, h, :], PTm[a][:, h, :], r=[('PTm', a), ('Pm', a)],
                                   w=pk(2048 + h * 128, 2048 + h * 128 + 128))
                        for bq in range(2):
                            hs_ = slice(4 * bq, 4 * bq + 4)
                            TT('dve', X[:, hs_, :], ps[:, bq * 512:(bq + 1) * 512].rearrange("p (h j) -> p h j", j=128), X[:, hs_, :], ALU.add,
                               r=pk(bq * 512, bq * 512 + 512) + ['X'], w=['X'])
                            if j < 5:
                                CP('act', Pm[b][:, hs_, :], ps[:, 1024 + bq * 512:1024 + (bq + 1) * 512].rearrange("p (h j) -> p h j", j=128),
                                   r=pk(1024 + bq * 512, 1536 + bq * 512), w=[('Pm', b)])
                                CP('act' if bq == 0 else 'dve', PTm[b][:, hs_, :], ps[:, 2048 + bq * 512:2048 + (bq + 1) * 512].rearrange("p (h j) -> p h j", j=128),
                                   r=pk(2048 + bq * 512, 2560 + bq * 512), w=[('PTm', b)])
                    if upto < 3.4:
                        S.disabled = True
                    for h in range(8):
                        c = h // 2; pb = 64 * (h % 2)
                        MM(ps[pb:pb + 64, c * 128:(c + 1) * 128], X[:, h, 0:64], SC1[:, h, 128:256], r=['X'] + sck, w=pk(0, 512))
                    TT('dve', RhatT[:], ps[:, 0:512].rearrange("p (c j) -> p c j", j=128), rs[:, :, tsl], ALU.add,
                       r=pk(0, 512) + [('rs', c) for c in range(4)], w=['RhatT'])
                    if upto < 3.5:
                        S.disabled = True
                    mreg = [(512, 768), (1024, 1280)]; nreg = [(1536, 1792), (2048, 2304)]
                    for ch in range(2):
                        chs = slice(ch * 64, (ch + 1) * 64)
                        for h in range(8):
                            c = h // 2; pb = 64 * (h % 2)
                            MM(ps[pb:pb + 64, mreg[ch][0] + c * 64: mreg[ch][0] + (c + 1) * 64], X[chs, h, 0:64], BcT[chs, h, :],
                               r=['X', 'BcT'], w=pk(*mreg[ch]))
                            no = ps[pb:pb + 64, nreg[ch][0] + c * 64: nreg[ch][0] + (c + 1) * 64]
                            MM(no, BcT[chs, h, :], X[chs, h, 64:128], start=True, stop=False, r=['BcT', 'X'], w=pk(*nreg[ch]))
                            MM(no, KcT[chs, h, :], VV[chs, h, 64:128], start=False, stop=True, r=['KcT', 'VV'], w=pk(*nreg[ch]))
                    for ch in range(2):
                        for c in range(4):
                            STT('dve', McT[:, c, ch, :], I2, GC[:, c, t * 2 + ch: t * 2 + ch + 1],
                                ps[:, mreg[ch][0] + c * 64: mreg[ch][0] + (c + 1) * 64], ALU.mult, ALU.add,
                                r=['cst', ('GC', c)] + pk(*mreg[ch]), w=['McT'])
                        CP('act', Nc[:, :, ch, :], ps[:, nreg[ch][0]:nreg[ch][1]].rearrange("p (c j) -> p c j", j=64), r=pk(*nreg[ch]), w=['Nc'])
                    if upto < 3.6:
                        S.disabled = True
                    for ch in range(2):
                        if ch == 1 and upto < 3.95:
                            S.disabled = True
                        if ch == 0 and t == 1 and upto < 3.97:
                            S.disabled = True
                        chs = slice(ch * 64, (ch + 1) * 64)
                        yreg = (2560, 2816)
                        for h in range(8):
                            c = h // 2; pb = 64 * (h % 2)
                            yo = ps[pb:pb + 64, 2560 + c * 64:2560 + (c + 1) * 64]
                            MM(yo, X[:, h, 64:128], SC1[:, h, 128 + ch * 64:128 + (ch + 1) * 64], start=True, stop=False, r=['X'] + sck, w=pk(*yreg))
                            MM(yo, VV[:, h, 64:128], SC2[:, h, 128 + ch * 64:128 + (ch + 1) * 64], start=False, stop=False, r=['VV'] + sck2, w=pk(*yreg))
                            MM(yo, H[pb:pb + 64, l, c, :], RhatT[pb:pb + 64, c, chs], start=False, stop=True, r=[('H', l), 'RhatT'], w=pk(*yreg))
                        if upto < 3.7:
                            S.disabled = True
                        for h in (0, 2, 4, 6, 1, 3, 5, 7):
                            c = h // 2; pb = 64 * (h % 2)
                            hb = 0 if pb == 0 else 512
                            MM(ps[pb:pb + 64, hb + c * 64:hb + (c + 1) * 64], McT[pb:pb + 64, c, ch, :], H[pb:pb + 64, l, c, :],
                               r=['McT', ('H', l)], w=pk(hb, hb + 256))
                        if upto < 3.8:
                            S.disabled = True
                        CP('act', YT[:, :, t * 128 + ch * 64: t * 128 + (ch + 1) * 64], ps[:, 2560:2816].rearrange("p (c j) -> p c j", j=64),
                           r=pk(*yreg), w=[('YT', c) for c in range(4)])
                        if upto < 3.9:
                            S.disabled = True
                        TT('dve', H[0:64, l, :, :], ps[0:64, 0:256].rearrange("p (c j) -> p c j", j=64), Nc[0:64, :, ch, :], ALU.add,
                           r=pk(0, 256) + ['Nc'], w=[('H', l)])
                        TT('dve', H[64:128, l, :, :], ps[64:128, 512:768].rearrange("p (c j) -> p c j", j=64), Nc[64:128, :, ch, :], ALU.add,
                           r=pk(512, 768) + ['Nc'], w=[('H', l)])
                dump("YT", YT[:], [('YT', c) for c in range(4)])
                if upto < 5:
                    S.disabled = True
                for c in range(4):
                    K = lambda i: ('Tm', i)
                    d_, dq_, sd = Tm[0 + 3 * (c % 2)], Tm[1 + 3 * (c % 2)], Tm[2 + 3 * (c % 2)]
                    k0, k1, k2 = K(0 + 3 * (c % 2)), K(1 + 3 * (c % 2)), K(2 + 3 * (c % 2))
                    p1, p1k = bigslot()
                    MM(p1, blockones, YT[:, c, :], r=['cst', ('YT', c)], w=p1k)
                    STT('dve', d_[:], p1, -1.0 / 64, YT[:, c, :], ALU.mult, ALU.add, r=p1k + [('YT', c)], w=[k0])
                    TT('pool', dq_[:], d_[:], d_[:], ALU.mult, r=[k0], w=[k1])
                    p2, p2k = bigslot()
                    MM(p2, blockones, dq_[:], r=['cst', k1], w=p2k)
                    ACT(sd[:], p2, AF.Sqrt, bias=epsln[:, 1:2], scale=1.0 / 64, r=p2k + ['eps'], w=[k2])
                    RCP(sd[:], sd[:], r=[k2], w=[k2])
                    TT('pool', d_[:], d_[:], sd[:], ALU.mult, r=[k0, k2], w=[k0])
                    TS('dve', d_[:], d_[:], P(l, 26 + c, 27 + c), ALU.mult, P(l, 30 + c, 31 + c), ALU.add, r=[k0, 'pv'], w=[k0])
                    TT('pool', d_[:], d_[:], bonus[:, c, :], ALU.add, r=[k0, ('bonus', c)], w=[k0])
                    TT('pool', yf[:, c, :], d_[:], gg[:, c, :], ALU.mult, r=[k0, ('gg', c)], w=[('yf', c)])
                dump("yf", yf[:], [('yf', c) for c in range(4)])
                if upto < 6:
                    S.disabled = True
                for t in range(TL):
                    tsl = slice(t * 128, (t + 1) * 128)
                    first_tile = (g == 0 and t == 0 and not samp)
                    for h in (0, 2, 4, 6, 1, 3, 5, 7):
                        c = h // 2; pb = 64 * (h % 2); kvh = h // 4
                        MM(ps[:, h * 256:h * 256 + 128], KT2[pb:pb + 64, l, kvh, t * 128:(t + 1) * 128], qT[pb:pb + 64, c, tsl],
                           r=[('KT2', l), ('qT', c)], w=pk(h * 256, h * 256 + 128))
                        MM(ps[:, h * 256 + 128:h * 256 + 256], KT2[pb:pb + 64, l, kvh, (t + 1) * 128:(t + 2) * 128], qT[pb:pb + 64, c, tsl],
                           r=[('KT2', l), ('qT', c)], w=pk(h * 256 + 128, h * 256 + 256))
                    for q4 in range(4):
                        ACT(pT[:, q4 * 2:q4 * 2 + 2, :, :].rearrange("p a b c -> p (a b c)"), ps[:, q4 * 512:(q4 + 1) * 512], AF.Exp, scale=0.125,
                            r=pk(q4 * 512, q4 * 512 + 512), w=[('pT', q4)])
                    mm_ = (mAtt0 if first_tile else mAtt)
                    ptk = [('pT', q4) for q4 in range(4)]
                    TT('pool', pT[:].rearrange("p a b c -> p a (b c)"), pT[:].rearrange("p a b c -> p a (b c)"), bc_mid(mm_, 8), ALU.mult,
                       r=ptk + ['cst'], w=ptk)
                    for h in range(8):
                        c = h // 2; pb = 64 * (h % 2); kvh = h // 4
                        oo = ps[pb:pb + 64, c * 128:(c + 1) * 128]
                        MM(oo, Vtm[:, l, t, kvh * 64:(kvh + 1) * 64], pT[:, h, 0, :], start=True, stop=False, r=[('Vtm', l)] + ptk, w=pk(0, 512))
                        MM(oo, Vtm[:, l, t + 1, kvh * 64:(kvh + 1) * 64], pT[:, h, 1, :], start=False, stop=True, r=[('Vtm', l)] + ptk, w=pk(0, 512))
                        do = ps[pb:pb + 64, 512 + c * 128:512 + (c + 1) * 128]
                        MM(do, onesb[:, 0:64], pT[:, h, 0, :], start=True, stop=False, r=['onesb'] + ptk, w=pk(512, 1024))
                        MM(do, onesb[:, 0:64], pT[:, h, 1, :], start=False, stop=True, r=['onesb'] + ptk, w=pk(512, 1024))
                    den = Tm[0]; den2 = Tm[1]
                    dv = lambda tl: tl[:].rearrange("p (a b) -> p a b", b=128)
                    for c in range(4):
                        tgt = (den if c < 2 else den2)[:, (c % 2) * 128:(c % 2 + 1) * 128]
                        TS('dve', tgt, ps[:, 512 + c * 128:512 + (c + 1) * 128], pd[:, l, 18 + c:19 + c], ALU.add,
                           r=pk(512, 1024) + [('pd', l)], w=[('Tm', 0 if c < 2 else 1)])
                    RCP(den[:], den[:], r=[('Tm', 0)], w=[('Tm', 0)])
                    RCP(den2[:], den2[:], r=[('Tm', 1)], w=[('Tm', 1)])
                    TT('dve', YA[:, 0:2, tsl], ps[:, 0:256].rearrange("p (a b) -> p a b", b=128), dv(den), ALU.mult,
                       r=pk(0, 512) + [('Tm', 0)], w=[('YA', 0), ('YA', 1)])
                    TT('dve', YA[:, 2:4, tsl], ps[:, 256:512].rearrange("p (a b) -> p a b", b=128), dv(den2), ALU.mult,
                       r=pk(0, 512) + [('Tm', 1)], w=[('YA', 2), ('YA', 3)])
                dump("YA", YA[:], [('YA', c) for c in range(4)])
                if upto < 7:
                    S.disabled = True
                for br in range(2):
                    Wb, wbk, wbn = w_get('wbr' if br == 0 else 'wba', l, 0)
                    wgn = None
                    src_act = yf if br == 0 else YA
                    sk = 'yf' if br == 0 else 'YA'
                    for m in range(8):
                        if m % 4 == 0:
                            if wgn is not None:
                                w_done(wgn)
                            Wg, wgk, wgn = w_get('wi', l, 5 + 2 * br + m // 4)
                        po, pkey = bigslot()
                        for k in range(8):
                            MM(po, Wg[:, k, (m % 4) * 128:(m % 4 + 1) * 128], xTb[:, k, :], start=(k == 0), stop=(k == 7), r=[wgk, ('xTb', k)], w=pkey)
                        gt = Gtmp[m % 2]; gk = ('Gtmp', m % 2)
                        ACT(gt[:], po, AF.Sigmoid, r=pkey, w=[gk])
                        p2, p2k = bigslot()
                        for c in range(4):
                            MM(p2, Wb[:, c, m * 128:(m + 1) * 128], src_act[:, c, :], start=(c == 0), stop=(c == 3), r=[wbk, (sk, c)], w=p2k)
                        if br == 0:
                            TT('dve', mixR[:, m, :], p2, gt[:], ALU.mult, r=p2k + [gk], w=[('mixR', m)])
                        else:
                            tm_ = Tm[m % 2]
                            TT('dve', tm_[:], p2, gt[:], ALU.mult, r=p2k + [gk], w=[('Tm', m % 2)])
                            TT('pool', mix[:, m, :], tm_[:], mixR[:, m, :], ALU.add, r=[('Tm', m % 2), ('mixR', m)], w=[('mix', m)])
                    w_done(wgn)
                    w_done(wbn)
                for m in range(8):
                    if m % 4 == 0:
                        if m > 0:
                            w_done(won)
                        Wo, wok, won = w_get('wo', l, m // 4)
                    po, pkey = bigslot()
                    for k in range(8):
                        MM(po, Wo[:, k, (m % 4) * 128:(m % 4 + 1) * 128], mix[:, k, :], start=(k == 0), stop=(k == 7), r=[wok, ('mix', k)], w=pkey)
                    STT('dve', x1[:, m, :], xT[:, m, :], ALPHA, po, ALU.mult, ALU.add, r=[('xT', m)] + pkey, w=[('x1', m)])
                dump("mix", mix[:], [('mix', k) for k in range(8)])
                dump("x1pre", x1[:], [('x1', k) for k in range(8)])
                w_done(won)
                layernorm(l, 42, 50, 'ln1')
                dump("xln1", xT[:], [('xT', k) for k in range(8)])
                if upto < 8:
                    S.disabled = True
                for fb in range(8):
                    Wu, wuk, wun = w_get('wu', l, fb)
                    for j in range(4):
                        f = fb * 4 + j
                        po, pkey = bigslot()
                        for k in range(8):
                            MM(po, Wu[:, k, j * 128:(j + 1) * 128], xTb[:, k, :], start=(k == 0), stop=(k == 7), r=[wuk, ('xTb', k)], w=pkey)
                        gt = Gtmp[f % 2]; gk = ('Gtmp', f % 2)
                        ACT(gt[:], po, AF.Relu, r=pkey, w=[gk])
                        TT('pool', hT[:, f, :], gt[:], gt[:], ALU.mult, r=[gk], w=[('hT', f)])
                    w_done(wun)
                for m in range(8):
                    Wd, wdk, wdn = w_get('wd', l, m)
                    po, pkey = bigslot()
                    for f in range(32):
                        MM(po, Wd[:, f, :], hT[:, f, :], start=(f == 0), stop=(f == 31), r=[wdk, ('hT', f)], w=pkey)
                    STT('dve', x1[:, m, :], xT[:, m, :], ALPHA, po, ALU.mult, ALU.add, r=[('xT', m)] + pkey, w=[('x1', m)])
                    w_done(wdn)
                layernorm(l, 58, 66, 'ln2')
                if upto < 9:
                    S.disabled = True
                CP('pool', KT2[:, l, :, 0:128], KT2[:, l, :, NT:NT + 128], r=[('KT2', l)], w=[('KT2', l)])
                CP('pool', Vtm[:, l, 0, :], Vtm[:, l, TL, :], r=[('Vtm', l)], w=[('Vtm', l)])
                if samp:
                    for c in range(4):
                        TR(ps[0:64, c * 128:(c + 1) * 128], H[:, l, c, :], identf, r=[('H', l), 'cst'], w=pk(0, 512))
                    CP('act', ostage[0:64, 0:256], ps[0:64, 0:256], r=pk(0, 512), w=['ostage'])
                    S.dma(QM, so_wkv[l, q, 0:4].rearrange("h v k -> v h k"), ostage[0:64, 0:256].rearrange("p (h k) -> p h k", k=64), r=['ostage'])
                    CP('act', ostage[0:64, 0:256], ps[0:64, 256:512], r=pk(0, 512), w=['ostage'])
                    S.dma(QM, so_wkv[l, q, 4:8].rearrange("h v k -> v h k"), ostage[0:64, 0:256].rearrange("p (h k) -> p h k", k=64), r=['ostage'])
                    S.dma(QM, so_shift[l, q].rearrange("(c p) -> p c", p=128), SHF[:, l, :], r=[('SHF', l)], allow_slow_non_contiguous=True)
                    for kvh in range(2):
                        TR(ps[:, 512 + kvh * 64:512 + (kvh + 1) * 64], kf[0:64, kvh, :], identf[0:64, 0:64], r=['kf', 'cst'], w=pk(512, 1024))
                    CP('act', ostage[:, 0:128], ps[:, 512:640], r=pk(512, 1024), w=['ostage'])
                    S.dma(QM, so_ck[l, q, 124:128].rearrange("t h d -> t (h d)"), ostage[0:4, 0:128], r=['ostage'])
                    S.dma(QM, so_cv[l, q, 124:128].rearrange("t h d -> t (h d)"), vf[0:4, :], r=['vf'])
                    S.dma(QM, so_ck[l, q, 0:124].rearrange("t h d -> t (h d)"), sck_i[l, q, 4:128].rearrange("t h d -> t (h d)"))
                    S.dma(QM, so_cv[l, q, 0:124].rearrange("t h d -> t (h d)"), scv_i[l, q, 4:128].rearrange("t h d -> t (h d)"))
                if last:
                    for c in range(4):
                        TR(ps[0:64, c * 128:(c + 1) * 128], H[:, l, c, :], identf, r=[('H', l), 'cst'], w=pk(0, 512))
                    CP('act', ostage[0:64, 0:512 // 2 * 0 + 256], ps[0:64, 0:256], r=pk(0, 512), w=['ostage'])
                    S.dma(QM, o_wkv[l, 0:4].rearrange("h v k -> v h k"), ostage[0:64, 0:256].rearrange("p (h k) -> p h k", k=64), r=['ostage'])
                    CP('act', ostage[0:64, 0:256], ps[0:64, 256:512], r=pk(0, 512), w=['ostage'])
                    S.dma(QM, o_wkv[l, 4:8].rearrange("h v k -> v h k"), ostage[0:64, 0:256].rearrange("p (h k) -> p h k", k=64), r=['ostage'])
                    S.dma(QM, o_shift[l].rearrange("(c p) -> p c", p=128), CARRY[:, l, :], r=[('CARRY', l)], allow_slow_non_contiguous=True)
                    for kvh in range(2):
                        TR(ps[:, 512 + kvh * 64:512 + (kvh + 1) * 64], kf[0:64, kvh, :], identf[0:64, 0:64], r=['kf', 'cst'], w=pk(512, 1024))
                    CP('act', ostage[:, 0:128], ps[:, 512:640], r=pk(512, 1024), w=['ostage'])
                    S.dma(QM, o_ck[l].rearrange("t h d -> t (h d)"), ostage[:, 0:128], r=['ostage'])
                    S.dma(QM, o_cv[l].rearrange("t h d -> t (h d)"), vf[:], r=['vf'])
            for t in range(TL):
                for kp in range(2):
                    for kk in range(4):
                        k = kp * 4 + kk
                        TR(ps[:, kp * 512 + kk * 128: kp * 512 + (kk + 1) * 128], xT[:, k, t * 128:(t + 1) * 128], identf,
                           r=[('xT', k), 'cst'], w=pk(kp * 512, kp * 512 + 512))
                    CP('act' if kp == 0 else 'dve', xio[:, t, kp * 512:(kp + 1) * 512], ps[:, kp * 512:(kp + 1) * 512],
                       r=pk(kp * 512, kp * 512 + 512), w=['xio'])
            if not samp:
                S.dma(QM, y_p[t0g:t0g + NT, :].rearrange("(t p) d -> p t d", p=128), xio[:], r=['xio'])
            else:
                S.dma(QM, y_s[q], xio[0:4, 0, :], r=['xio'])
        stats = S.emit()
    return nc, stats


def pack_pv(inp):
    pv = np.zeros((128, 2 * PVL), np.float32)
    for l in range(2):
        b = l * PVL
        pv[:, b:b + 14] = inp['mu_shift'][l].reshape(14, 128).T
        for off, name in ((14, 'k_k'), (18, 'k_a'), (26, 'lnx_g'), (30, 'lnx_b'), (34, 'decay_base'), (38, 'iclr_base')):
            pv[:, b + off:b + off + 4] = inp[name][l].reshape(4, 128).T
        pv[:, b + 22:b + 26] = inp['r_k'][l].reshape(512).reshape(4, 128).T
        for off, name in ((42, 'ln1_g'), (50, 'ln1_b'), (58, 'ln2_g'), (66, 'ln2_b')):
            pv[:, b + off:b + off + 8] = inp[name][l].reshape(8, 128).T
        pv[:, b + 74:b + 78] = np.repeat(inp['sinks'][l].reshape(4, 2), 64, axis=1).T
    return pv


_CACHE = {}


def host_inputs(inp, c, SEQ, NSS, consts):
    cst, cosT, sinT, coss, sins, tmask, pv = consts
    wnames = ['w_in', 'w_br_rwkv', 'w_br_attn', 'w_out', 'w_ff_up', 'w_ff_down', 'decay_up', 'iclr_up', 'gate_up']
    m = {k: inp[k] for k in wnames}
    sl = slice(c * NSS, (c + 1) * NSS)
    m.update(xp=inp['x_prompt'][(c * 2) // 8][:SEQ], pv_in=pv, cst_in=cst, cos_in=cosT, sin_in=sinT, coss_in=coss, sins_in=sins, tmask_in=tmask,
             xs=np.ascontiguousarray(inp['x_sample'][sl]), swkv_i=np.ascontiguousarray(inp['state_wkv'][:, sl]),
             sshift_i=np.ascontiguousarray(inp['state_shift'][:, sl]), sck_i=np.ascontiguousarray(inp['cache_k_win'][:, sl]),
             scv_i=np.ascontiguousarray(inp['cache_v_win'][:, sl]))
    return m


def host_consts(inp, SEQ, past_len=8192):
    cst = make_consts()
    cosT, sinT = rope_tables(np.arange(SEQ))
    coss, sins = rope_tables(past_len + np.arange(NT))
    tmask = np.zeros((128, NT), np.float32)
    tmask[:, 0:4] = 1.0
    return cst, cosT, sinT, coss, sins, tmask, pack_pv(inp)


def kernel(**inputs):
    inp = {k: np.ascontiguousarray(np.asarray(v)) for k, v in inputs.items()}
    B, SEQ, _ = inp['x_prompt'].shape
    NSS = inp['x_sample'].shape[0] // 8
    if SEQ not in _CACHE:
        _CACHE[SEQ] = build(SEQ, NSS)
    nc, _ = _CACHE[SEQ]
    consts = host_consts(inp, SEQ)
    in_maps = [host_inputs(inp, c, SEQ, NSS, consts) for c in range(8)]
    res = run_bass_kernel_spmd(nc, in_maps, core_ids=list(range(8))).results
    y_p = np.stack([res[0]['y_p'], res[4]['y_p']])
    p_wkv = np.stack([res[0]['p_wkv'], res[4]['p_wkv']], 1)
    p_shift = np.stack([res[0]['p_shift'], res[4]['p_shift']], 1)
    p_ck = np.stack([res[0]['p_ck'], res[4]['p_ck']], 1)
    p_cv = np.stack([res[0]['p_cv'], res[4]['p_cv']], 1)
    y_s = np.concatenate([res[c]['y_s'] for c in range(8)], 0)
    s_wkv = np.concatenate([res[c]['s_wkv'] for c in range(8)], 1)
    s_shift = np.concatenate([res[c]['s_shift'] for c in range(8)], 1)
    s_ck = np.concatenate([res[c]['s_ck'] for c in range(8)], 1)
    s_cv = np.concatenate([res[c]['s_cv'] for c in range(8)], 1)
    return (y_p, y_s, p_wkv, p_shift, p_ck, p_cv, s_wkv, s_shift, s_ck, s_cv)
```

```python
import numpy as np
from contextlib import ExitStack
from itertools import zip_longest
import concourse.bass as bass
import concourse.mybir as mybir
from concourse.ap import AP
from concourse.bass_utils import run_bass_kernel_spmd

F32 = mybir.dt.float32
BF16 = mybir.dt.bfloat16
AF = mybir.ActivationFunctionType
ALU = mybir.AluOpType

D = 1024
NT = 256
TL = NT // 128
SHIFT_W = 1792
IN_W = 4608
DFF = 4096
ALPHA = 4 ** 0.25
LN_EPS = 1e-5
GN_EPS = 64e-5
DECAY_C = 0.6065306597126334
PVL = 78
QM = 'pool'
NCST = 1792


class Sched:
    COMPUTE = ('pe', 'act', 'dve', 'pool')

    def __init__(self, nc, es, n_dma_sems=24):
        self.nc = nc
        self.h = {'pe': nc.tensor, 'act': nc.scalar, 'dve': nc.vector, 'pool': nc.gpsimd, 'sp': nc.sync}
        self.ops = []
        self.last_w = {}
        self.readers = {}
        self.sem = {e: es.enter_context(nc.semaphore("s_" + e)) for e in self.COMPUTE}
        self.dsem = []
        self.dq = {}
        for q, n in (('sp', n_dma_sems), ('act', 8), ('pool', 12)):
            self.dq[q] = list(range(len(self.dsem), len(self.dsem) + n))
            self.dsem += [es.enter_context(nc.semaphore("d%s%d" % (q, i))) for i in range(n)]

    def _add(self, kind, eng, fn, r, w):
        isps = lambda k: isinstance(k, tuple) and k[0] in ('ps', 'pst')
        w = list(w) + [k for k in r if isps(k)]
        r = [k for k in r if not isps(k)]
        oid = len(self.ops)
        deps = set()
        for k in r:
            if k in self.last_w:
                deps.add(self.last_w[k])
        for k in w:
            if k in self.last_w:
                deps.add(self.last_w[k])
            deps |= self.readers.get(k, set())
        for k in r:
            self.readers.setdefault(k, set()).add(oid)
        for k in w:
            self.last_w[k] = oid
            self.readers[k] = set()
        deps.discard(oid)
        self.ops.append(dict(kind=kind, eng=eng, fn=fn, deps=deps))
        return oid

    disabled = False

    def op(self, eng, fn, r=(), w=()):
        if self.disabled:
            return None
        return self._add('c', eng, fn, r, w)

    def dma(self, eng, out, in_, r=(), w=(), **kw):
        if self.disabled:
            return None
        return self._add('d', eng, (out, in_, kw), r, w)

    def emit(self):
        ops = self.ops
        need = [False] * len(ops)
        for i, o in enumerate(ops):
            for d in o['deps']:
                p = ops[d]
                if p['kind'] == 'c':
                    if p['eng'] == o['eng'] and o['kind'] == 'c' and p['eng'] == 'pe':
                        continue
                    need[d] = True
        cnt = {e: 0 for e in self.COMPUTE}
        tok = [None] * len(ops)
        seen = {}
        dcount = [0] * len(self.dsem)
        dk = {q: 0 for q in self.dq}
        nwaits = 0
        acts = {e: [] for e in self.h}
        for i, o in enumerate(ops):
            e = o['eng']
            wl = {}
            for d in o['deps']:
                p = ops[d]
                if p['kind'] == 'c' and p['eng'] == e and o['kind'] == 'c' and e == 'pe':
                    continue
                t = tok[d]
                if t is None:
                    continue
                ts, tv = t
                if tv > wl.get(id(ts), (ts, 0))[1]:
                    wl[id(ts)] = (ts, tv)
            if o['kind'] == 'd':
                j = self.dq[e][dk[e] % len(self.dq[e])]
                dk[e] += 1
                dsj = self.dsem[j]
                if dcount[j] > 0 and dcount[j] > wl.get(id(dsj), (dsj, 0))[1]:
                    wl[id(dsj)] = (dsj, dcount[j])
            for ws, wv in wl.values():
                key = (e, id(ws))
                if seen.get(key, 0) >= wv:
                    continue
                acts[e].append((lambda s_, v_: (lambda h: h.wait_ge(s_, v_)))(ws, wv))
                nwaits += 1
                seen[key] = wv
            if o['kind'] == 'c':
                if need[i]:
                    cnt[e] += 1
                    acts[e].append((lambda fn_, sm_: (lambda h: fn_(h).then_inc(sm_, 1)))(o['fn'], self.sem[e]))
                    tok[i] = (self.sem[e], cnt[e])
                else:
                    acts[e].append(o['fn'])
            else:
                out, in_, kw = o['fn']
                dcount[j] += 16
                acts[e].append((lambda o_, i_, k_, s_: (lambda h: h.dma_start(out=o_, in_=i_, **k_).then_inc(s_, 16)))(out, in_, kw, dsj))
                tok[i] = (dsj, dcount[j])
        for j, fs in enumerate(self.dsem):
            if dcount[j] > 0:
                acts['sp'].append((lambda s_, v_: (lambda h: h.wait_ge(s_, v_)))(fs, dcount[j]))
        with self.nc.Block() as block:
            @block.sync
            def _(h):
                for a in acts['sp']:
                    a(h)

            @block.tensor
            def _(h):
                for a in acts['pe']:
                    a(h)

            @block.scalar
            def _(h):
                for a in acts['act']:
                    a(h)

            @block.vector
            def _(h):
                for a in acts['dve']:
                    a(h)

            @block.gpsimd
            def _(h):
                for a in acts['pool']:
                    a(h)
        return dict(n_ops=len(ops), n_waits=nwaits, signals=dict(cnt))


def make_consts():
    c = np.zeros((128, NCST), np.float32)
    idx = np.arange(128)
    c[:, 0:128] = np.eye(128)
    c[:, 128:256] = (idx[:, None] // 64 == idx[None, :] // 64)
    same = (idx[:, None] // 64) == (idx[None, :] // 64)
    mstrict = same & (idx[:, None] < idx[None, :])
    mincl = same & (idx[:, None] <= idx[None, :])
    c[:, 256:384] = mstrict.T
    c[:, 384:512] = mstrict
    c[:, 512:640] = mincl
    c[:, 640:768] = idx[:, None] >= idx[None, :]
    c[:, 768:896] = idx[:, None] <= idx[None, :]
    c[:, 896:1024] = 0.0
    c[:, 1024:1152] = idx[:, None] <= idx[None, :]
    prot = np.zeros((128, 128), np.float32)
    for hb in (0, 64):
        for dd in range(32):
            prot[hb + dd + 32, hb + dd] = -1.0
            prot[hb + dd, hb + dd + 32] = 1.0
    c[:, 1152:1280] = prot
    rm = np.ones((128, 256), np.float32)
    rm[:, 0::64] = 0.0
    c[:, 1280:1536] = rm
    c[:, 1536:1664] = 1.0
    c[:, 1664:1728] = (idx[:, None] % 64) == np.arange(64)[None, :]
    return c


def rope_tables(pos):
    half = 32
    inv = (10000.0 ** (-np.arange(half, dtype=np.float32) / half)).astype(np.float32)
    ang = pos.astype(np.float32)[None, :] * inv[:, None]
    cos = np.cos(ang).astype(np.float32)
    sin = np.sin(ang).astype(np.float32)
    return np.tile(cos, (4, 1)), np.tile(sin, (4, 1))


def build(SEQ, NSS=16, dbg=(), upto=99, noconv=False):
    NG = SEQ // NT
    NTS = NSS * 4
    nc = bass.Bass("TRN2", target_bir_lowering=False)
    din = lambda name, shape, dt=F32: nc.dram_tensor(name, list(shape), dt, kind="ExternalInput").ap()
    dout = lambda name, shape, dt=F32: nc.dram_tensor(name, list(shape), dt, kind="ExternalOutput").ap()
    dscr = lambda name, shape, dt=BF16: nc.dram_tensor(name, list(shape), dt).ap()

    xp = din("xp", [SEQ, D])
    w_in = din("w_in", [2, D, IN_W]); w_brr = din("w_br_rwkv", [2, 512, D]); w_bra = din("w_br_attn", [2, 512, D])
    w_out = din("w_out", [2, D, D]); w_up = din("w_ff_up", [2, D, DFF]); w_dn = din("w_ff_down", [2, DFF, D])
    d_up = din("decay_up", [2, 64, 512]); i_up = din("iclr_up", [2, 64, 512]); g_up = din("gate_up", [2, 128, 512])
    pv_d = din("pv_in", [128, 2 * PVL]); cst_d = din("cst_in", [128, NCST])
    cos_d = din("cos_in", [128, SEQ]); sin_d = din("sin_in", [128, SEQ])

    xs = din("xs", [NSS, 4, D]); swkv_i = din("swkv_i", [2, NSS, 8, 64, 64]); sshift_i = din("sshift_i", [2, NSS, SHIFT_W])
    sck_i = din("sck_i", [2, NSS, 128, 2, 64]); scv_i = din("scv_i", [2, NSS, 128, 2, 64])
    coss_d = din("coss_in", [128, NT]); sins_d = din("sins_in", [128, NT]); tmask_d = din("tmask_in", [128, NT])
    y_s = dout("y_s", [NSS, 4, D]); so_wkv = dout("s_wkv", [2, NSS, 8, 64, 64]); so_shift = dout("s_shift", [2, NSS, SHIFT_W])
    so_ck = dout("s_ck", [2, NSS, 128, 2, 64]); so_cv = dout("s_cv", [2, NSS, 128, 2, 64])
    y_p = dout("y_p", [SEQ, D]); o_wkv = dout("p_wkv", [2, 8, 64, 64]); o_shift = dout("p_shift", [2, SHIFT_W])
    o_ck = dout("p_ck", [2, 128, 2, 64]); o_cv = dout("p_cv", [2, 128, 2, 64])
    dbg_out = {}

    wi_b = dscr("wi_b", [2, D, IN_W]); wbr_b = dscr("wbr_b", [2, 512, D]); wba_b = dscr("wba_b", [2, 512, D])
    wo_b = dscr("wo_b", [2, D, D]); wu_b = dscr("wu_b", [2, D, DFF]); wd_b = dscr("wd_b", [2, DFF, D])

    with ExitStack() as es:
        S = Sched(nc, es)
        T = lambda name, shape, dt=F32: es.enter_context(nc.sbuf_tensor(name, list(shape), dt))
        def MM(out, lhsT, rhs, start=True, stop=True, r=(), w=()):
            S.op('pe', lambda e: e.matmul(out, lhsT=lhsT, rhs=rhs, start=start, stop=stop), r=r, w=w)

        def TR(out, in_, ident, r=(), w=()):
            S.op('pe', lambda e: e.transpose(out, in_, ident), r=r, w=w)

        def TT(eng, out, in0, in1, op, r=(), w=()):
            S.op(eng, lambda e: e.tensor_tensor(out=out, in0=in0, in1=in1, op=op), r=r, w=w)

        def TS(eng, out, in0, s1, op0, s2=None, op1=None, r=(), w=()):
            if op1 is None:
                S.op(eng, lambda e: e.tensor_scalar(out=out, in0=in0, scalar1=s1, scalar2=None, op0=op0), r=r, w=w)
            else:
                S.op(eng, lambda e: e.tensor_scalar(out=out, in0=in0, scalar1=s1, scalar2=s2, op0=op0, op1=op1), r=r, w=w)

        def STT(eng, out, in0, scalar, in1, op0, op1, r=(), w=()):
            S.op(eng, lambda e: e.scalar_tensor_tensor(out=out, in0=in0, scalar=scalar, in1=in1, op0=op0, op1=op1), r=r, w=w)

        def ACT(out, in_, func, bias=None, scale=1.0, r=(), w=()):
            if bias is None:
                S.op('act', lambda e: e.activation(out=out, in_=in_, func=func, scale=scale), r=r, w=w)
            else:
                S.op('act', lambda e: e.activation(out=out, in_=in_, func=func, bias=bias, scale=scale), r=r, w=w)

        def CP(eng, out, in_, r=(), w=()):
            if eng == 'act':
                S.op('act', lambda e: e.copy(out=out, in_=in_), r=r, w=w)
            else:
                S.op(eng, lambda e: e.tensor_copy(out=out, in_=in_), r=r, w=w)

        def RCP(out, in_, r=(), w=()):
            S.op('dve', lambda e: e.reciprocal(out=out, in_=in_), r=r, w=w)

        def bc_mid(ap2d, n):
            return ap2d.unsqueeze(1).broadcast_to([ap2d.shape[0], n, ap2d.shape[1]])

        def dump(name, ap, keys, shape=None):
            if name not in dbg or name in dbg_out:
                return
            dbg_out[name] = 1
            shp = list(ap.shape)
            o = dout("dbg_" + name, shp, ap.dtype)
            full = o if len(shp) == 2 else o
            S.dma(QM, o[tuple(slice(None) for _ in shp)], ap, r=keys)

        cst = T("cst", [128, NCST])
        S.dma(QM, cst[:], cst_d[:, :], w=['cst'])
        identf = cst[:, 0:128]; blockones = cst[:, 128:256]; maskT = cst[:, 256:384]; mask12 = cst[:, 384:640]
        mAtt = cst[:, 640:896]; mAtt0 = cst[:, 896:1152]; resetm = cst[:, 1280:1536]; I2 = cst[:, 1664:1728]
        identb = T("identb", [128, 128], BF16); protb = T("protb", [128, 128], BF16); onesb = T("onesb", [128, 128], BF16)
        CP('pool', identb[:], cst[:, 0:128], r=['cst'], w=['identb'])
        CP('pool', protb[:], cst[:, 1152:1280], r=['cst'], w=['protb'])
        CP('pool', onesb[:], cst[:, 1536:1664], r=['cst'], w=['onesb'])
        pv = T("pv", [128, 2 * PVL])
        S.dma(QM, pv[:], pv_d[:, :], w=['pv'])
        pd = T("pd", [128, 2, 24])
        for l in range(2):
            b0 = l * PVL
            TS('dve', pd[:, l, 0:14], pv[:, b0:b0 + 14], -1.0, ALU.mult, 1.0, ALU.add, r=['pv'], w=[('pd', l)])
            TS('dve', pd[:, l, 14:18], pv[:, b0 + 18:b0 + 22], -1.0, ALU.mult, 1.0, ALU.add, r=['pv'], w=[('pd', l)])
            ACT(pd[:, l, 18:22], pv[:, b0 + 74:b0 + 78], AF.Exp, r=['pv'], w=[('pd', l)])
        P = lambda l, a, b: pv[:, l * PVL + a: l * PVL + b]
        lora = T("lora", [128, 2, 2, 512], BF16)
        for l in range(2):
            S.dma('pool', lora[0:64, l, 0, :], d_up[l], w=[('lora', l)])
            S.dma('pool', lora[64:128, l, 0, :], i_up[l], w=[('lora', l)])
            S.dma('pool', lora[:, l, 1, :], g_up[l], w=[('lora', l)])

        def conv(dst, src, l, rows, key):
            for k in range(rows // 128):
                S.dma('pool', dst[l, k * 128:(k + 1) * 128, :], src[l, k * 128:(k + 1) * 128, :], w=[(key, l, k)])
        convspec = dict(wi=(wi_b, w_in, D), wbr=(wbr_b, w_brr, 512), wba=(wba_b, w_bra, 512), wo=(wo_b, w_out, D),
                        wu=(wu_b, w_up, D), wd=(wd_b, w_dn, DFF))
        converted = set()

        def ensure_conv(kind, l):
            if (kind, l) in converted:
                return
            converted.add((kind, l))
            dst, src, rows = convspec[kind]
            conv(dst, src, l, rows, kind)

        NSLOT = 3
        ring = [T("ring%d" % i, [128, 4096], BF16) for i in range(NSLOT)]
        wk2 = T("wk2", [128, 8, 2, 128], BF16)

        def wsrc(kind, l, i):
            if kind == 'wi':
                return wi_b[l].rearrange("(k p) n -> p k n", p=128)[:, :, i * 512:(i + 1) * 512], [('wi', l, k) for k in range(8)], [8, 512]
            if kind == 'wbr':
                return wbr_b[l].rearrange("(k p) n -> p k n", p=128), [('wbr', l, k) for k in range(4)], [4, 1024]
            if kind == 'wba':
                return wba_b[l].rearrange("(k p) n -> p k n", p=128), [('wba', l, k) for k in range(4)], [4, 1024]
            if kind == 'wo':
                return wo_b[l].rearrange("(k p) n -> p k n", p=128)[:, :, i * 512:(i + 1) * 512], [('wo', l, k) for k in range(8)], [8, 512]
            if kind == 'wu':
                return wu_b[l].rearrange("(k p) n -> p k n", p=128)[:, :, i * 512:(i + 1) * 512], [('wu', l, k) for k in range(8)], [8, 512]
            if kind == 'wd':
                return wd_b[l].rearrange("(f p) n -> p f n", p=128)[:, :, i * 128:(i + 1) * 128], [('wd', l, k) for k in range(32)], [32, 128]
        layer_loads = ([('wi', i) for i in range(5)] + [('wbr', 0), ('wi', 5), ('wi', 6), ('wba', 0), ('wi', 7), ('wi', 8),
                       ('wo', 0), ('wo', 1)] + [('wu', i) for i in range(8)] + [('wd', i) for i in range(8)])
        all_loads = [(kind, l, i) for g in range(NG + NSS) for l in range(2) for (kind, i) in layer_loads]
        wstate = dict(issued=0, used=0, done=set())

        def w_can_issue(n):
            return n < len(all_loads) and (n - NSLOT < 0 or (n - NSLOT) in wstate['done'])

        def w_issue():
            n = wstate['issued']
            kind, l, i = all_loads[n]
            ensure_conv(kind, l)
            src, keys, shp = wsrc(kind, l, i)
            slot = n % NSLOT
            dst = ring[slot][:].rearrange("p (a b) -> p a b", a=shp[0])
            S.dma('sp', dst, src, r=keys, w=[('ring', slot)])
            wstate['issued'] = n + 1

        def w_prefetch():
            while wstate['issued'] < min(wstate['used'] + NSLOT, len(all_loads)) and w_can_issue(wstate['issued']):
                w_issue()

        def w_get(kind, l, i):
            n = wstate['used']
            assert all_loads[n] == (kind, l, i), (all_loads[n], kind, l, i)
            wstate['used'] = n + 1
            while wstate['issued'] <= n:
                assert w_can_issue(wstate['issued']), ("ring slot still live", n)
                w_issue()
            w_prefetch()
            _, _, shp = wsrc(kind, l, i)
            slot = n % NSLOT
            return ring[slot][:].rearrange("p (a b) -> p a b", a=shp[0]), ('ring', slot), n

        def w_done(n):
            wstate['done'].add(n)
            w_prefetch()

        ps = es.enter_context(nc.psum_tensor("ps", [128, 3072], F32))
        pst = es.enter_context(nc.psum_tensor("pst", [128, 2048], BF16))

        def pk(c0, c1):
            return [('ps', b) for b in range(c0 // 512, (c1 - 1) // 512 + 1)]
        big = dict(i=0)

        def bigslot():
            i = big['i'] % 2
            big['i'] += 1
            c0 = 2048 + i * 512
            return ps[:, c0:c0 + 256], [('ps', 4 + i)]

        xT = T("xT", [128, 8, NT]); xTb = T("xTb", [128, 8, NT], BF16)
        xio = T("xio", [128, TL, D])
        Zc = [T("Zc%d" % i, [128, NT + 1]) for i in range(2)]
        CARRY = T("CARRY", [128, 2, 14])
        rT = T("rT", [128, 4, NT]); kraw = T("kraw", [128, 4, NT]); vT = T("vT", [128, 4, NT])
        tw = T("tw", [128, NT], BF16); sgd = T("sgd", [128, NT], BF16)
        ACRC = T("ACRC", [128, 4, TL, 2, 128], BF16)
        bcb = T("bcb", [128, 4, NT], BF16); kcb = T("kcb", [128, 4, NT], BF16); asb = T("asb", [128, 4, NT], BF16)
        vb = T("vb", [128, 4, NT], BF16); rs = T("rs", [128, 4, NT]); bonus = T("bonus", [128, 4, NT], BF16)
        gg = T("gg", [128, 4, NT], BF16); GC = T("GC", [128, 4, NT // 64])
        NTM = 9
        Tm = [T("Tm%d" % i, [128, NT]) for i in range(NTM)]
        Tn = [T("Tn%d" % i, [128, NT]) for i in range(NTM)]
        X = T("X", [128, 8, 128], BF16); VV = T("VV", [128, 8, 128], BF16)
        BcT = T("BcT", [128, 8, 64], BF16); KcT = T("KcT", [128, 8, 64], BF16)
        SC1 = T("SC1", [128, 8, 256], BF16); SC2 = T("SC2", [128, 8, 256], BF16)
        Pm = [T("Pm%d" % i, [128, 8, 128], BF16) for i in range(2)]
        PTm = [T("PTm%d" % i, [128, 8, 128], BF16) for i in range(2)]
        RhatT = T("RhatT", [128, 4, 128]); McT = T("McT", [128, 4, 2, 64]); H = T("H", [128, 2, 4, 64]); Nc = T("Nc", [128, 4, 2, 64])
        YT = T("YT", [128, 4, NT])
        yf = T("yf", [128, 4, NT], BF16); qT = T("qT", [128, 4, NT], BF16)
        KT2 = T("KT2", [128, 2, 2, 128 + NT], BF16); Vtm = T("Vtm", [128, 2, 1 + TL, 128], BF16)
        pT = T("pT", [128, 8, 2, 128], BF16); YA = T("YA", [128, 4, NT], BF16)
        cosT = T("cosT", [128, NT]); sinT = T("sinT", [128, NT])
        qraw = [T("qraw%d" % i, [128, NT], BF16) for i in range(2)]
        kf = T("kf", [128, 2, 128]); vf = T("vf", [128, 128])
        Gtmp = [T("Gtmp%d" % i, [128, NT], BF16) for i in range(2)]
        mixR = T("mixR", [128, 8, NT], BF16); mix = T("mix", [128, 8, NT], BF16)
        x1 = T("x1", [128, 8, NT])
        x1b = [T("x1b%d" % i, [128, NT], BF16) for i in range(2)]
        x1q = [T("x1q%d" % i, [128, NT], BF16) for i in range(2)]
        hT = T("hT", [128, 32, NT], BF16)
        ostage = T("ostage", [128, 256])
        tmask = T("tmask", [128, NT]); SHF = T("SHF", [128, 2, 14]); Snat = T("Snat", [64, 512]); ckd = T("ckd", [128, 2, 2, 64])
        S.dma(QM, tmask[:], tmask_d[:, :], w=['tmask'])

        S.op('pool', lambda e: e.memset(H[:], 0.0), w=[('H', 0), ('H', 1)])
        S.op('pool', lambda e: e.memset(CARRY[:], 0.0), w=[('CARRY', 0), ('CARRY', 1)])
        S.op('pool', lambda e: e.memset(KT2[:], 0.0), w=[('KT2', 0), ('KT2', 1)])
        S.op('pool', lambda e: e.memset(Vtm[:], 0.0), w=[('Vtm', 0), ('Vtm', 1)])
        S.op('pool', lambda e: e.memset(VV[:], 0.0), w=['VV'])

        def layernorm(l, ga, gb_, tag):
            s1 = ps[:, 0:NT]; s2 = ps[:, 512:512 + NT]
            for k in range(8):
                j = k % 2
                CP('act', x1b[j][:], x1[:, k, :], r=[('x1', k)], w=[('x1b', j)])
                ACT(x1q[j][:], x1[:, k, :], AF.Square, r=[('x1', k)], w=[('x1q', j)])
                MM(s1, onesb[:], x1b[j][:], start=(k == 0), stop=(k == 7), r=['onesb', ('x1b', j)], w=pk(0, NT))
                MM(s2, onesb[:], x1q[j][:], start=(k == 0), stop=(k == 7), r=['onesb', ('x1q', j)], w=pk(512, 512 + NT))
            mean, msq, var, rstd = Tm[0], Tm[1], Tm[2], Tm[3]
            ACT(mean[:], s1, AF.Copy, scale=1.0 / D, r=pk(0, NT), w=[('Tm', 0)])
            TT('pool', msq[:], mean[:], mean[:], ALU.mult, r=[('Tm', 0)], w=[('Tm', 1)])
            STT('dve', var[:], s2, 1.0 / D, msq[:], ALU.mult, ALU.subtract, r=pk(512, 512 + NT) + [('Tm', 1)], w=[('Tm', 2)])
            ACT(var[:], var[:], AF.Sqrt, bias=epsln[:, 0:1], r=[('Tm', 2), 'eps'], w=[('Tm', 2)])
            RCP(rstd[:], var[:], r=[('Tm', 2)], w=[('Tm', 3)])
            for k in range(8):
                d = Tm[4 + (k % 2)]
                TT('pool', d[:], x1[:, k, :], mean[:], ALU.subtract, r=[('x1', k), ('Tm', 0)], w=[('Tm', 4 + k % 2)])
                TT('pool', d[:], d[:], rstd[:], ALU.mult, r=[('Tm', 4 + k % 2), ('Tm', 3)], w=[('Tm', 4 + k % 2)])
                TS('dve', xT[:, k, :], d[:], P(l, ga + k, ga + k + 1), ALU.mult, P(l, gb_ + k, gb_ + k + 1), ALU.add,
                   r=[('Tm', 4 + k % 2), 'pv'], w=[('xT', k)])
                CP('act', xTb[:, k, :], xT[:, k, :], r=[('xT', k)], w=[('xTb', k)])

        epsln = T("epsln", [128, 2])
        S.op('pool', lambda e: e.memset(epsln[:, 0:1], LN_EPS), w=['eps'])
        S.op('pool', lambda e: e.memset(epsln[:, 1:2], GN_EPS), w=['eps'])

        for gi in range(NG + NSS):
            samp = gi >= NG
            g = gi if not samp else -1
            q = gi - NG
            t0g = g * NT
            if not samp:
                S.dma('pool', cosT[:], cos_d[:, t0g:t0g + NT], w=['cosT'])
                S.dma('pool', sinT[:], sin_d[:, t0g:t0g + NT], w=['sinT'])
            else:
                S.dma('pool', cosT[:], coss_d[:, :], w=['cosT'])
                S.dma('pool', sinT[:], sins_d[:, :], w=['sinT'])
            if upto < 1:
                S.disabled = True
            if not samp:
                S.dma('pool', xio[:], xp[t0g:t0g + NT, :].rearrange("(t p) d -> p t d", p=128), w=['xio'])
            else:
                S.op('pool', lambda e: e.memset(xio[:], 0.0), w=['xio'])
                S.dma('pool', xio[0:4, 0, :], xs[q], w=['xio'])
            for kp in range(4):
                reg = ps[:, kp * 512:(kp + 1) * 512]
                for kk in range(2):
                    k = kp * 2 + kk
                    for t in range(TL):
                        TR(ps[:, kp * 512 + kk * 256 + t * 128: kp * 512 + kk * 256 + (t + 1) * 128],
                           xio[:, t, k * 128:(k + 1) * 128], identf, r=['xio', 'cst'], w=pk(kp * 512, kp * 512 + 512))
                CP('act', xT[:, 2 * kp:2 * kp + 2, :], reg.rearrange("p (a b) -> p a b", a=2), r=pk(kp * 512, kp * 512 + 512),
                   w=[('xT', 2 * kp), ('xT', 2 * kp + 1)])
                CP('dve', xTb[:, 2 * kp:2 * kp + 2, :], reg.rearrange("p (a b) -> p a b", a=2), r=pk(kp * 512, kp * 512 + 512),
                   w=[('xTb', 2 * kp), ('xTb', 2 * kp + 1)])
            for l in range(2):
                last = (g == NG - 1)
                if samp:
                    S.dma(QM, Snat[:].rearrange("v (h k) -> v h k", k=64), swkv_i[l, q].rearrange("h v k -> v h k"), w=['Snat'])
                    for c in range(4):
                        TR(ps[:, c * 64:(c + 1) * 64], Snat[:, c * 128:(c + 1) * 128], identf[0:64, 0:64], r=['Snat', 'cst'], w=pk(0, 256))
                    CP('dve', H[:, l, :, :], ps[:, 0:256].rearrange("p (c j) -> p c j", j=64), r=pk(0, 256), w=[('H', l)])
                    S.dma(QM, CARRY[:, l, :], sshift_i[l, q].rearrange("(c p) -> p c", p=128), w=[('CARRY', l)], allow_slow_non_contiguous=True)
                    for dup in range(2):
                        S.dma(QM, ckd[:, :, dup, :], sck_i[l, q], w=['ckd'])
                    for kvh in range(2):
                        TR(ps[:, 512 + kvh * 128:512 + (kvh + 1) * 128], ckd[:, kvh, :, :].rearrange("p a b -> p (a b)"), identf, r=['ckd', 'cst'], w=pk(512, 1024))
                    CP('act', KT2[:, l, :, 0:128], ps[:, 512:768].rearrange("p (a b) -> p a b", b=128), r=pk(512, 1024), w=[('KT2', l)])
                    S.dma('pool', Vtm[:, l, 0, :], scv_i[l, q].rearrange("t h d -> t (h d)"), w=[('Vtm', l)])
                allx = [('xTb', k) for k in range(8)]
                ensure_conv('wi', l)
                for kvh in range(2):
                    for dup in range(2):
                        S.dma('pool', wk2[:, :, kvh, dup * 64:(dup + 1) * 64],
                              wi_b[l].rearrange("(k p) n -> p k n", p=128)[:, :, 2304 + kvh * 64: 2304 + (kvh + 1) * 64],
                              r=[('wi', l, k) for k in range(8)], w=['wk2'])
                if upto < 2:
                    S.disabled = True
                for blk in range(5):
                    W, wkey, wn = w_get('wi', l, blk)
                    for j in range(4):
                        c = blk * 4 + j
                        if c in (18, 19):
                            continue
                        po, pkey = bigslot()
                        for k in range(8):
                            MM(po, W[:, k, j * 128:(j + 1) * 128], xTb[:, k, :], start=(k == 0), stop=(k == 7),
                               r=[wkey, ('xTb', k)], w=pkey)
                        if c < 14:
                            z = Zc[c % 2]; zk = ('Zc', c % 2)
                            CP('act', z[:, 1:NT + 1], po, r=pkey, w=[zk])
                            CP('pool', z[:, 0:1], CARRY[:, l, c:c + 1], r=[('CARRY', l)], w=[zk])
                            tmp = Tm[c % 2]
                            TS('dve', tmp[:], z[:, 1:NT + 1], pd[:, l, c:c + 1], ALU.mult, r=[zk, ('pd', l)], w=[('Tm', c % 2)])
                            if c < 4:
                                dst, dk_ = rT[:, c, :], ('rT', c)
                            elif c < 8:
                                dst, dk_ = kraw[:, c - 4, :], ('kraw', c - 4)
                            elif c < 12:
                                dst, dk_ = vT[:, c - 8, :], ('vT', c - 8)
                            else:
                                dst, dk_ = Tm[2 + c % 2][:], ('Tm', 2 + c % 2)
                            STT('dve', dst, z[:, 0:NT], P(l, c, c + 1), tmp[:], ALU.mult, ALU.add,
                                r=[zk, 'pv', ('Tm', c % 2)], w=[dk_])
                            CP('pool', CARRY[:, l, c:c + 1], z[:, NT:NT + 1], r=[zk], w=[('CARRY', l)])
                            if samp:
                                CP('pool', SHF[:, l, c:c + 1], z[:, 4:5], r=[zk], w=[('SHF', l)])
                            if c == 12:
                                ACT(tw[0:64, :], dst[0:64, :], AF.Tanh, r=[dk_], w=['tw'])
                                CP('act', tw[64:128, :], dst[64:128, :], r=[dk_], w=['tw'])
                            if c == 13:
                                ACT(sgd[:], dst, AF.Sigmoid, r=[dk_], w=['sgd'])
                        else:
                            qi = c - 14
                            qr = qraw[qi % 2]; qk = ('qraw', qi % 2)
                            CP('act', qr[:], po, r=pkey, w=[qk])
                            p2, p2k = bigslot()
                            MM(p2, protb[:], qr[:], r=['protb', qk], w=p2k)
                            ta = Tm[4 + qi % 2]; tb_ = Tm[6 + qi % 2]
                            TT('dve', ta[:], p2, sinT[:], ALU.mult, r=p2k + ['sinT'], w=[('Tm', 4 + qi % 2)])
                            TT('pool', tb_[:], qr[:], cosT[:], ALU.mult, r=[qk, 'cosT'], w=[('Tm', 6 + qi % 2)])
                            TT('pool', qT[:, qi, :], ta[:], tb_[:], ALU.add, r=[('Tm', 4 + qi % 2), ('Tm', 6 + qi % 2)], w=[('qT', qi)])
                    if blk == 4:
                        for t in range(TL):
                            po, pkey = bigslot()
                            for k in range(8):
                                MM(po[:, 0:128], xTb[:, k, t * 128:(t + 1) * 128], W[:, k, 384:512], start=(k == 0), stop=(k == 7),
                                   r=[wkey, ('xTb', k)], w=pkey)
                            CP('act', Vtm[:, l, 1 + t, :], po[:, 0:128], r=pkey, w=[('Vtm', l)])
                            if (t == TL - 1 and last) or (samp and t == 0):
                                CP('dve', vf[:], po[:, 0:128], r=pkey, w=['vf'])
                    w_done(wn)
                for kvh in range(2):
                    po, pkey = bigslot()
                    for k in range(8):
                        MM(po, wk2[:, k, kvh, :], xTb[:, k, :], start=(k == 0), stop=(k == 7), r=['wk2', ('xTb', k)], w=pkey)
                    qr = qraw[kvh]; qk = ('qraw', kvh)
                    CP('act', qr[:], po, r=pkey, w=[qk])
                    p2, p2k = bigslot()
                    MM(p2, protb[:], qr[:], r=['protb', qk], w=p2k)
                    ta = Tm[4 + kvh]; tb_ = Tm[6 + kvh]
                    TT('dve', ta[:], p2, sinT[:], ALU.mult, r=p2k + ['sinT'], w=[('Tm', 4 + kvh)])
                    TT('pool', tb_[:], qr[:], cosT[:], ALU.mult, r=[qk, 'cosT'], w=[('Tm', 6 + kvh)])
                    TT('pool', KT2[:, l, kvh, 128:128 + NT], ta[:], tb_[:], ALU.add, r=[('Tm', 4 + kvh), ('Tm', 6 + kvh)], w=[('KT2', l)])
                    if last:
                        TT('pool', kf[:, kvh, :], ta[:, NT - 128:NT], tb_[:, NT - 128:NT], ALU.add,
                           r=[('Tm', 4 + kvh), ('Tm', 6 + kvh)], w=['kf'])
                    if samp:
                        TT('pool', kf[:, kvh, :], ta[:, 0:128], tb_[:, 0:128], ALU.add,
                           r=[('Tm', 4 + kvh), ('Tm', 6 + kvh)], w=['kf'])
                if upto < 3:
                    S.disabled = True
                def st2(c, TS_, TK_):
                    sg, ic, kk_, t4, bT_, gs, E3, E1, rkk = TS_[0], TS_[1], TS_[2], TS_[3], TS_[4], TS_[5], TS_[6], TS_[7], TS_[8]
                    K = lambda i: (TK_, i)
                    cs = slice(c * 128, (c + 1) * 128)
                    p1, p1k = bigslot()
                    MM(p1, lora[0:64, l, 0, cs], tw[0:64, :], r=[('lora', l), 'tw'], w=p1k)
                    yield
                    ACT(sg[:], p1, AF.Sigmoid, bias=P(l, 34 + c, 35 + c), r=p1k + ['pv'], w=[K(0)])
                    yield
                    if samp:
                        TT('pool', sg[:], sg[:], tmask[:], ALU.mult, r=[K(0), 'tmask'], w=[K(0)])
                        yield
                    p2, p2k = bigslot()
                    MM(p2, lora[64:128, l, 0, cs], tw[64:128, :], r=[('lora', l), 'tw'], w=p2k)
                    yield
                    ACT(ic[:], p2, AF.Sigmoid, bias=P(l, 38 + c, 39 + c), r=p2k + ['pv'], w=[K(1)])
                    yield
                    p3, p3k = bigslot()
                    MM(p3, lora[:, l, 1, cs], sgd[:], r=[('lora', l), 'sgd'], w=p3k)
                    yield
                    CP('act', gg[:, c, :], p3, r=p3k, w=[('gg', c)])
                    yield
                    TS('dve', kk_[:], kraw[:, c, :], P(l, 14 + c, 15 + c), ALU.mult, r=[('kraw', c), 'pv'], w=[K(2)])
                    yield
                    TT('pool', t4[:], kk_[:], kk_[:], ALU.mult, r=[K(2)], w=[K(3)])
                    yield
                    p4, p4k = bigslot()
                    MM(p4, blockones, t4[:], r=['cst', K(3)], w=p4k)
                    yield
                    TS('dve', t4[:], p4, 1e-24, ALU.max, r=p4k, w=[K(3)])
                    yield
                    ACT(t4[:], t4[:], AF.Sqrt, r=[K(3)], w=[K(3)])
                    yield
                    RCP(t4[:], t4[:], r=[K(3)], w=[K(3)])
                    yield
                    TT('pool', kk_[:], kk_[:], t4[:], ALU.mult, r=[K(2), K(3)], w=[K(2)])
                    yield
                    if samp:
                        TT('pool', kk_[:], kk_[:], tmask[:], ALU.mult, r=[K(2), 'tmask'], w=[K(2)])
                        yield
                    TT('pool', bT_[:], kk_[:], ic[:], ALU.mult, r=[K(2), K(1)], w=[K(4)])
                    yield
                    TS('dve', ic[:], ic[:], P(l, 18 + c, 19 + c), ALU.mult, pd[:, l, 14 + c:15 + c], ALU.add,
                       r=[K(1), 'pv', ('pd', l)], w=[K(1)])
                    yield
                    TT('pool', ic[:], kraw[:, c, :], ic[:], ALU.mult, r=[('kraw', c), K(1)], w=[K(1)])
                    yield
                    if samp:
                        TT('pool', ic[:], ic[:], tmask[:], ALU.mult, r=[K(1), 'tmask'], w=[K(1)])
                        yield
                    S.op('dve', (lambda o_, d0, d1: (lambda e: e.tensor_tensor_scan(out=o_, data0=d0, data1=d1, initial=0.0,
                                                                                    op0=ALU.mult, op1=ALU.add)))(gs[:], resetm, sg[:]),
                         r=['cst', K(0)], w=[K(5)])
                    yield
                    ACT(E3[:], gs[:], AF.Exp, scale=-DECAY_C, r=[K(5)], w=[K(6)])
                    yield
                    ACT(gs[:], gs[:], AF.Exp, scale=DECAY_C, r=[K(5)], w=[K(5)])
                    yield
                    ACT(sg[:], sg[:], AF.Exp, scale=DECAY_C, r=[K(0)], w=[K(0)])
                    yield
                    nch = NT // 64
                    e3v = E3[:].rearrange("p (a b) -> p a b", b=64)
                    i3v = gs[:].rearrange("p (a b) -> p a b", b=64)
                    CP('pool', GC[:, c, :], e3v[:, :, 63], r=[K(6)], w=[('GC', c)])
                    yield
                    TT('dve', E1[:].rearrange("p (a b) -> p a b", b=64), e3v, i3v[:, :, 63:64].broadcast_to([128, nch, 64]),
                       ALU.mult, r=[K(6), K(5)], w=[K(7)])
                    yield
                    TT('dve', i3v, i3v, e3v[:, :, 63:64].broadcast_to([128, nch, 64]), ALU.mult,
                       r=[K(5), K(6)], w=[K(5)])
                    yield
                    acv = ACRC[:, c, :, 0, :]
                    rcv = ACRC[:, c, :, 1, :]
                    v3 = lambda ap: ap.rearrange("p (a b) -> p a b", b=128)
                    TT('pool', rcv, v3(rT[:, c, :]), v3(E1[:]), ALU.mult, r=[('rT', c), K(7)], w=[('ACRC', c)])
                    yield
                    TT('pool', rs[:, c, :], rT[:, c, :], E3[:], ALU.mult, r=[('rT', c), K(6)], w=[('rs', c)])
                    yield
                    STT('dve', sg[:], kk_[:], -1.0, sg[:], ALU.mult, ALU.mult, r=[K(2), K(0)], w=[K(0)])
                    yield
                    TT('dve', acv, v3(sg[:]), v3(E1[:]), ALU.mult, r=[K(0), K(7)], w=[('ACRC', c)])
                    yield
                    TT('pool', asb[:, c, :], sg[:], E3[:], ALU.mult, r=[K(0), K(6)], w=[('asb', c)])
                    yield
                    TT('pool', bcb[:, c, :], bT_[:], gs[:], ALU.mult, r=[K(4), K(5)], w=[('bcb', c)])
                    yield
                    TT('dve', kcb[:, c, :], ic[:], gs[:], ALU.mult, r=[K(1), K(5)], w=[('kcb', c)])
                    yield
                    CP('act', vb[:, c, :], vT[:, c, :], r=[('vT', c)], w=[('vb', c)])
                    yield
                    TT('pool', rkk[:], rT[:, c, :], ic[:], ALU.mult, r=[('rT', c), K(1)], w=[K(8)])
                    yield
                    TS('dve', rkk[:], rkk[:], P(l, 22 + c, 23 + c), ALU.mult, r=[K(8), 'pv'], w=[K(8)])
                    yield
                    p5, p5k = bigslot()
                    MM(p5, blockones, rkk[:], r=['cst', K(8)], w=p5k)
                    yield
                    TT('dve', bonus[:, c, :], p5, vT[:, c, :], ALU.mult, r=p5k + [('vT', c)], w=[('bonus', c)])
                    yield

                for pair_ in ((0, 1), (2, 3)):
                    gens_ = [st2(c, Tm if c % 2 == 0 else Tn, 'Tm' if c % 2 == 0 else 'Tn') for c in pair_]
                    for _ in zip_longest(*gens_):
                        pass
                dump("rT", rT[:], [('rT', c) for c in range(4)])
                dump("rs", rs[:], [('rs', c) for c in range(4)])
                dump("asb", asb[:], [('asb', c) for c in range(4)])
                dump("bcb", bcb[:], [('bcb', c) for c in range(4)])
                dump("kcb", kcb[:], [('kcb', c) for c in range(4)])
                dump("ACRC", ACRC[:].rearrange("p a b c d -> p (a b c d)"), [('ACRC', c) for c in range(4)])
                if upto < 3.05:
                    S.disabled = True
                allp = [('asb', c) for c in range(4)] + [('bcb', c) for c in range(4)] + [('kcb', c) for c in range(4)] + [('vb', c) for c in range(4)]
                for t in range(1 if samp else TL):
                    tsl = slice(t * 128, (t + 1) * 128)
                    for c in range(4):
                        MM(ps[:, c * 128:(c + 1) * 128], asb[:, c, tsl], identb[:], r=[('asb', c), 'identb'], w=pk(0, 512))
                        MM(ps[:, 512 + c * 128:512 + (c + 1) * 128], bcb[:, c, tsl], identb[:], r=[('bcb', c), 'identb'], w=pk(512, 1024))
                        MM(ps[:, 1024 + c * 128:1024 + (c + 1) * 128], kcb[:, c, tsl], identb[:], r=[('kcb', c), 'identb'], w=pk(1024, 1536))
                        MM(ps[:, 1536 + c * 128:1536 + (c + 1) * 128], vb[:, c, tsl], identb[:], r=[('vb', c), 'identb'], w=pk(1536, 2048))
                    h64 = lambda ap: ap.rearrange("p (h j) -> p h j", j=64)
                    CP('act', X[:, :, 0:64], h64(ps[:, 0:512]), r=pk(0, 512), w=[('X', 0), ('X', 1)])
                    CP('dve', BcT[:], h64(ps[:, 512:1024]), r=pk(512, 1024), w=['BcT'])
                    CP('act', KcT[:], h64(ps[:, 1024:1536]), r=pk(1024, 1536), w=['KcT'])
                    CP('dve', VV[:, :, 64:128], h64(ps[:, 1536:2048]), r=pk(1536, 2048), w=['VV'])
                    if upto < 3.1:
                        S.disabled = True
                    m12 = bc_mid(mask12, 4)
                    for hg in range(2):
                        for hi in (0, 2, 1, 3):
                            h = hg * 4 + hi; c = h // 2; pb = 64 * (h % 2)
                            rhs2 = ACRC[pb:pb + 64, c, t, :, :].rearrange("p a b -> p (a b)")
                            MM(ps[:, hi * 256:(hi + 1) * 256], bcb[pb:pb + 64, c, tsl], rhs2, r=[('bcb', c), ('ACRC', c)], w=pk(hi * 256, hi * 256 + 256))
                            MM(ps[:, 1024 + hi * 256:1024 + (hi + 1) * 256], kcb[pb:pb + 64, c, tsl], rhs2, r=[('kcb', c), ('ACRC', c)],
                               w=pk(1024 + hi * 256, 1024 + hi * 256 + 256))
                        m12h = bc_mid(mask12, 2)
                        for bq in range(2):
                            TT('dve', SC1[:, hg * 4 + 2 * bq:hg * 4 + 2 * bq + 2, :], ps[:, bq * 512:(bq + 1) * 512].rearrange("p (h j) -> p h j", j=256), m12h, ALU.mult,
                               r=pk(bq * 512, bq * 512 + 512) + ['cst'], w=[('SC1', hg)])
                            TT('dve', SC2[:, hg * 4 + 2 * bq:hg * 4 + 2 * bq + 2, :], ps[:, 1024 + bq * 512:1024 + (bq + 1) * 512].rearrange("p (h j) -> p h j", j=256), m12h, ALU.mult,
                               r=pk(1024 + bq * 512, 1024 + bq * 512 + 512) + ['cst'], w=[('SC2', hg)])
                    if upto < 3.2:
                        S.disabled = True
                    sck = [('SC1', 0), ('SC1', 1)]; sck2 = [('SC2', 0), ('SC2', 1)]
                    for h in (0, 2, 4, 6, 1, 3, 5, 7):
                        c = h // 2; pb = 64 * (h % 2)
                        MM(ps[:, h * 128:(h + 1) * 128], ACRC[pb:pb + 64, c, t, 0, :], bcb[pb:pb + 64, c, tsl], r=[('ACRC', c), ('bcb', c)],
                           w=pk(h * 128, h * 128 + 128))
                    for bq in range(2):
                        TT('dve', Pm[0][:, 4 * bq:4 * bq + 4, :], ps[:, bq * 512:(bq + 1) * 512].rearrange("p (h j) -> p h j", j=128), bc_mid(maskT, 4), ALU.mult,
                           r=pk(bq * 512, bq * 512 + 512) + ['cst'], w=[('Pm', 0, bq)])
                    CP('pool', PTm[0][:], SC1[:, :, 0:128], r=sck, w=[('PTm', 0, 0), ('PTm', 0, 1)])
                    for h in range(8):
                        MM(ps[:, 1024 + h * 64:1024 + (h + 1) * 64], SC2[:, h, 0:128], VV[:, h, 64:128], r=sck2 + ['VV'],
                           w=pk(1024 + h * 64, 1024 + h * 64 + 64))
                    CP('act', X[:, :, 64:128], h64(ps[:, 1024:1536]), r=pk(1024, 1536), w=[('X', 0), ('X', 1)])
                    if upto < 3.3:
                        S.disabled = True
                    for j in range(6):
                        a = j % 2; b = 1 - a
                        for bq in range(2):
                            hs_ = range(4 * bq, 4 * bq + 4)
                            for h in hs_:
                                MM(ps[:, h * 128:(h + 1) * 128], PTm[a][:, h, :], X[:, h, :], r=[('PTm', a, bq), ('X', bq)], w=pk(h * 128, h * 128 + 128))
                            if j < 5:
                                for h in hs_:
                                    MM(ps[:, 1024 + h * 128:1024 + (h + 1) * 128], PTm[a][:, h, :], Pm[a][:, h, :], r=[('PTm', a, bq), ('Pm', a, bq)],
                                       w=pk(1024 + h * 128, 1024 + h * 128 + 128))
                                for h in hs_:
                                    MM(ps[:, 2048 + h * 128:2048 + (h + 1) * 128], Pm[a][:, h, :], PTm[a][:, h, :], r=[('PTm', a, bq), ('Pm', a, bq)],
                                       w=pk(2048 + h * 128, 2048 + h * 128 + 128))
                        for bq in range(2):
                            hs_ = slice(4 * bq, 4 * bq + 4)
                            TT('dve', X[:, hs_, :], ps[:, bq * 512:(bq + 1) * 512].rearrange("p (h j) -> p h j", j=128), X[:, hs_, :], ALU.add,
                               r=pk(bq * 512, bq * 512 + 512) + [('X', bq)], w=[('X', bq)])
                            if j < 5:
                                CP('act', Pm[b][:, hs_, :], ps[:, 1024 + bq * 512:1024 + (bq + 1) * 512].rearrange("p (h j) -> p h j", j=128),
                                   r=pk(1024 + bq * 512, 1536 + bq * 512), w=[('Pm', b, bq)])
                                CP('act' if bq == 0 else 'dve', PTm[b][:, hs_, :], ps[:, 2048 + bq * 512:2048 + (bq + 1) * 512].rearrange("p (h j) -> p h j", j=128),
                                   r=pk(2048 + bq * 512, 2560 + bq * 512), w=[('PTm', b, bq)])
                    if upto < 3.4:
                        S.disabled = True
                    for h in range(8):
                        c = h // 2; pb = 64 * (h % 2)
                        MM(ps[pb:pb + 64, c * 128:(c + 1) * 128], X[:, h, 0:64], SC1[:, h, 128:256], r=[('X', 0), ('X', 1)] + sck, w=pk(0, 512))
                    TT('dve', RhatT[:], ps[:, 0:512].rearrange("p (c j) -> p c j", j=128), rs[:, :, tsl], ALU.add,
                       r=pk(0, 512) + [('rs', c) for c in range(4)], w=['RhatT'])
                    if upto < 3.5:
                        S.disabled = True
                    mreg = [(512, 768), (1024, 1280)]; nreg = [(1536, 1792), (2048, 2304)]
                    for ch in range(2):
                        chs = slice(ch * 64, (ch + 1) * 64)
                        for h in range(8):
                            c = h // 2; pb = 64 * (h % 2)
                            MM(ps[pb:pb + 64, mreg[ch][0] + c * 64: mreg[ch][0] + (c + 1) * 64], X[chs, h, 0:64], BcT[chs, h, :],
                               r=[('X', 0), ('X', 1), 'BcT'], w=pk(*mreg[ch]))
                            no = ps[pb:pb + 64, nreg[ch][0] + c * 64: nreg[ch][0] + (c + 1) * 64]
                            MM(no, BcT[chs, h, :], X[chs, h, 64:128], start=True, stop=False, r=['BcT', ('X', 0), ('X', 1)], w=pk(*nreg[ch]))
                            MM(no, KcT[chs, h, :], VV[chs, h, 64:128], start=False, stop=True, r=['KcT', 'VV'], w=pk(*nreg[ch]))
                    for ch in range(2):
                        for c in range(4):
                            STT('dve', McT[:, c, ch, :], I2, GC[:, c, t * 2 + ch: t * 2 + ch + 1],
                                ps[:, mreg[ch][0] + c * 64: mreg[ch][0] + (c + 1) * 64], ALU.mult, ALU.add,
                                r=['cst', ('GC', c)] + pk(*mreg[ch]), w=['McT'])
                        CP('act', Nc[:, :, ch, :], ps[:, nreg[ch][0]:nreg[ch][1]].rearrange("p (c j) -> p c j", j=64), r=pk(*nreg[ch]), w=['Nc'])
                    if upto < 3.6:
                        S.disabled = True
                    for ch in range(2):
                        if ch == 1 and upto < 3.95:
                            S.disabled = True
                        if ch == 0 and t == 1 and upto < 3.97:
                            S.disabled = True
                        chs = slice(ch * 64, (ch + 1) * 64)
                        yreg = (2560, 2816)
                        for h in range(8):
                            c = h // 2; pb = 64 * (h % 2)
                            yo = ps[pb:pb + 64, 2560 + c * 64:2560 + (c + 1) * 64]
                            MM(yo, X[:, h, 64:128], SC1[:, h, 128 + ch * 64:128 + (ch + 1) * 64], start=True, stop=False, r=[('X', 0), ('X', 1)] + sck, w=pk(*yreg))
                            MM(yo, VV[:, h, 64:128], SC2[:, h, 128 + ch * 64:128 + (ch + 1) * 64], start=False, stop=False, r=['VV'] + sck2, w=pk(*yreg))
                            MM(yo, H[pb:pb + 64, l, c, :], RhatT[pb:pb + 64, c, chs], start=False, stop=True, r=[('H', l), 'RhatT'], w=pk(*yreg))
                        if upto < 3.7:
                            S.disabled = True
                        for h in (0, 2, 4, 6, 1, 3, 5, 7):
                            c = h // 2; pb = 64 * (h % 2)
                            hb = 0 if pb == 0 else 512
                            MM(ps[pb:pb + 64, hb + c * 64:hb + (c + 1) * 64], McT[pb:pb + 64, c, ch, :], H[pb:pb + 64, l, c, :],
                               r=['McT', ('H', l)], w=pk(hb, hb + 256))
                        if upto < 3.8:
                            S.disabled = True
                        CP('act', YT[:, :, t * 128 + ch * 64: t * 128 + (ch + 1) * 64], ps[:, 2560:2816].rearrange("p (c j) -> p c j", j=64),
                           r=pk(*yreg), w=[('YT', c) for c in range(4)])
                        if upto < 3.9:
                            S.disabled = True
                        TT('dve', H[0:64, l, :, :], ps[0:64, 0:256].rearrange("p (c j) -> p c j", j=64), Nc[0:64, :, ch, :], ALU.add,
                           r=pk(0, 256) + ['Nc'], w=[('H', l)])
                        TT('dve', H[64:128, l, :, :], ps[64:128, 512:768].rearrange("p (c j) -> p c j", j=64), Nc[64:128, :, ch, :], ALU.add,
                           r=pk(512, 768) + ['Nc'], w=[('H', l)])
                dump("YT", YT[:], [('YT', c) for c in range(4)])
                if upto < 5:
                    S.disabled = True
                def st5(c):
                    K = lambda i: ('Tm', i)
                    d_, dq_, sd = Tm[0 + 3 * (c % 2)], Tm[1 + 3 * (c % 2)], Tm[2 + 3 * (c % 2)]
                    k0, k1, k2 = K(0 + 3 * (c % 2)), K(1 + 3 * (c % 2)), K(2 + 3 * (c % 2))
                    p1, p1k = bigslot()
                    MM(p1, blockones, YT[:, c, :], r=['cst', ('YT', c)], w=p1k)
                    yield
                    STT('dve', d_[:], p1, -1.0 / 64, YT[:, c, :], ALU.mult, ALU.add, r=p1k + [('YT', c)], w=[k0])
                    yield
                    TT('pool', dq_[:], d_[:], d_[:], ALU.mult, r=[k0], w=[k1])
                    yield
                    p2, p2k = bigslot()
                    MM(p2, blockones, dq_[:], r=['cst', k1], w=p2k)
                    yield
                    ACT(sd[:], p2, AF.Sqrt, bias=epsln[:, 1:2], scale=1.0 / 64, r=p2k + ['eps'], w=[k2])
                    yield
                    RCP(sd[:], sd[:], r=[k2], w=[k2])
                    yield
                    TT('pool', d_[:], d_[:], sd[:], ALU.mult, r=[k0, k2], w=[k0])
                    yield
                    TS('dve', d_[:], d_[:], P(l, 26 + c, 27 + c), ALU.mult, P(l, 30 + c, 31 + c), ALU.add, r=[k0, 'pv'], w=[k0])
                    yield
                    TT('pool', d_[:], d_[:], bonus[:, c, :], ALU.add, r=[k0, ('bonus', c)], w=[k0])
                    yield
                    TT('pool', yf[:, c, :], d_[:], gg[:, c, :], ALU.mult, r=[k0, ('gg', c)], w=[('yf', c)])
                    yield

                for pair_ in ((0, 1), (2, 3)):
                    gens_ = [st5(c) for c in pair_]
                    for _ in zip_longest(*gens_):
                        pass
                dump("yf", yf[:], [('yf', c) for c in range(4)])
                if upto < 6:
                    S.disabled = True
                for t in range(1 if samp else TL):
                    tsl = slice(t * 128, (t + 1) * 128)
                    first_tile = (g == 0 and t == 0 and not samp)
                    for h in (0, 2, 4, 6, 1, 3, 5, 7):
                        c = h // 2; pb = 64 * (h % 2); kvh = h // 4
                        MM(ps[:, h * 256:h * 256 + 128], KT2[pb:pb + 64, l, kvh, t * 128:(t + 1) * 128], qT[pb:pb + 64, c, tsl],
                           r=[('KT2', l), ('qT', c)], w=pk(h * 256, h * 256 + 128))
                        MM(ps[:, h * 256 + 128:h * 256 + 256], KT2[pb:pb + 64, l, kvh, (t + 1) * 128:(t + 2) * 128], qT[pb:pb + 64, c, tsl],
                           r=[('KT2', l), ('qT', c)], w=pk(h * 256 + 128, h * 256 + 256))
                    for q4 in range(4):
                        ACT(pT[:, q4 * 2:q4 * 2 + 2, :, :].rearrange("p a b c -> p (a b c)"), ps[:, q4 * 512:(q4 + 1) * 512], AF.Exp, scale=0.125,
                            r=pk(q4 * 512, q4 * 512 + 512), w=[('pT', q4)])
                    mm_ = (mAtt0 if first_tile else mAtt)
                    ptk = [('pT', q4) for q4 in range(4)]
                    TT('pool', pT[:].rearrange("p a b c -> p a (b c)"), pT[:].rearrange("p a b c -> p a (b c)"), bc_mid(mm_, 8), ALU.mult,
                       r=ptk + ['cst'], w=ptk)
                    for h in range(8):
                        c = h // 2; pb = 64 * (h % 2); kvh = h // 4
                        oo = ps[pb:pb + 64, c * 128:(c + 1) * 128]
                        MM(oo, Vtm[:, l, t, kvh * 64:(kvh + 1) * 64], pT[:, h, 0, :], start=True, stop=False, r=[('Vtm', l)] + ptk, w=pk(0, 512))
                        MM(oo, Vtm[:, l, t + 1, kvh * 64:(kvh + 1) * 64], pT[:, h, 1, :], start=False, stop=True, r=[('Vtm', l)] + ptk, w=pk(0, 512))
                        do = ps[pb:pb + 64, 512 + c * 128:512 + (c + 1) * 128]
                        MM(do, onesb[:, 0:64], pT[:, h, 0, :], start=True, stop=False, r=['onesb'] + ptk, w=pk(512, 1024))
                        MM(do, onesb[:, 0:64], pT[:, h, 1, :], start=False, stop=True, r=['onesb'] + ptk, w=pk(512, 1024))
                    den = Tm[0]; den2 = Tm[1]
                    dv = lambda tl: tl[:].rearrange("p (a b) -> p a b", b=128)
                    for c in range(4):
                        tgt = (den if c < 2 else den2)[:, (c % 2) * 128:(c % 2 + 1) * 128]
                        TS('dve', tgt, ps[:, 512 + c * 128:512 + (c + 1) * 128], pd[:, l, 18 + c:19 + c], ALU.add,
                           r=pk(512, 1024) + [('pd', l)], w=[('Tm', 0 if c < 2 else 1)])
                    RCP(den[:], den[:], r=[('Tm', 0)], w=[('Tm', 0)])
                    RCP(den2[:], den2[:], r=[('Tm', 1)], w=[('Tm', 1)])
                    TT('dve', YA[:, 0:2, tsl], ps[:, 0:256].rearrange("p (a b) -> p a b", b=128), dv(den), ALU.mult,
                       r=pk(0, 512) + [('Tm', 0)], w=[('YA', 0), ('YA', 1)])
                    TT('dve', YA[:, 2:4, tsl], ps[:, 256:512].rearrange("p (a b) -> p a b", b=128), dv(den2), ALU.mult,
                       r=pk(0, 512) + [('Tm', 1)], w=[('YA', 2), ('YA', 3)])
                dump("YA", YA[:], [('YA', c) for c in range(4)])
                if upto < 7:
                    S.disabled = True
                for br in range(2):
                    Wb, wbk, wbn = w_get('wbr' if br == 0 else 'wba', l, 0)
                    wgn = None
                    src_act = yf if br == 0 else YA
                    sk = 'yf' if br == 0 else 'YA'
                    for m in range(8):
                        if m % 4 == 0:
                            if wgn is not None:
                                w_done(wgn)
                            Wg, wgk, wgn = w_get('wi', l, 5 + 2 * br + m // 4)
                        po, pkey = bigslot()
                        for k in range(8):
                            MM(po, Wg[:, k, (m % 4) * 128:(m % 4 + 1) * 128], xTb[:, k, :], start=(k == 0), stop=(k == 7), r=[wgk, ('xTb', k)], w=pkey)
                        gt = Gtmp[m % 2]; gk = ('Gtmp', m % 2)
                        ACT(gt[:], po, AF.Sigmoid, r=pkey, w=[gk])
                        p2, p2k = bigslot()
                        for c in range(4):
                            MM(p2, Wb[:, c, m * 128:(m + 1) * 128], src_act[:, c, :], start=(c == 0), stop=(c == 3), r=[wbk, (sk, c)], w=p2k)
                        if br == 0:
                            TT('dve', mixR[:, m, :], p2, gt[:], ALU.mult, r=p2k + [gk], w=[('mixR', m)])
                        else:
                            tm_ = Tm[m % 2]
                            TT('dve', tm_[:], p2, gt[:], ALU.mult, r=p2k + [gk], w=[('Tm', m % 2)])
                            TT('pool', mix[:, m, :], tm_[:], mixR[:, m, :], ALU.add, r=[('Tm', m % 2), ('mixR', m)], w=[('mix', m)])
                    w_done(wgn)
                    w_done(wbn)
                for m in range(8):
                    if m % 4 == 0:
                        if m > 0:
                            w_done(won)
                        Wo, wok, won = w_get('wo', l, m // 4)
                    po, pkey = bigslot()
                    for k in range(8):
                        MM(po, Wo[:, k, (m % 4) * 128:(m % 4 + 1) * 128], mix[:, k, :], start=(k == 0), stop=(k == 7), r=[wok, ('mix', k)], w=pkey)
                    STT('dve', x1[:, m, :], xT[:, m, :], ALPHA, po, ALU.mult, ALU.add, r=[('xT', m)] + pkey, w=[('x1', m)])
                dump("mix", mix[:], [('mix', k) for k in range(8)])
                dump("x1pre", x1[:], [('x1', k) for k in range(8)])
                w_done(won)
                layernorm(l, 42, 50, 'ln1')
                dump("xln1", xT[:], [('xT', k) for k in range(8)])
                if upto < 8:
                    S.disabled = True
                for fb in range(8):
                    Wu, wuk, wun = w_get('wu', l, fb)
                    for j in range(4):
                        f = fb * 4 + j
                        po, pkey = bigslot()
                        for k in range(8):
                            MM(po, Wu[:, k, j * 128:(j + 1) * 128], xTb[:, k, :], start=(k == 0), stop=(k == 7), r=[wuk, ('xTb', k)], w=pkey)
                        gt = Gtmp[f % 2]; gk = ('Gtmp', f % 2)
                        ACT(gt[:], po, AF.Relu, r=pkey, w=[gk])
                        TT('pool', hT[:, f, :], gt[:], gt[:], ALU.mult, r=[gk], w=[('hT', f)])
                    w_done(wun)
                for m in range(8):
                    Wd, wdk, wdn = w_get('wd', l, m)
                    po, pkey = bigslot()
                    for f in range(32):
                        MM(po, Wd[:, f, :], hT[:, f, :], start=(f == 0), stop=(f == 31), r=[wdk, ('hT', f)], w=pkey)
                    STT('dve', x1[:, m, :], xT[:, m, :], ALPHA, po, ALU.mult, ALU.add, r=[('xT', m)] + pkey, w=[('x1', m)])
                    w_done(wdn)
                layernorm(l, 58, 66, 'ln2')
                if upto < 9:
                    S.disabled = True
                CP('pool', KT2[:, l, :, 0:128], KT2[:, l, :, NT:NT + 128], r=[('KT2', l)], w=[('KT2', l)])
                CP('pool', Vtm[:, l, 0, :], Vtm[:, l, TL, :], r=[('Vtm', l)], w=[('Vtm', l)])
                if samp:
                    for c in range(4):
                        TR(ps[0:64, c * 128:(c + 1) * 128], H[:, l, c, :], identf, r=[('H', l), 'cst'], w=pk(0, 512))
                    CP('act', ostage[0:64, 0:256], ps[0:64, 0:256], r=pk(0, 512), w=['ostage'])
                    S.dma(QM, so_wkv[l, q, 0:4].rearrange("h v k -> v h k"), ostage[0:64, 0:256].rearrange("p (h k) -> p h k", k=64), r=['ostage'])
                    CP('act', ostage[0:64, 0:256], ps[0:64, 256:512], r=pk(0, 512), w=['ostage'])
                    S.dma(QM, so_wkv[l, q, 4:8].rearrange("h v k -> v h k"), ostage[0:64, 0:256].rearrange("p (h k) -> p h k", k=64), r=['ostage'])
                    S.dma(QM, so_shift[l, q].rearrange("(c p) -> p c", p=128), SHF[:, l, :], r=[('SHF', l)], allow_slow_non_contiguous=True)
                    for kvh in range(2):
                        TR(ps[:, 512 + kvh * 64:512 + (kvh + 1) * 64], kf[0:64, kvh, :], identf[0:64, 0:64], r=['kf', 'cst'], w=pk(512, 1024))
                    CP('act', ostage[:, 0:128], ps[:, 512:640], r=pk(512, 1024), w=['ostage'])
                    S.dma(QM, so_ck[l, q, 124:128].rearrange("t h d -> t (h d)"), ostage[0:4, 0:128], r=['ostage'])
                    S.dma(QM, so_cv[l, q, 124:128].rearrange("t h d -> t (h d)"), vf[0:4, :], r=['vf'])
                    S.dma(QM, so_ck[l, q, 0:124].rearrange("t h d -> t (h d)"), sck_i[l, q, 4:128].rearrange("t h d -> t (h d)"))
                    S.dma(QM, so_cv[l, q, 0:124].rearrange("t h d -> t (h d)"), scv_i[l, q, 4:128].rearrange("t h d -> t (h d)"))
                if last:
                    for c in range(4):
                        TR(ps[0:64, c * 128:(c + 1) * 128], H[:, l, c, :], identf, r=[('H', l), 'cst'], w=pk(0, 512))
                    CP('act', ostage[0:64, 0:512 // 2 * 0 + 256], ps[0:64, 0:256], r=pk(0, 512), w=['ostage'])
                    S.dma(QM, o_wkv[l, 0:4].rearrange("h v k -> v h k"), ostage[0:64, 0:256].rearrange("p (h k) -> p h k", k=64), r=['ostage'])
                    CP('act', ostage[0:64, 0:256], ps[0:64, 256:512], r=pk(0, 512), w=['ostage'])
                    S.dma(QM, o_wkv[l, 4:8].rearrange("h v k -> v h k"), ostage[0:64, 0:256].rearrange("p (h k) -> p h k", k=64), r=['ostage'])
                    S.dma(QM, o_shift[l].rearrange("(c p) -> p c", p=128), CARRY[:, l, :], r=[('CARRY', l)], allow_slow_non_contiguous=True)
                    for kvh in range(2):
                        TR(ps[:, 512 + kvh * 64:512 + (kvh + 1) * 64], kf[0:64, kvh, :], identf[0:64, 0:64], r=['kf', 'cst'], w=pk(512, 1024))
                    CP('act', ostage[:, 0:128], ps[:, 512:640], r=pk(512, 1024), w=['ostage'])
                    S.dma(QM, o_ck[l].rearrange("t h d -> t (h d)"), ostage[:, 0:128], r=['ostage'])
                    S.dma(QM, o_cv[l].rearrange("t h d -> t (h d)"), vf[:], r=['vf'])
            for t in range(TL):
                for kp in range(2):
                    for kk in range(4):
                        k = kp * 4 + kk
                        TR(ps[:, kp * 512 + kk * 128: kp * 512 + (kk + 1) * 128], xT[:, k, t * 128:(t + 1) * 128], identf,
                           r=[('xT', k), 'cst'], w=pk(kp * 512, kp * 512 + 512))
                    CP('act' if kp == 0 else 'dve', xio[:, t, kp * 512:(kp + 1) * 512], ps[:, kp * 512:(kp + 1) * 512],
                       r=pk(kp * 512, kp * 512 + 512), w=['xio'])
            if not samp:
                S.dma(QM, y_p[t0g:t0g + NT, :].rearrange("(t p) d -> p t d", p=128), xio[:], r=['xio'])
            else:
                S.dma(QM, y_s[q], xio[0:4, 0, :], r=['xio'])
        stats = S.emit()
    return nc, stats


def pack_pv(inp):
    pv = np.zeros((128, 2 * PVL), np.float32)
    for l in range(2):
        b = l * PVL
        pv[:, b:b + 14] = inp['mu_shift'][l].reshape(14, 128).T
        for off, name in ((14, 'k_k'), (18, 'k_a'), (26, 'lnx_g'), (30, 'lnx_b'), (34, 'decay_base'), (38, 'iclr_base')):
            pv[:, b + off:b + off + 4] = inp[name][l].reshape(4, 128).T
        pv[:, b + 22:b + 26] = inp['r_k'][l].reshape(512).reshape(4, 128).T
        for off, name in ((42, 'ln1_g'), (50, 'ln1_b'), (58, 'ln2_g'), (66, 'ln2_b')):
            pv[:, b + off:b + off + 8] = inp[name][l].reshape(8, 128).T
        pv[:, b + 74:b + 78] = np.repeat(inp['sinks'][l].reshape(4, 2), 64, axis=1).T
    return pv


_CACHE = {}


def host_inputs(inp, c, SEQ, NSS, consts):
    cst, cosT, sinT, coss, sins, tmask, pv = consts
    wnames = ['w_in', 'w_br_rwkv', 'w_br_attn', 'w_out', 'w_ff_up', 'w_ff_down', 'decay_up', 'iclr_up', 'gate_up']
    m = {k: inp[k] for k in wnames}
    sl = slice(c * NSS, (c + 1) * NSS)
    m.update(xp=inp['x_prompt'][(c * 2) // 8][:SEQ], pv_in=pv, cst_in=cst, cos_in=cosT, sin_in=sinT, coss_in=coss, sins_in=sins, tmask_in=tmask,
             xs=np.ascontiguousarray(inp['x_sample'][sl]), swkv_i=np.ascontiguousarray(inp['state_wkv'][:, sl]),
             sshift_i=np.ascontiguousarray(inp['state_shift'][:, sl]), sck_i=np.ascontiguousarray(inp['cache_k_win'][:, sl]),
             scv_i=np.ascontiguousarray(inp['cache_v_win'][:, sl]))
    return m


def host_consts(inp, SEQ, past_len=8192):
    cst = make_consts()
    cosT, sinT = rope_tables(np.arange(SEQ))
    coss, sins = rope_tables(past_len + np.arange(NT))
    tmask = np.zeros((128, NT), np.float32)
    tmask[:, 0:4] = 1.0
    return cst, cosT, sinT, coss, sins, tmask, pack_pv(inp)


def kernel(**inputs):
    inp = {k: np.ascontiguousarray(np.asarray(v)) for k, v in inputs.items()}
    B, SEQ, _ = inp['x_prompt'].shape
    NSS = inp['x_sample'].shape[0] // 8
    if SEQ not in _CACHE:
        _CACHE[SEQ] = build(SEQ, NSS)
    nc, _ = _CACHE[SEQ]
    consts = host_consts(inp, SEQ)
    in_maps = [host_inputs(inp, c, SEQ, NSS, consts) for c in range(8)]
    res = run_bass_kernel_spmd(nc, in_maps, core_ids=list(range(8))).results
    y_p = np.stack([res[0]['y_p'], res[4]['y_p']])
    p_wkv = np.stack([res[0]['p_wkv'], res[4]['p_wkv']], 1)
    p_shift = np.stack([res[0]['p_shift'], res[4]['p_shift']], 1)
    p_ck = np.stack([res[0]['p_ck'], res[4]['p_ck']], 1)
    p_cv = np.stack([res[0]['p_cv'], res[4]['p_cv']], 1)
    y_s = np.concatenate([res[c]['y_s'] for c in range(8)], 0)
    s_wkv = np.concatenate([res[c]['s_wkv'] for c in range(8)], 1)
    s_shift = np.concatenate([res[c]['s_shift'] for c in range(8)], 1)
    s_ck = np.concatenate([res[c]['s_ck'] for c in range(8)], 1)
    s_cv = np.concatenate([res[c]['s_cv'] for c in range(8)], 1)
    return (y_p, y_s, p_wkv, p_shift, p_ck, p_cv, s_wkv, s_shift, s_ck, s_cv)
```

```python
import numpy as np
from contextlib import ExitStack
from itertools import zip_longest
import concourse.bass as bass
import concourse.mybir as mybir
from concourse.ap import AP
from concourse.bass_utils import run_bass_kernel_spmd

F32 = mybir.dt.float32
BF16 = mybir.dt.bfloat16
AF = mybir.ActivationFunctionType
ALU = mybir.AluOpType

D = 1024
NT = 256
TL = NT // 128
SHIFT_W = 1792
IN_W = 4608
DFF = 4096
ALPHA = 4 ** 0.25
LN_EPS = 1e-5
GN_EPS = 64e-5
DECAY_C = 0.6065306597126334
PVL = 78
QM = 'pool'
NCST = 1792


class Sched:
    COMPUTE = ('pe', 'act', 'dve', 'pool')

    def __init__(self, nc, es, n_dma_sems=24):
        self.nc = nc
        self.h = {'pe': nc.tensor, 'act': nc.scalar, 'dve': nc.vector, 'pool': nc.gpsimd, 'sp': nc.sync}
        self.ops = []
        self.last_w = {}
        self.readers = {}
        self.sem = {e: es.enter_context(nc.semaphore("s_" + e)) for e in self.COMPUTE}
        self.dsem = []
        self.dq = {}
        for q, n in (('sp', n_dma_sems), ('act', 8), ('pool', 12)):
            self.dq[q] = list(range(len(self.dsem), len(self.dsem) + n))
            self.dsem += [es.enter_context(nc.semaphore("d%s%d" % (q, i))) for i in range(n)]

    def _add(self, kind, eng, fn, r, w):
        isps = lambda k: isinstance(k, tuple) and k[0] in ('ps', 'pst')
        w = list(w) + [k for k in r if isps(k)]
        r = [k for k in r if not isps(k)]
        oid = len(self.ops)
        deps = set()
        for k in r:
            if k in self.last_w:
                deps.add(self.last_w[k])
        for k in w:
            if k in self.last_w:
                deps.add(self.last_w[k])
            deps |= self.readers.get(k, set())
        for k in r:
            self.readers.setdefault(k, set()).add(oid)
        for k in w:
            self.last_w[k] = oid
            self.readers[k] = set()
        deps.discard(oid)
        self.ops.append(dict(kind=kind, eng=eng, fn=fn, deps=deps))
        return oid

    disabled = False

    def op(self, eng, fn, r=(), w=()):
        if self.disabled:
            return None
        return self._add('c', eng, fn, r, w)

    def dma(self, eng, out, in_, r=(), w=(), **kw):
        if self.disabled:
            return None
        return self._add('d', eng, (out, in_, kw), r, w)

    def emit(self):
        ops = self.ops
        need = [False] * len(ops)
        for i, o in enumerate(ops):
            for d in o['deps']:
                p = ops[d]
                if p['kind'] == 'c':
                    if p['eng'] == o['eng'] and o['kind'] == 'c' and p['eng'] == 'pe':
                        continue
                    need[d] = True
        cnt = {e: 0 for e in self.COMPUTE}
        tok = [None] * len(ops)
        seen = {}
        dcount = [0] * len(self.dsem)
        dk = {q: 0 for q in self.dq}
        nwaits = 0
        acts = {e: [] for e in self.h}
        for i, o in enumerate(ops):
            e = o['eng']
            wl = {}
            for d in o['deps']:
                p = ops[d]
                if p['kind'] == 'c' and p['eng'] == e and o['kind'] == 'c' and e == 'pe':
                    continue
                t = tok[d]
                if t is None:
                    continue
                ts, tv = t
                if tv > wl.get(id(ts), (ts, 0))[1]:
                    wl[id(ts)] = (ts, tv)
            if o['kind'] == 'd':
                j = self.dq[e][dk[e] % len(self.dq[e])]
                dk[e] += 1
                dsj = self.dsem[j]
                if dcount[j] > 0 and dcount[j] > wl.get(id(dsj), (dsj, 0))[1]:
                    wl[id(dsj)] = (dsj, dcount[j])
            for ws, wv in wl.values():
                key = (e, id(ws))
                if seen.get(key, 0) >= wv:
                    continue
                acts[e].append((lambda s_, v_: (lambda h: h.wait_ge(s_, v_)))(ws, wv))
                nwaits += 1
                seen[key] = wv
            if o['kind'] == 'c':
                if need[i]:
                    cnt[e] += 1
                    acts[e].append((lambda fn_, sm_: (lambda h: fn_(h).then_inc(sm_, 1)))(o['fn'], self.sem[e]))
                    tok[i] = (self.sem[e], cnt[e])
                else:
                    acts[e].append(o['fn'])
            else:
                out, in_, kw = o['fn']
                dcount[j] += 16
                acts[e].append((lambda o_, i_, k_, s_: (lambda h: h.dma_start(out=o_, in_=i_, **k_).then_inc(s_, 16)))(out, in_, kw, dsj))
                tok[i] = (dsj, dcount[j])
        for j, fs in enumerate(self.dsem):
            if dcount[j] > 0:
                acts['sp'].append((lambda s_, v_: (lambda h: h.wait_ge(s_, v_)))(fs, dcount[j]))
        with self.nc.Block() as block:
            @block.sync
            def _(h):
                for a in acts['sp']:
                    a(h)

            @block.tensor
            def _(h):
                for a in acts['pe']:
                    a(h)

            @block.scalar
            def _(h):
                for a in acts['act']:
                    a(h)

            @block.vector
            def _(h):
                for a in acts['dve']:
                    a(h)

            @block.gpsimd
            def _(h):
                for a in acts['pool']:
                    a(h)
        return dict(n_ops=len(ops), n_waits=nwaits, signals=dict(cnt))


def make_consts():
    c = np.zeros((128, NCST), np.float32)
    idx = np.arange(128)
    c[:, 0:128] = np.eye(128)
    c[:, 128:256] = (idx[:, None] // 64 == idx[None, :] // 64)
    same = (idx[:, None] // 64) == (idx[None, :] // 64)
    mstrict = same & (idx[:, None] < idx[None, :])
    mincl = same & (idx[:, None] <= idx[None, :])
    c[:, 256:384] = mstrict.T
    c[:, 384:512] = mstrict
    c[:, 512:640] = mincl
    c[:, 640:768] = idx[:, None] >= idx[None, :]
    c[:, 768:896] = idx[:, None] <= idx[None, :]
    c[:, 896:1024] = 0.0
    c[:, 1024:1152] = idx[:, None] <= idx[None, :]
    prot = np.zeros((128, 128), np.float32)
    for hb in (0, 64):
        for dd in range(32):
            prot[hb + dd + 32, hb + dd] = -1.0
            prot[hb + dd, hb + dd + 32] = 1.0
    c[:, 1152:1280] = prot
    rm = np.ones((128, 256), np.float32)
    rm[:, 0::64] = 0.0
    c[:, 1280:1536] = rm
    c[:, 1536:1664] = 1.0
    c[:, 1664:1728] = (idx[:, None] % 64) == np.arange(64)[None, :]
    return c


def rope_tables(pos):
    half = 32
    inv = (10000.0 ** (-np.arange(half, dtype=np.float32) / half)).astype(np.float32)
    ang = pos.astype(np.float32)[None, :] * inv[:, None]
    cos = np.cos(ang).astype(np.float32)
    sin = np.sin(ang).astype(np.float32)
    return np.tile(cos, (4, 1)), np.tile(sin, (4, 1))


def build(SEQ, NSS=16, dbg=(), upto=99, noconv=False):
    NG = SEQ // NT
    NTS = NSS * 4
    nc = bass.Bass("TRN2", target_bir_lowering=False)
    din = lambda name, shape, dt=F32: nc.dram_tensor(name, list(shape), dt, kind="ExternalInput").ap()
    dout = lambda name, shape, dt=F32: nc.dram_tensor(name, list(shape), dt, kind="ExternalOutput").ap()
    dscr = lambda name, shape, dt=BF16: nc.dram_tensor(name, list(shape), dt).ap()

    xp = din("xp", [SEQ, D])
    w_in = din("w_in", [2, D, IN_W]); w_brr = din("w_br_rwkv", [2, 512, D]); w_bra = din("w_br_attn", [2, 512, D])
    w_out = din("w_out", [2, D, D]); w_up = din("w_ff_up", [2, D, DFF]); w_dn = din("w_ff_down", [2, DFF, D])
    d_up = din("decay_up", [2, 64, 512]); i_up = din("iclr_up", [2, 64, 512]); g_up = din("gate_up", [2, 128, 512])
    pv_d = din("pv_in", [128, 2 * PVL]); cst_d = din("cst_in", [128, NCST])
    cos_d = din("cos_in", [128, SEQ]); sin_d = din("sin_in", [128, SEQ])

    xs = din("xs", [NSS, 4, D]); swkv_i = din("swkv_i", [2, NSS, 8, 64, 64]); sshift_i = din("sshift_i", [2, NSS, SHIFT_W])
    sck_i = din("sck_i", [2, NSS, 128, 2, 64]); scv_i = din("scv_i", [2, NSS, 128, 2, 64])
    coss_d = din("coss_in", [128, NT]); sins_d = din("sins_in", [128, NT]); tmask_d = din("tmask_in", [128, NT])
    y_s = dout("y_s", [NSS, 4, D]); so_wkv = dout("s_wkv", [2, NSS, 8, 64, 64]); so_shift = dout("s_shift", [2, NSS, SHIFT_W])
    so_ck = dout("s_ck", [2, NSS, 128, 2, 64]); so_cv = dout("s_cv", [2, NSS, 128, 2, 64])
    y_p = dout("y_p", [SEQ, D]); o_wkv = dout("p_wkv", [2, 8, 64, 64]); o_shift = dout("p_shift", [2, SHIFT_W])
    o_ck = dout("p_ck", [2, 128, 2, 64]); o_cv = dout("p_cv", [2, 128, 2, 64])
    dbg_out = {}

    wi_b = dscr("wi_b", [2, D, IN_W]); wbr_b = dscr("wbr_b", [2, 512, D]); wba_b = dscr("wba_b", [2, 512, D])
    wo_b = dscr("wo_b", [2, D, D]); wu_b = dscr("wu_b", [2, D, DFF]); wd_b = dscr("wd_b", [2, DFF, D])

    with ExitStack() as es:
        S = Sched(nc, es)
        T = lambda name, shape, dt=F32: es.enter_context(nc.sbuf_tensor(name, list(shape), dt))
        def MM(out, lhsT, rhs, start=True, stop=True, r=(), w=()):
            S.op('pe', lambda e: e.matmul(out, lhsT=lhsT, rhs=rhs, start=start, stop=stop), r=r, w=w)

        def TR(out, in_, ident, r=(), w=()):
            S.op('pe', lambda e: e.transpose(out, in_, ident), r=r, w=w)

        def TT(eng, out, in0, in1, op, r=(), w=()):
            S.op(eng, lambda e: e.tensor_tensor(out=out, in0=in0, in1=in1, op=op), r=r, w=w)

        def TS(eng, out, in0, s1, op0, s2=None, op1=None, r=(), w=()):
            if op1 is None:
                S.op(eng, lambda e: e.tensor_scalar(out=out, in0=in0, scalar1=s1, scalar2=None, op0=op0), r=r, w=w)
            else:
                S.op(eng, lambda e: e.tensor_scalar(out=out, in0=in0, scalar1=s1, scalar2=s2, op0=op0, op1=op1), r=r, w=w)

        def STT(eng, out, in0, scalar, in1, op0, op1, r=(), w=()):
            S.op(eng, lambda e: e.scalar_tensor_tensor(out=out, in0=in0, scalar=scalar, in1=in1, op0=op0, op1=op1), r=r, w=w)

        def ACT(out, in_, func, bias=None, scale=1.0, r=(), w=()):
            if bias is None:
                S.op('act', lambda e: e.activation(out=out, in_=in_, func=func, scale=scale), r=r, w=w)
            else:
                S.op('act', lambda e: e.activation(out=out, in_=in_, func=func, bias=bias, scale=scale), r=r, w=w)

        def CP(eng, out, in_, r=(), w=()):
            if eng == 'act':
                S.op('act', lambda e: e.copy(out=out, in_=in_), r=r, w=w)
            else:
                S.op(eng, lambda e: e.tensor_copy(out=out, in_=in_), r=r, w=w)

        def RCP(out, in_, r=(), w=()):
            S.op('dve', lambda e: e.reciprocal(out=out, in_=in_), r=r, w=w)

        def bc_mid(ap2d, n):
            return ap2d.unsqueeze(1).broadcast_to([ap2d.shape[0], n, ap2d.shape[1]])

        def dump(name, ap, keys, shape=None):
            if name not in dbg or name in dbg_out:
                return
            dbg_out[name] = 1
            shp = list(ap.shape)
            o = dout("dbg_" + name, shp, ap.dtype)
            full = o if len(shp) == 2 else o
            S.dma(QM, o[tuple(slice(None) for _ in shp)], ap, r=keys)

        cst = T("cst", [128, NCST])
        S.dma(QM, cst[:], cst_d[:, :], w=['cst'])
        identf = cst[:, 0:128]; blockones = cst[:, 128:256]; maskT = cst[:, 256:384]; mask12 = cst[:, 384:640]
        mAtt = cst[:, 640:896]; mAtt0 = cst[:, 896:1152]; resetm = cst[:, 1280:1536]; I2 = cst[:, 1664:1728]
        identb = T("identb", [128, 128], BF16); protb = T("protb", [128, 128], BF16); onesb = T("onesb", [128, 128], BF16)
        CP('pool', identb[:], cst[:, 0:128], r=['cst'], w=['identb'])
        CP('pool', protb[:], cst[:, 1152:1280], r=['cst'], w=['protb'])
        CP('pool', onesb[:], cst[:, 1536:1664], r=['cst'], w=['onesb'])
        pv = T("pv", [128, 2 * PVL])
        S.dma(QM, pv[:], pv_d[:, :], w=['pv'])
        pd = T("pd", [128, 2, 24])
        for l in range(2):
            b0 = l * PVL
            TS('dve', pd[:, l, 0:14], pv[:, b0:b0 + 14], -1.0, ALU.mult, 1.0, ALU.add, r=['pv'], w=[('pd', l)])
            TS('dve', pd[:, l, 14:18], pv[:, b0 + 18:b0 + 22], -1.0, ALU.mult, 1.0, ALU.add, r=['pv'], w=[('pd', l)])
            ACT(pd[:, l, 18:22], pv[:, b0 + 74:b0 + 78], AF.Exp, r=['pv'], w=[('pd', l)])
        P = lambda l, a, b: pv[:, l * PVL + a: l * PVL + b]
        lora = T("lora", [128, 2, 2, 512], BF16)
        for l in range(2):
            S.dma('pool', lora[0:64, l, 0, :], d_up[l], w=[('lora', l)])
            S.dma('pool', lora[64:128, l, 0, :], i_up[l], w=[('lora', l)])
            S.dma('pool', lora[:, l, 1, :], g_up[l], w=[('lora', l)])

        def conv(dst, src, l, rows, key):
            for k in range(rows // 128):
                S.dma('pool', dst[l, k * 128:(k + 1) * 128, :], src[l, k * 128:(k + 1) * 128, :], w=[(key, l, k)])
        convspec = dict(wi=(wi_b, w_in, D), wbr=(wbr_b, w_brr, 512), wba=(wba_b, w_bra, 512), wo=(wo_b, w_out, D),
                        wu=(wu_b, w_up, D), wd=(wd_b, w_dn, DFF))
        converted = set()

        def ensure_conv(kind, l):
            if (kind, l) in converted:
                return
            converted.add((kind, l))
            dst, src, rows = convspec[kind]
            conv(dst, src, l, rows, kind)

        NSLOT = 3
        ring = [T("ring%d" % i, [128, 4096], BF16) for i in range(NSLOT)]
        wk2 = T("wk2", [128, 8, 2, 128], BF16)

        def wsrc(kind, l, i):
            if kind == 'wi':
                return wi_b[l].rearrange("(k p) n -> p k n", p=128)[:, :, i * 512:(i + 1) * 512], [('wi', l, k) for k in range(8)], [8, 512]
            if kind == 'wbr':
                return wbr_b[l].rearrange("(k p) n -> p k n", p=128), [('wbr', l, k) for k in range(4)], [4, 1024]
            if kind == 'wba':
                return wba_b[l].rearrange("(k p) n -> p k n", p=128), [('wba', l, k) for k in range(4)], [4, 1024]
            if kind == 'wo':
                return wo_b[l].rearrange("(k p) n -> p k n", p=128)[:, :, i * 512:(i + 1) * 512], [('wo', l, k) for k in range(8)], [8, 512]
            if kind == 'wu':
                return wu_b[l].rearrange("(k p) n -> p k n", p=128)[:, :, i * 512:(i + 1) * 512], [('wu', l, k) for k in range(8)], [8, 512]
            if kind == 'wd':
                return wd_b[l].rearrange("(f p) n -> p f n", p=128)[:, :, i * 128:(i + 1) * 128], [('wd', l, k) for k in range(32)], [32, 128]
        layer_loads = ([('wi', i) for i in range(5)] + [('wbr', 0), ('wi', 5), ('wi', 6), ('wba', 0), ('wi', 7), ('wi', 8),
                       ('wo', 0), ('wo', 1)] + [('wu', i) for i in range(8)] + [('wd', i) for i in range(8)])
        all_loads = [(kind, l, i) for g in range(NG + NSS) for l in range(2) for (kind, i) in layer_loads]
        wstate = dict(issued=0, used=0, done=set())

        def w_can_issue(n):
            return n < len(all_loads) and (n - NSLOT < 0 or (n - NSLOT) in wstate['done'])

        def w_issue():
            n = wstate['issued']
            kind, l, i = all_loads[n]
            ensure_conv(kind, l)
            src, keys, shp = wsrc(kind, l, i)
            slot = n % NSLOT
            dst = ring[slot][:].rearrange("p (a b) -> p a b", a=shp[0])
            S.dma('sp', dst, src, r=keys, w=[('ring', slot)])
            wstate['issued'] = n + 1

        def w_prefetch():
            while wstate['issued'] < min(wstate['used'] + NSLOT, len(all_loads)) and w_can_issue(wstate['issued']):
                w_issue()

        def w_get(kind, l, i):
            n = wstate['used']
            assert all_loads[n] == (kind, l, i), (all_loads[n], kind, l, i)
            wstate['used'] = n + 1
            while wstate['issued'] <= n:
                assert w_can_issue(wstate['issued']), ("ring slot still live", n)
                w_issue()
            w_prefetch()
            _, _, shp = wsrc(kind, l, i)
            slot = n % NSLOT
            return ring[slot][:].rearrange("p (a b) -> p a b", a=shp[0]), ('ring', slot), n

        def w_done(n):
            wstate['done'].add(n)
            w_prefetch()

        ps = es.enter_context(nc.psum_tensor("ps", [128, 3072], F32))
        pst = es.enter_context(nc.psum_tensor("pst", [128, 2048], BF16))

        def pk(c0, c1):
            return [('ps', b) for b in range(c0 // 512, (c1 - 1) // 512 + 1)]
        big = dict(i=0)

        def bigslot():
            i = big['i'] % 2
            big['i'] += 1
            c0 = 2048 + i * 512
            return ps[:, c0:c0 + 256], [('ps', 4 + i)]

        xT = T("xT", [128, 8, NT]); xTb = T("xTb", [128, 8, NT], BF16)
        xio = T("xio", [128, TL, D])
        Zc = [T("Zc%d" % i, [128, NT + 1]) for i in range(2)]
        CARRY = T("CARRY", [128, 2, 14])
        rT = T("rT", [128, 4, NT]); kraw = T("kraw", [128, 4, NT]); vT = T("vT", [128, 4, NT])
        tw = T("tw", [128, NT], BF16); sgd = T("sgd", [128, NT], BF16)
        ACRC = T("ACRC", [128, 4, TL, 2, 128], BF16)
        bcb = T("bcb", [128, 4, NT], BF16); kcb = T("kcb", [128, 4, NT], BF16); asb = T("asb", [128, 4, NT], BF16)
        vb = T("vb", [128, 4, NT], BF16); rs = T("rs", [128, 4, NT]); bonus = T("bonus", [128, 4, NT], BF16)
        gg = T("gg", [128, 4, NT], BF16); GC = T("GC", [128, 4, NT // 64])
        NTM = 9
        Tm = [T("Tm%d" % i, [128, NT]) for i in range(NTM)]
        Tn = [T("Tn%d" % i, [128, NT]) for i in range(NTM)]
        X = T("X", [128, 8, 128], BF16); VV = T("VV", [128, 8, 128], BF16)
        BcT = T("BcT", [128, 8, 64], BF16); KcT = T("KcT", [128, 8, 64], BF16)
        SC1 = T("SC1", [128, 8, 256], BF16); SC2 = T("SC2", [128, 8, 256], BF16)
        Pm = [T("Pm%d" % i, [128, 8, 128], BF16) for i in range(2)]
        PTm = [T("PTm%d" % i, [128, 8, 128], BF16) for i in range(2)]
        RhatT = T("RhatT", [128, 4, 128]); McT = T("McT", [128, 4, 2, 64]); H = T("H", [128, 2, 4, 64]); Nc = T("Nc", [128, 4, 2, 64])
        YT = T("YT", [128, 4, NT])
        yf = T("yf", [128, 4, NT], BF16); qT = T("qT", [128, 4, NT], BF16)
        KT2 = T("KT2", [128, 2, 2, 128 + NT], BF16); Vtm = T("Vtm", [128, 2, 1 + TL, 128], BF16)
        pT = T("pT", [128, 8, 2, 128], BF16); YA = T("YA", [128, 4, NT], BF16)
        cosT = T("cosT", [128, NT]); sinT = T("sinT", [128, NT])
        qraw = [T("qraw%d" % i, [128, NT], BF16) for i in range(2)]
        kf = T("kf", [128, 2, 128]); vf = T("vf", [128, 128])
        Gtmp = [T("Gtmp%d" % i, [128, NT], BF16) for i in range(2)]
        mixR = T("mixR", [128, 8, NT], BF16); mix = T("mix", [128, 8, NT], BF16)
        x1 = T("x1", [128, 8, NT])
        x1b = [T("x1b%d" % i, [128, NT], BF16) for i in range(2)]
        x1q = [T("x1q%d" % i, [128, NT], BF16) for i in range(2)]
        hT = T("hT", [128, 32, NT], BF16)
        ostage = T("ostage", [128, 256])
        dena = T("dena", [128, NT]); denb = T("denb", [128, NT])
        tmask = T("tmask", [128, NT]); SHF = T("SHF", [128, 2, 14])
        Snat = xio[0:64, 0, 0:512]; ckd = xio[:, 1, 0:256].rearrange("p (a b c) -> p a b c", a=2, b=2)
        S.dma(QM, tmask[:], tmask_d[:, :], w=['tmask'])

        S.op('pool', lambda e: e.memset(H[:], 0.0), w=[('H', 0), ('H', 1)])
        S.op('pool', lambda e: e.memset(CARRY[:], 0.0), w=[('CARRY', 0), ('CARRY', 1)])
        S.op('pool', lambda e: e.memset(KT2[:], 0.0), w=[('KT2', 0), ('KT2', 1)])
        S.op('pool', lambda e: e.memset(Vtm[:], 0.0), w=[('Vtm', 0), ('Vtm', 1)])
        S.op('pool', lambda e: e.memset(VV[:], 0.0), w=['VV'])

        def layernorm(l, ga, gb_, tag):
            s1 = ps[:, 0:NT]; s2 = ps[:, 512:512 + NT]
            for k in range(8):
                j = k % 2
                CP('act', x1b[j][:], x1[:, k, :], r=[('x1', k)], w=[('x1b', j)])
                ACT(x1q[j][:], x1[:, k, :], AF.Square, r=[('x1', k)], w=[('x1q', j)])
                MM(s1, onesb[:], x1b[j][:], start=(k == 0), stop=(k == 7), r=['onesb', ('x1b', j)], w=pk(0, NT))
                MM(s2, onesb[:], x1q[j][:], start=(k == 0), stop=(k == 7), r=['onesb', ('x1q', j)], w=pk(512, 512 + NT))
            mean, msq, var, rstd = Tm[0], Tm[1], Tm[2], Tm[3]
            ACT(mean[:], s1, AF.Copy, scale=1.0 / D, r=pk(0, NT), w=[('Tm', 0)])
            TT('pool', msq[:], mean[:], mean[:], ALU.mult, r=[('Tm', 0)], w=[('Tm', 1)])
            STT('dve', var[:], s2, 1.0 / D, msq[:], ALU.mult, ALU.subtract, r=pk(512, 512 + NT) + [('Tm', 1)], w=[('Tm', 2)])
            ACT(var[:], var[:], AF.Sqrt, bias=epsln[:, 0:1], r=[('Tm', 2), 'eps'], w=[('Tm', 2)])
            RCP(rstd[:], var[:], r=[('Tm', 2)], w=[('Tm', 3)])
            for k in range(8):
                d = Tm[4 + (k % 2)]
                TT('pool', d[:], x1[:, k, :], mean[:], ALU.subtract, r=[('x1', k), ('Tm', 0)], w=[('Tm', 4 + k % 2)])
                TT('dve', d[:], d[:], rstd[:], ALU.mult, r=[('Tm', 4 + k % 2), ('Tm', 3)], w=[('Tm', 4 + k % 2)])
                TS('dve', xT[:, k, :], d[:], P(l, ga + k, ga + k + 1), ALU.mult, P(l, gb_ + k, gb_ + k + 1), ALU.add,
                   r=[('Tm', 4 + k % 2), 'pv'], w=[('xT', k)])
                CP('act', xTb[:, k, :], xT[:, k, :], r=[('xT', k)], w=[('xTb', k)])

        epsln = T("epsln", [128, 2])
        S.op('pool', lambda e: e.memset(epsln[:, 0:1], LN_EPS), w=['eps'])
        S.op('pool', lambda e: e.memset(epsln[:, 1:2], GN_EPS), w=['eps'])

        for gi in range(NG + NSS):
            samp = gi >= NG
            g = gi if not samp else -1
            q = gi - NG
            t0g = g * NT
            if not samp:
                S.dma('pool', cosT[:], cos_d[:, t0g:t0g + NT], w=['cosT'])
                S.dma('pool', sinT[:], sin_d[:, t0g:t0g + NT], w=['sinT'])
            else:
                S.dma('pool', cosT[:], coss_d[:, :], w=['cosT'])
                S.dma('pool', sinT[:], sins_d[:, :], w=['sinT'])
            if upto < 1:
                S.disabled = True
            if not samp:
                S.dma('pool', xio[:], xp[t0g:t0g + NT, :].rearrange("(t p) d -> p t d", p=128), w=['xio'])
            else:
                S.op('pool', lambda e: e.memset(xio[:], 0.0), w=['xio'])
                S.dma('pool', xio[0:4, 0, :], xs[q], w=['xio'])
            for kp in range(4):
                reg = ps[:, kp * 512:(kp + 1) * 512]
                for kk in range(2):
                    k = kp * 2 + kk
                    for t in range(TL):
                        TR(ps[:, kp * 512 + kk * 256 + t * 128: kp * 512 + kk * 256 + (t + 1) * 128],
                           xio[:, t, k * 128:(k + 1) * 128], identf, r=['xio', 'cst'], w=pk(kp * 512, kp * 512 + 512))
                CP('act', xT[:, 2 * kp:2 * kp + 2, :], reg.rearrange("p (a b) -> p a b", a=2), r=pk(kp * 512, kp * 512 + 512),
                   w=[('xT', 2 * kp), ('xT', 2 * kp + 1)])
                CP('dve', xTb[:, 2 * kp:2 * kp + 2, :], reg.rearrange("p (a b) -> p a b", a=2), r=pk(kp * 512, kp * 512 + 512),
                   w=[('xTb', 2 * kp), ('xTb', 2 * kp + 1)])
            for l in range(2):
                last = (g == NG - 1)
                if samp:
                    S.dma(QM, Snat.rearrange("v (h k) -> v h k", k=64), swkv_i[l, q].rearrange("h v k -> v h k"), w=['xio'])
                    for c in range(4):
                        TR(ps[:, c * 64:(c + 1) * 64], Snat[:, c * 128:(c + 1) * 128], identf[0:64, 0:64], r=['xio', 'cst'], w=pk(0, 256))
                    CP('dve', H[:, l, :, :], ps[:, 0:256].rearrange("p (c j) -> p c j", j=64), r=pk(0, 256), w=[('H', l)])
                    S.dma(QM, CARRY[:, l, :], sshift_i[l, q].rearrange("(c p) -> p c", p=128), w=[('CARRY', l)], allow_slow_non_contiguous=True)
                    for dup in range(2):
                        S.dma(QM, ckd[:, :, dup, :], sck_i[l, q], w=['xio'])
                    for kvh in range(2):
                        TR(ps[:, 512 + kvh * 128:512 + (kvh + 1) * 128], ckd[:, kvh, :, :].rearrange("p a b -> p (a b)"), identf, r=['xio', 'cst'], w=pk(512, 1024))
                    CP('act', KT2[:, l, :, 0:128], ps[:, 512:768].rearrange("p (a b) -> p a b", b=128), r=pk(512, 1024), w=[('KT2', l)])
                    S.dma('pool', Vtm[:, l, 0, :], scv_i[l, q].rearrange("t h d -> t (h d)"), w=[('Vtm', l)])
                allx = [('xTb', k) for k in range(8)]
                ensure_conv('wi', l)
                for kvh in range(2):
                    for dup in range(2):
                        S.dma('pool', wk2[:, :, kvh, dup * 64:(dup + 1) * 64],
                              wi_b[l].rearrange("(k p) n -> p k n", p=128)[:, :, 2304 + kvh * 64: 2304 + (kvh + 1) * 64],
                              r=[('wi', l, k) for k in range(8)], w=['wk2'])
                if upto < 2:
                    S.disabled = True
                for blk in range(5):
                    W, wkey, wn = w_get('wi', l, blk)
                    for j in range(4):
                        c = blk * 4 + j
                        if c in (18, 19):
                            continue
                        po, pkey = bigslot()
                        for k in range(8):
                            MM(po, W[:, k, j * 128:(j + 1) * 128], xTb[:, k, :], start=(k == 0), stop=(k == 7),
                               r=[wkey, ('xTb', k)], w=pkey)
                        if c < 14:
                            z = Zc[c % 2]; zk = ('Zc', c % 2)
                            CP('act', z[:, 1:NT + 1], po, r=pkey, w=[zk])
                            CP('pool', z[:, 0:1], CARRY[:, l, c:c + 1], r=[('CARRY', l)], w=[zk])
                            tmp = Tm[c % 2]
                            TS('dve', tmp[:], z[:, 1:NT + 1], pd[:, l, c:c + 1], ALU.mult, r=[zk, ('pd', l)], w=[('Tm', c % 2)])
                            if c < 4:
                                dst, dk_ = rT[:, c, :], ('rT', c)
                            elif c < 8:
                                dst, dk_ = kraw[:, c - 4, :], ('kraw', c - 4)
                            elif c < 12:
                                dst, dk_ = vT[:, c - 8, :], ('vT', c - 8)
                            else:
                                dst, dk_ = Tm[2 + c % 2][:], ('Tm', 2 + c % 2)
                            STT('dve', dst, z[:, 0:NT], P(l, c, c + 1), tmp[:], ALU.mult, ALU.add,
                                r=[zk, 'pv', ('Tm', c % 2)], w=[dk_])
                            CP('pool', CARRY[:, l, c:c + 1], z[:, NT:NT + 1], r=[zk], w=[('CARRY', l)])
                            if samp:
                                CP('pool', SHF[:, l, c:c + 1], z[:, 4:5], r=[zk], w=[('SHF', l)])
                            if c == 12:
                                ACT(tw[0:64, :], dst[0:64, :], AF.Tanh, r=[dk_], w=['tw'])
                                CP('act', tw[64:128, :], dst[64:128, :], r=[dk_], w=['tw'])
                            if c == 13:
                                ACT(sgd[:], dst, AF.Sigmoid, r=[dk_], w=['sgd'])
                        else:
                            qi = c - 14
                            qr = qraw[qi % 2]; qk = ('qraw', qi % 2)
                            CP('act', qr[:], po, r=pkey, w=[qk])
                            p2, p2k = bigslot()
                            MM(p2, protb[:], qr[:], r=['protb', qk], w=p2k)
                            ta = Tm[4 + qi % 2]; tb_ = Tm[6 + qi % 2]
                            TT('dve', ta[:], p2, sinT[:], ALU.mult, r=p2k + ['sinT'], w=[('Tm', 4 + qi % 2)])
                            TT('pool', tb_[:], qr[:], cosT[:], ALU.mult, r=[qk, 'cosT'], w=[('Tm', 6 + qi % 2)])
                            TT('pool', qT[:, qi, :], ta[:], tb_[:], ALU.add, r=[('Tm', 4 + qi % 2), ('Tm', 6 + qi % 2)], w=[('qT', qi)])
                    if blk == 4:
                        for t in range(TL):
                            po, pkey = bigslot()
                            for k in range(8):
                                MM(po[:, 0:128], xTb[:, k, t * 128:(t + 1) * 128], W[:, k, 384:512], start=(k == 0), stop=(k == 7),
                                   r=[wkey, ('xTb', k)], w=pkey)
                            CP('act', Vtm[:, l, 1 + t, :], po[:, 0:128], r=pkey, w=[('Vtm', l)])
                            if (t == TL - 1 and last) or (samp and t == 0):
                                CP('dve', vf[:], po[:, 0:128], r=pkey, w=['vf'])
                    w_done(wn)
                for kvh in range(2):
                    po, pkey = bigslot()
                    for k in range(8):
                        MM(po, wk2[:, k, kvh, :], xTb[:, k, :], start=(k == 0), stop=(k == 7), r=['wk2', ('xTb', k)], w=pkey)
                    qr = qraw[kvh]; qk = ('qraw', kvh)
                    CP('act', qr[:], po, r=pkey, w=[qk])
                    p2, p2k = bigslot()
                    MM(p2, protb[:], qr[:], r=['protb', qk], w=p2k)
                    ta = Tm[4 + kvh]; tb_ = Tm[6 + kvh]
                    TT('dve', ta[:], p2, sinT[:], ALU.mult, r=p2k + ['sinT'], w=[('Tm', 4 + kvh)])
                    TT('pool', tb_[:], qr[:], cosT[:], ALU.mult, r=[qk, 'cosT'], w=[('Tm', 6 + kvh)])
                    TT('pool', KT2[:, l, kvh, 128:128 + NT], ta[:], tb_[:], ALU.add, r=[('Tm', 4 + kvh), ('Tm', 6 + kvh)], w=[('KT2', l)])
                    if last:
                        TT('pool', kf[:, kvh, :], ta[:, NT - 128:NT], tb_[:, NT - 128:NT], ALU.add,
                           r=[('Tm', 4 + kvh), ('Tm', 6 + kvh)], w=['kf'])
                    if samp:
                        TT('pool', kf[:, kvh, :], ta[:, 0:128], tb_[:, 0:128], ALU.add,
                           r=[('Tm', 4 + kvh), ('Tm', 6 + kvh)], w=['kf'])
                if upto < 3:
                    S.disabled = True
                def st6(t):
                    tsl = slice(t * 128, (t + 1) * 128)
                    first_tile = (g == 0 and t == 0 and not samp)
                    for h in (0, 2, 4, 6, 1, 3, 5, 7):
                        c = h // 2; pb = 64 * (h % 2); kvh = h // 4
                        MM(ps[:, h * 256:h * 256 + 128], KT2[pb:pb + 64, l, kvh, t * 128:(t + 1) * 128], qT[pb:pb + 64, c, tsl],
                           r=[('KT2', l), ('qT', c)], w=pk(h * 256, h * 256 + 128))
                        yield
                        MM(ps[:, h * 256 + 128:h * 256 + 256], KT2[pb:pb + 64, l, kvh, (t + 1) * 128:(t + 2) * 128], qT[pb:pb + 64, c, tsl],
                           r=[('KT2', l), ('qT', c)], w=pk(h * 256 + 128, h * 256 + 256))
                        yield
                    for q4 in range(4):
                        ACT(pT[:, q4 * 2:q4 * 2 + 2, :, :].rearrange("p a b c -> p (a b c)"), ps[:, q4 * 512:(q4 + 1) * 512], AF.Exp, scale=0.125,
                            r=pk(q4 * 512, q4 * 512 + 512), w=[('pT', q4)])
                        yield
                    mm_ = (mAtt0 if first_tile else mAtt)
                    ptk = [('pT', q4) for q4 in range(4)]
                    TT('pool', pT[:].rearrange("p a b c -> p a (b c)"), pT[:].rearrange("p a b c -> p a (b c)"), bc_mid(mm_, 8), ALU.mult,
                       r=ptk + ['cst'], w=ptk)
                    yield
                    for h in range(8):
                        c = h // 2; pb = 64 * (h % 2); kvh = h // 4
                        oo = ps[pb:pb + 64, c * 128:(c + 1) * 128]
                        MM(oo, Vtm[:, l, t, kvh * 64:(kvh + 1) * 64], pT[:, h, 0, :], start=True, stop=False, r=[('Vtm', l)] + ptk, w=pk(0, 512))
                        yield
                        MM(oo, Vtm[:, l, t + 1, kvh * 64:(kvh + 1) * 64], pT[:, h, 1, :], start=False, stop=True, r=[('Vtm', l)] + ptk, w=pk(0, 512))
                        yield
                        do = ps[pb:pb + 64, 512 + c * 128:512 + (c + 1) * 128]
                        MM(do, onesb[:, 0:64], pT[:, h, 0, :], start=True, stop=False, r=['onesb'] + ptk, w=pk(512, 1024))
                        yield
                        MM(do, onesb[:, 0:64], pT[:, h, 1, :], start=False, stop=True, r=['onesb'] + ptk, w=pk(512, 1024))
                        yield
                    den = dena; den2 = denb
                    dv = lambda tl: tl[:].rearrange("p (a b) -> p a b", b=128)
                    for c in range(4):
                        tgt = (den if c < 2 else den2)[:, (c % 2) * 128:(c % 2 + 1) * 128]
                        TS('dve', tgt, ps[:, 512 + c * 128:512 + (c + 1) * 128], pd[:, l, 18 + c:19 + c], ALU.add,
                           r=pk(512, 1024) + [('pd', l)], w=[('den', 0 if c < 2 else 1)])
                        yield
                    RCP(den[:], den[:], r=[('den', 0)], w=[('den', 0)])
                    yield
                    RCP(den2[:], den2[:], r=[('den', 1)], w=[('den', 1)])
                    yield
                    TT('dve', YA[:, 0:2, tsl], ps[:, 0:256].rearrange("p (a b) -> p a b", b=128), dv(den), ALU.mult,
                       r=pk(0, 512) + [('den', 0)], w=[('YA', 0), ('YA', 1)])
                    yield
                    TT('dve', YA[:, 2:4, tsl], ps[:, 256:512].rearrange("p (a b) -> p a b", b=128), dv(den2), ALU.mult,
                       r=pk(0, 512) + [('den', 1)], w=[('YA', 2), ('YA', 3)])
                    yield
                def st2(c, TS_, TK_):
                    sg, ic, kk_, t4, bT_, gs, E3, E1, rkk = TS_[0], TS_[1], TS_[2], TS_[3], TS_[4], TS_[5], TS_[6], TS_[7], TS_[8]
                    K = lambda i: (TK_, i)
                    cs = slice(c * 128, (c + 1) * 128)
                    p1, p1k = bigslot()
                    MM(p1, lora[0:64, l, 0, cs], tw[0:64, :], r=[('lora', l), 'tw'], w=p1k)
                    yield
                    ACT(sg[:], p1, AF.Sigmoid, bias=P(l, 34 + c, 35 + c), r=p1k + ['pv'], w=[K(0)])
                    yield
                    if samp:
                        TT('pool', sg[:], sg[:], tmask[:], ALU.mult, r=[K(0), 'tmask'], w=[K(0)])
                        yield
                    p2, p2k = bigslot()
                    MM(p2, lora[64:128, l, 0, cs], tw[64:128, :], r=[('lora', l), 'tw'], w=p2k)
                    yield
                    ACT(ic[:], p2, AF.Sigmoid, bias=P(l, 38 + c, 39 + c), r=p2k + ['pv'], w=[K(1)])
                    yield
                    p3, p3k = bigslot()
                    MM(p3, lora[:, l, 1, cs], sgd[:], r=[('lora', l), 'sgd'], w=p3k)
                    yield
                    CP('act', gg[:, c, :], p3, r=p3k, w=[('gg', c)])
                    yield
                    TS('dve', kk_[:], kraw[:, c, :], P(l, 14 + c, 15 + c), ALU.mult, r=[('kraw', c), 'pv'], w=[K(2)])
                    yield
                    TT('pool', t4[:], kk_[:], kk_[:], ALU.mult, r=[K(2)], w=[K(3)])
                    yield
                    p4, p4k = bigslot()
                    MM(p4, blockones, t4[:], r=['cst', K(3)], w=p4k)
                    yield
                    TS('dve', t4[:], p4, 1e-24, ALU.max, r=p4k, w=[K(3)])
                    yield
                    ACT(t4[:], t4[:], AF.Sqrt, r=[K(3)], w=[K(3)])
                    yield
                    RCP(t4[:], t4[:], r=[K(3)], w=[K(3)])
                    yield
                    TT('pool', kk_[:], kk_[:], t4[:], ALU.mult, r=[K(2), K(3)], w=[K(2)])
                    yield
                    if samp:
                        TT('pool', kk_[:], kk_[:], tmask[:], ALU.mult, r=[K(2), 'tmask'], w=[K(2)])
                        yield
                    TT('pool', bT_[:], kk_[:], ic[:], ALU.mult, r=[K(2), K(1)], w=[K(4)])
                    yield
                    TS('dve', ic[:], ic[:], P(l, 18 + c, 19 + c), ALU.mult, pd[:, l, 14 + c:15 + c], ALU.add,
                       r=[K(1), 'pv', ('pd', l)], w=[K(1)])
                    yield
                    TT('pool', ic[:], kraw[:, c, :], ic[:], ALU.mult, r=[('kraw', c), K(1)], w=[K(1)])
                    yield
                    if samp:
                        TT('pool', ic[:], ic[:], tmask[:], ALU.mult, r=[K(1), 'tmask'], w=[K(1)])
                        yield
                    S.op('dve', (lambda o_, d0, d1: (lambda e: e.tensor_tensor_scan(out=o_, data0=d0, data1=d1, initial=0.0,
                                                                                    op0=ALU.mult, op1=ALU.add)))(gs[:], resetm, sg[:]),
                         r=['cst', K(0)], w=[K(5)])
                    yield
                    ACT(E3[:], gs[:], AF.Exp, scale=-DECAY_C, r=[K(5)], w=[K(6)])
                    yield
                    ACT(gs[:], gs[:], AF.Exp, scale=DECAY_C, r=[K(5)], w=[K(5)])
                    yield
                    ACT(sg[:], sg[:], AF.Exp, scale=DECAY_C, r=[K(0)], w=[K(0)])
                    yield
                    nch = NT // 64
                    e3v = E3[:].rearrange("p (a b) -> p a b", b=64)
                    i3v = gs[:].rearrange("p (a b) -> p a b", b=64)
                    CP('pool', GC[:, c, :], e3v[:, :, 63], r=[K(6)], w=[('GC', c)])
                    yield
                    TT('dve', E1[:].rearrange("p (a b) -> p a b", b=64), e3v, i3v[:, :, 63:64].broadcast_to([128, nch, 64]),
                       ALU.mult, r=[K(6), K(5)], w=[K(7)])
                    yield
                    TT('dve', i3v, i3v, e3v[:, :, 63:64].broadcast_to([128, nch, 64]), ALU.mult,
                       r=[K(5), K(6)], w=[K(5)])
                    yield
                    acv = ACRC[:, c, :, 0, :]
                    rcv = ACRC[:, c, :, 1, :]
                    v3 = lambda ap: ap.rearrange("p (a b) -> p a b", b=128)
                    TT('pool', rcv, v3(rT[:, c, :]), v3(E1[:]), ALU.mult, r=[('rT', c), K(7)], w=[('ACRC', c)])
                    yield
                    TT('pool', rs[:, c, :], rT[:, c, :], E3[:], ALU.mult, r=[('rT', c), K(6)], w=[('rs', c)])
                    yield
                    STT('dve', sg[:], kk_[:], -1.0, sg[:], ALU.mult, ALU.mult, r=[K(2), K(0)], w=[K(0)])
                    yield
                    TT('dve', acv, v3(sg[:]), v3(E1[:]), ALU.mult, r=[K(0), K(7)], w=[('ACRC', c)])
                    yield
                    TT('pool', asb[:, c, :], sg[:], E3[:], ALU.mult, r=[K(0), K(6)], w=[('asb', c)])
                    yield
                    TT('pool', bcb[:, c, :], bT_[:], gs[:], ALU.mult, r=[K(4), K(5)], w=[('bcb', c)])
                    yield
                    TT('dve', kcb[:, c, :], ic[:], gs[:], ALU.mult, r=[K(1), K(5)], w=[('kcb', c)])
                    yield
                    CP('act', vb[:, c, :], vT[:, c, :], r=[('vT', c)], w=[('vb', c)])
                    yield
                    TT('pool', rkk[:], rT[:, c, :], ic[:], ALU.mult, r=[('rT', c), K(1)], w=[K(8)])
                    yield
                    TS('dve', rkk[:], rkk[:], P(l, 22 + c, 23 + c), ALU.mult, r=[K(8), 'pv'], w=[K(8)])
                    yield
                    p5, p5k = bigslot()
                    MM(p5, blockones, rkk[:], r=['cst', K(8)], w=p5k)
                    yield
                    TT('dve', bonus[:, c, :], p5, vT[:, c, :], ALU.mult, r=p5k + [('vT', c)], w=[('bonus', c)])
                    yield

                ntile6 = 1 if samp else TL
                for pi_, pair_ in enumerate(((0, 1), (2, 3))):
                    gens_ = [st2(c, Tm if c % 2 == 0 else Tn, 'Tm' if c % 2 == 0 else 'Tn') for c in pair_]
                    if pi_ < ntile6 and upto >= 6:
                        gens_.append(st6(pi_))
                    for _ in zip_longest(*gens_):
                        pass
                dump("rT", rT[:], [('rT', c) for c in range(4)])
                dump("rs", rs[:], [('rs', c) for c in range(4)])
                dump("asb", asb[:], [('asb', c) for c in range(4)])
                dump("bcb", bcb[:], [('bcb', c) for c in range(4)])
                dump("kcb", kcb[:], [('kcb', c) for c in range(4)])
                dump("ACRC", ACRC[:].rearrange("p a b c d -> p (a b c d)"), [('ACRC', c) for c in range(4)])
                if upto < 3.05:
                    S.disabled = True
                allp = [('asb', c) for c in range(4)] + [('bcb', c) for c in range(4)] + [('kcb', c) for c in range(4)] + [('vb', c) for c in range(4)]
                for t in range(1 if samp else TL):
                    tsl = slice(t * 128, (t + 1) * 128)
                    for c in range(4):
                        MM(ps[:, c * 128:(c + 1) * 128], asb[:, c, tsl], identb[:], r=[('asb', c), 'identb'], w=pk(0, 512))
                        MM(ps[:, 512 + c * 128:512 + (c + 1) * 128], bcb[:, c, tsl], identb[:], r=[('bcb', c), 'identb'], w=pk(512, 1024))
                        MM(ps[:, 1024 + c * 128:1024 + (c + 1) * 128], kcb[:, c, tsl], identb[:], r=[('kcb', c), 'identb'], w=pk(1024, 1536))
                        MM(ps[:, 1536 + c * 128:1536 + (c + 1) * 128], vb[:, c, tsl], identb[:], r=[('vb', c), 'identb'], w=pk(1536, 2048))
                    h64 = lambda ap: ap.rearrange("p (h j) -> p h j", j=64)
                    CP('act', X[:, :, 0:64], h64(ps[:, 0:512]), r=pk(0, 512), w=[('X', 0), ('X', 1)])
                    CP('dve', BcT[:], h64(ps[:, 512:1024]), r=pk(512, 1024), w=['BcT'])
                    CP('act', KcT[:], h64(ps[:, 1024:1536]), r=pk(1024, 1536), w=['KcT'])
                    CP('dve', VV[:, :, 64:128], h64(ps[:, 1536:2048]), r=pk(1536, 2048), w=['VV'])
                    if upto < 3.1:
                        S.disabled = True
                    m12 = bc_mid(mask12, 4)
                    for hg in range(2):
                        for hi in (0, 2, 1, 3):
                            h = hg * 4 + hi; c = h // 2; pb = 64 * (h % 2)
                            rhs2 = ACRC[pb:pb + 64, c, t, :, :].rearrange("p a b -> p (a b)")
                            MM(ps[:, hi * 256:(hi + 1) * 256], bcb[pb:pb + 64, c, tsl], rhs2, r=[('bcb', c), ('ACRC', c)], w=pk(hi * 256, hi * 256 + 256))
                            MM(ps[:, 1024 + hi * 256:1024 + (hi + 1) * 256], kcb[pb:pb + 64, c, tsl], rhs2, r=[('kcb', c), ('ACRC', c)],
                               w=pk(1024 + hi * 256, 1024 + hi * 256 + 256))
                        m12h = bc_mid(mask12, 2)
                        for bq in range(2):
                            TT('dve', SC1[:, hg * 4 + 2 * bq:hg * 4 + 2 * bq + 2, :], ps[:, bq * 512:(bq + 1) * 512].rearrange("p (h j) -> p h j", j=256), m12h, ALU.mult,
                               r=pk(bq * 512, bq * 512 + 512) + ['cst'], w=[('SC1', hg)])
                            TT('dve', SC2[:, hg * 4 + 2 * bq:hg * 4 + 2 * bq + 2, :], ps[:, 1024 + bq * 512:1024 + (bq + 1) * 512].rearrange("p (h j) -> p h j", j=256), m12h, ALU.mult,
                               r=pk(1024 + bq * 512, 1024 + bq * 512 + 512) + ['cst'], w=[('SC2', hg)])
                    if upto < 3.2:
                        S.disabled = True
                    sck = [('SC1', 0), ('SC1', 1)]; sck2 = [('SC2', 0), ('SC2', 1)]
                    for h in (0, 2, 4, 6, 1, 3, 5, 7):
                        c = h // 2; pb = 64 * (h % 2)
                        MM(ps[:, h * 128:(h + 1) * 128], ACRC[pb:pb + 64, c, t, 0, :], bcb[pb:pb + 64, c, tsl], r=[('ACRC', c), ('bcb', c)],
                           w=pk(h * 128, h * 128 + 128))
                    for bq in range(2):
                        TT('dve', Pm[0][:, 4 * bq:4 * bq + 4, :], ps[:, bq * 512:(bq + 1) * 512].rearrange("p (h j) -> p h j", j=128), bc_mid(maskT, 4), ALU.mult,
                           r=pk(bq * 512, bq * 512 + 512) + ['cst'], w=[('Pm', 0, bq)])
                    CP('pool', PTm[0][:], SC1[:, :, 0:128], r=sck, w=[('PTm', 0, 0), ('PTm', 0, 1)])
                    for h in range(8):
                        MM(ps[:, 1024 + h * 64:1024 + (h + 1) * 64], SC2[:, h, 0:128], VV[:, h, 64:128], r=sck2 + ['VV'],
                           w=pk(1024 + h * 64, 1024 + h * 64 + 64))
                    CP('act', X[:, :, 64:128], h64(ps[:, 1024:1536]), r=pk(1024, 1536), w=[('X', 0), ('X', 1)])
                    if upto < 3.3:
                        S.disabled = True
                    for j in range(6):
                        a = j % 2; b = 1 - a
                        for bq in range(2):
                            hs_ = range(4 * bq, 4 * bq + 4)
                            for h in hs_:
                                MM(ps[:, h * 128:(h + 1) * 128], PTm[a][:, h, :], X[:, h, :], r=[('PTm', a, bq), ('X', bq)], w=pk(h * 128, h * 128 + 128))
                            if j < 5:
                                for h in hs_:
                                    MM(ps[:, 1024 + h * 128:1024 + (h + 1) * 128], PTm[a][:, h, :], Pm[a][:, h, :], r=[('PTm', a, bq), ('Pm', a, bq)],
                                       w=pk(1024 + h * 128, 1024 + h * 128 + 128))
                                for h in hs_:
                                    MM(ps[:, 2048 + h * 128:2048 + (h + 1) * 128], Pm[a][:, h, :], PTm[a][:, h, :], r=[('PTm', a, bq), ('Pm', a, bq)],
                                       w=pk(2048 + h * 128, 2048 + h * 128 + 128))
                        for bq in range(2):
                            hs_ = slice(4 * bq, 4 * bq + 4)
                            TT('dve', X[:, hs_, :], ps[:, bq * 512:(bq + 1) * 512].rearrange("p (h j) -> p h j", j=128), X[:, hs_, :], ALU.add,
                               r=pk(bq * 512, bq * 512 + 512) + [('X', bq)], w=[('X', bq)])
                            if j < 5:
                                CP('act', Pm[b][:, hs_, :], ps[:, 1024 + bq * 512:1024 + (bq + 1) * 512].rearrange("p (h j) -> p h j", j=128),
                                   r=pk(1024 + bq * 512, 1536 + bq * 512), w=[('Pm', b, bq)])
                                CP('act' if bq == 0 else 'dve', PTm[b][:, hs_, :], ps[:, 2048 + bq * 512:2048 + (bq + 1) * 512].rearrange("p (h j) -> p h j", j=128),
                                   r=pk(2048 + bq * 512, 2560 + bq * 512), w=[('PTm', b, bq)])
                    if upto < 3.4:
                        S.disabled = True
                    for h in range(8):
                        c = h // 2; pb = 64 * (h % 2)
                        MM(ps[pb:pb + 64, c * 128:(c + 1) * 128], X[:, h, 0:64], SC1[:, h, 128:256], r=[('X', 0), ('X', 1)] + sck, w=pk(0, 512))
                    TT('dve', RhatT[:], ps[:, 0:512].rearrange("p (c j) -> p c j", j=128), rs[:, :, tsl], ALU.add,
                       r=pk(0, 512) + [('rs', c) for c in range(4)], w=['RhatT'])
                    if upto < 3.5:
                        S.disabled = True
                    mreg = [(512, 768), (1024, 1280)]; nreg = [(1536, 1792), (2048, 2304)]
                    for ch in range(2):
                        chs = slice(ch * 64, (ch + 1) * 64)
                        for h in range(8):
                            c = h // 2; pb = 64 * (h % 2)
                            MM(ps[pb:pb + 64, mreg[ch][0] + c * 64: mreg[ch][0] + (c + 1) * 64], X[chs, h, 0:64], BcT[chs, h, :],
                               r=[('X', 0), ('X', 1), 'BcT'], w=pk(*mreg[ch]))
                            no = ps[pb:pb + 64, nreg[ch][0] + c * 64: nreg[ch][0] + (c + 1) * 64]
                            MM(no, BcT[chs, h, :], X[chs, h, 64:128], start=True, stop=False, r=['BcT', ('X', 0), ('X', 1)], w=pk(*nreg[ch]))
                            MM(no, KcT[chs, h, :], VV[chs, h, 64:128], start=False, stop=True, r=['KcT', 'VV'], w=pk(*nreg[ch]))
                    for ch in range(2):
                        for c in range(4):
                            STT('dve', McT[:, c, ch, :], I2, GC[:, c, t * 2 + ch: t * 2 + ch + 1],
                                ps[:, mreg[ch][0] + c * 64: mreg[ch][0] + (c + 1) * 64], ALU.mult, ALU.add,
                                r=['cst', ('GC', c)] + pk(*mreg[ch]), w=['McT'])
                        CP('act', Nc[:, :, ch, :], ps[:, nreg[ch][0]:nreg[ch][1]].rearrange("p (c j) -> p c j", j=64), r=pk(*nreg[ch]), w=['Nc'])
                    if upto < 3.6:
                        S.disabled = True
                    for ch in range(2):
                        if ch == 1 and upto < 3.95:
                            S.disabled = True
                        if ch == 0 and t == 1 and upto < 3.97:
                            S.disabled = True
                        chs = slice(ch * 64, (ch + 1) * 64)
                        yreg = (2560, 2816)
                        for h in range(8):
                            c = h // 2; pb = 64 * (h % 2)
                            yo = ps[pb:pb + 64, 2560 + c * 64:2560 + (c + 1) * 64]
                            MM(yo, X[:, h, 64:128], SC1[:, h, 128 + ch * 64:128 + (ch + 1) * 64], start=True, stop=False, r=[('X', 0), ('X', 1)] + sck, w=pk(*yreg))
                            MM(yo, VV[:, h, 64:128], SC2[:, h, 128 + ch * 64:128 + (ch + 1) * 64], start=False, stop=False, r=['VV'] + sck2, w=pk(*yreg))
                            MM(yo, H[pb:pb + 64, l, c, :], RhatT[pb:pb + 64, c, chs], start=False, stop=True, r=[('H', l), 'RhatT'], w=pk(*yreg))
                        if upto < 3.7:
                            S.disabled = True
                        for h in (0, 2, 4, 6, 1, 3, 5, 7):
                            c = h // 2; pb = 64 * (h % 2)
                            hb = 0 if pb == 0 else 512
                            MM(ps[pb:pb + 64, hb + c * 64:hb + (c + 1) * 64], McT[pb:pb + 64, c, ch, :], H[pb:pb + 64, l, c, :],
                               r=['McT', ('H', l)], w=pk(hb, hb + 256))
                        if upto < 3.8:
                            S.disabled = True
                        CP('act', YT[:, :, t * 128 + ch * 64: t * 128 + (ch + 1) * 64], ps[:, 2560:2816].rearrange("p (c j) -> p c j", j=64),
                           r=pk(*yreg), w=[('YT', c) for c in range(4)])
                        if upto < 3.9:
                            S.disabled = True
                        TT('dve', H[0:64, l, :, :], ps[0:64, 0:256].rearrange("p (c j) -> p c j", j=64), Nc[0:64, :, ch, :], ALU.add,
                           r=pk(0, 256) + ['Nc'], w=[('H', l)])
                        TT('dve', H[64:128, l, :, :], ps[64:128, 512:768].rearrange("p (c j) -> p c j", j=64), Nc[64:128, :, ch, :], ALU.add,
                           r=pk(512, 768) + ['Nc'], w=[('H', l)])
                dump("YT", YT[:], [('YT', c) for c in range(4)])
                if upto < 5:
                    S.disabled = True
                def st5(c):
                    K = lambda i: ('Tm', i)
                    d_, dq_, sd = Tm[0 + 3 * (c % 2)], Tm[1 + 3 * (c % 2)], Tm[2 + 3 * (c % 2)]
                    k0, k1, k2 = K(0 + 3 * (c % 2)), K(1 + 3 * (c % 2)), K(2 + 3 * (c % 2))
                    p1, p1k = bigslot()
                    MM(p1, blockones, YT[:, c, :], r=['cst', ('YT', c)], w=p1k)
                    yield
                    STT('dve', d_[:], p1, -1.0 / 64, YT[:, c, :], ALU.mult, ALU.add, r=p1k + [('YT', c)], w=[k0])
                    yield
                    TT('pool', dq_[:], d_[:], d_[:], ALU.mult, r=[k0], w=[k1])
                    yield
                    p2, p2k = bigslot()
                    MM(p2, blockones, dq_[:], r=['cst', k1], w=p2k)
                    yield
                    ACT(sd[:], p2, AF.Sqrt, bias=epsln[:, 1:2], scale=1.0 / 64, r=p2k + ['eps'], w=[k2])
                    yield
                    RCP(sd[:], sd[:], r=[k2], w=[k2])
                    yield
                    TT('dve', d_[:], d_[:], sd[:], ALU.mult, r=[k0, k2], w=[k0])
                    yield
                    TS('dve', d_[:], d_[:], P(l, 26 + c, 27 + c), ALU.mult, P(l, 30 + c, 31 + c), ALU.add, r=[k0, 'pv'], w=[k0])
                    yield
                    TT('pool', d_[:], d_[:], bonus[:, c, :], ALU.add, r=[k0, ('bonus', c)], w=[k0])
                    yield
                    TT('pool', yf[:, c, :], d_[:], gg[:, c, :], ALU.mult, r=[k0, ('gg', c)], w=[('yf', c)])
                    yield

                for pair_ in ((0, 1), (2, 3)):
                    gens_ = [st5(c) for c in pair_]
                    for _ in zip_longest(*gens_):
                        pass
                dump("yf", yf[:], [('yf', c) for c in range(4)])
                dump("YA", YA[:], [('YA', c) for c in range(4)])
                if upto < 7:
                    S.disabled = True
                for br in range(2):
                    Wb, wbk, wbn = w_get('wbr' if br == 0 else 'wba', l, 0)
                    wgn = None
                    src_act = yf if br == 0 else YA
                    sk = 'yf' if br == 0 else 'YA'
                    for m in range(8):
                        if m % 4 == 0:
                            if wgn is not None:
                                w_done(wgn)
                            Wg, wgk, wgn = w_get('wi', l, 5 + 2 * br + m // 4)
                        po, pkey = bigslot()
                        for k in range(8):
                            MM(po, Wg[:, k, (m % 4) * 128:(m % 4 + 1) * 128], xTb[:, k, :], start=(k == 0), stop=(k == 7), r=[wgk, ('xTb', k)], w=pkey)
                        gt = Gtmp[m % 2]; gk = ('Gtmp', m % 2)
                        ACT(gt[:], po, AF.Sigmoid, r=pkey, w=[gk])
                        p2, p2k = bigslot()
                        for c in range(4):
                            MM(p2, Wb[:, c, m * 128:(m + 1) * 128], src_act[:, c, :], start=(c == 0), stop=(c == 3), r=[wbk, (sk, c)], w=p2k)
                        if br == 0:
                            TT('dve', mixR[:, m, :], p2, gt[:], ALU.mult, r=p2k + [gk], w=[('mixR', m)])
                        else:
                            tm_ = Tm[m % 2]
                            TT('dve', tm_[:], p2, gt[:], ALU.mult, r=p2k + [gk], w=[('Tm', m % 2)])
                            TT('pool', mix[:, m, :], tm_[:], mixR[:, m, :], ALU.add, r=[('Tm', m % 2), ('mixR', m)], w=[('mix', m)])
                    w_done(wgn)
                    w_done(wbn)
                for m in range(8):
                    if m % 4 == 0:
                        if m > 0:
                            w_done(won)
                        Wo, wok, won = w_get('wo', l, m // 4)
                    po, pkey = bigslot()
                    for k in range(8):
                        MM(po, Wo[:, k, (m % 4) * 128:(m % 4 + 1) * 128], mix[:, k, :], start=(k == 0), stop=(k == 7), r=[wok, ('mix', k)], w=pkey)
                    STT('dve', x1[:, m, :], xT[:, m, :], ALPHA, po, ALU.mult, ALU.add, r=[('xT', m)] + pkey, w=[('x1', m)])
                dump("mix", mix[:], [('mix', k) for k in range(8)])
                dump("x1pre", x1[:], [('x1', k) for k in range(8)])
                w_done(won)
                layernorm(l, 42, 50, 'ln1')
                dump("xln1", xT[:], [('xT', k) for k in range(8)])
                if upto < 8:
                    S.disabled = True
                for fb in range(8):
                    Wu, wuk, wun = w_get('wu', l, fb)
                    for j in range(4):
                        f = fb * 4 + j
                        po, pkey = bigslot()
                        for k in range(8):
                            MM(po, Wu[:, k, j * 128:(j + 1) * 128], xTb[:, k, :], start=(k == 0), stop=(k == 7), r=[wuk, ('xTb', k)], w=pkey)
                        gt = Gtmp[f % 2]; gk = ('Gtmp', f % 2)
                        ACT(gt[:], po, AF.Relu, r=pkey, w=[gk])
                        TT('dve', hT[:, f, :], gt[:], gt[:], ALU.mult, r=[gk], w=[('hT', f)])
                    w_done(wun)
                for m in range(8):
                    Wd, wdk, wdn = w_get('wd', l, m)
                    po, pkey = bigslot()
                    for f in range(32):
                        MM(po, Wd[:, f, :], hT[:, f, :], start=(f == 0), stop=(f == 31), r=[wdk, ('hT', f)], w=pkey)
                    STT('dve', x1[:, m, :], xT[:, m, :], ALPHA, po, ALU.mult, ALU.add, r=[('xT', m)] + pkey, w=[('x1', m)])
                    w_done(wdn)
                layernorm(l, 58, 66, 'ln2')
                if upto < 9:
                    S.disabled = True
                CP('pool', KT2[:, l, :, 0:128], KT2[:, l, :, NT:NT + 128], r=[('KT2', l)], w=[('KT2', l)])
                CP('pool', Vtm[:, l, 0, :], Vtm[:, l, TL, :], r=[('Vtm', l)], w=[('Vtm', l)])
                if samp:
                    for c in range(4):
                        TR(ps[0:64, c * 128:(c + 1) * 128], H[:, l, c, :], identf, r=[('H', l), 'cst'], w=pk(0, 512))
                    CP('act', ostage[0:64, 0:256], ps[0:64, 0:256], r=pk(0, 512), w=['ostage'])
                    S.dma(QM, so_wkv[l, q, 0:4].rearrange("h v k -> v h k"), ostage[0:64, 0:256].rearrange("p (h k) -> p h k", k=64), r=['ostage'])
                    CP('act', ostage[0:64, 0:256], ps[0:64, 256:512], r=pk(0, 512), w=['ostage'])
                    S.dma(QM, so_wkv[l, q, 4:8].rearrange("h v k -> v h k"), ostage[0:64, 0:256].rearrange("p (h k) -> p h k", k=64), r=['ostage'])
                    S.dma(QM, so_shift[l, q].rearrange("(c p) -> p c", p=128), SHF[:, l, :], r=[('SHF', l)], allow_slow_non_contiguous=True)
                    for kvh in range(2):
                        TR(ps[:, 512 + kvh * 64:512 + (kvh + 1) * 64], kf[0:64, kvh, :], identf[0:64, 0:64], r=['kf', 'cst'], w=pk(512, 1024))
                    CP('act', ostage[:, 0:128], ps[:, 512:640], r=pk(512, 1024), w=['ostage'])
                    S.dma(QM, so_ck[l, q, 124:128].rearrange("t h d -> t (h d)"), ostage[0:4, 0:128], r=['ostage'])
                    S.dma(QM, so_cv[l, q, 124:128].rearrange("t h d -> t (h d)"), vf[0:4, :], r=['vf'])
                    S.dma(QM, so_ck[l, q, 0:124].rearrange("t h d -> t (h d)"), sck_i[l, q, 4:128].rearrange("t h d -> t (h d)"))
                    S.dma(QM, so_cv[l, q, 0:124].rearrange("t h d -> t (h d)"), scv_i[l, q, 4:128].rearrange("t h d -> t (h d)"))
                if last:
                    for c in range(4):
                        TR(ps[0:64, c * 128:(c + 1) * 128], H[:, l, c, :], identf, r=[('H', l), 'cst'], w=pk(0, 512))
                    CP('act', ostage[0:64, 0:512 // 2 * 0 + 256], ps[0:64, 0:256], r=pk(0, 512), w=['ostage'])
                    S.dma(QM, o_wkv[l, 0:4].rearrange("h v k -> v h k"), ostage[0:64, 0:256].rearrange("p (h k) -> p h k", k=64), r=['ostage'])
                    CP('act', ostage[0:64, 0:256], ps[0:64, 256:512], r=pk(0, 512), w=['ostage'])
                    S.dma(QM, o_wkv[l, 4:8].rearrange("h v k -> v h k"), ostage[0:64, 0:256].rearrange("p (h k) -> p h k", k=64), r=['ostage'])
                    S.dma(QM, o_shift[l].rearrange("(c p) -> p c", p=128), CARRY[:, l, :], r=[('CARRY', l)], allow_slow_non_contiguous=True)
                    for kvh in range(2):
                        TR(ps[:, 512 + kvh * 64:512 + (kvh + 1) * 64], kf[0:64, kvh, :], identf[0:64, 0:64], r=['kf', 'cst'], w=pk(512, 1024))
                    CP('act', ostage[:, 0:128], ps[:, 512:640], r=pk(512, 1024), w=['ostage'])
                    S.dma(QM, o_ck[l].rearrange("t h d -> t (h d)"), ostage[:, 0:128], r=['ostage'])
                    S.dma(QM, o_cv[l].rearrange("t h d -> t (h d)"), vf[:], r=['vf'])
            for t in range(TL):
                for kp in range(2):
                    for kk in range(4):
                        k = kp * 4 + kk
                        TR(ps[:, kp * 512 + kk * 128: kp * 512 + (kk + 1) * 128], xT[:, k, t * 128:(t + 1) * 128], identf,
                           r=[('xT', k), 'cst'], w=pk(kp * 512, kp * 512 + 512))
                    CP('act' if kp == 0 else 'dve', xio[:, t, kp * 512:(kp + 1) * 512], ps[:, kp * 512:(kp + 1) * 512],
                       r=pk(kp * 512, kp * 512 + 512), w=['xio'])
            if not samp:
                S.dma(QM, y_p[t0g:t0g + NT, :].rearrange("(t p) d -> p t d", p=128), xio[:], r=['xio'])
            else:
                S.dma(QM, y_s[q], xio[0:4, 0, :], r=['xio'])
        stats = S.emit()
    return nc, stats


def pack_pv(inp):
    pv = np.zeros((128, 2 * PVL), np.float32)
    for l in range(2):
        b = l * PVL
        pv[:, b:b + 14] = inp['mu_shift'][l].reshape(14, 128).T
        for off, name in ((14, 'k_k'), (18, 'k_a'), (26, 'lnx_g'), (30, 'lnx_b'), (34, 'decay_base'), (38, 'iclr_base')):
            pv[:, b + off:b + off + 4] = inp[name][l].reshape(4, 128).T
        pv[:, b + 22:b + 26] = inp['r_k'][l].reshape(512).reshape(4, 128).T
        for off, name in ((42, 'ln1_g'), (50, 'ln1_b'), (58, 'ln2_g'), (66, 'ln2_b')):
            pv[:, b + off:b + off + 8] = inp[name][l].reshape(8, 128).T
        pv[:, b + 74:b + 78] = np.repeat(inp['sinks'][l].reshape(4, 2), 64, axis=1).T
    return pv


_CACHE = {}


def host_inputs(inp, c, SEQ, NSS, consts):
    cst, cosT, sinT, coss, sins, tmask, pv = consts
    wnames = ['w_in', 'w_br_rwkv', 'w_br_attn', 'w_out', 'w_ff_up', 'w_ff_down', 'decay_up', 'iclr_up', 'gate_up']
    m = {k: inp[k] for k in wnames}
    sl = slice(c * NSS, (c + 1) * NSS)
    m.update(xp=inp['x_prompt'][(c * 2) // 8][:SEQ], pv_in=pv, cst_in=cst, cos_in=cosT, sin_in=sinT, coss_in=coss, sins_in=sins, tmask_in=tmask,
             xs=np.ascontiguousarray(inp['x_sample'][sl]), swkv_i=np.ascontiguousarray(inp['state_wkv'][:, sl]),
             sshift_i=np.ascontiguousarray(inp['state_shift'][:, sl]), sck_i=np.ascontiguousarray(inp['cache_k_win'][:, sl]),
             scv_i=np.ascontiguousarray(inp['cache_v_win'][:, sl]))
    return m


def host_consts(inp, SEQ, past_len=8192):
    cst = make_consts()
    cosT, sinT = rope_tables(np.arange(SEQ))
    coss, sins = rope_tables(past_len + np.arange(NT))
    tmask = np.zeros((128, NT), np.float32)
    tmask[:, 0:4] = 1.0
    return cst, cosT, sinT, coss, sins, tmask, pack_pv(inp)


def kernel(**inputs):
    inp = {k: np.ascontiguousarray(np.asarray(v)) for k, v in inputs.items()}
    B, SEQ, _ = inp['x_prompt'].shape
    NSS = inp['x_sample'].shape[0] // 8
    if SEQ not in _CACHE:
        _CACHE[SEQ] = build(SEQ, NSS)
    nc, _ = _CACHE[SEQ]
    consts = host_consts(inp, SEQ)
    in_maps = [host_inputs(inp, c, SEQ, NSS, consts) for c in range(8)]
    res = run_bass_kernel_spmd(nc, in_maps, core_ids=list(range(8))).results
    y_p = np.stack([res[0]['y_p'], res[4]['y_p']])
    p_wkv = np.stack([res[0]['p_wkv'], res[4]['p_wkv']], 1)
    p_shift = np.stack([res[0]['p_shift'], res[4]['p_shift']], 1)
    p_ck = np.stack([res[0]['p_ck'], res[4]['p_ck']], 1)
    p_cv = np.stack([res[0]['p_cv'], res[4]['p_cv']], 1)
    y_s = np.concatenate([res[c]['y_s'] for c in range(8)], 0)
    s_wkv = np.concatenate([res[c]['s_wkv'] for c in range(8)], 1)
    s_shift = np.concatenate([res[c]['s_shift'] for c in range(8)], 1)
    s_ck = np.concatenate([res[c]['s_ck'] for c in range(8)], 1)
    s_cv = np.concatenate([res[c]['s_cv'] for c in range(8)], 1)
    return (y_p, y_s, p_wkv, p_shift, p_ck, p_cv, s_wkv, s_shift, s_ck, s_cv)
```

```python
import numpy as np
from contextlib import ExitStack
from itertools import zip_longest
import concourse.bass as bass
import concourse.mybir as mybir
from concourse.ap import AP
from concourse.bass_utils import run_bass_kernel_spmd

F32 = mybir.dt.float32
BF16 = mybir.dt.bfloat16
AF = mybir.ActivationFunctionType
ALU = mybir.AluOpType

D = 1024
NT = 256
TL = NT // 128
SHIFT_W = 1792
IN_W = 4608
DFF = 4096
ALPHA = 4 ** 0.25
LN_EPS = 1e-5
GN_EPS = 64e-5
DECAY_C = 0.6065306597126334
PVL = 78
QM = 'pool'
NCST = 1728


class Sched:
    COMPUTE = ('pe', 'act', 'dve', 'pool')

    def __init__(self, nc, es, n_dma_sems=24):
        self.nc = nc
        self.h = {'pe': nc.tensor, 'act': nc.scalar, 'dve': nc.vector, 'pool': nc.gpsimd, 'sp': nc.sync}
        self.ops = []
        self.last_w = {}
        self.readers = {}
        self.sem = {e: es.enter_context(nc.semaphore("s_" + e)) for e in self.COMPUTE}
        self.dsem = []
        self.dq = {}
        for q, n in (('sp', n_dma_sems), ('act', 8), ('pool', 12)):
            self.dq[q] = list(range(len(self.dsem), len(self.dsem) + n))
            self.dsem += [es.enter_context(nc.semaphore("d%s%d" % (q, i))) for i in range(n)]

    def _add(self, kind, eng, fn, r, w):
        isps = lambda k: isinstance(k, tuple) and k[0] in ('ps', 'pst')
        w = list(w) + [k for k in r if isps(k)]
        r = [k for k in r if not isps(k)]
        oid = len(self.ops)
        deps = set()
        for k in r:
            if k in self.last_w:
                deps.add(self.last_w[k])
        for k in w:
            if k in self.last_w:
                deps.add(self.last_w[k])
            deps |= self.readers.get(k, set())
        for k in r:
            self.readers.setdefault(k, set()).add(oid)
        for k in w:
            self.last_w[k] = oid
            self.readers[k] = set()
        deps.discard(oid)
        self.ops.append(dict(kind=kind, eng=eng, fn=fn, deps=deps))
        return oid

    disabled = False

    def op(self, eng, fn, r=(), w=()):
        if self.disabled:
            return None
        return self._add('c', eng, fn, r, w)

    def dma(self, eng, out, in_, r=(), w=(), **kw):
        if self.disabled:
            return None
        return self._add('d', eng, (out, in_, kw), r, w)

    def emit(self):
        ops = self.ops
        need = [False] * len(ops)
        for i, o in enumerate(ops):
            for d in o['deps']:
                p = ops[d]
                if p['kind'] == 'c':
                    if p['eng'] == o['eng'] and o['kind'] == 'c' and p['eng'] == 'pe':
                        continue
                    need[d] = True
        cnt = {e: 0 for e in self.COMPUTE}
        tok = [None] * len(ops)
        seen = {}
        dcount = [0] * len(self.dsem)
        dk = {q: 0 for q in self.dq}
        nwaits = 0
        acts = {e: [] for e in self.h}
        for i, o in enumerate(ops):
            e = o['eng']
            wl = {}
            for d in o['deps']:
                p = ops[d]
                if p['kind'] == 'c' and p['eng'] == e and o['kind'] == 'c' and e == 'pe':
                    continue
                t = tok[d]
                if t is None:
                    continue
                ts, tv = t
                if tv > wl.get(id(ts), (ts, 0))[1]:
                    wl[id(ts)] = (ts, tv)
            if o['kind'] == 'd':
                j = self.dq[e][dk[e] % len(self.dq[e])]
                dk[e] += 1
                dsj = self.dsem[j]
                if dcount[j] > 0 and dcount[j] > wl.get(id(dsj), (dsj, 0))[1]:
                    wl[id(dsj)] = (dsj, dcount[j])
            for ws, wv in wl.values():
                key = (e, id(ws))
                if seen.get(key, 0) >= wv:
                    continue
                acts[e].append((lambda s_, v_: (lambda h: h.wait_ge(s_, v_)))(ws, wv))
                nwaits += 1
                seen[key] = wv
            if o['kind'] == 'c':
                if need[i]:
                    cnt[e] += 1
                    acts[e].append((lambda fn_, sm_: (lambda h: fn_(h).then_inc(sm_, 1)))(o['fn'], self.sem[e]))
                    tok[i] = (self.sem[e], cnt[e])
                else:
                    acts[e].append(o['fn'])
            else:
                out, in_, kw = o['fn']
                dcount[j] += 16
                acts[e].append((lambda o_, i_, k_, s_: (lambda h: h.dma_start(out=o_, in_=i_, **k_).then_inc(s_, 16)))(out, in_, kw, dsj))
                tok[i] = (dsj, dcount[j])
        for j, fs in enumerate(self.dsem):
            if dcount[j] > 0:
                acts['sp'].append((lambda s_, v_: (lambda h: h.wait_ge(s_, v_)))(fs, dcount[j]))
        with self.nc.Block() as block:
            @block.sync
            def _(h):
                for a in acts['sp']:
                    a(h)

            @block.tensor
            def _(h):
                for a in acts['pe']:
                    a(h)

            @block.scalar
            def _(h):
                for a in acts['act']:
                    a(h)

            @block.vector
            def _(h):
                for a in acts['dve']:
                    a(h)

            @block.gpsimd
            def _(h):
                for a in acts['pool']:
                    a(h)
        return dict(n_ops=len(ops), n_waits=nwaits, signals=dict(cnt))


def make_consts():
    c = np.zeros((128, NCST), np.float32)
    idx = np.arange(128)
    c[:, 0:128] = np.eye(128)
    c[:, 128:256] = (idx[:, None] // 64 == idx[None, :] // 64)
    same = (idx[:, None] // 64) == (idx[None, :] // 64)
    mstrict = same & (idx[:, None] < idx[None, :])
    mincl = same & (idx[:, None] <= idx[None, :])
    c[:, 256:384] = mstrict.T
    c[:, 384:512] = mstrict
    c[:, 512:640] = mincl
    c[:, 640:768] = idx[:, None] >= idx[None, :]
    c[:, 768:896] = idx[:, None] <= idx[None, :]
    c[:, 896:1024] = 0.0
    c[:, 1024:1152] = idx[:, None] <= idx[None, :]
    prot = np.zeros((128, 128), np.float32)
    for hb in (0, 64):
        for dd in range(32):
            prot[hb + dd + 32, hb + dd] = -1.0
            prot[hb + dd, hb + dd + 32] = 1.0
    c[:, 1152:1280] = prot
    rm = np.ones((128, 256), np.float32)
    rm[:, 0::64] = 0.0
    c[:, 1280:1536] = rm
    c[:, 1536:1664] = 1.0
    c[:, 1664:1728] = (idx[:, None] % 64) == np.arange(64)[None, :]
    return c


def rope_tables(pos):
    half = 32
    inv = (10000.0 ** (-np.arange(half, dtype=np.float32) / half)).astype(np.float32)
    ang = pos.astype(np.float32)[None, :] * inv[:, None]
    cos = np.cos(ang).astype(np.float32)
    sin = np.sin(ang).astype(np.float32)
    return np.tile(cos, (4, 1)), np.tile(sin, (4, 1))


def build(SEQ, NSS=16, dbg=(), upto=99, noconv=False):
    NG = SEQ // NT
    NTS = NSS * 4
    nc = bass.Bass("TRN2", target_bir_lowering=False)
    din = lambda name, shape, dt=F32: nc.dram_tensor(name, list(shape), dt, kind="ExternalInput").ap()
    dout = lambda name, shape, dt=F32: nc.dram_tensor(name, list(shape), dt, kind="ExternalOutput").ap()
    dscr = lambda name, shape, dt=BF16: nc.dram_tensor(name, list(shape), dt).ap()

    xp = din("xp", [SEQ, D])
    w_in = din("w_in", [2, D, IN_W]); w_brr = din("w_br_rwkv", [2, 512, D]); w_bra = din("w_br_attn", [2, 512, D])
    w_out = din("w_out", [2, D, D]); w_up = din("w_ff_up", [2, D, DFF]); w_dn = din("w_ff_down", [2, DFF, D])
    d_up = din("decay_up", [2, 64, 512]); i_up = din("iclr_up", [2, 64, 512]); g_up = din("gate_up", [2, 128, 512])
    pv_d = din("pv_in", [128, 2 * PVL]); cst_d = din("cst_in", [128, NCST])
    cos_d = din("cos_in", [128, SEQ]); sin_d = din("sin_in", [128, SEQ])

    xs = din("xs", [NSS, 4, D]); swkv_i = din("swkv_i", [2, NSS, 8, 64, 64]); sshift_i = din("sshift_i", [2, NSS, SHIFT_W])
    sck_i = din("sck_i", [2, NSS, 128, 2, 64]); scv_i = din("scv_i", [2, NSS, 128, 2, 64])
    coss_d = din("coss_in", [128, NT]); sins_d = din("sins_in", [128, NT]); tmask_d = din("tmask_in", [128, NT])
    y_s = dout("y_s", [NSS, 4, D]); so_wkv = dout("s_wkv", [2, NSS, 8, 64, 64]); so_shift = dout("s_shift", [2, NSS, SHIFT_W])
    so_ck = dout("s_ck", [2, NSS, 128, 2, 64]); so_cv = dout("s_cv", [2, NSS, 128, 2, 64])
    y_p = dout("y_p", [SEQ, D]); o_wkv = dout("p_wkv", [2, 8, 64, 64]); o_shift = dout("p_shift", [2, SHIFT_W])
    o_ck = dout("p_ck", [2, 128, 2, 64]); o_cv = dout("p_cv", [2, 128, 2, 64])
    dbg_out = {}

    wi_b = dscr("wi_b", [2, D, IN_W]); wbr_b = dscr("wbr_b", [2, 512, D]); wba_b = dscr("wba_b", [2, 512, D])
    wo_b = dscr("wo_b", [2, D, D]); wu_b = dscr("wu_b", [2, D, DFF]); wd_b = dscr("wd_b", [2, DFF, D])

    with ExitStack() as es:
        S = Sched(nc, es)
        T = lambda name, shape, dt=F32: es.enter_context(nc.sbuf_tensor(name, list(shape), dt))
        def MM(out, lhsT, rhs, start=True, stop=True, r=(), w=()):
            S.op('pe', lambda e: e.matmul(out, lhsT=lhsT, rhs=rhs, start=start, stop=stop), r=r, w=w)

        def TR(out, in_, ident, r=(), w=()):
            S.op('pe', lambda e: e.transpose(out, in_, ident), r=r, w=w)

        def TT(eng, out, in0, in1, op, r=(), w=()):
            S.op(eng, lambda e: e.tensor_tensor(out=out, in0=in0, in1=in1, op=op), r=r, w=w)

        def TS(eng, out, in0, s1, op0, s2=None, op1=None, r=(), w=()):
            if op1 is None:
                S.op(eng, lambda e: e.tensor_scalar(out=out, in0=in0, scalar1=s1, scalar2=None, op0=op0), r=r, w=w)
            else:
                S.op(eng, lambda e: e.tensor_scalar(out=out, in0=in0, scalar1=s1, scalar2=s2, op0=op0, op1=op1), r=r, w=w)

        def STT(eng, out, in0, scalar, in1, op0, op1, r=(), w=()):
            S.op(eng, lambda e: e.scalar_tensor_tensor(out=out, in0=in0, scalar=scalar, in1=in1, op0=op0, op1=op1), r=r, w=w)

        def ACT(out, in_, func, bias=None, scale=1.0, r=(), w=()):
            if bias is None:
                S.op('act', lambda e: e.activation(out=out, in_=in_, func=func, scale=scale), r=r, w=w)
            else:
                S.op('act', lambda e: e.activation(out=out, in_=in_, func=func, bias=bias, scale=scale), r=r, w=w)

        def CP(eng, out, in_, r=(), w=()):
            if eng == 'act':
                S.op('act', lambda e: e.copy(out=out, in_=in_), r=r, w=w)
            else:
                S.op(eng, lambda e: e.tensor_copy(out=out, in_=in_), r=r, w=w)

        def RCP(out, in_, r=(), w=()):
            S.op('dve', lambda e: e.reciprocal(out=out, in_=in_), r=r, w=w)

        def bc_mid(ap2d, n):
            return ap2d.unsqueeze(1).broadcast_to([ap2d.shape[0], n, ap2d.shape[1]])

        def dump(name, ap, keys, shape=None):
            if name not in dbg or name in dbg_out:
                return
            dbg_out[name] = 1
            shp = list(ap.shape)
            o = dout("dbg_" + name, shp, ap.dtype)
            full = o if len(shp) == 2 else o
            S.dma(QM, o[tuple(slice(None) for _ in shp)], ap, r=keys)

        cst = T("cst", [128, NCST])
        S.dma(QM, cst[:], cst_d[:, :], w=['cst'])
        identf = cst[:, 0:128]; blockones = cst[:, 128:256]; maskT = cst[:, 256:384]; mask12 = cst[:, 384:640]
        mAtt = cst[:, 640:896]; mAtt0 = cst[:, 896:1152]; resetm = cst[:, 1280:1536]; I2 = cst[:, 1664:1728]
        identb = T("identb", [128, 128], BF16); protb = T("protb", [128, 128], BF16); onesb = T("onesb", [128, 128], BF16)
        CP('pool', identb[:], cst[:, 0:128], r=['cst'], w=['identb'])
        CP('pool', protb[:], cst[:, 1152:1280], r=['cst'], w=['protb'])
        CP('pool', onesb[:], cst[:, 1536:1664], r=['cst'], w=['onesb'])
        pv = T("pv", [128, 2 * PVL])
        S.dma(QM, pv[:], pv_d[:, :], w=['pv'])
        pd = T("pd", [128, 2, 24])
        for l in range(2):
            b0 = l * PVL
            TS('dve', pd[:, l, 0:14], pv[:, b0:b0 + 14], -1.0, ALU.mult, 1.0, ALU.add, r=['pv'], w=[('pd', l)])
            TS('dve', pd[:, l, 14:18], pv[:, b0 + 18:b0 + 22], -1.0, ALU.mult, 1.0, ALU.add, r=['pv'], w=[('pd', l)])
            ACT(pd[:, l, 18:22], pv[:, b0 + 74:b0 + 78], AF.Exp, r=['pv'], w=[('pd', l)])
        P = lambda l, a, b: pv[:, l * PVL + a: l * PVL + b]
        lora = T("lora", [128, 2, 2, 512], BF16)
        for l in range(2):
            S.dma('pool', lora[0:64, l, 0, :], d_up[l], w=[('lora', l)])
            S.dma('pool', lora[64:128, l, 0, :], i_up[l], w=[('lora', l)])
            S.dma('pool', lora[:, l, 1, :], g_up[l], w=[('lora', l)])

        def conv(dst, src, l, rows, key):
            for k in range(rows // 128):
                S.dma('pool', dst[l, k * 128:(k + 1) * 128, :], src[l, k * 128:(k + 1) * 128, :], w=[(key, l, k)])
        convspec = dict(wi=(wi_b, w_in, D), wbr=(wbr_b, w_brr, 512), wba=(wba_b, w_bra, 512), wo=(wo_b, w_out, D),
                        wu=(wu_b, w_up, D), wd=(wd_b, w_dn, DFF))
        converted = set()

        def ensure_conv(kind, l):
            if (kind, l) in converted:
                return
            converted.add((kind, l))
            dst, src, rows = convspec[kind]
            conv(dst, src, l, rows, kind)

        NSLOT = 3
        ring = [T("ring%d" % i, [128, 4096], BF16) for i in range(NSLOT)]
        wk2 = T("wk2", [128, 8, 2, 128], BF16)

        def wsrc(kind, l, i):
            if kind == 'wi':
                return wi_b[l].rearrange("(k p) n -> p k n", p=128)[:, :, i * 512:(i + 1) * 512], [('wi', l, k) for k in range(8)], [8, 512]
            if kind == 'wbr':
                return wbr_b[l].rearrange("(k p) n -> p k n", p=128), [('wbr', l, k) for k in range(4)], [4, 1024]
            if kind == 'wba':
                return wba_b[l].rearrange("(k p) n -> p k n", p=128), [('wba', l, k) for k in range(4)], [4, 1024]
            if kind == 'wo':
                return wo_b[l].rearrange("(k p) n -> p k n", p=128)[:, :, i * 512:(i + 1) * 512], [('wo', l, k) for k in range(8)], [8, 512]
            if kind == 'wu':
                return wu_b[l].rearrange("(k p) n -> p k n", p=128)[:, :, i * 512:(i + 1) * 512], [('wu', l, k) for k in range(8)], [8, 512]
            if kind == 'wd':
                return wd_b[l].rearrange("(f p) n -> p f n", p=128)[:, :, i * 128:(i + 1) * 128], [('wd', l, k) for k in range(32)], [32, 128]
        layer_loads = ([('wi', i) for i in range(5)] + [('wbr', 0), ('wi', 5), ('wi', 6), ('wba', 0), ('wi', 7), ('wi', 8),
                       ('wo', 0), ('wo', 1)] + [('wu', i) for i in range(8)] + [('wd', i) for i in range(8)])
        NSG = NSS // 2
        all_loads = [(kind, l, i) for g in range(NG + NSG) for l in range(2) for (kind, i) in layer_loads]
        wstate = dict(issued=0, used=0, done=set())

        def w_can_issue(n):
            return n < len(all_loads) and (n - NSLOT < 0 or (n - NSLOT) in wstate['done'])

        def w_issue():
            n = wstate['issued']
            kind, l, i = all_loads[n]
            ensure_conv(kind, l)
            src, keys, shp = wsrc(kind, l, i)
            slot = n % NSLOT
            dst = ring[slot][:].rearrange("p (a b) -> p a b", a=shp[0])
            S.dma('sp', dst, src, r=keys, w=[('ring', slot)])
            wstate['issued'] = n + 1

        def w_prefetch():
            while wstate['issued'] < min(wstate['used'] + NSLOT, len(all_loads)) and w_can_issue(wstate['issued']):
                w_issue()

        def w_get(kind, l, i):
            n = wstate['used']
            assert all_loads[n] == (kind, l, i), (all_loads[n], kind, l, i)
            wstate['used'] = n + 1
            while wstate['issued'] <= n:
                assert w_can_issue(wstate['issued']), ("ring slot still live", n)
                w_issue()
            w_prefetch()
            _, _, shp = wsrc(kind, l, i)
            slot = n % NSLOT
            return ring[slot][:].rearrange("p (a b) -> p a b", a=shp[0]), ('ring', slot), n

        def w_done(n):
            wstate['done'].add(n)
            w_prefetch()

        ps = es.enter_context(nc.psum_tensor("ps", [128, 3072], F32))
        pst = es.enter_context(nc.psum_tensor("pst", [128, 2048], BF16))

        def pk(c0, c1):
            return [('ps', b) for b in range(c0 // 512, (c1 - 1) // 512 + 1)]
        big = dict(i=0)

        def bigslot():
            i = big['i'] % 2
            big['i'] += 1
            c0 = 2048 + i * 512
            return ps[:, c0:c0 + 256], [('ps', 4 + i)]

        xT = T("xT", [128, 8, NT]); xTb = T("xTb", [128, 8, NT], BF16)
        xio = T("xio", [128, TL, D])
        Zc = [T("Zc%d" % i, [128, NT + 1]) for i in range(2)]
        CARRY = T("CARRY", [128, 2, 14])
        rT = T("rT", [128, 4, NT]); kraw = T("kraw", [128, 4, NT]); vT = T("vT", [128, 4, NT])
        tw = T("tw", [128, NT], BF16); sgd = T("sgd", [128, NT], BF16)
        ACRC = T("ACRC", [128, 4, TL, 2, 128], BF16)
        bcb = T("bcb", [128, 4, NT], BF16); kcb = T("kcb", [128, 4, NT], BF16); asb = T("asb", [128, 4, NT], BF16)
        vb = T("vb", [128, 4, NT], BF16); rs = T("rs", [128, 4, NT]); bonus = T("bonus", [128, 4, NT], BF16)
        gg = T("gg", [128, 4, NT], BF16); GC = T("GC", [128, 4, NT // 64])
        NTM = 9
        Tm = [T("Tm%d" % i, [128, NT]) for i in range(NTM)]
        Tn = [T("Tn%d" % i, [128, NT]) for i in range(NTM)]
        X = T("X", [128, 8, 128], BF16); VV = T("VV", [128, 8, 128], BF16)
        BcT = T("BcT", [128, 8, 64], BF16); KcT = T("KcT", [128, 8, 64], BF16)
        SC1 = T("SC1", [128, 8, 256], BF16); SC2 = T("SC2", [128, 8, 256], BF16)
        Pm = [T("Pm%d" % i, [128, 8, 128], BF16) for i in range(2)]
        PTm = [T("PTm%d" % i, [128, 8, 128], BF16) for i in range(2)]
        RhatT = T("RhatT", [128, 4, 128]); McT = T("McT", [128, 4, 2, 64]); H = T("H", [128, 2, 4, 64]); Nc = T("Nc", [128, 4, 2, 64])
        YT = T("YT", [128, 4, NT])
        yf = T("yf", [128, 4, NT], BF16); qT = T("qT", [128, 4, NT], BF16)
        KT2 = T("KT2", [128, 2, 2, 128 + NT], BF16); Vtm = T("Vtm", [128, 2, 1 + TL, 128], BF16)
        pT = T("pT", [128, 8, 2, 128], BF16); YA = T("YA", [128, 4, NT], BF16)
        cosT = T("cosT", [128, NT]); sinT = T("sinT", [128, NT])
        qraw = [T("qraw%d" % i, [128, NT], BF16) for i in range(2)]
        kf = T("kf", [128, 2, 128]); vf = T("vf", [128, 128]); kfB = T("kfB", [128, 2, 128]); vfB = T("vfB", [128, 128])
        Gtmp = [T("Gtmp%d" % i, [128, NT], BF16) for i in range(2)]
        mixR = T("mixR", [128, 8, NT], BF16); mix = T("mix", [128, 8, NT], BF16)
        x1 = T("x1", [128, 8, NT])
        x1b = [T("x1b%d" % i, [128, NT], BF16) for i in range(2)]
        x1q = [T("x1q%d" % i, [128, NT], BF16) for i in range(2)]
        hT = T("hT", [128, 32, NT], BF16)
        dena = T("dena", [128, NT]); denb = T("denb", [128, NT])
        HB = xio[:, 1, 512:768].rearrange("p (c j) -> p c j", j=64)
        ostage = xio[:, 1, 768:1024]
        CARB = T("CARB", [128, 2, 14]); SHFB = T("SHFB", [128, 2, 14]); KTB = T("KTB", [128, 2, 128], BF16); VtmB = T("VtmB", [128, 128], BF16)
        tmask = T("tmask", [128, NT]); SHF = T("SHF", [128, 2, 14])
        Snat = xio[0:64, 0, 0:512]; ckd = xio[:, 1, 0:256].rearrange("p (a b c) -> p a b c", a=2, b=2)
        S.dma(QM, tmask[:], tmask_d[:, :], w=['tmask'])

        S.op('pool', lambda e: e.memset(H[:], 0.0), w=[('H', 0), ('H', 1)])
        S.op('pool', lambda e: e.memset(CARRY[:], 0.0), w=[('CARRY', 0), ('CARRY', 1)])
        S.op('pool', lambda e: e.memset(KT2[:], 0.0), w=[('KT2', 0), ('KT2', 1)])
        S.op('pool', lambda e: e.memset(Vtm[:], 0.0), w=[('Vtm', 0), ('Vtm', 1)])
        S.op('pool', lambda e: e.memset(VV[:], 0.0), w=['VV'])

        def layernorm(l, ga, gb_, tag):
            s1 = ps[:, 0:NT]; s2 = ps[:, 512:512 + NT]
            for k in range(8):
                j = k % 2
                CP('act', x1b[j][:], x1[:, k, :], r=[('x1', k)], w=[('x1b', j)])
                ACT(x1q[j][:], x1[:, k, :], AF.Square, r=[('x1', k)], w=[('x1q', j)])
                MM(s1, onesb[:], x1b[j][:], start=(k == 0), stop=(k == 7), r=['onesb', ('x1b', j)], w=pk(0, NT))
                MM(s2, onesb[:], x1q[j][:], start=(k == 0), stop=(k == 7), r=['onesb', ('x1q', j)], w=pk(512, 512 + NT))
            mean, msq, var, rstd = Tm[0], Tm[1], Tm[2], Tm[3]
            ACT(mean[:], s1, AF.Copy, scale=1.0 / D, r=pk(0, NT), w=[('Tm', 0)])
            TT('pool', msq[:], mean[:], mean[:], ALU.mult, r=[('Tm', 0)], w=[('Tm', 1)])
            STT('dve', var[:], s2, 1.0 / D, msq[:], ALU.mult, ALU.subtract, r=pk(512, 512 + NT) + [('Tm', 1)], w=[('Tm', 2)])
            ACT(var[:], var[:], AF.Sqrt, bias=epsln[:, 0:1], r=[('Tm', 2), 'eps'], w=[('Tm', 2)])
            RCP(rstd[:], var[:], r=[('Tm', 2)], w=[('Tm', 3)])
            for k in range(8):
                d = Tm[4 + (k % 2)]
                TT('pool', d[:], x1[:, k, :], mean[:], ALU.subtract, r=[('x1', k), ('Tm', 0)], w=[('Tm', 4 + k % 2)])
                TT('dve', d[:], d[:], rstd[:], ALU.mult, r=[('Tm', 4 + k % 2), ('Tm', 3)], w=[('Tm', 4 + k % 2)])
                TS('dve', xT[:, k, :], d[:], P(l, ga + k, ga + k + 1), ALU.mult, P(l, gb_ + k, gb_ + k + 1), ALU.add,
                   r=[('Tm', 4 + k % 2), 'pv'], w=[('xT', k)])
                CP('act', xTb[:, k, :], xT[:, k, :], r=[('xT', k)], w=[('xTb', k)])

        def emit_wkv(dst):
            for c in range(4):
                TR(ps[0:64, c * 128:(c + 1) * 128], H[:, l, c, :], identf, r=[('H', l), 'cst'], w=pk(0, 512))
            for hh in range(2):
                CP('act', ostage[0:64, 0:256], ps[0:64, hh * 256:(hh + 1) * 256], r=pk(0, 512), w=['xio'])
                S.dma(QM, dst[hh * 4:(hh + 1) * 4].rearrange("h v k -> v h k"), ostage[0:64, 0:256].rearrange("p (h k) -> p h k", k=64), r=['xio'])

        epsln = T("epsln", [128, 2])
        S.op('pool', lambda e: e.memset(epsln[:, 0:1], LN_EPS), w=['eps'])
        S.op('pool', lambda e: e.memset(epsln[:, 1:2], GN_EPS), w=['eps'])

        for gi in range(NG + NSG):
            samp = gi >= NG
            g = gi if not samp else -1
            q = 2 * (gi - NG)
            qb = q + 1
            t0g = g * NT
            if not samp:
                S.dma('pool', cosT[:], cos_d[:, t0g:t0g + NT], w=['cosT'])
                S.dma('pool', sinT[:], sin_d[:, t0g:t0g + NT], w=['sinT'])
            else:
                S.dma('pool', cosT[:], coss_d[:, :], w=['cosT'])
                S.dma('pool', sinT[:], sins_d[:, :], w=['sinT'])
            if upto < 1:
                S.disabled = True
            if not samp:
                S.dma('pool', xio[:], xp[t0g:t0g + NT, :].rearrange("(t p) d -> p t d", p=128), w=['xio'])
            else:
                S.op('pool', lambda e: e.memset(xio[:], 0.0), w=['xio'])
                S.dma('pool', xio[0:4, 0, :], xs[q], w=['xio'])
                S.dma('pool', xio[0:4, 1, :], xs[qb], w=['xio'])
            for kp in range(4):
                reg = ps[:, kp * 512:(kp + 1) * 512]
                for kk in range(2):
                    k = kp * 2 + kk
                    for t in range(TL):
                        TR(ps[:, kp * 512 + kk * 256 + t * 128: kp * 512 + kk * 256 + (t + 1) * 128],
                           xio[:, t, k * 128:(k + 1) * 128], identf, r=['xio', 'cst'], w=pk(kp * 512, kp * 512 + 512))
                CP('act', xT[:, 2 * kp:2 * kp + 2, :], reg.rearrange("p (a b) -> p a b", a=2), r=pk(kp * 512, kp * 512 + 512),
                   w=[('xT', 2 * kp), ('xT', 2 * kp + 1)])
                CP('dve', xTb[:, 2 * kp:2 * kp + 2, :], reg.rearrange("p (a b) -> p a b", a=2), r=pk(kp * 512, kp * 512 + 512),
                   w=[('xTb', 2 * kp), ('xTb', 2 * kp + 1)])
            for l in range(2):
                last = (g == NG - 1)
                if samp:
                    S.dma(QM, Snat.rearrange("v (h k) -> v h k", k=64), swkv_i[l, q].rearrange("h v k -> v h k"), w=['xio'])
                    for c in range(4):
                        TR(ps[:, c * 64:(c + 1) * 64], Snat[:, c * 128:(c + 1) * 128], identf[0:64, 0:64], r=['xio', 'cst'], w=pk(0, 256))
                    CP('dve', H[:, l, :, :], ps[:, 0:256].rearrange("p (c j) -> p c j", j=64), r=pk(0, 256), w=[('H', l)])
                    S.dma(QM, CARRY[:, l, :], sshift_i[l, q].rearrange("(c p) -> p c", p=128), w=[('CARRY', l)], allow_slow_non_contiguous=True)
                    for dup in range(2):
                        S.dma(QM, ckd[:, :, dup, :], sck_i[l, q], w=['xio'])
                    for kvh in range(2):
                        TR(ps[:, 512 + kvh * 128:512 + (kvh + 1) * 128], ckd[:, kvh, :, :].rearrange("p a b -> p (a b)"), identf, r=['xio', 'cst'], w=pk(512, 1024))
                    CP('act', KT2[:, l, :, 0:128], ps[:, 512:768].rearrange("p (a b) -> p a b", b=128), r=pk(512, 1024), w=[('KT2', l)])
                    S.dma('pool', Vtm[:, l, 0, :], scv_i[l, q].rearrange("t h d -> t (h d)"), w=[('Vtm', l)])
                    S.dma(QM, Snat.rearrange("v (h k) -> v h k", k=64), swkv_i[l, qb].rearrange("h v k -> v h k"), w=['xio'])
                    for c in range(4):
                        TR(ps[:, c * 64:(c + 1) * 64], Snat[:, c * 128:(c + 1) * 128], identf[0:64, 0:64], r=['xio', 'cst'], w=pk(0, 256))
                    CP('dve', HB, ps[:, 0:256].rearrange("p (c j) -> p c j", j=64), r=pk(0, 256), w=['xio', 'HB'])
                    S.dma(QM, CARB[:, l, :], sshift_i[l, qb].rearrange("(c p) -> p c", p=128), w=[('CARB', l)], allow_slow_non_contiguous=True)
                    for dup in range(2):
                        S.dma(QM, ckd[:, :, dup, :], sck_i[l, qb], w=['xio'])
                    for kvh in range(2):
                        TR(ps[:, 512 + kvh * 128:512 + (kvh + 1) * 128], ckd[:, kvh, :, :].rearrange("p a b -> p (a b)"), identf, r=['xio', 'cst'], w=pk(512, 1024))
                    CP('act', KTB[:], ps[:, 512:768].rearrange("p (a b) -> p a b", b=128), r=pk(512, 1024), w=['KTB'])
                    S.dma('pool', VtmB[:], scv_i[l, qb].rearrange("t h d -> t (h d)"), w=['VtmB'])
                allx = [('xTb', k) for k in range(8)]
                ensure_conv('wi', l)
                for kvh in range(2):
                    for dup in range(2):
                        S.dma('pool', wk2[:, :, kvh, dup * 64:(dup + 1) * 64],
                              wi_b[l].rearrange("(k p) n -> p k n", p=128)[:, :, 2304 + kvh * 64: 2304 + (kvh + 1) * 64],
                              r=[('wi', l, k) for k in range(8)], w=['wk2'])
                if upto < 2:
                    S.disabled = True
                for blk in range(5):
                    W, wkey, wn = w_get('wi', l, blk)
                    for j in range(4):
                        c = blk * 4 + j
                        if c in (18, 19):
                            continue
                        po, pkey = bigslot()
                        for k in range(8):
                            MM(po, W[:, k, j * 128:(j + 1) * 128], xTb[:, k, :], start=(k == 0), stop=(k == 7),
                               r=[wkey, ('xTb', k)], w=pkey)
                        if c < 14:
                            z = Zc[c % 2]; zk = ('Zc', c % 2)
                            CP('act', z[:, 1:NT + 1], po, r=pkey, w=[zk])
                            CP('pool', z[:, 0:1], CARRY[:, l, c:c + 1], r=[('CARRY', l)], w=[zk])
                            tmp = Tm[c % 2]
                            TS('dve', tmp[:], z[:, 1:NT + 1], pd[:, l, c:c + 1], ALU.mult, r=[zk, ('pd', l)], w=[('Tm', c % 2)])
                            if c < 4:
                                dst, dk_ = rT[:, c, :], ('rT', c)
                            elif c < 8:
                                dst, dk_ = kraw[:, c - 4, :], ('kraw', c - 4)
                            elif c < 12:
                                dst, dk_ = vT[:, c - 8, :], ('vT', c - 8)
                            else:
                                dst, dk_ = Tm[2 + c % 2][:], ('Tm', 2 + c % 2)
                            if samp:
                                CP('pool', SHFB[:, l, c:c + 1], z[:, 132:133], r=[zk], w=[('SHFB', l)])
                                CP('pool', z[:, 128:129], CARB[:, l, c:c + 1], r=[zk, ('CARB', l), ('Tm', c % 2)], w=[zk])
                            STT('dve', dst, z[:, 0:NT], P(l, c, c + 1), tmp[:], ALU.mult, ALU.add,
                                r=[zk, 'pv', ('Tm', c % 2)], w=[dk_])
                            CP('pool', CARRY[:, l, c:c + 1], z[:, NT:NT + 1], r=[zk], w=[('CARRY', l)])
                            if samp:
                                CP('pool', SHF[:, l, c:c + 1], z[:, 4:5], r=[zk], w=[('SHF', l)])
                            if c == 12:
                                ACT(tw[0:64, :], dst[0:64, :], AF.Tanh, r=[dk_], w=['tw'])
                                CP('act', tw[64:128, :], dst[64:128, :], r=[dk_], w=['tw'])
                            if c == 13:
                                ACT(sgd[:], dst, AF.Sigmoid, r=[dk_], w=['sgd'])
                        else:
                            qi = c - 14
                            qr = qraw[qi % 2]; qk = ('qraw', qi % 2)
                            CP('act', qr[:], po, r=pkey, w=[qk])
                            p2, p2k = bigslot()
                            MM(p2, protb[:], qr[:], r=['protb', qk], w=p2k)
                            ta = Tm[4 + qi % 2]; tb_ = Tm[6 + qi % 2]
                            TT('dve', ta[:], p2, sinT[:], ALU.mult, r=p2k + ['sinT'], w=[('Tm', 4 + qi % 2)])
                            TT('pool', tb_[:], qr[:], cosT[:], ALU.mult, r=[qk, 'cosT'], w=[('Tm', 6 + qi % 2)])
                            TT('pool', qT[:, qi, :], ta[:], tb_[:], ALU.add, r=[('Tm', 4 + qi % 2), ('Tm', 6 + qi % 2)], w=[('qT', qi)])
                    if blk == 4:
                        for t in range(TL):
                            po, pkey = bigslot()
                            for k in range(8):
                                MM(po[:, 0:128], xTb[:, k, t * 128:(t + 1) * 128], W[:, k, 384:512], start=(k == 0), stop=(k == 7),
                                   r=[wkey, ('xTb', k)], w=pkey)
                            CP('act', Vtm[:, l, 1 + t, :], po[:, 0:128], r=pkey, w=[('Vtm', l)])
                            if (t == TL - 1 and last) or (samp and t == 0):
                                CP('dve', vf[:], po[:, 0:128], r=pkey, w=['vf'])
                            if samp and t == 1:
                                CP('dve', vfB[:], po[:, 0:128], r=pkey, w=['vfB'])
                    w_done(wn)
                for kvh in range(2):
                    po, pkey = bigslot()
                    for k in range(8):
                        MM(po, wk2[:, k, kvh, :], xTb[:, k, :], start=(k == 0), stop=(k == 7), r=['wk2', ('xTb', k)], w=pkey)
                    qr = qraw[kvh]; qk = ('qraw', kvh)
                    CP('act', qr[:], po, r=pkey, w=[qk])
                    p2, p2k = bigslot()
                    MM(p2, protb[:], qr[:], r=['protb', qk], w=p2k)
                    ta = Tm[4 + kvh]; tb_ = Tm[6 + kvh]
                    TT('dve', ta[:], p2, sinT[:], ALU.mult, r=p2k + ['sinT'], w=[('Tm', 4 + kvh)])
                    TT('pool', tb_[:], qr[:], cosT[:], ALU.mult, r=[qk, 'cosT'], w=[('Tm', 6 + kvh)])
                    TT('pool', KT2[:, l, kvh, 128:128 + NT], ta[:], tb_[:], ALU.add, r=[('Tm', 4 + kvh), ('Tm', 6 + kvh)], w=[('KT2', l)])
                    if last:
                        TT('pool', kf[:, kvh, :], ta[:, NT - 128:NT], tb_[:, NT - 128:NT], ALU.add,
                           r=[('Tm', 4 + kvh), ('Tm', 6 + kvh)], w=['kf'])
                    if samp:
                        TT('pool', kf[:, kvh, :], ta[:, 0:128], tb_[:, 0:128], ALU.add,
                           r=[('Tm', 4 + kvh), ('Tm', 6 + kvh)], w=['kf'])
                        TT('pool', kfB[:, kvh, :], ta[:, 128:256], tb_[:, 128:256], ALU.add,
                           r=[('Tm', 4 + kvh), ('Tm', 6 + kvh)], w=['kfB'])
                if upto < 3:
                    S.disabled = True
                def st6(t):
                    tsl = slice(t * 128, (t + 1) * 128)
                    first_tile = (g == 0 and t == 0 and not samp)
                    for h in (0, 2, 4, 6, 1, 3, 5, 7):
                        c = h // 2; pb = 64 * (h % 2); kvh = h // 4
                        kprev_ = KTB[pb:pb + 64, kvh, :] if (samp and t == 1) else KT2[pb:pb + 64, l, kvh, t * 128:(t + 1) * 128]
                        MM(ps[:, h * 256:h * 256 + 128], kprev_, qT[pb:pb + 64, c, tsl],
                           r=[('KT2', l), ('qT', c), 'KTB'], w=pk(h * 256, h * 256 + 128))
                        yield
                        MM(ps[:, h * 256 + 128:h * 256 + 256], KT2[pb:pb + 64, l, kvh, (t + 1) * 128:(t + 2) * 128], qT[pb:pb + 64, c, tsl],
                           r=[('KT2', l), ('qT', c)], w=pk(h * 256 + 128, h * 256 + 256))
                        yield
                    for q4 in range(4):
                        ACT(pT[:, q4 * 2:q4 * 2 + 2, :, :].rearrange("p a b c -> p (a b c)"), ps[:, q4 * 512:(q4 + 1) * 512], AF.Exp, scale=0.125,
                            r=pk(q4 * 512, q4 * 512 + 512), w=[('pT', q4)])
                        yield
                    mm_ = (mAtt0 if first_tile else mAtt)
                    ptk = [('pT', q4) for q4 in range(4)]
                    TT('pool', pT[:].rearrange("p a b c -> p a (b c)"), pT[:].rearrange("p a b c -> p a (b c)"), bc_mid(mm_, 8), ALU.mult,
                       r=ptk + ['cst'], w=ptk)
                    yield
                    for h in range(8):
                        c = h // 2; pb = 64 * (h % 2); kvh = h // 4
                        oo = ps[pb:pb + 64, c * 128:(c + 1) * 128]
                        vprev_ = VtmB[:, kvh * 64:(kvh + 1) * 64] if (samp and t == 1) else Vtm[:, l, t, kvh * 64:(kvh + 1) * 64]
                        MM(oo, vprev_, pT[:, h, 0, :], start=True, stop=False, r=[('Vtm', l), 'VtmB'] + ptk, w=pk(0, 512))
                        yield
                        MM(oo, Vtm[:, l, t + 1, kvh * 64:(kvh + 1) * 64], pT[:, h, 1, :], start=False, stop=True, r=[('Vtm', l)] + ptk, w=pk(0, 512))
                        yield
                        do = ps[pb:pb + 64, 512 + c * 128:512 + (c + 1) * 128]
                        MM(do, onesb[:, 0:64], pT[:, h, 0, :], start=True, stop=False, r=['onesb'] + ptk, w=pk(512, 1024))
                        yield
                        MM(do, onesb[:, 0:64], pT[:, h, 1, :], start=False, stop=True, r=['onesb'] + ptk, w=pk(512, 1024))
                        yield
                    den = dena; den2 = denb
                    dv = lambda tl: tl[:].rearrange("p (a b) -> p a b", b=128)
                    for c in range(4):
                        tgt = (den if c < 2 else den2)[:, (c % 2) * 128:(c % 2 + 1) * 128]
                        TS('dve', tgt, ps[:, 512 + c * 128:512 + (c + 1) * 128], pd[:, l, 18 + c:19 + c], ALU.add,
                           r=pk(512, 1024) + [('pd', l)], w=[('den', 0 if c < 2 else 1)])
                        yield
                    RCP(den[:], den[:], r=[('den', 0)], w=[('den', 0)])
                    yield
                    RCP(den2[:], den2[:], r=[('den', 1)], w=[('den', 1)])
                    yield
                    TT('dve', YA[:, 0:2, tsl], ps[:, 0:256].rearrange("p (a b) -> p a b", b=128), dv(den), ALU.mult,
                       r=pk(0, 512) + [('den', 0)], w=[('YA', 0), ('YA', 1)])
                    yield
                    TT('dve', YA[:, 2:4, tsl], ps[:, 256:512].rearrange("p (a b) -> p a b", b=128), dv(den2), ALU.mult,
                       r=pk(0, 512) + [('den', 1)], w=[('YA', 2), ('YA', 3)])
                    yield
                def st2(c, TS_, TK_):
                    sg, ic, kk_, t4, bT_, gs, E3, E1, rkk = TS_[0], TS_[1], TS_[2], TS_[3], TS_[4], TS_[5], TS_[6], TS_[7], TS_[8]
                    K = lambda i: (TK_, i)
                    cs = slice(c * 128, (c + 1) * 128)
                    p1, p1k = bigslot()
                    MM(p1, lora[0:64, l, 0, cs], tw[0:64, :], r=[('lora', l), 'tw'], w=p1k)
                    yield
                    ACT(sg[:], p1, AF.Sigmoid, bias=P(l, 34 + c, 35 + c), r=p1k + ['pv'], w=[K(0)])
                    yield
                    if samp:
                        TT('pool', sg[:], sg[:], tmask[:], ALU.mult, r=[K(0), 'tmask'], w=[K(0)])
                        yield
                    p2, p2k = bigslot()
                    MM(p2, lora[64:128, l, 0, cs], tw[64:128, :], r=[('lora', l), 'tw'], w=p2k)
                    yield
                    ACT(ic[:], p2, AF.Sigmoid, bias=P(l, 38 + c, 39 + c), r=p2k + ['pv'], w=[K(1)])
                    yield
                    p3, p3k = bigslot()
                    MM(p3, lora[:, l, 1, cs], sgd[:], r=[('lora', l), 'sgd'], w=p3k)
                    yield
                    CP('act', gg[:, c, :], p3, r=p3k, w=[('gg', c)])
                    yield
                    TS('dve', kk_[:], kraw[:, c, :], P(l, 14 + c, 15 + c), ALU.mult, r=[('kraw', c), 'pv'], w=[K(2)])
                    yield
                    TT('pool', t4[:], kk_[:], kk_[:], ALU.mult, r=[K(2)], w=[K(3)])
                    yield
                    p4, p4k = bigslot()
                    MM(p4, blockones, t4[:], r=['cst', K(3)], w=p4k)
                    yield
                    TS('dve', t4[:], p4, 1e-24, ALU.max, r=p4k, w=[K(3)])
                    yield
                    ACT(t4[:], t4[:], AF.Sqrt, r=[K(3)], w=[K(3)])
                    yield
                    RCP(t4[:], t4[:], r=[K(3)], w=[K(3)])
                    yield
                    TT('pool', kk_[:], kk_[:], t4[:], ALU.mult, r=[K(2), K(3)], w=[K(2)])
                    yield
                    if samp:
                        TT('pool', kk_[:], kk_[:], tmask[:], ALU.mult, r=[K(2), 'tmask'], w=[K(2)])
                        yield
                    TT('pool', bT_[:], kk_[:], ic[:], ALU.mult, r=[K(2), K(1)], w=[K(4)])
                    yield
                    TS('dve', ic[:], ic[:], P(l, 18 + c, 19 + c), ALU.mult, pd[:, l, 14 + c:15 + c], ALU.add,
                       r=[K(1), 'pv', ('pd', l)], w=[K(1)])
                    yield
                    TT('pool', ic[:], kraw[:, c, :], ic[:], ALU.mult, r=[('kraw', c), K(1)], w=[K(1)])
                    yield
                    if samp:
                        TT('pool', ic[:], ic[:], tmask[:], ALU.mult, r=[K(1), 'tmask'], w=[K(1)])
                        yield
                    S.op('dve', (lambda o_, d0, d1: (lambda e: e.tensor_tensor_scan(out=o_, data0=d0, data1=d1, initial=0.0,
                                                                                    op0=ALU.mult, op1=ALU.add)))(gs[:], resetm, sg[:]),
                         r=['cst', K(0)], w=[K(5)])
                    yield
                    ACT(E3[:], gs[:], AF.Exp, scale=-DECAY_C, r=[K(5)], w=[K(6)])
                    yield
                    ACT(gs[:], gs[:], AF.Exp, scale=DECAY_C, r=[K(5)], w=[K(5)])
                    yield
                    ACT(sg[:], sg[:], AF.Exp, scale=DECAY_C, r=[K(0)], w=[K(0)])
                    yield
                    nch = NT // 64
                    e3v = E3[:].rearrange("p (a b) -> p a b", b=64)
                    i3v = gs[:].rearrange("p (a b) -> p a b", b=64)
                    CP('pool', GC[:, c, :], e3v[:, :, 63], r=[K(6)], w=[('GC', c)])
                    yield
                    TT('dve', E1[:].rearrange("p (a b) -> p a b", b=64), e3v, i3v[:, :, 63:64].broadcast_to([128, nch, 64]),
                       ALU.mult, r=[K(6), K(5)], w=[K(7)])
                    yield
                    TT('dve', i3v, i3v, e3v[:, :, 63:64].broadcast_to([128, nch, 64]), ALU.mult,
                       r=[K(5), K(6)], w=[K(5)])
                    yield
                    acv = ACRC[:, c, :, 0, :]
                    rcv = ACRC[:, c, :, 1, :]
                    v3 = lambda ap: ap.rearrange("p (a b) -> p a b", b=128)
                    TT('pool', rcv, v3(rT[:, c, :]), v3(E1[:]), ALU.mult, r=[('rT', c), K(7)], w=[('ACRC', c)])
                    yield
                    TT('pool', rs[:, c, :], rT[:, c, :], E3[:], ALU.mult, r=[('rT', c), K(6)], w=[('rs', c)])
                    yield
                    STT('dve', sg[:], kk_[:], -1.0, sg[:], ALU.mult, ALU.mult, r=[K(2), K(0)], w=[K(0)])
                    yield
                    TT('dve', acv, v3(sg[:]), v3(E1[:]), ALU.mult, r=[K(0), K(7)], w=[('ACRC', c)])
                    yield
                    TT('pool', asb[:, c, :], sg[:], E3[:], ALU.mult, r=[K(0), K(6)], w=[('asb', c)])
                    yield
                    TT('pool', bcb[:, c, :], bT_[:], gs[:], ALU.mult, r=[K(4), K(5)], w=[('bcb', c)])
                    yield
                    TT('dve', kcb[:, c, :], ic[:], gs[:], ALU.mult, r=[K(1), K(5)], w=[('kcb', c)])
                    yield
                    CP('act', vb[:, c, :], vT[:, c, :], r=[('vT', c)], w=[('vb', c)])
                    yield
                    TT('pool', rkk[:], rT[:, c, :], ic[:], ALU.mult, r=[('rT', c), K(1)], w=[K(8)])
                    yield
                    TS('dve', rkk[:], rkk[:], P(l, 22 + c, 23 + c), ALU.mult, r=[K(8), 'pv'], w=[K(8)])
                    yield
                    p5, p5k = bigslot()
                    MM(p5, blockones, rkk[:], r=['cst', K(8)], w=p5k)
                    yield
                    TT('dve', bonus[:, c, :], p5, vT[:, c, :], ALU.mult, r=p5k + [('vT', c)], w=[('bonus', c)])
                    yield

                ntile6 = TL
                for pi_, pair_ in enumerate(((0, 1), (2, 3))):
                    gens_ = [st2(c, Tm if c % 2 == 0 else Tn, 'Tm' if c % 2 == 0 else 'Tn') for c in pair_]
                    if pi_ < ntile6 and upto >= 6:
                        gens_.append(st6(pi_))
                    for _ in zip_longest(*gens_):
                        pass
                dump("rT", rT[:], [('rT', c) for c in range(4)])
                dump("rs", rs[:], [('rs', c) for c in range(4)])
                dump("asb", asb[:], [('asb', c) for c in range(4)])
                dump("bcb", bcb[:], [('bcb', c) for c in range(4)])
                dump("kcb", kcb[:], [('kcb', c) for c in range(4)])
                dump("ACRC", ACRC[:].rearrange("p a b c d -> p (a b c d)"), [('ACRC', c) for c in range(4)])
                if upto < 3.05:
                    S.disabled = True
                allp = [('asb', c) for c in range(4)] + [('bcb', c) for c in range(4)] + [('kcb', c) for c in range(4)] + [('vb', c) for c in range(4)]
                for t in range(TL):
                    tsl = slice(t * 128, (t + 1) * 128)
                    for c in range(4):
                        MM(ps[:, c * 128:(c + 1) * 128], asb[:, c, tsl], identb[:], r=[('asb', c), 'identb'], w=pk(0, 512))
                        MM(ps[:, 512 + c * 128:512 + (c + 1) * 128], bcb[:, c, tsl], identb[:], r=[('bcb', c), 'identb'], w=pk(512, 1024))
                        MM(ps[:, 1024 + c * 128:1024 + (c + 1) * 128], kcb[:, c, tsl], identb[:], r=[('kcb', c), 'identb'], w=pk(1024, 1536))
                        MM(ps[:, 1536 + c * 128:1536 + (c + 1) * 128], vb[:, c, tsl], identb[:], r=[('vb', c), 'identb'], w=pk(1536, 2048))
                    h64 = lambda ap: ap.rearrange("p (h j) -> p h j", j=64)
                    CP('act', X[:, :, 0:64], h64(ps[:, 0:512]), r=pk(0, 512), w=[('X', 0), ('X', 1)])
                    CP('dve', BcT[:], h64(ps[:, 512:1024]), r=pk(512, 1024), w=['BcT'])
                    CP('act', KcT[:], h64(ps[:, 1024:1536]), r=pk(1024, 1536), w=['KcT'])
                    CP('dve', VV[:, :, 64:128], h64(ps[:, 1536:2048]), r=pk(1536, 2048), w=['VV'])
                    if upto < 3.1:
                        S.disabled = True
                    m12 = bc_mid(mask12, 4)
                    for hg in range(2):
                        for hi in (0, 2, 1, 3):
                            h = hg * 4 + hi; c = h // 2; pb = 64 * (h % 2)
                            rhs2 = ACRC[pb:pb + 64, c, t, :, :].rearrange("p a b -> p (a b)")
                            MM(ps[:, hi * 256:(hi + 1) * 256], bcb[pb:pb + 64, c, tsl], rhs2, r=[('bcb', c), ('ACRC', c)], w=pk(hi * 256, hi * 256 + 256))
                            MM(ps[:, 1024 + hi * 256:1024 + (hi + 1) * 256], kcb[pb:pb + 64, c, tsl], rhs2, r=[('kcb', c), ('ACRC', c)],
                               w=pk(1024 + hi * 256, 1024 + hi * 256 + 256))
                        m12h = bc_mid(mask12, 2)
                        for bq in range(2):
                            TT('dve', SC1[:, hg * 4 + 2 * bq:hg * 4 + 2 * bq + 2, :], ps[:, bq * 512:(bq + 1) * 512].rearrange("p (h j) -> p h j", j=256), m12h, ALU.mult,
                               r=pk(bq * 512, bq * 512 + 512) + ['cst'], w=[('SC1', hg)])
                            TT('dve', SC2[:, hg * 4 + 2 * bq:hg * 4 + 2 * bq + 2, :], ps[:, 1024 + bq * 512:1024 + (bq + 1) * 512].rearrange("p (h j) -> p h j", j=256), m12h, ALU.mult,
                               r=pk(1024 + bq * 512, 1024 + bq * 512 + 512) + ['cst'], w=[('SC2', hg)])
                    if upto < 3.2:
                        S.disabled = True
                    sck = [('SC1', 0), ('SC1', 1)]; sck2 = [('SC2', 0), ('SC2', 1)]
                    for h in (0, 2, 4, 6, 1, 3, 5, 7):
                        c = h // 2; pb = 64 * (h % 2)
                        MM(ps[:, h * 128:(h + 1) * 128], ACRC[pb:pb + 64, c, t, 0, :], bcb[pb:pb + 64, c, tsl], r=[('ACRC', c), ('bcb', c)],
                           w=pk(h * 128, h * 128 + 128))
                    for bq in range(2):
                        TT('dve', Pm[0][:, 4 * bq:4 * bq + 4, :], ps[:, bq * 512:(bq + 1) * 512].rearrange("p (h j) -> p h j", j=128), bc_mid(maskT, 4), ALU.mult,
                           r=pk(bq * 512, bq * 512 + 512) + ['cst'], w=[('Pm', 0, bq)])
                    CP('pool', PTm[0][:], SC1[:, :, 0:128], r=sck, w=[('PTm', 0, 0), ('PTm', 0, 1)])
                    for h in range(8):
                        MM(ps[:, 1024 + h * 64:1024 + (h + 1) * 64], SC2[:, h, 0:128], VV[:, h, 64:128], r=sck2 + ['VV'],
                           w=pk(1024 + h * 64, 1024 + h * 64 + 64))
                    CP('act', X[:, :, 64:128], h64(ps[:, 1024:1536]), r=pk(1024, 1536), w=[('X', 0), ('X', 1)])
                    if upto < 3.3:
                        S.disabled = True
                    for j in range(6):
                        a = j % 2; b = 1 - a
                        for bq in range(2):
                            hs_ = range(4 * bq, 4 * bq + 4)
                            for h in hs_:
                                MM(ps[:, h * 128:(h + 1) * 128], PTm[a][:, h, :], X[:, h, :], r=[('PTm', a, bq), ('X', bq)], w=pk(h * 128, h * 128 + 128))
                            if j < 5:
                                for h in hs_:
                                    MM(ps[:, 1024 + h * 128:1024 + (h + 1) * 128], PTm[a][:, h, :], Pm[a][:, h, :], r=[('PTm', a, bq), ('Pm', a, bq)],
                                       w=pk(1024 + h * 128, 1024 + h * 128 + 128))
                                for h in hs_:
                                    MM(ps[:, 2048 + h * 128:2048 + (h + 1) * 128], Pm[a][:, h, :], PTm[a][:, h, :], r=[('PTm', a, bq), ('Pm', a, bq)],
                                       w=pk(2048 + h * 128, 2048 + h * 128 + 128))
                        for bq in range(2):
                            hs_ = slice(4 * bq, 4 * bq + 4)
                            TT('dve', X[:, hs_, :], ps[:, bq * 512:(bq + 1) * 512].rearrange("p (h j) -> p h j", j=128), X[:, hs_, :], ALU.add,
                               r=pk(bq * 512, bq * 512 + 512) + [('X', bq)], w=[('X', bq)])
                            if j < 5:
                                CP('act', Pm[b][:, hs_, :], ps[:, 1024 + bq * 512:1024 + (bq + 1) * 512].rearrange("p (h j) -> p h j", j=128),
                                   r=pk(1024 + bq * 512, 1536 + bq * 512), w=[('Pm', b, bq)])
                                CP('act' if bq == 0 else 'dve', PTm[b][:, hs_, :], ps[:, 2048 + bq * 512:2048 + (bq + 1) * 512].rearrange("p (h j) -> p h j", j=128),
                                   r=pk(2048 + bq * 512, 2560 + bq * 512), w=[('PTm', b, bq)])
                    if upto < 3.4:
                        S.disabled = True
                    for h in range(8):
                        c = h // 2; pb = 64 * (h % 2)
                        MM(ps[pb:pb + 64, c * 128:(c + 1) * 128], X[:, h, 0:64], SC1[:, h, 128:256], r=[('X', 0), ('X', 1)] + sck, w=pk(0, 512))
                    TT('dve', RhatT[:], ps[:, 0:512].rearrange("p (c j) -> p c j", j=128), rs[:, :, tsl], ALU.add,
                       r=pk(0, 512) + [('rs', c) for c in range(4)], w=['RhatT'])
                    if upto < 3.5:
                        S.disabled = True
                    mreg = [(512, 768), (1024, 1280)]; nreg = [(1536, 1792), (2048, 2304)]
                    for ch in range(2):
                        chs = slice(ch * 64, (ch + 1) * 64)
                        for h in range(8):
                            c = h // 2; pb = 64 * (h % 2)
                            MM(ps[pb:pb + 64, mreg[ch][0] + c * 64: mreg[ch][0] + (c + 1) * 64], X[chs, h, 0:64], BcT[chs, h, :],
                               r=[('X', 0), ('X', 1), 'BcT'], w=pk(*mreg[ch]))
                            no = ps[pb:pb + 64, nreg[ch][0] + c * 64: nreg[ch][0] + (c + 1) * 64]
                            MM(no, BcT[chs, h, :], X[chs, h, 64:128], start=True, stop=False, r=['BcT', ('X', 0), ('X', 1)], w=pk(*nreg[ch]))
                            MM(no, KcT[chs, h, :], VV[chs, h, 64:128], start=False, stop=True, r=['KcT', 'VV'], w=pk(*nreg[ch]))
                    for ch in range(2):
                        for c in range(4):
                            STT('dve', McT[:, c, ch, :], I2, GC[:, c, t * 2 + ch: t * 2 + ch + 1],
                                ps[:, mreg[ch][0] + c * 64: mreg[ch][0] + (c + 1) * 64], ALU.mult, ALU.add,
                                r=['cst', ('GC', c)] + pk(*mreg[ch]), w=['McT'])
                        CP('act', Nc[:, :, ch, :], ps[:, nreg[ch][0]:nreg[ch][1]].rearrange("p (c j) -> p c j", j=64), r=pk(*nreg[ch]), w=['Nc'])
                    if upto < 3.6:
                        S.disabled = True
                    if samp and t == 1:
                        emit_wkv(so_wkv[l, q])
                        CP('dve', H[:, l, :, :], HB, r=['xio', 'HB'], w=[('H', l)])
                    for ch in range(2):
                        if ch == 1 and upto < 3.95:
                            S.disabled = True
                        if ch == 0 and t == 1 and upto < 3.97:
                            S.disabled = True
                        chs = slice(ch * 64, (ch + 1) * 64)
                        yreg = (2560, 2816)
                        for h in range(8):
                            c = h // 2; pb = 64 * (h % 2)
                            yo = ps[pb:pb + 64, 2560 + c * 64:2560 + (c + 1) * 64]
                            MM(yo, X[:, h, 64:128], SC1[:, h, 128 + ch * 64:128 + (ch + 1) * 64], start=True, stop=False, r=[('X', 0), ('X', 1)] + sck, w=pk(*yreg))
                            MM(yo, VV[:, h, 64:128], SC2[:, h, 128 + ch * 64:128 + (ch + 1) * 64], start=False, stop=False, r=['VV'] + sck2, w=pk(*yreg))
                            MM(yo, H[pb:pb + 64, l, c, :], RhatT[pb:pb + 64, c, chs], start=False, stop=True, r=[('H', l), 'RhatT'], w=pk(*yreg))
                        if upto < 3.7:
                            S.disabled = True
                        for h in (0, 2, 4, 6, 1, 3, 5, 7):
                            c = h // 2; pb = 64 * (h % 2)
                            hb = 0 if pb == 0 else 512
                            MM(ps[pb:pb + 64, hb + c * 64:hb + (c + 1) * 64], McT[pb:pb + 64, c, ch, :], H[pb:pb + 64, l, c, :],
                               r=['McT', ('H', l)], w=pk(hb, hb + 256))
                        if upto < 3.8:
                            S.disabled = True
                        CP('act', YT[:, :, t * 128 + ch * 64: t * 128 + (ch + 1) * 64], ps[:, 2560:2816].rearrange("p (c j) -> p c j", j=64),
                           r=pk(*yreg), w=[('YT', c) for c in range(4)])
                        if upto < 3.9:
                            S.disabled = True
                        TT('dve', H[0:64, l, :, :], ps[0:64, 0:256].rearrange("p (c j) -> p c j", j=64), Nc[0:64, :, ch, :], ALU.add,
                           r=pk(0, 256) + ['Nc'], w=[('H', l)])
                        TT('dve', H[64:128, l, :, :], ps[64:128, 512:768].rearrange("p (c j) -> p c j", j=64), Nc[64:128, :, ch, :], ALU.add,
                           r=pk(512, 768) + ['Nc'], w=[('H', l)])
                dump("YT", YT[:], [('YT', c) for c in range(4)])
                if upto < 5:
                    S.disabled = True
                def st5(c):
                    K = lambda i: ('Tm', i)
                    d_, dq_, sd = Tm[0 + 3 * (c % 2)], Tm[1 + 3 * (c % 2)], Tm[2 + 3 * (c % 2)]
                    k0, k1, k2 = K(0 + 3 * (c % 2)), K(1 + 3 * (c % 2)), K(2 + 3 * (c % 2))
                    p1, p1k = bigslot()
                    MM(p1, blockones, YT[:, c, :], r=['cst', ('YT', c)], w=p1k)
                    yield
                    STT('dve', d_[:], p1, -1.0 / 64, YT[:, c, :], ALU.mult, ALU.add, r=p1k + [('YT', c)], w=[k0])
                    yield
                    TT('pool', dq_[:], d_[:], d_[:], ALU.mult, r=[k0], w=[k1])
                    yield
                    p2, p2k = bigslot()
                    MM(p2, blockones, dq_[:], r=['cst', k1], w=p2k)
                    yield
                    ACT(sd[:], p2, AF.Sqrt, bias=epsln[:, 1:2], scale=1.0 / 64, r=p2k + ['eps'], w=[k2])
                    yield
                    RCP(sd[:], sd[:], r=[k2], w=[k2])
                    yield
                    TT('dve', d_[:], d_[:], sd[:], ALU.mult, r=[k0, k2], w=[k0])
                    yield
                    TS('dve', d_[:], d_[:], P(l, 26 + c, 27 + c), ALU.mult, P(l, 30 + c, 31 + c), ALU.add, r=[k0, 'pv'], w=[k0])
                    yield
                    TT('pool', d_[:], d_[:], bonus[:, c, :], ALU.add, r=[k0, ('bonus', c)], w=[k0])
                    yield
                    TT('pool', yf[:, c, :], d_[:], gg[:, c, :], ALU.mult, r=[k0, ('gg', c)], w=[('yf', c)])
                    yield

                for pair_ in ((0, 1), (2, 3)):
                    gens_ = [st5(c) for c in pair_]
                    for _ in zip_longest(*gens_):
                        pass
                dump("yf", yf[:], [('yf', c) for c in range(4)])
                dump("YA", YA[:], [('YA', c) for c in range(4)])
                if upto < 7:
                    S.disabled = True
                for br in range(2):
                    Wb, wbk, wbn = w_get('wbr' if br == 0 else 'wba', l, 0)
                    wgn = None
                    src_act = yf if br == 0 else YA
                    sk = 'yf' if br == 0 else 'YA'
                    for m in range(8):
                        if m % 4 == 0:
                            if wgn is not None:
                                w_done(wgn)
                            Wg, wgk, wgn = w_get('wi', l, 5 + 2 * br + m // 4)
                        po, pkey = bigslot()
                        for k in range(8):
                            MM(po, Wg[:, k, (m % 4) * 128:(m % 4 + 1) * 128], xTb[:, k, :], start=(k == 0), stop=(k == 7), r=[wgk, ('xTb', k)], w=pkey)
                        gt = Gtmp[m % 2]; gk = ('Gtmp', m % 2)
                        ACT(gt[:], po, AF.Sigmoid, r=pkey, w=[gk])
                        p2, p2k = bigslot()
                        for c in range(4):
                            MM(p2, Wb[:, c, m * 128:(m + 1) * 128], src_act[:, c, :], start=(c == 0), stop=(c == 3), r=[wbk, (sk, c)], w=p2k)
                        if br == 0:
                            TT('dve', mixR[:, m, :], p2, gt[:], ALU.mult, r=p2k + [gk], w=[('mixR', m)])
                        else:
                            tm_ = Tm[m % 2]
                            TT('dve', tm_[:], p2, gt[:], ALU.mult, r=p2k + [gk], w=[('Tm', m % 2)])
                            TT('pool', mix[:, m, :], tm_[:], mixR[:, m, :], ALU.add, r=[('Tm', m % 2), ('mixR', m)], w=[('mix', m)])
                    w_done(wgn)
                    w_done(wbn)
                for m in range(8):
                    if m % 4 == 0:
                        if m > 0:
                            w_done(won)
                        Wo, wok, won = w_get('wo', l, m // 4)
                    po, pkey = bigslot()
                    for k in range(8):
                        MM(po, Wo[:, k, (m % 4) * 128:(m % 4 + 1) * 128], mix[:, k, :], start=(k == 0), stop=(k == 7), r=[wok, ('mix', k)], w=pkey)
                    STT('dve', x1[:, m, :], xT[:, m, :], ALPHA, po, ALU.mult, ALU.add, r=[('xT', m)] + pkey, w=[('x1', m)])
                dump("mix", mix[:], [('mix', k) for k in range(8)])
                dump("x1pre", x1[:], [('x1', k) for k in range(8)])
                w_done(won)
                layernorm(l, 42, 50, 'ln1')
                dump("xln1", xT[:], [('xT', k) for k in range(8)])
                if upto < 8:
                    S.disabled = True
                for fb in range(8):
                    Wu, wuk, wun = w_get('wu', l, fb)
                    for j in range(4):
                        f = fb * 4 + j
                        po, pkey = bigslot()
                        for k in range(8):
                            MM(po, Wu[:, k, j * 128:(j + 1) * 128], xTb[:, k, :], start=(k == 0), stop=(k == 7), r=[wuk, ('xTb', k)], w=pkey)
                        gt = Gtmp[f % 2]; gk = ('Gtmp', f % 2)
                        ACT(gt[:], po, AF.Relu, r=pkey, w=[gk])
                        TT('dve', hT[:, f, :], gt[:], gt[:], ALU.mult, r=[gk], w=[('hT', f)])
                    w_done(wun)
                for m in range(8):
                    Wd, wdk, wdn = w_get('wd', l, m)
                    po, pkey = bigslot()
                    for f in range(32):
                        MM(po, Wd[:, f, :], hT[:, f, :], start=(f == 0), stop=(f == 31), r=[wdk, ('hT', f)], w=pkey)
                    STT('dve', x1[:, m, :], xT[:, m, :], ALPHA, po, ALU.mult, ALU.add, r=[('xT', m)] + pkey, w=[('x1', m)])
                    w_done(wdn)
                layernorm(l, 58, 66, 'ln2')
                if upto < 9:
                    S.disabled = True
                CP('pool', KT2[:, l, :, 0:128], KT2[:, l, :, NT:NT + 128], r=[('KT2', l)], w=[('KT2', l)])
                CP('pool', Vtm[:, l, 0, :], Vtm[:, l, TL, :], r=[('Vtm', l)], w=[('Vtm', l)])
                if samp:
                    emit_wkv(so_wkv[l, qb])
                    for (qq_, shf_, kf_, vf_) in ((q, SHF, kf, vf), (qb, SHFB, kfB, vfB)):
                        S.dma(QM, so_shift[l, qq_].rearrange("(c p) -> p c", p=128), shf_[:, l, :], r=[('SHF', l), ('SHFB', l)], allow_slow_non_contiguous=True)
                        for kvh in range(2):
                            TR(ps[:, 512 + kvh * 64:512 + (kvh + 1) * 64], kf_[0:64, kvh, :], identf[0:64, 0:64], r=['kf', 'kfB', 'cst'], w=pk(512, 1024))
                        CP('act', ostage[:, 0:128], ps[:, 512:640], r=pk(512, 1024), w=['xio'])
                        S.dma(QM, so_ck[l, qq_, 124:128].rearrange("t h d -> t (h d)"), ostage[0:4, 0:128], r=['xio'])
                        S.dma(QM, so_cv[l, qq_, 124:128].rearrange("t h d -> t (h d)"), vf_[0:4, :], r=['vf', 'vfB'])
                        S.dma(QM, so_ck[l, qq_, 0:124].rearrange("t h d -> t (h d)"), sck_i[l, qq_, 4:128].rearrange("t h d -> t (h d)"))
                        S.dma(QM, so_cv[l, qq_, 0:124].rearrange("t h d -> t (h d)"), scv_i[l, qq_, 4:128].rearrange("t h d -> t (h d)"))
                if last:
                    for c in range(4):
                        TR(ps[0:64, c * 128:(c + 1) * 128], H[:, l, c, :], identf, r=[('H', l), 'cst'], w=pk(0, 512))
                    CP('act', ostage[0:64, 0:512 // 2 * 0 + 256], ps[0:64, 0:256], r=pk(0, 512), w=['xio'])
                    S.dma(QM, o_wkv[l, 0:4].rearrange("h v k -> v h k"), ostage[0:64, 0:256].rearrange("p (h k) -> p h k", k=64), r=['xio'])
                    CP('act', ostage[0:64, 0:256], ps[0:64, 256:512], r=pk(0, 512), w=['xio'])
                    S.dma(QM, o_wkv[l, 4:8].rearrange("h v k -> v h k"), ostage[0:64, 0:256].rearrange("p (h k) -> p h k", k=64), r=['xio'])
                    S.dma(QM, o_shift[l].rearrange("(c p) -> p c", p=128), CARRY[:, l, :], r=[('CARRY', l)], allow_slow_non_contiguous=True)
                    for kvh in range(2):
                        TR(ps[:, 512 + kvh * 64:512 + (kvh + 1) * 64], kf[0:64, kvh, :], identf[0:64, 0:64], r=['kf', 'cst'], w=pk(512, 1024))
                    CP('act', ostage[:, 0:128], ps[:, 512:640], r=pk(512, 1024), w=['xio'])
                    S.dma(QM, o_ck[l].rearrange("t h d -> t (h d)"), ostage[:, 0:128], r=['xio'])
                    S.dma(QM, o_cv[l].rearrange("t h d -> t (h d)"), vf[:], r=['vf'])
            for t in range(TL):
                for kp in range(2):
                    for kk in range(4):
                        k = kp * 4 + kk
                        TR(ps[:, kp * 512 + kk * 128: kp * 512 + (kk + 1) * 128], xT[:, k, t * 128:(t + 1) * 128], identf,
                           r=[('xT', k), 'cst'], w=pk(kp * 512, kp * 512 + 512))
                    CP('act' if kp == 0 else 'dve', xio[:, t, kp * 512:(kp + 1) * 512], ps[:, kp * 512:(kp + 1) * 512],
                       r=pk(kp * 512, kp * 512 + 512), w=['xio'])
            if not samp:
                S.dma(QM, y_p[t0g:t0g + NT, :].rearrange("(t p) d -> p t d", p=128), xio[:], r=['xio'])
            else:
                S.dma(QM, y_s[q], xio[0:4, 0, :], r=['xio'])
                S.dma(QM, y_s[qb], xio[0:4, 1, :], r=['xio'])
        stats = S.emit()
    return nc, stats


def pack_pv(inp):
    pv = np.zeros((128, 2 * PVL), np.float32)
    for l in range(2):
        b = l * PVL
        pv[:, b:b + 14] = inp['mu_shift'][l].reshape(14, 128).T
        for off, name in ((14, 'k_k'), (18, 'k_a'), (26, 'lnx_g'), (30, 'lnx_b'), (34, 'decay_base'), (38, 'iclr_base')):
            pv[:, b + off:b + off + 4] = inp[name][l].reshape(4, 128).T
        pv[:, b + 22:b + 26] = inp['r_k'][l].reshape(512).reshape(4, 128).T
        for off, name in ((42, 'ln1_g'), (50, 'ln1_b'), (58, 'ln2_g'), (66, 'ln2_b')):
            pv[:, b + off:b + off + 8] = inp[name][l].reshape(8, 128).T
        pv[:, b + 74:b + 78] = np.repeat(inp['sinks'][l].reshape(4, 2), 64, axis=1).T
    return pv


_CACHE = {}


def host_inputs(inp, c, SEQ, NSS, consts):
    cst, cosT, sinT, coss, sins, tmask, pv = consts
    wnames = ['w_in', 'w_br_rwkv', 'w_br_attn', 'w_out', 'w_ff_up', 'w_ff_down', 'decay_up', 'iclr_up', 'gate_up']
    m = {k: inp[k] for k in wnames}
    sl = slice(c * NSS, (c + 1) * NSS)
    m.update(xp=inp['x_prompt'][(c * 2) // 8][:SEQ], pv_in=pv, cst_in=cst, cos_in=cosT, sin_in=sinT, coss_in=coss, sins_in=sins, tmask_in=tmask,
             xs=np.ascontiguousarray(inp['x_sample'][sl]), swkv_i=np.ascontiguousarray(inp['state_wkv'][:, sl]),
             sshift_i=np.ascontiguousarray(inp['state_shift'][:, sl]), sck_i=np.ascontiguousarray(inp['cache_k_win'][:, sl]),
             scv_i=np.ascontiguousarray(inp['cache_v_win'][:, sl]))
    return m


def host_consts(inp, SEQ, past_len=8192):
    cst = make_consts()
    cosT, sinT = rope_tables(np.arange(SEQ))
    coss, sins = rope_tables(past_len + (np.arange(NT) % 128))
    tmask = np.zeros((128, NT), np.float32)
    tmask[:, 0:4] = 1.0
    tmask[:, 128:132] = 1.0
    return cst, cosT, sinT, coss, sins, tmask, pack_pv(inp)


def kernel(**inputs):
    inp = {k: np.ascontiguousarray(np.asarray(v)) for k, v in inputs.items()}
    B, SEQ, _ = inp['x_prompt'].shape
    NSS = inp['x_sample'].shape[0] // 8
    if SEQ not in _CACHE:
        _CACHE[SEQ] = build(SEQ, NSS)
    nc, _ = _CACHE[SEQ]
    consts = host_consts(inp, SEQ)
    in_maps = [host_inputs(inp, c, SEQ, NSS, consts) for c in range(8)]
    res = run_bass_kernel_spmd(nc, in_maps, core_ids=list(range(8))).results
    y_p = np.stack([res[0]['y_p'], res[4]['y_p']])
    p_wkv = np.stack([res[0]['p_wkv'], res[4]['p_wkv']], 1)
    p_shift = np.stack([res[0]['p_shift'], res[4]['p_shift']], 1)
    p_ck = np.stack([res[0]['p_ck'], res[4]['p_ck']], 1)
    p_cv = np.stack([res[0]['p_cv'], res[4]['p_cv']], 1)
    y_s = np.concatenate([res[c]['y_s'] for c in range(8)], 0)
    s_wkv = np.concatenate([res[c]['s_wkv'] for c in range(8)], 1)
    s_shift = np.concatenate([res[c]['s_shift'] for c in range(8)], 1)
    s_ck = np.concatenate([res[c]['s_ck'] for c in range(8)], 1)
    s_cv = np.concatenate([res[c]['s_cv'] for c in range(8)], 1)
    return (y_p, y_s, p_wkv, p_shift, p_ck, p_cv, s_wkv, s_shift, s_ck, s_cv)
```

```python
import numpy as np
from contextlib import ExitStack
from itertools import zip_longest
import concourse.bass as bass
import concourse.mybir as mybir
from concourse.ap import AP
from concourse.bass_utils import run_bass_kernel_spmd

F32 = mybir.dt.float32
BF16 = mybir.dt.bfloat16
AF = mybir.ActivationFunctionType
ALU = mybir.AluOpType

D = 1024
NT = 256
TL = NT // 128
SHIFT_W = 1792
IN_W = 4608
DFF = 4096
ALPHA = 4 ** 0.25
LN_EPS = 1e-5
GN_EPS = 64e-5
DECAY_C = 0.6065306597126334
PVL = 78
QM = 'pool'
NCST = 1728


class Sched:
    COMPUTE = ('pe', 'act', 'dve', 'pool')

    def __init__(self, nc, es, n_dma_sems=24):
        self.nc = nc
        self.h = {'pe': nc.tensor, 'act': nc.scalar, 'dve': nc.vector, 'pool': nc.gpsimd, 'sp': nc.sync}
        self.ops = []
        self.last_w = {}
        self.readers = {}
        self.sem = {e: es.enter_context(nc.semaphore("s_" + e)) for e in self.COMPUTE}
        self.dsem = []
        self.dq = {}
        for q, n in (('sp', n_dma_sems), ('act', 8), ('pool', 12)):
            self.dq[q] = list(range(len(self.dsem), len(self.dsem) + n))
            self.dsem += [es.enter_context(nc.semaphore("d%s%d" % (q, i))) for i in range(n)]

    def _add(self, kind, eng, fn, r, w):
        isps = lambda k: isinstance(k, tuple) and k[0] in ('ps', 'pst')
        w = list(w) + [k for k in r if isps(k)]
        r = [k for k in r if not isps(k)]
        oid = len(self.ops)
        deps = set()
        for k in r:
            if k in self.last_w:
                deps.add(self.last_w[k])
        for k in w:
            if k in self.last_w:
                deps.add(self.last_w[k])
            deps |= self.readers.get(k, set())
        for k in r:
            self.readers.setdefault(k, set()).add(oid)
        for k in w:
            self.last_w[k] = oid
            self.readers[k] = set()
        deps.discard(oid)
        self.ops.append(dict(kind=kind, eng=eng, fn=fn, deps=deps))
        return oid

    disabled = False

    def op(self, eng, fn, r=(), w=()):
        if self.disabled:
            return None
        return self._add('c', eng, fn, r, w)

    def dma(self, eng, out, in_, r=(), w=(), **kw):
        if self.disabled:
            return None
        return self._add('d', eng, (out, in_, kw), r, w)

    def emit(self):
        ops = self.ops
        need = [False] * len(ops)
        for i, o in enumerate(ops):
            for d in o['deps']:
                p = ops[d]
                if p['kind'] == 'c':
                    if p['eng'] == o['eng'] and o['kind'] == 'c' and p['eng'] == 'pe':
                        continue
                    need[d] = True
        cnt = {e: 0 for e in self.COMPUTE}
        tok = [None] * len(ops)
        seen = {}
        dcount = [0] * len(self.dsem)
        dk = {q: 0 for q in self.dq}
        nwaits = 0
        acts = {e: [] for e in self.h}
        for i, o in enumerate(ops):
            e = o['eng']
            wl = {}
            for d in o['deps']:
                p = ops[d]
                if p['kind'] == 'c' and p['eng'] == e and o['kind'] == 'c' and e == 'pe':
                    continue
                t = tok[d]
                if t is None:
                    continue
                ts, tv = t
                if tv > wl.get(id(ts), (ts, 0))[1]:
                    wl[id(ts)] = (ts, tv)
            if o['kind'] == 'd':
                j = self.dq[e][dk[e] % len(self.dq[e])]
                dk[e] += 1
                dsj = self.dsem[j]
                if dcount[j] > 0 and dcount[j] > wl.get(id(dsj), (dsj, 0))[1]:
                    wl[id(dsj)] = (dsj, dcount[j])
            for ws, wv in wl.values():
                key = (e, id(ws))
                if seen.get(key, 0) >= wv:
                    continue
                acts[e].append((lambda s_, v_: (lambda h: h.wait_ge(s_, v_)))(ws, wv))
                nwaits += 1
                seen[key] = wv
            if o['kind'] == 'c':
                if need[i]:
                    cnt[e] += 1
                    acts[e].append((lambda fn_, sm_: (lambda h: fn_(h).then_inc(sm_, 1)))(o['fn'], self.sem[e]))
                    tok[i] = (self.sem[e], cnt[e])
                else:
                    acts[e].append(o['fn'])
            else:
                out, in_, kw = o['fn']
                dcount[j] += 16
                acts[e].append((lambda o_, i_, k_, s_: (lambda h: h.dma_start(out=o_, in_=i_, **k_).then_inc(s_, 16)))(out, in_, kw, dsj))
                tok[i] = (dsj, dcount[j])
        for j, fs in enumerate(self.dsem):
            if dcount[j] > 0:
                acts['sp'].append((lambda s_, v_: (lambda h: h.wait_ge(s_, v_)))(fs, dcount[j]))
        with self.nc.Block() as block:
            @block.sync
            def _(h):
                for a in acts['sp']:
                    a(h)

            @block.tensor
            def _(h):
                for a in acts['pe']:
                    a(h)

            @block.scalar
            def _(h):
                for a in acts['act']:
                    a(h)

            @block.vector
            def _(h):
                for a in acts['dve']:
                    a(h)

            @block.gpsimd
            def _(h):
                for a in acts['pool']:
                    a(h)
        return dict(n_ops=len(ops), n_waits=nwaits, signals=dict(cnt))


def make_consts():
    c = np.zeros((128, NCST), np.float32)
    idx = np.arange(128)
    c[:, 0:128] = np.eye(128)
    c[:, 128:256] = (idx[:, None] // 64 == idx[None, :] // 64)
    same = (idx[:, None] // 64) == (idx[None, :] // 64)
    mstrict = same & (idx[:, None] < idx[None, :])
    mincl = same & (idx[:, None] <= idx[None, :])
    c[:, 256:384] = mstrict.T
    c[:, 384:512] = mstrict
    c[:, 512:640] = mincl
    c[:, 640:768] = idx[:, None] >= idx[None, :]
    c[:, 768:896] = idx[:, None] <= idx[None, :]
    c[:, 896:1024] = 0.0
    c[:, 1024:1152] = idx[:, None] <= idx[None, :]
    prot = np.zeros((128, 128), np.float32)
    for hb in (0, 64):
        for dd in range(32):
            prot[hb + dd + 32, hb + dd] = -1.0
            prot[hb + dd, hb + dd + 32] = 1.0
    c[:, 1152:1280] = prot
    rm = np.ones((128, 256), np.float32)
    rm[:, 0::64] = 0.0
    c[:, 1280:1536] = rm
    c[:, 1536:1664] = 1.0
    c[:, 1664:1728] = (idx[:, None] % 64) == np.arange(64)[None, :]
    return c


def rope_tables(pos):
    half = 32
    inv = (10000.0 ** (-np.arange(half, dtype=np.float32) / half)).astype(np.float32)
    ang = pos.astype(np.float32)[None, :] * inv[:, None]
    cos = np.cos(ang).astype(np.float32)
    sin = np.sin(ang).astype(np.float32)
    return np.tile(cos, (4, 1)), np.tile(sin, (4, 1))


def build(SEQ, NSS=16, dbg=(), upto=99, noconv=False):
    NG = SEQ // NT
    NTS = NSS * 4
    nc = bass.Bass("TRN2", target_bir_lowering=False)
    din = lambda name, shape, dt=F32: nc.dram_tensor(name, list(shape), dt, kind="ExternalInput").ap()
    dout = lambda name, shape, dt=F32: nc.dram_tensor(name, list(shape), dt, kind="ExternalOutput").ap()
    dscr = lambda name, shape, dt=BF16: nc.dram_tensor(name, list(shape), dt).ap()

    xp = din("xp", [SEQ, D])
    w_in = din("w_in", [2, D, IN_W]); w_brr = din("w_br_rwkv", [2, 512, D]); w_bra = din("w_br_attn", [2, 512, D])
    w_out = din("w_out", [2, D, D]); w_up = din("w_ff_up", [2, D, DFF]); w_dn = din("w_ff_down", [2, DFF, D])
    d_up = din("decay_up", [2, 64, 512]); i_up = din("iclr_up", [2, 64, 512]); g_up = din("gate_up", [2, 128, 512])
    pv_d = din("pv_in", [128, 2 * PVL]); cst_d = din("cst_in", [128, NCST])
    cos_d = din("cos_in", [128, SEQ]); sin_d = din("sin_in", [128, SEQ])

    xs = din("xs", [NSS, 4, D]); swkv_i = din("swkv_i", [2, NSS, 8, 64, 64]); sshift_i = din("sshift_i", [2, NSS, SHIFT_W])
    sck_i = din("sck_i", [2, NSS, 128, 2, 64]); scv_i = din("scv_i", [2, NSS, 128, 2, 64])
    coss_d = din("coss_in", [128, NT]); sins_d = din("sins_in", [128, NT]); tmask_d = din("tmask_in", [128, NT])
    y_s = dout("y_s", [NSS, 4, D]); so_wkv = dout("s_wkv", [2, NSS, 8, 64, 64]); so_shift = dout("s_shift", [2, NSS, SHIFT_W])
    so_ck = dout("s_ck", [2, NSS, 128, 2, 64]); so_cv = dout("s_cv", [2, NSS, 128, 2, 64])
    y_p = dout("y_p", [SEQ, D]); o_wkv = dout("p_wkv", [2, 8, 64, 64]); o_shift = dout("p_shift", [2, SHIFT_W])
    o_ck = dout("p_ck", [2, 128, 2, 64]); o_cv = dout("p_cv", [2, 128, 2, 64])
    dbg_out = {}

    wi_b = dscr("wi_b", [2, D, IN_W]); wbr_b = dscr("wbr_b", [2, 512, D]); wba_b = dscr("wba_b", [2, 512, D])
    wo_b = dscr("wo_b", [2, D, D]); wu_b = dscr("wu_b", [2, D, DFF]); wd_b = dscr("wd_b", [2, DFF, D])

    with ExitStack() as es:
        S = Sched(nc, es)
        T = lambda name, shape, dt=F32: es.enter_context(nc.sbuf_tensor(name, list(shape), dt))
        def MM(out, lhsT, rhs, start=True, stop=True, r=(), w=()):
            S.op('pe', lambda e: e.matmul(out, lhsT=lhsT, rhs=rhs, start=start, stop=stop), r=r, w=w)

        def TR(out, in_, ident, r=(), w=()):
            S.op('pe', lambda e: e.transpose(out, in_, ident), r=r, w=w)

        def TT(eng, out, in0, in1, op, r=(), w=()):
            S.op(eng, lambda e: e.tensor_tensor(out=out, in0=in0, in1=in1, op=op), r=r, w=w)

        def TS(eng, out, in0, s1, op0, s2=None, op1=None, r=(), w=()):
            if op1 is None:
                S.op(eng, lambda e: e.tensor_scalar(out=out, in0=in0, scalar1=s1, scalar2=None, op0=op0), r=r, w=w)
            else:
                S.op(eng, lambda e: e.tensor_scalar(out=out, in0=in0, scalar1=s1, scalar2=s2, op0=op0, op1=op1), r=r, w=w)

        def STT(eng, out, in0, scalar, in1, op0, op1, r=(), w=()):
            S.op(eng, lambda e: e.scalar_tensor_tensor(out=out, in0=in0, scalar=scalar, in1=in1, op0=op0, op1=op1), r=r, w=w)

        def ACT(out, in_, func, bias=None, scale=1.0, r=(), w=()):
            if bias is None:
                S.op('act', lambda e: e.activation(out=out, in_=in_, func=func, scale=scale), r=r, w=w)
            else:
                S.op('act', lambda e: e.activation(out=out, in_=in_, func=func, bias=bias, scale=scale), r=r, w=w)

        def CP(eng, out, in_, r=(), w=()):
            if eng == 'act':
                S.op('act', lambda e: e.copy(out=out, in_=in_), r=r, w=w)
            else:
                S.op(eng, lambda e: e.tensor_copy(out=out, in_=in_), r=r, w=w)

        def RCP(out, in_, r=(), w=()):
            S.op('dve', lambda e: e.reciprocal(out=out, in_=in_), r=r, w=w)

        def bc_mid(ap2d, n):
            return ap2d.unsqueeze(1).broadcast_to([ap2d.shape[0], n, ap2d.shape[1]])

        def dump(name, ap, keys, shape=None):
            if name not in dbg or name in dbg_out:
                return
            dbg_out[name] = 1
            shp = list(ap.shape)
            o = dout("dbg_" + name, shp, ap.dtype)
            full = o if len(shp) == 2 else o
            S.dma(QM, o[tuple(slice(None) for _ in shp)], ap, r=keys)

        cst = T("cst", [128, NCST])
        S.dma(QM, cst[:], cst_d[:, :], w=['cst'])
        identf = cst[:, 0:128]; blockones = cst[:, 128:256]; maskT = cst[:, 256:384]; mask12 = cst[:, 384:640]
        mAtt = cst[:, 640:896]; mAtt0 = cst[:, 896:1152]; resetm = cst[:, 1280:1536]; I2 = cst[:, 1664:1728]
        identb = T("identb", [128, 128], BF16); protb = T("protb", [128, 128], BF16); onesb = T("onesb", [128, 128], BF16)
        CP('pool', identb[:], cst[:, 0:128], r=['cst'], w=['identb'])
        CP('pool', protb[:], cst[:, 1152:1280], r=['cst'], w=['protb'])
        CP('pool', onesb[:], cst[:, 1536:1664], r=['cst'], w=['onesb'])
        pv = T("pv", [128, 2 * PVL])
        S.dma(QM, pv[:], pv_d[:, :], w=['pv'])
        pd = T("pd", [128, 2, 24])
        for l in range(2):
            b0 = l * PVL
            TS('dve', pd[:, l, 0:14], pv[:, b0:b0 + 14], -1.0, ALU.mult, 1.0, ALU.add, r=['pv'], w=[('pd', l)])
            TS('dve', pd[:, l, 14:18], pv[:, b0 + 18:b0 + 22], -1.0, ALU.mult, 1.0, ALU.add, r=['pv'], w=[('pd', l)])
            ACT(pd[:, l, 18:22], pv[:, b0 + 74:b0 + 78], AF.Exp, r=['pv'], w=[('pd', l)])
        P = lambda l, a, b: pv[:, l * PVL + a: l * PVL + b]
        lora = T("lora", [128, 2, 2, 512], BF16)
        for l in range(2):
            S.dma('pool', lora[0:64, l, 0, :], d_up[l], w=[('lora', l)])
            S.dma('pool', lora[64:128, l, 0, :], i_up[l], w=[('lora', l)])
            S.dma('pool', lora[:, l, 1, :], g_up[l], w=[('lora', l)])

        def conv(dst, src, l, rows, key):
            for k in range(rows // 128):
                S.dma('pool', dst[l, k * 128:(k + 1) * 128, :], src[l, k * 128:(k + 1) * 128, :], w=[(key, l, k)])
        convspec = dict(wi=(wi_b, w_in, D), wbr=(wbr_b, w_brr, 512), wba=(wba_b, w_bra, 512), wo=(wo_b, w_out, D),
                        wu=(wu_b, w_up, D), wd=(wd_b, w_dn, DFF))
        converted = set()

        def ensure_conv(kind, l):
            if (kind, l) in converted:
                return
            converted.add((kind, l))
            dst, src, rows = convspec[kind]
            conv(dst, src, l, rows, kind)

        NSLOT = 3
        ring = [T("ring%d" % i, [128, 4096], BF16) for i in range(NSLOT)]
        wk2 = T("wk2", [128, 8, 2, 128], BF16)

        def wsrc(kind, l, i):
            if kind == 'wi':
                return wi_b[l].rearrange("(k p) n -> p k n", p=128)[:, :, i * 512:(i + 1) * 512], [('wi', l, k) for k in range(8)], [8, 512]
            if kind == 'wbr':
                return wbr_b[l].rearrange("(k p) n -> p k n", p=128), [('wbr', l, k) for k in range(4)], [4, 1024]
            if kind == 'wba':
                return wba_b[l].rearrange("(k p) n -> p k n", p=128), [('wba', l, k) for k in range(4)], [4, 1024]
            if kind == 'wo':
                return wo_b[l].rearrange("(k p) n -> p k n", p=128)[:, :, i * 512:(i + 1) * 512], [('wo', l, k) for k in range(8)], [8, 512]
            if kind == 'wu':
                return wu_b[l].rearrange("(k p) n -> p k n", p=128)[:, :, i * 512:(i + 1) * 512], [('wu', l, k) for k in range(8)], [8, 512]
            if kind == 'wd':
                return wd_b[l].rearrange("(f p) n -> p f n", p=128)[:, :, i * 128:(i + 1) * 128], [('wd', l, k) for k in range(32)], [32, 128]
        layer_loads = ([('wi', i) for i in range(9)] + [('wbr', 0), ('wba', 0), ('wo', 0), ('wo', 1)] + [('wu', i) for i in range(8)] + [('wd', i) for i in range(8)])
        NSG = NSS // 2
        all_loads = [(kind, l, i) for g in range(NG + NSG) for l in range(2) for (kind, i) in layer_loads]
        wstate = dict(issued=0, used=0, done=set())

        def w_can_issue(n):
            return n < len(all_loads) and (n - NSLOT < 0 or (n - NSLOT) in wstate['done'])

        def w_issue():
            n = wstate['issued']
            kind, l, i = all_loads[n]
            ensure_conv(kind, l)
            src, keys, shp = wsrc(kind, l, i)
            slot = n % NSLOT
            dst = ring[slot][:].rearrange("p (a b) -> p a b", a=shp[0])
            S.dma('sp', dst, src, r=keys, w=[('ring', slot)])
            wstate['issued'] = n + 1

        def w_prefetch():
            while wstate['issued'] < min(wstate['used'] + NSLOT, len(all_loads)) and w_can_issue(wstate['issued']):
                w_issue()

        def w_get(kind, l, i):
            n = wstate['used']
            assert all_loads[n] == (kind, l, i), (all_loads[n], kind, l, i)
            wstate['used'] = n + 1
            while wstate['issued'] <= n:
                assert w_can_issue(wstate['issued']), ("ring slot still live", n)
                w_issue()
            w_prefetch()
            _, _, shp = wsrc(kind, l, i)
            slot = n % NSLOT
            return ring[slot][:].rearrange("p (a b) -> p a b", a=shp[0]), ('ring', slot), n

        def w_done(n):
            wstate['done'].add(n)
            w_prefetch()

        ps = es.enter_context(nc.psum_tensor("ps", [128, 3072], F32))
        pst = es.enter_context(nc.psum_tensor("pst", [128, 2048], BF16))

        def pk(c0, c1):
            return [('ps', b) for b in range(c0 // 512, (c1 - 1) // 512 + 1)]
        big = dict(i=0)

        def bigslot():
            i = big['i'] % 2
            big['i'] += 1
            c0 = 2048 + i * 512
            return ps[:, c0:c0 + 256], [('ps', 4 + i)]

        xT = T("xT", [128, 8, NT]); xTb = T("xTb", [128, 8, NT], BF16)
        xio = T("xio", [128, TL, D])
        Zc = [T("Zc%d" % i, [128, NT + 1]) for i in range(2)]
        CARRY = T("CARRY", [128, 2, 14])
        rT = T("rT", [128, 4, NT]); kraw = T("kraw", [128, 4, NT]); vT = T("vT", [128, 4, NT])
        tw = T("tw", [128, NT], BF16); sgd = T("sgd", [128, NT], BF16)
        ACRC = T("ACRC", [128, 4, TL, 2, 128], BF16)
        bcb = T("bcb", [128, 4, NT], BF16); kcb = T("kcb", [128, 4, NT], BF16); asb = T("asb", [128, 4, NT], BF16)
        vb = T("vb", [128, 4, NT], BF16); rs = T("rs", [128, 4, NT]); bonus = T("bonus", [128, 4, NT], BF16)
        gg = T("gg", [128, 4, NT], BF16); GC = T("GC", [128, 4, NT // 64])
        NTM = 9
        Tm = [T("Tm%d" % i, [128, NT]) for i in range(NTM)]
        Tn = [T("Tn%d" % i, [128, NT]) for i in range(NTM)]
        X = T("X", [128, 8, 128], BF16); VV = T("VV", [128, 8, 128], BF16)
        BcT = T("BcT", [128, 8, 64], BF16); KcT = T("KcT", [128, 8, 64], BF16)
        SC1 = T("SC1", [128, 8, 256], BF16); SC2 = T("SC2", [128, 8, 256], BF16)
        Pm = [T("Pm%d" % i, [128, 8, 128], BF16) for i in range(2)]
        PTm = [T("PTm%d" % i, [128, 8, 128], BF16) for i in range(2)]
        RhatT = T("RhatT", [128, 4, 128]); McT = T("McT", [128, 4, 2, 64]); H = T("H", [128, 2, 4, 64]); Nc = T("Nc", [128, 4, 2, 64])
        YT = T("YT", [128, 4, NT])
        yf = T("yf", [128, 4, NT], BF16); qT = T("qT", [128, 4, NT], BF16)
        KT2 = T("KT2", [128, 2, 2, 128 + NT], BF16); Vtm = T("Vtm", [128, 2, 1 + TL, 128], BF16)
        pT = T("pT", [128, 8, 2, 128], BF16); YA = T("YA", [128, 4, NT], BF16)
        cosT = T("cosT", [128, NT]); sinT = T("sinT", [128, NT])
        qraw = [T("qraw%d" % i, [128, NT], BF16) for i in range(2)]
        kf = T("kf", [128, 2, 128]); vf = T("vf", [128, 128]); kfB = T("kfB", [128, 2, 128]); vfB = T("vfB", [128, 128])
        Gtmp = [T("Gtmp%d" % i, [128, NT], BF16) for i in range(2)]
        mixR = T("mixR", [128, 8, NT], BF16); mix = T("mix", [128, 8, NT], BF16)
        x1 = T("x1", [128, 8, NT])
        x1b = [T("x1b%d" % i, [128, NT], BF16) for i in range(2)]
        x1q = [T("x1q%d" % i, [128, NT], BF16) for i in range(2)]
        hT = T("hT", [128, 32, NT], BF16)
        dena = T("dena", [128, NT]); denb = T("denb", [128, NT])
        HB = xio[:, 1, 512:768].rearrange("p (c j) -> p c j", j=64)
        ostage = xio[:, 1, 768:1024]
        CARB = T("CARB", [128, 2, 14]); SHFB = T("SHFB", [128, 2, 14]); KTB = T("KTB", [128, 2, 128], BF16); VtmB = T("VtmB", [128, 128], BF16)
        tmask = T("tmask", [128, NT]); SHF = T("SHF", [128, 2, 14])
        Snat = xio[0:64, 0, 0:512]; ckd = xio[:, 1, 0:256].rearrange("p (a b c) -> p a b c", a=2, b=2)
        S.dma(QM, tmask[:], tmask_d[:, :], w=['tmask'])

        S.op('pool', lambda e: e.memset(H[:], 0.0), w=[('H', 0), ('H', 1)])
        S.op('pool', lambda e: e.memset(CARRY[:], 0.0), w=[('CARRY', 0), ('CARRY', 1)])
        S.op('pool', lambda e: e.memset(KT2[:], 0.0), w=[('KT2', 0), ('KT2', 1)])
        S.op('pool', lambda e: e.memset(Vtm[:], 0.0), w=[('Vtm', 0), ('Vtm', 1)])
        S.op('pool', lambda e: e.memset(VV[:], 0.0), w=['VV'])

        def layernorm(l, ga, gb_, tag):
            s1 = ps[:, 0:NT]; s2 = ps[:, 512:512 + NT]
            for k in range(8):
                j = k % 2
                CP('act', x1b[j][:], x1[:, k, :], r=[('x1', k)], w=[('x1b', j)])
                TT('dve', x1q[j][:], x1[:, k, :], x1[:, k, :], ALU.mult, r=[('x1', k)], w=[('x1q', j)])
                MM(s1, onesb[:], x1b[j][:], start=(k == 0), stop=(k == 7), r=['onesb', ('x1b', j)], w=pk(0, NT))
                MM(s2, onesb[:], x1q[j][:], start=(k == 0), stop=(k == 7), r=['onesb', ('x1q', j)], w=pk(512, 512 + NT))
            mean, msq, var, rstd = Tm[0], Tm[1], Tm[2], Tm[3]
            ACT(mean[:], s1, AF.Copy, scale=1.0 / D, r=pk(0, NT), w=[('Tm', 0)])
            TT('pool', msq[:], mean[:], mean[:], ALU.mult, r=[('Tm', 0)], w=[('Tm', 1)])
            STT('dve', var[:], s2, 1.0 / D, msq[:], ALU.mult, ALU.subtract, r=pk(512, 512 + NT) + [('Tm', 1)], w=[('Tm', 2)])
            ACT(var[:], var[:], AF.Sqrt, bias=epsln[:, 0:1], r=[('Tm', 2), 'eps'], w=[('Tm', 2)])
            RCP(rstd[:], var[:], r=[('Tm', 2)], w=[('Tm', 3)])
            for k in range(8):
                d = Tm[4 + (k % 2)]
                TT('pool', d[:], x1[:, k, :], mean[:], ALU.subtract, r=[('x1', k), ('Tm', 0)], w=[('Tm', 4 + k % 2)])
                TT('dve', d[:], d[:], rstd[:], ALU.mult, r=[('Tm', 4 + k % 2), ('Tm', 3)], w=[('Tm', 4 + k % 2)])
                TS('dve', xT[:, k, :], d[:], P(l, ga + k, ga + k + 1), ALU.mult, P(l, gb_ + k, gb_ + k + 1), ALU.add,
                   r=[('Tm', 4 + k % 2), 'pv'], w=[('xT', k)])
                CP('act', xTb[:, k, :], xT[:, k, :], r=[('xT', k)], w=[('xTb', k)])

        def emit_wkv(dst):
            for c in range(4):
                TR(ps[0:64, c * 128:(c + 1) * 128], H[:, l, c, :], identf, r=[('H', l), 'cst'], w=pk(0, 512))
            for hh in range(2):
                CP('act', ostage[0:64, 0:256], ps[0:64, hh * 256:(hh + 1) * 256], r=pk(0, 512), w=['xio'])
                S.dma(QM, dst[hh * 4:(hh + 1) * 4].rearrange("h v k -> v h k"), ostage[0:64, 0:256].rearrange("p (h k) -> p h k", k=64), r=['xio'])

        epsln = T("epsln", [128, 2])
        S.op('pool', lambda e: e.memset(epsln[:, 0:1], LN_EPS), w=['eps'])
        S.op('pool', lambda e: e.memset(epsln[:, 1:2], GN_EPS), w=['eps'])

        for gi in range(NG + NSG):
            samp = gi >= NG
            g = gi if not samp else -1
            q = 2 * (gi - NG)
            qb = q + 1
            t0g = g * NT
            if not samp:
                S.dma('pool', cosT[:], cos_d[:, t0g:t0g + NT], w=['cosT'])
                S.dma('pool', sinT[:], sin_d[:, t0g:t0g + NT], w=['sinT'])
            else:
                S.dma('pool', cosT[:], coss_d[:, :], w=['cosT'])
                S.dma('pool', sinT[:], sins_d[:, :], w=['sinT'])
            if upto < 1:
                S.disabled = True
            if not samp:
                S.dma('pool', xio[:], xp[t0g:t0g + NT, :].rearrange("(t p) d -> p t d", p=128), w=['xio'])
            else:
                S.op('pool', lambda e: e.memset(xio[:], 0.0), w=['xio'])
                S.dma('pool', xio[0:4, 0, :], xs[q], w=['xio'])
                S.dma('pool', xio[0:4, 1, :], xs[qb], w=['xio'])
            for kp in range(4):
                reg = ps[:, kp * 512:(kp + 1) * 512]
                for kk in range(2):
                    k = kp * 2 + kk
                    for t in range(TL):
                        TR(ps[:, kp * 512 + kk * 256 + t * 128: kp * 512 + kk * 256 + (t + 1) * 128],
                           xio[:, t, k * 128:(k + 1) * 128], identf, r=['xio', 'cst'], w=pk(kp * 512, kp * 512 + 512))
                CP('act', xT[:, 2 * kp:2 * kp + 2, :], reg.rearrange("p (a b) -> p a b", a=2), r=pk(kp * 512, kp * 512 + 512),
                   w=[('xT', 2 * kp), ('xT', 2 * kp + 1)])
                CP('dve', xTb[:, 2 * kp:2 * kp + 2, :], reg.rearrange("p (a b) -> p a b", a=2), r=pk(kp * 512, kp * 512 + 512),
                   w=[('xTb', 2 * kp), ('xTb', 2 * kp + 1)])
            for l in range(2):
                last = (g == NG - 1)
                if samp:
                    S.dma(QM, Snat.rearrange("v (h k) -> v h k", k=64), swkv_i[l, q].rearrange("h v k -> v h k"), w=['xio'])
                    for c in range(4):
                        TR(ps[:, c * 64:(c + 1) * 64], Snat[:, c * 128:(c + 1) * 128], identf[0:64, 0:64], r=['xio', 'cst'], w=pk(0, 256))
                    CP('dve', H[:, l, :, :], ps[:, 0:256].rearrange("p (c j) -> p c j", j=64), r=pk(0, 256), w=[('H', l)])
                    S.dma(QM, CARRY[:, l, :], sshift_i[l, q].rearrange("(c p) -> p c", p=128), w=[('CARRY', l)], allow_slow_non_contiguous=True)
                    for dup in range(2):
                        S.dma(QM, ckd[:, :, dup, :], sck_i[l, q], w=['xio'])
                    for kvh in range(2):
                        TR(ps[:, 512 + kvh * 128:512 + (kvh + 1) * 128], ckd[:, kvh, :, :].rearrange("p a b -> p (a b)"), identf, r=['xio', 'cst'], w=pk(512, 1024))
                    CP('act', KT2[:, l, :, 0:128], ps[:, 512:768].rearrange("p (a b) -> p a b", b=128), r=pk(512, 1024), w=[('KT2', l)])
                    S.dma('pool', Vtm[:, l, 0, :], scv_i[l, q].rearrange("t h d -> t (h d)"), w=[('Vtm', l)])
                    S.dma(QM, Snat.rearrange("v (h k) -> v h k", k=64), swkv_i[l, qb].rearrange("h v k -> v h k"), w=['xio'])
                    for c in range(4):
                        TR(ps[:, c * 64:(c + 1) * 64], Snat[:, c * 128:(c + 1) * 128], identf[0:64, 0:64], r=['xio', 'cst'], w=pk(0, 256))
                    CP('dve', HB, ps[:, 0:256].rearrange("p (c j) -> p c j", j=64), r=pk(0, 256), w=['xio', 'HB'])
                    S.dma(QM, CARB[:, l, :], sshift_i[l, qb].rearrange("(c p) -> p c", p=128), w=[('CARB', l)], allow_slow_non_contiguous=True)
                    for dup in range(2):
                        S.dma(QM, ckd[:, :, dup, :], sck_i[l, qb], w=['xio'])
                    for kvh in range(2):
                        TR(ps[:, 512 + kvh * 128:512 + (kvh + 1) * 128], ckd[:, kvh, :, :].rearrange("p a b -> p (a b)"), identf, r=['xio', 'cst'], w=pk(512, 1024))
                    CP('act', KTB[:], ps[:, 512:768].rearrange("p (a b) -> p a b", b=128), r=pk(512, 1024), w=['KTB'])
                    S.dma('pool', VtmB[:], scv_i[l, qb].rearrange("t h d -> t (h d)"), w=['VtmB'])
                allx = [('xTb', k) for k in range(8)]
                ensure_conv('wi', l)
                for kvh in range(2):
                    for dup in range(2):
                        S.dma('pool', wk2[:, :, kvh, dup * 64:(dup + 1) * 64],
                              wi_b[l].rearrange("(k p) n -> p k n", p=128)[:, :, 2304 + kvh * 64: 2304 + (kvh + 1) * 64],
                              r=[('wi', l, k) for k in range(8)], w=['wk2'])
                if upto < 2:
                    S.disabled = True
                for blk in range(5):
                    W, wkey, wn = w_get('wi', l, blk)
                    for j in range(4):
                        c = blk * 4 + j
                        if c in (18, 19):
                            continue
                        po, pkey = bigslot()
                        for k in range(8):
                            MM(po, W[:, k, j * 128:(j + 1) * 128], xTb[:, k, :], start=(k == 0), stop=(k == 7),
                               r=[wkey, ('xTb', k)], w=pkey)
                        if c < 14:
                            z = Zc[c % 2]; zk = ('Zc', c % 2)
                            CP('act', z[:, 1:NT + 1], po, r=pkey, w=[zk])
                            CP('pool', z[:, 0:1], CARRY[:, l, c:c + 1], r=[('CARRY', l)], w=[zk])
                            tmp = Tm[c % 2]
                            TS('dve', tmp[:], z[:, 1:NT + 1], pd[:, l, c:c + 1], ALU.mult, r=[zk, ('pd', l)], w=[('Tm', c % 2)])
                            if c < 4:
                                dst, dk_ = rT[:, c, :], ('rT', c)
                            elif c < 8:
                                dst, dk_ = kraw[:, c - 4, :], ('kraw', c - 4)
                            elif c < 12:
                                dst, dk_ = vT[:, c - 8, :], ('vT', c - 8)
                            else:
                                dst, dk_ = Tm[2 + c % 2][:], ('Tm', 2 + c % 2)
                            if samp:
                                CP('pool', SHFB[:, l, c:c + 1], z[:, 132:133], r=[zk], w=[('SHFB', l)])
                                CP('pool', z[:, 128:129], CARB[:, l, c:c + 1], r=[zk, ('CARB', l), ('Tm', c % 2)], w=[zk])
                            STT('dve', dst, z[:, 0:NT], P(l, c, c + 1), tmp[:], ALU.mult, ALU.add,
                                r=[zk, 'pv', ('Tm', c % 2)], w=[dk_])
                            CP('pool', CARRY[:, l, c:c + 1], z[:, NT:NT + 1], r=[zk], w=[('CARRY', l)])
                            if samp:
                                CP('pool', SHF[:, l, c:c + 1], z[:, 4:5], r=[zk], w=[('SHF', l)])
                            if c == 12:
                                ACT(tw[0:64, :], dst[0:64, :], AF.Tanh, r=[dk_], w=['tw'])
                                CP('act', tw[64:128, :], dst[64:128, :], r=[dk_], w=['tw'])
                            if c == 13:
                                ACT(sgd[:], dst, AF.Sigmoid, r=[dk_], w=['sgd'])
                        else:
                            qi = c - 14
                            qr = qraw[qi % 2]; qk = ('qraw', qi % 2)
                            CP('act', qr[:], po, r=pkey, w=[qk])
                            p2, p2k = bigslot()
                            MM(p2, protb[:], qr[:], r=['protb', qk], w=p2k)
                            ta = Tm[4 + qi % 2]; tb_ = Tm[6 + qi % 2]
                            TT('dve', ta[:], p2, sinT[:], ALU.mult, r=p2k + ['sinT'], w=[('Tm', 4 + qi % 2)])
                            TT('pool', tb_[:], qr[:], cosT[:], ALU.mult, r=[qk, 'cosT'], w=[('Tm', 6 + qi % 2)])
                            TT('pool', qT[:, qi, :], ta[:], tb_[:], ALU.add, r=[('Tm', 4 + qi % 2), ('Tm', 6 + qi % 2)], w=[('qT', qi)])
                    if blk == 4:
                        for t in range(TL):
                            po, pkey = bigslot()
                            for k in range(8):
                                MM(po[:, 0:128], xTb[:, k, t * 128:(t + 1) * 128], W[:, k, 384:512], start=(k == 0), stop=(k == 7),
                                   r=[wkey, ('xTb', k)], w=pkey)
                            CP('act', Vtm[:, l, 1 + t, :], po[:, 0:128], r=pkey, w=[('Vtm', l)])
                            if (t == TL - 1 and last) or (samp and t == 0):
                                CP('dve', vf[:], po[:, 0:128], r=pkey, w=['vf'])
                            if samp and t == 1:
                                CP('dve', vfB[:], po[:, 0:128], r=pkey, w=['vfB'])
                    w_done(wn)
                for kvh in range(2):
                    po, pkey = bigslot()
                    for k in range(8):
                        MM(po, wk2[:, k, kvh, :], xTb[:, k, :], start=(k == 0), stop=(k == 7), r=['wk2', ('xTb', k)], w=pkey)
                    qr = qraw[kvh]; qk = ('qraw', kvh)
                    CP('act', qr[:], po, r=pkey, w=[qk])
                    p2, p2k = bigslot()
                    MM(p2, protb[:], qr[:], r=['protb', qk], w=p2k)
                    ta = Tm[4 + kvh]; tb_ = Tm[6 + kvh]
                    TT('dve', ta[:], p2, sinT[:], ALU.mult, r=p2k + ['sinT'], w=[('Tm', 4 + kvh)])
                    TT('pool', tb_[:], qr[:], cosT[:], ALU.mult, r=[qk, 'cosT'], w=[('Tm', 6 + kvh)])
                    TT('pool', KT2[:, l, kvh, 128:128 + NT], ta[:], tb_[:], ALU.add, r=[('Tm', 4 + kvh), ('Tm', 6 + kvh)], w=[('KT2', l)])
                    if last:
                        TT('pool', kf[:, kvh, :], ta[:, NT - 128:NT], tb_[:, NT - 128:NT], ALU.add,
                           r=[('Tm', 4 + kvh), ('Tm', 6 + kvh)], w=['kf'])
                    if samp:
                        TT('pool', kf[:, kvh, :], ta[:, 0:128], tb_[:, 0:128], ALU.add,
                           r=[('Tm', 4 + kvh), ('Tm', 6 + kvh)], w=['kf'])
                        TT('pool', kfB[:, kvh, :], ta[:, 128:256], tb_[:, 128:256], ALU.add,
                           r=[('Tm', 4 + kvh), ('Tm', 6 + kvh)], w=['kfB'])
                if upto < 3:
                    S.disabled = True
                def stg(half):
                    wgn = None
                    for gi_ in range(half * 8, half * 8 + 8):
                        if gi_ % 4 == 0:
                            if wgn is not None:
                                w_done(wgn)
                            Wg, wgk, wgn = w_get('wi', l, 5 + gi_ // 4)
                        po, pkey = bigslot()
                        for k in range(8):
                            MM(po, Wg[:, k, (gi_ % 4) * 128:(gi_ % 4 + 1) * 128], xTb[:, k, :], start=(k == 0), stop=(k == 7), r=[wgk, ('xTb', k)], w=pkey)
                        ACT(hT[:, gi_, :], po, AF.Sigmoid, r=pkey, w=[('hT', gi_)])
                        yield
                    w_done(wgn)

                def st6(t):
                    tsl = slice(t * 128, (t + 1) * 128)
                    first_tile = (g == 0 and t == 0 and not samp)
                    for h in (0, 2, 4, 6, 1, 3, 5, 7):
                        c = h // 2; pb = 64 * (h % 2); kvh = h // 4
                        kprev_ = KTB[pb:pb + 64, kvh, :] if (samp and t == 1) else KT2[pb:pb + 64, l, kvh, t * 128:(t + 1) * 128]
                        MM(ps[:, h * 256:h * 256 + 128], kprev_, qT[pb:pb + 64, c, tsl],
                           r=[('KT2', l), ('qT', c), 'KTB'], w=pk(h * 256, h * 256 + 128))
                        yield
                        MM(ps[:, h * 256 + 128:h * 256 + 256], KT2[pb:pb + 64, l, kvh, (t + 1) * 128:(t + 2) * 128], qT[pb:pb + 64, c, tsl],
                           r=[('KT2', l), ('qT', c)], w=pk(h * 256 + 128, h * 256 + 256))
                        yield
                    for q4 in range(4):
                        ACT(pT[:, q4 * 2:q4 * 2 + 2, :, :].rearrange("p a b c -> p (a b c)"), ps[:, q4 * 512:(q4 + 1) * 512], AF.Exp, scale=0.125,
                            r=pk(q4 * 512, q4 * 512 + 512), w=[('pT', q4)])
                        yield
                    mm_ = (mAtt0 if first_tile else mAtt)
                    ptk = [('pT', q4) for q4 in range(4)]
                    TT('pool', pT[:].rearrange("p a b c -> p a (b c)"), pT[:].rearrange("p a b c -> p a (b c)"), bc_mid(mm_, 8), ALU.mult,
                       r=ptk + ['cst'], w=ptk)
                    yield
                    for h in range(8):
                        c = h // 2; pb = 64 * (h % 2); kvh = h // 4
                        oo = ps[pb:pb + 64, c * 128:(c + 1) * 128]
                        vprev_ = VtmB[:, kvh * 64:(kvh + 1) * 64] if (samp and t == 1) else Vtm[:, l, t, kvh * 64:(kvh + 1) * 64]
                        MM(oo, vprev_, pT[:, h, 0, :], start=True, stop=False, r=[('Vtm', l), 'VtmB'] + ptk, w=pk(0, 512))
                        yield
                        MM(oo, Vtm[:, l, t + 1, kvh * 64:(kvh + 1) * 64], pT[:, h, 1, :], start=False, stop=True, r=[('Vtm', l)] + ptk, w=pk(0, 512))
                        yield
                        do = ps[pb:pb + 64, 512 + c * 128:512 + (c + 1) * 128]
                        MM(do, onesb[:, 0:64], pT[:, h, 0, :], start=True, stop=False, r=['onesb'] + ptk, w=pk(512, 1024))
                        yield
                        MM(do, onesb[:, 0:64], pT[:, h, 1, :], start=False, stop=True, r=['onesb'] + ptk, w=pk(512, 1024))
                        yield
                    den = dena; den2 = denb
                    dv = lambda tl: tl[:].rearrange("p (a b) -> p a b", b=128)
                    for c in range(4):
                        tgt = (den if c < 2 else den2)[:, (c % 2) * 128:(c % 2 + 1) * 128]
                        TS('dve', tgt, ps[:, 512 + c * 128:512 + (c + 1) * 128], pd[:, l, 18 + c:19 + c], ALU.add,
                           r=pk(512, 1024) + [('pd', l)], w=[('den', 0 if c < 2 else 1)])
                        yield
                    RCP(den[:], den[:], r=[('den', 0)], w=[('den', 0)])
                    yield
                    RCP(den2[:], den2[:], r=[('den', 1)], w=[('den', 1)])
                    yield
                    TT('dve', YA[:, 0:2, tsl], ps[:, 0:256].rearrange("p (a b) -> p a b", b=128), dv(den), ALU.mult,
                       r=pk(0, 512) + [('den', 0)], w=[('YA', 0), ('YA', 1)])
                    yield
                    TT('dve', YA[:, 2:4, tsl], ps[:, 256:512].rearrange("p (a b) -> p a b", b=128), dv(den2), ALU.mult,
                       r=pk(0, 512) + [('den', 1)], w=[('YA', 2), ('YA', 3)])
                    yield
                def st2(c, TS_, TK_):
                    sg, ic, kk_, t4, bT_, gs, E3, E1, rkk = TS_[0], TS_[1], TS_[2], TS_[3], TS_[4], TS_[5], TS_[6], TS_[7], TS_[8]
                    K = lambda i: (TK_, i)
                    cs = slice(c * 128, (c + 1) * 128)
                    p1, p1k = bigslot()
                    MM(p1, lora[0:64, l, 0, cs], tw[0:64, :], r=[('lora', l), 'tw'], w=p1k)
                    ACT(sg[:], p1, AF.Sigmoid, bias=P(l, 34 + c, 35 + c), r=p1k + ['pv'], w=[K(0)])
                    yield
                    if samp:
                        TT('pool', sg[:], sg[:], tmask[:], ALU.mult, r=[K(0), 'tmask'], w=[K(0)])
                        yield
                    p2, p2k = bigslot()
                    MM(p2, lora[64:128, l, 0, cs], tw[64:128, :], r=[('lora', l), 'tw'], w=p2k)
                    ACT(ic[:], p2, AF.Sigmoid, bias=P(l, 38 + c, 39 + c), r=p2k + ['pv'], w=[K(1)])
                    yield
                    p3, p3k = bigslot()
                    MM(p3, lora[:, l, 1, cs], sgd[:], r=[('lora', l), 'sgd'], w=p3k)
                    CP('act', gg[:, c, :], p3, r=p3k, w=[('gg', c)])
                    yield
                    TS('dve', kk_[:], kraw[:, c, :], P(l, 14 + c, 15 + c), ALU.mult, r=[('kraw', c), 'pv'], w=[K(2)])
                    yield
                    TT('pool', t4[:], kk_[:], kk_[:], ALU.mult, r=[K(2)], w=[K(3)])
                    yield
                    p4, p4k = bigslot()
                    MM(p4, blockones, t4[:], r=['cst', K(3)], w=p4k)
                    TS('dve', t4[:], p4, 1e-24, ALU.max, r=p4k, w=[K(3)])
                    yield
                    ACT(t4[:], t4[:], AF.Sqrt, r=[K(3)], w=[K(3)])
                    yield
                    RCP(t4[:], t4[:], r=[K(3)], w=[K(3)])
                    yield
                    TT('pool', kk_[:], kk_[:], t4[:], ALU.mult, r=[K(2), K(3)], w=[K(2)])
                    yield
                    if samp:
                        TT('pool', kk_[:], kk_[:], tmask[:], ALU.mult, r=[K(2), 'tmask'], w=[K(2)])
                        yield
                    TT('pool', bT_[:], kk_[:], ic[:], ALU.mult, r=[K(2), K(1)], w=[K(4)])
                    yield
                    TS('dve', ic[:], ic[:], P(l, 18 + c, 19 + c), ALU.mult, pd[:, l, 14 + c:15 + c], ALU.add,
                       r=[K(1), 'pv', ('pd', l)], w=[K(1)])
                    yield
                    TT('pool', ic[:], kraw[:, c, :], ic[:], ALU.mult, r=[('kraw', c), K(1)], w=[K(1)])
                    yield
                    if samp:
                        TT('pool', ic[:], ic[:], tmask[:], ALU.mult, r=[K(1), 'tmask'], w=[K(1)])
                        yield
                    S.op('dve', (lambda o_, d0, d1: (lambda e: e.tensor_tensor_scan(out=o_, data0=d0, data1=d1, initial=0.0,
                                                                                    op0=ALU.mult, op1=ALU.add)))(gs[:], resetm, sg[:]),
                         r=['cst', K(0)], w=[K(5)])
                    yield
                    ACT(E3[:], gs[:], AF.Exp, scale=-DECAY_C, r=[K(5)], w=[K(6)])
                    yield
                    ACT(gs[:], gs[:], AF.Exp, scale=DECAY_C, r=[K(5)], w=[K(5)])
                    yield
                    ACT(sg[:], sg[:], AF.Exp, scale=DECAY_C, r=[K(0)], w=[K(0)])
                    yield
                    nch = NT // 64
                    e3v = E3[:].rearrange("p (a b) -> p a b", b=64)
                    i3v = gs[:].rearrange("p (a b) -> p a b", b=64)
                    CP('pool', GC[:, c, :], e3v[:, :, 63], r=[K(6)], w=[('GC', c)])
                    yield
                    TT('dve', E1[:].rearrange("p (a b) -> p a b", b=64), e3v, i3v[:, :, 63:64].broadcast_to([128, nch, 64]),
                       ALU.mult, r=[K(6), K(5)], w=[K(7)])
                    yield
                    TT('dve', i3v, i3v, e3v[:, :, 63:64].broadcast_to([128, nch, 64]), ALU.mult,
                       r=[K(5), K(6)], w=[K(5)])
                    yield
                    acv = ACRC[:, c, :, 0, :]
                    rcv = ACRC[:, c, :, 1, :]
                    v3 = lambda ap: ap.rearrange("p (a b) -> p a b", b=128)
                    TT('pool', rcv, v3(rT[:, c, :]), v3(E1[:]), ALU.mult, r=[('rT', c), K(7)], w=[('ACRC', c)])
                    yield
                    TT('pool', rs[:, c, :], rT[:, c, :], E3[:], ALU.mult, r=[('rT', c), K(6)], w=[('rs', c)])
                    yield
                    STT('dve', sg[:], kk_[:], -1.0, sg[:], ALU.mult, ALU.mult, r=[K(2), K(0)], w=[K(0)])
                    yield
                    TT('dve', acv, v3(sg[:]), v3(E1[:]), ALU.mult, r=[K(0), K(7)], w=[('ACRC', c)])
                    yield
                    TT('pool', asb[:, c, :], sg[:], E3[:], ALU.mult, r=[K(0), K(6)], w=[('asb', c)])
                    yield
                    TT('pool', bcb[:, c, :], bT_[:], gs[:], ALU.mult, r=[K(4), K(5)], w=[('bcb', c)])
                    yield
                    TT('dve', kcb[:, c, :], ic[:], gs[:], ALU.mult, r=[K(1), K(5)], w=[('kcb', c)])
                    yield
                    CP('act', vb[:, c, :], vT[:, c, :], r=[('vT', c)], w=[('vb', c)])
                    yield
                    TT('pool', rkk[:], rT[:, c, :], ic[:], ALU.mult, r=[('rT', c), K(1)], w=[K(8)])
                    yield
                    TS('dve', rkk[:], rkk[:], P(l, 22 + c, 23 + c), ALU.mult, r=[K(8), 'pv'], w=[K(8)])
                    yield
                    p5, p5k = bigslot()
                    MM(p5, blockones, rkk[:], r=['cst', K(8)], w=p5k)
                    TT('dve', bonus[:, c, :], p5, vT[:, c, :], ALU.mult, r=p5k + [('vT', c)], w=[('bonus', c)])
                    yield

                ntile6 = TL
                for pi_, pair_ in enumerate(((0, 1), (2, 3))):
                    gens_ = [st2(c, Tm if c % 2 == 0 else Tn, 'Tm' if c % 2 == 0 else 'Tn') for c in pair_]
                    if pi_ < ntile6 and upto >= 6:
                        gens_.append(st6(pi_))
                    gens_.append(stg(pi_))
                    for _ in zip_longest(*gens_):
                        pass
                dump("rT", rT[:], [('rT', c) for c in range(4)])
                dump("rs", rs[:], [('rs', c) for c in range(4)])
                dump("asb", asb[:], [('asb', c) for c in range(4)])
                dump("bcb", bcb[:], [('bcb', c) for c in range(4)])
                dump("kcb", kcb[:], [('kcb', c) for c in range(4)])
                dump("ACRC", ACRC[:].rearrange("p a b c d -> p (a b c d)"), [('ACRC', c) for c in range(4)])
                if upto < 3.05:
                    S.disabled = True
                allp = [('asb', c) for c in range(4)] + [('bcb', c) for c in range(4)] + [('kcb', c) for c in range(4)] + [('vb', c) for c in range(4)]
                for t in range(TL):
                    tsl = slice(t * 128, (t + 1) * 128)
                    for c in range(4):
                        MM(ps[:, c * 128:(c + 1) * 128], asb[:, c, tsl], identb[:], r=[('asb', c), 'identb'], w=pk(0, 512))
                        MM(ps[:, 512 + c * 128:512 + (c + 1) * 128], bcb[:, c, tsl], identb[:], r=[('bcb', c), 'identb'], w=pk(512, 1024))
                        MM(ps[:, 1024 + c * 128:1024 + (c + 1) * 128], kcb[:, c, tsl], identb[:], r=[('kcb', c), 'identb'], w=pk(1024, 1536))
                        MM(ps[:, 1536 + c * 128:1536 + (c + 1) * 128], vb[:, c, tsl], identb[:], r=[('vb', c), 'identb'], w=pk(1536, 2048))
                    h64 = lambda ap: ap.rearrange("p (h j) -> p h j", j=64)
                    CP('act', X[:, :, 0:64], h64(ps[:, 0:512]), r=pk(0, 512), w=[('X', 0), ('X', 1)])
                    CP('dve', BcT[:], h64(ps[:, 512:1024]), r=pk(512, 1024), w=['BcT'])
                    CP('act', KcT[:], h64(ps[:, 1024:1536]), r=pk(1024, 1536), w=['KcT'])
                    CP('dve', VV[:, :, 64:128], h64(ps[:, 1536:2048]), r=pk(1536, 2048), w=['VV'])
                    if upto < 3.1:
                        S.disabled = True
                    m12 = bc_mid(mask12, 4)
                    for hg in range(2):
                        for hi in (0, 2, 1, 3):
                            h = hg * 4 + hi; c = h // 2; pb = 64 * (h % 2)
                            rhs2 = ACRC[pb:pb + 64, c, t, :, :].rearrange("p a b -> p (a b)")
                            MM(ps[:, hi * 256:(hi + 1) * 256], bcb[pb:pb + 64, c, tsl], rhs2, r=[('bcb', c), ('ACRC', c)], w=pk(hi * 256, hi * 256 + 256))
                            MM(ps[:, 1024 + hi * 256:1024 + (hi + 1) * 256], kcb[pb:pb + 64, c, tsl], rhs2, r=[('kcb', c), ('ACRC', c)],
                               w=pk(1024 + hi * 256, 1024 + hi * 256 + 256))
                        m12h = bc_mid(mask12, 2)
                        for bq in range(2):
                            TT('dve', SC1[:, hg * 4 + 2 * bq:hg * 4 + 2 * bq + 2, :], ps[:, bq * 512:(bq + 1) * 512].rearrange("p (h j) -> p h j", j=256), m12h, ALU.mult,
                               r=pk(bq * 512, bq * 512 + 512) + ['cst'], w=[('SC1', hg)])
                            TT('dve', SC2[:, hg * 4 + 2 * bq:hg * 4 + 2 * bq + 2, :], ps[:, 1024 + bq * 512:1024 + (bq + 1) * 512].rearrange("p (h j) -> p h j", j=256), m12h, ALU.mult,
                               r=pk(1024 + bq * 512, 1024 + bq * 512 + 512) + ['cst'], w=[('SC2', hg)])
                    if upto < 3.2:
                        S.disabled = True
                    sck = [('SC1', 0), ('SC1', 1)]; sck2 = [('SC2', 0), ('SC2', 1)]
                    for h in (0, 2, 4, 6, 1, 3, 5, 7):
                        c = h // 2; pb = 64 * (h % 2)
                        MM(ps[:, h * 128:(h + 1) * 128], ACRC[pb:pb + 64, c, t, 0, :], bcb[pb:pb + 64, c, tsl], r=[('ACRC', c), ('bcb', c)],
                           w=pk(h * 128, h * 128 + 128))
                    for bq in range(2):
                        TT('dve', Pm[0][:, 4 * bq:4 * bq + 4, :], ps[:, bq * 512:(bq + 1) * 512].rearrange("p (h j) -> p h j", j=128), bc_mid(maskT, 4), ALU.mult,
                           r=pk(bq * 512, bq * 512 + 512) + ['cst'], w=[('Pm', 0, bq)])
                    for h in range(8):
                        MM(ps[:, 1024 + h * 64:1024 + (h + 1) * 64], SC2[:, h, 0:128], VV[:, h, 64:128], r=sck2 + ['VV'],
                           w=pk(1024 + h * 64, 1024 + h * 64 + 64))
                    CP('act', X[:, :, 64:128], h64(ps[:, 1024:1536]), r=pk(1024, 1536), w=[('X', 0), ('X', 1)])
                    if upto < 3.3:
                        S.disabled = True
                    for j in range(6):
                        a = j % 2; b = 1 - a
                        for bq in range(2):
                            hs_ = range(4 * bq, 4 * bq + 4)
                            ptv = (lambda h_: SC1[:, h_, 0:128]) if j == 0 else (lambda h_: PTm[a][:, h_, :])
                            ptk_ = sck if j == 0 else [('PTm', a, bq)]
                            for h in hs_:
                                MM(ps[:, h * 128:(h + 1) * 128], ptv(h), X[:, h, :], r=ptk_ + [('X', bq)], w=pk(h * 128, h * 128 + 128))
                            if j < 5:
                                for h in hs_:
                                    MM(ps[:, 1024 + h * 128:1024 + (h + 1) * 128], ptv(h), Pm[a][:, h, :], r=ptk_ + [('Pm', a, bq)],
                                       w=pk(1024 + h * 128, 1024 + h * 128 + 128))
                                for h in hs_:
                                    MM(ps[:, 2048 + h * 128:2048 + (h + 1) * 128], Pm[a][:, h, :], ptv(h), r=ptk_ + [('Pm', a, bq)],
                                       w=pk(2048 + h * 128, 2048 + h * 128 + 128))
                        for bq in range(2):
                            hs_ = slice(4 * bq, 4 * bq + 4)
                            TT('dve', X[:, hs_, :], ps[:, bq * 512:(bq + 1) * 512].rearrange("p (h j) -> p h j", j=128), X[:, hs_, :], ALU.add,
                               r=pk(bq * 512, bq * 512 + 512) + [('X', bq)], w=[('X', bq)])
                            if j < 5:
                                CP('act', Pm[b][:, hs_, :], ps[:, 1024 + bq * 512:1024 + (bq + 1) * 512].rearrange("p (h j) -> p h j", j=128),
                                   r=pk(1024 + bq * 512, 1536 + bq * 512), w=[('Pm', b, bq)])
                                CP('act' if bq == 0 else 'dve', PTm[b][:, hs_, :], ps[:, 2048 + bq * 512:2048 + (bq + 1) * 512].rearrange("p (h j) -> p h j", j=128),
                                   r=pk(2048 + bq * 512, 2560 + bq * 512), w=[('PTm', b, bq)])
                    if upto < 3.4:
                        S.disabled = True
                    for h in range(8):
                        c = h // 2; pb = 64 * (h % 2)
                        MM(ps[pb:pb + 64, c * 128:(c + 1) * 128], X[:, h, 0:64], SC1[:, h, 128:256], r=[('X', 0), ('X', 1)] + sck, w=pk(0, 512))
                    TT('dve', RhatT[:], ps[:, 0:512].rearrange("p (c j) -> p c j", j=128), rs[:, :, tsl], ALU.add,
                       r=pk(0, 512) + [('rs', c) for c in range(4)], w=['RhatT'])
                    if upto < 3.5:
                        S.disabled = True
                    mreg = [(512, 768), (1024, 1280)]; nreg = [(1536, 1792), (2048, 2304)]
                    for ch in range(2):
                        chs = slice(ch * 64, (ch + 1) * 64)
                        for h in range(8):
                            c = h // 2; pb = 64 * (h % 2)
                            MM(ps[pb:pb + 64, mreg[ch][0] + c * 64: mreg[ch][0] + (c + 1) * 64], X[chs, h, 0:64], BcT[chs, h, :],
                               r=[('X', 0), ('X', 1), 'BcT'], w=pk(*mreg[ch]))
                            no = ps[pb:pb + 64, nreg[ch][0] + c * 64: nreg[ch][0] + (c + 1) * 64]
                            MM(no, BcT[chs, h, :], X[chs, h, 64:128], start=True, stop=False, r=['BcT', ('X', 0), ('X', 1)], w=pk(*nreg[ch]))
                            MM(no, KcT[chs, h, :], VV[chs, h, 64:128], start=False, stop=True, r=['KcT', 'VV'], w=pk(*nreg[ch]))
                    for ch in range(2):
                        for c in range(4):
                            STT('dve', McT[:, c, ch, :], I2, GC[:, c, t * 2 + ch: t * 2 + ch + 1],
                                ps[:, mreg[ch][0] + c * 64: mreg[ch][0] + (c + 1) * 64], ALU.mult, ALU.add,
                                r=['cst', ('GC', c)] + pk(*mreg[ch]), w=['McT'])
                        CP('act', Nc[:, :, ch, :], ps[:, nreg[ch][0]:nreg[ch][1]].rearrange("p (c j) -> p c j", j=64), r=pk(*nreg[ch]), w=['Nc'])
                    if upto < 3.6:
                        S.disabled = True
                    if samp and t == 1:
                        emit_wkv(so_wkv[l, q])
                        CP('dve', H[:, l, :, :], HB, r=['xio', 'HB'], w=[('H', l)])
                    for ch in range(2):
                        if ch == 1 and upto < 3.95:
                            S.disabled = True
                        if ch == 0 and t == 1 and upto < 3.97:
                            S.disabled = True
                        chs = slice(ch * 64, (ch + 1) * 64)
                        yreg = (2560, 2816)
                        for h in range(8):
                            c = h // 2; pb = 64 * (h % 2)
                            yo = ps[pb:pb + 64, 2560 + c * 64:2560 + (c + 1) * 64]
                            MM(yo, X[:, h, 64:128], SC1[:, h, 128 + ch * 64:128 + (ch + 1) * 64], start=True, stop=False, r=[('X', 0), ('X', 1)] + sck, w=pk(*yreg))
                            MM(yo, VV[:, h, 64:128], SC2[:, h, 128 + ch * 64:128 + (ch + 1) * 64], start=False, stop=False, r=['VV'] + sck2, w=pk(*yreg))
                            MM(yo, H[pb:pb + 64, l, c, :], RhatT[pb:pb + 64, c, chs], start=False, stop=True, r=[('H', l), 'RhatT'], w=pk(*yreg))
                        if upto < 3.7:
                            S.disabled = True
                        for h in (0, 2, 4, 6, 1, 3, 5, 7):
                            c = h // 2; pb = 64 * (h % 2)
                            hb = 0 if pb == 0 else 512
                            MM(ps[pb:pb + 64, hb + c * 64:hb + (c + 1) * 64], McT[pb:pb + 64, c, ch, :], H[pb:pb + 64, l, c, :],
                               r=['McT', ('H', l)], w=pk(hb, hb + 256))
                        if upto < 3.8:
                            S.disabled = True
                        CP('act', YT[:, :, t * 128 + ch * 64: t * 128 + (ch + 1) * 64], ps[:, 2560:2816].rearrange("p (c j) -> p c j", j=64),
                           r=pk(*yreg), w=[('YT', c) for c in range(4)])
                        if upto < 3.9:
                            S.disabled = True
                        TT('dve', H[0:64, l, :, :], ps[0:64, 0:256].rearrange("p (c j) -> p c j", j=64), Nc[0:64, :, ch, :], ALU.add,
                           r=pk(0, 256) + ['Nc'], w=[('H', l)])
                        TT('dve', H[64:128, l, :, :], ps[64:128, 512:768].rearrange("p (c j) -> p c j", j=64), Nc[64:128, :, ch, :], ALU.add,
                           r=pk(512, 768) + ['Nc'], w=[('H', l)])
                dump("YT", YT[:], [('YT', c) for c in range(4)])
                if upto < 5:
                    S.disabled = True
                def st5(c):
                    K = lambda i: ('Tm', i)
                    d_, dq_, sd = Tm[0 + 3 * (c % 2)], Tm[1 + 3 * (c % 2)], Tm[2 + 3 * (c % 2)]
                    k0, k1, k2 = K(0 + 3 * (c % 2)), K(1 + 3 * (c % 2)), K(2 + 3 * (c % 2))
                    p1, p1k = bigslot()
                    MM(p1, blockones, YT[:, c, :], r=['cst', ('YT', c)], w=p1k)
                    STT('dve', d_[:], p1, -1.0 / 64, YT[:, c, :], ALU.mult, ALU.add, r=p1k + [('YT', c)], w=[k0])
                    yield
                    TT('pool', dq_[:], d_[:], d_[:], ALU.mult, r=[k0], w=[k1])
                    yield
                    p2, p2k = bigslot()
                    MM(p2, blockones, dq_[:], r=['cst', k1], w=p2k)
                    ACT(sd[:], p2, AF.Sqrt, bias=epsln[:, 1:2], scale=1.0 / 64, r=p2k + ['eps'], w=[k2])
                    yield
                    RCP(sd[:], sd[:], r=[k2], w=[k2])
                    yield
                    TT('dve', d_[:], d_[:], sd[:], ALU.mult, r=[k0, k2], w=[k0])
                    yield
                    TS('dve', d_[:], d_[:], P(l, 26 + c, 27 + c), ALU.mult, P(l, 30 + c, 31 + c), ALU.add, r=[k0, 'pv'], w=[k0])
                    yield
                    TT('pool', d_[:], d_[:], bonus[:, c, :], ALU.add, r=[k0, ('bonus', c)], w=[k0])
                    yield
                    TT('pool', yf[:, c, :], d_[:], gg[:, c, :], ALU.mult, r=[k0, ('gg', c)], w=[('yf', c)])
                    yield

                for pair_ in ((0, 1), (2, 3)):
                    gens_ = [st5(c) for c in pair_]
                    for _ in zip_longest(*gens_):
                        pass
                dump("yf", yf[:], [('yf', c) for c in range(4)])
                dump("YA", YA[:], [('YA', c) for c in range(4)])
                if upto < 7:
                    S.disabled = True
                for br in range(2):
                    Wb, wbk, wbn = w_get('wbr' if br == 0 else 'wba', l, 0)
                    src_act = yf if br == 0 else YA
                    sk = 'yf' if br == 0 else 'YA'
                    for m in range(8):
                        gt = hT[:, br * 8 + m, :]; gk = ('hT', br * 8 + m)
                        p2, p2k = bigslot()
                        for c in range(4):
                            MM(p2, Wb[:, c, m * 128:(m + 1) * 128], src_act[:, c, :], start=(c == 0), stop=(c == 3), r=[wbk, (sk, c)], w=p2k)
                        if br == 0:
                            TT('dve', mixR[:, m, :], p2, gt, ALU.mult, r=p2k + [gk], w=[('mixR', m)])
                        else:
                            tm_ = Tm[m % 2]
                            TT('dve', tm_[:], p2, gt, ALU.mult, r=p2k + [gk], w=[('Tm', m % 2)])
                            TT('pool', mix[:, m, :], tm_[:], mixR[:, m, :], ALU.add, r=[('Tm', m % 2), ('mixR', m)], w=[('mix', m)])
                    w_done(wbn)
                for m in range(8):
                    if m % 4 == 0:
                        if m > 0:
                            w_done(won)
                        Wo, wok, won = w_get('wo', l, m // 4)
                    po, pkey = bigslot()
                    for k in range(8):
                        MM(po, Wo[:, k, (m % 4) * 128:(m % 4 + 1) * 128], mix[:, k, :], start=(k == 0), stop=(k == 7), r=[wok, ('mix', k)], w=pkey)
                    STT('dve', x1[:, m, :], xT[:, m, :], ALPHA, po, ALU.mult, ALU.add, r=[('xT', m)] + pkey, w=[('x1', m)])
                dump("mix", mix[:], [('mix', k) for k in range(8)])
                dump("x1pre", x1[:], [('x1', k) for k in range(8)])
                w_done(won)
                layernorm(l, 42, 50, 'ln1')
                dump("xln1", xT[:], [('xT', k) for k in range(8)])
                if upto < 8:
                    S.disabled = True
                for fb in range(8):
                    Wu, wuk, wun = w_get('wu', l, fb)
                    for j in range(4):
                        f = fb * 4 + j
                        po, pkey = bigslot()
                        for k in range(8):
                            MM(po, Wu[:, k, j * 128:(j + 1) * 128], xTb[:, k, :], start=(k == 0), stop=(k == 7), r=[wuk, ('xTb', k)], w=pkey)
                        gt = Gtmp[f % 2]; gk = ('Gtmp', f % 2)
                        ACT(gt[:], po, AF.Relu, r=pkey, w=[gk])
                        TT('dve', hT[:, f, :], gt[:], gt[:], ALU.mult, r=[gk], w=[('hT', f)])
                    w_done(wun)
                for m in range(8):
                    Wd, wdk, wdn = w_get('wd', l, m)
                    po, pkey = bigslot()
                    for f in range(32):
                        MM(po, Wd[:, f, :], hT[:, f, :], start=(f == 0), stop=(f == 31), r=[wdk, ('hT', f)], w=pkey)
                    STT('dve', x1[:, m, :], xT[:, m, :], ALPHA, po, ALU.mult, ALU.add, r=[('xT', m)] + pkey, w=[('x1', m)])
                    w_done(wdn)
                layernorm(l, 58, 66, 'ln2')
                if upto < 9:
                    S.disabled = True
                CP('pool', KT2[:, l, :, 0:128], KT2[:, l, :, NT:NT + 128], r=[('KT2', l)], w=[('KT2', l)])
                CP('pool', Vtm[:, l, 0, :], Vtm[:, l, TL, :], r=[('Vtm', l)], w=[('Vtm', l)])
                if samp:
                    emit_wkv(so_wkv[l, qb])
                    for (qq_, shf_, kf_, vf_) in ((q, SHF, kf, vf), (qb, SHFB, kfB, vfB)):
                        S.dma(QM, so_shift[l, qq_].rearrange("(c p) -> p c", p=128), shf_[:, l, :], r=[('SHF', l), ('SHFB', l)], allow_slow_non_contiguous=True)
                        for kvh in range(2):
                            TR(ps[:, 512 + kvh * 64:512 + (kvh + 1) * 64], kf_[0:64, kvh, :], identf[0:64, 0:64], r=['kf', 'kfB', 'cst'], w=pk(512, 1024))
                        CP('act', ostage[:, 0:128], ps[:, 512:640], r=pk(512, 1024), w=['xio'])
                        S.dma(QM, so_ck[l, qq_, 124:128].rearrange("t h d -> t (h d)"), ostage[0:4, 0:128], r=['xio'])
                        S.dma(QM, so_cv[l, qq_, 124:128].rearrange("t h d -> t (h d)"), vf_[0:4, :], r=['vf', 'vfB'])
                        S.dma(QM, so_ck[l, qq_, 0:124].rearrange("t h d -> t (h d)"), sck_i[l, qq_, 4:128].rearrange("t h d -> t (h d)"))
                        S.dma(QM, so_cv[l, qq_, 0:124].rearrange("t h d -> t (h d)"), scv_i[l, qq_, 4:128].rearrange("t h d -> t (h d)"))
                if last:
                    for c in range(4):
                        TR(ps[0:64, c * 128:(c + 1) * 128], H[:, l, c, :], identf, r=[('H', l), 'cst'], w=pk(0, 512))
                    CP('act', ostage[0:64, 0:512 // 2 * 0 + 256], ps[0:64, 0:256], r=pk(0, 512), w=['xio'])
                    S.dma(QM, o_wkv[l, 0:4].rearrange("h v k -> v h k"), ostage[0:64, 0:256].rearrange("p (h k) -> p h k", k=64), r=['xio'])
                    CP('act', ostage[0:64, 0:256], ps[0:64, 256:512], r=pk(0, 512), w=['xio'])
                    S.dma(QM, o_wkv[l, 4:8].rearrange("h v k -> v h k"), ostage[0:64, 0:256].rearrange("p (h k) -> p h k", k=64), r=['xio'])
                    S.dma(QM, o_shift[l].rearrange("(c p) -> p c", p=128), CARRY[:, l, :], r=[('CARRY', l)], allow_slow_non_contiguous=True)
                    for kvh in range(2):
                        TR(ps[:, 512 + kvh * 64:512 + (kvh + 1) * 64], kf[0:64, kvh, :], identf[0:64, 0:64], r=['kf', 'cst'], w=pk(512, 1024))
                    CP('act', ostage[:, 0:128], ps[:, 512:640], r=pk(512, 1024), w=['xio'])
                    S.dma(QM, o_ck[l].rearrange("t h d -> t (h d)"), ostage[:, 0:128], r=['xio'])
                    S.dma(QM, o_cv[l].rearrange("t h d -> t (h d)"), vf[:], r=['vf'])
            for t in range(TL):
                for kp in range(2):
                    for kk in range(4):
                        k = kp * 4 + kk
                        TR(ps[:, kp * 512 + kk * 128: kp * 512 + (kk + 1) * 128], xT[:, k, t * 128:(t + 1) * 128], identf,
                           r=[('xT', k), 'cst'], w=pk(kp * 512, kp * 512 + 512))
                    CP('act' if kp == 0 else 'dve', xio[:, t, kp * 512:(kp + 1) * 512], ps[:, kp * 512:(kp + 1) * 512],
                       r=pk(kp * 512, kp * 512 + 512), w=['xio'])
            if not samp:
                S.dma(QM, y_p[t0g:t0g + NT, :].rearrange("(t p) d -> p t d", p=128), xio[:], r=['xio'])
            else:
                S.dma(QM, y_s[q], xio[0:4, 0, :], r=['xio'])
                S.dma(QM, y_s[qb], xio[0:4, 1, :], r=['xio'])
        stats = S.emit()
    return nc, stats


def pack_pv(inp):
    pv = np.zeros((128, 2 * PVL), np.float32)
    for l in range(2):
        b = l * PVL
        pv[:, b:b + 14] = inp['mu_shift'][l].reshape(14, 128).T
        for off, name in ((14, 'k_k'), (18, 'k_a'), (26, 'lnx_g'), (30, 'lnx_b'), (34, 'decay_base'), (38, 'iclr_base')):
            pv[:, b + off:b + off + 4] = inp[name][l].reshape(4, 128).T
        pv[:, b + 22:b + 26] = inp['r_k'][l].reshape(512).reshape(4, 128).T
        for off, name in ((42, 'ln1_g'), (50, 'ln1_b'), (58, 'ln2_g'), (66, 'ln2_b')):
            pv[:, b + off:b + off + 8] = inp[name][l].reshape(8, 128).T
        pv[:, b + 74:b + 78] = np.repeat(inp['sinks'][l].reshape(4, 2), 64, axis=1).T
    return pv


_CACHE = {}


def host_inputs(inp, c, SEQ, NSS, consts):
    cst, cosT, sinT, coss, sins, tmask, pv = consts
    wnames = ['w_in', 'w_br_rwkv', 'w_br_attn', 'w_out', 'w_ff_up', 'w_ff_down', 'decay_up', 'iclr_up', 'gate_up']
    m = {k: inp[k] for k in wnames}
    sl = slice(c * NSS, (c + 1) * NSS)
    m.update(xp=inp['x_prompt'][(c * 2) // 8][:SEQ], pv_in=pv, cst_in=cst, cos_in=cosT, sin_in=sinT, coss_in=coss, sins_in=sins, tmask_in=tmask,
             xs=np.ascontiguousarray(inp['x_sample'][sl]), swkv_i=np.ascontiguousarray(inp['state_wkv'][:, sl]),
             sshift_i=np.ascontiguousarray(inp['state_shift'][:, sl]), sck_i=np.ascontiguousarray(inp['cache_k_win'][:, sl]),
             scv_i=np.ascontiguousarray(inp['cache_v_win'][:, sl]))
    return m


def host_consts(inp, SEQ, past_len=8192):
    cst = make_consts()
    cosT, sinT = rope_tables(np.arange(SEQ))
    coss, sins = rope_tables(past_len + (np.arange(NT) % 128))
    tmask = np.zeros((128, NT), np.float32)
    tmask[:, 0:4] = 1.0
    tmask[:, 128:132] = 1.0
    return cst, cosT, sinT, coss, sins, tmask, pack_pv(inp)


def kernel(**inputs):
    inp = {k: np.ascontiguousarray(np.asarray(v)) for k, v in inputs.items()}
    B, SEQ, _ = inp['x_prompt'].shape
    NSS = inp['x_sample'].shape[0] // 8
    if SEQ not in _CACHE:
        _CACHE[SEQ] = build(SEQ, NSS)
    nc, _ = _CACHE[SEQ]
    consts = host_consts(inp, SEQ)
    in_maps = [host_inputs(inp, c, SEQ, NSS, consts) for c in range(8)]
    res = run_bass_kernel_spmd(nc, in_maps, core_ids=list(range(8))).results
    y_p = np.stack([res[0]['y_p'], res[4]['y_p']])
    p_wkv = np.stack([res[0]['p_wkv'], res[4]['p_wkv']], 1)
    p_shift = np.stack([res[0]['p_shift'], res[4]['p_shift']], 1)
    p_ck = np.stack([res[0]['p_ck'], res[4]['p_ck']], 1)
    p_cv = np.stack([res[0]['p_cv'], res[4]['p_cv']], 1)
    y_s = np.concatenate([res[c]['y_s'] for c in range(8)], 0)
    s_wkv = np.concatenate([res[c]['s_wkv'] for c in range(8)], 1)
    s_shift = np.concatenate([res[c]['s_shift'] for c in range(8)], 1)
    s_ck = np.concatenate([res[c]['s_ck'] for c in range(8)], 1)
    s_cv = np.concatenate([res[c]['s_cv'] for c in range(8)], 1)
    return (y_p, y_s, p_wkv, p_shift, p_ck, p_cv, s_wkv, s_shift, s_ck, s_cv)
```

```python
import numpy as np
from contextlib import ExitStack
from itertools import zip_longest
import concourse.bass as bass
import concourse.mybir as mybir
from concourse.ap import AP
from concourse.bass_utils import run_bass_kernel_spmd

F32 = mybir.dt.float32
BF16 = mybir.dt.bfloat16
AF = mybir.ActivationFunctionType
ALU = mybir.AluOpType

D = 1024
NT = 256
TL = NT // 128
SHIFT_W = 1792
IN_W = 4608
DFF = 4096
ALPHA = 4 ** 0.25
LN_EPS = 1e-5
GN_EPS = 64e-5
DECAY_C = 0.6065306597126334
PVL = 78
QM = 'pool'
NCST = 1728


class Sched:
    COMPUTE = ('pe', 'act', 'dve', 'pool')

    def __init__(self, nc, es, n_dma_sems=24):
        self.nc = nc
        self.h = {'pe': nc.tensor, 'act': nc.scalar, 'dve': nc.vector, 'pool': nc.gpsimd, 'sp': nc.sync}
        self.ops = []
        self.last_w = {}
        self.readers = {}
        self.sem = {e: es.enter_context(nc.semaphore("s_" + e)) for e in self.COMPUTE}
        self.dsem = []
        self.dq = {}
        for q, n in (('sp', n_dma_sems), ('act', 8), ('pool', 12)):
            self.dq[q] = list(range(len(self.dsem), len(self.dsem) + n))
            self.dsem += [es.enter_context(nc.semaphore("d%s%d" % (q, i))) for i in range(n)]

    def _add(self, kind, eng, fn, r, w):
        isps = lambda k: isinstance(k, tuple) and k[0] in ('ps', 'pst')
        w = list(w) + [k for k in r if isps(k)]
        r = [k for k in r if not isps(k)]
        oid = len(self.ops)
        deps = set()
        for k in r:
            if k in self.last_w:
                deps.add(self.last_w[k])
        for k in w:
            if k in self.last_w:
                deps.add(self.last_w[k])
            deps |= self.readers.get(k, set())
        for k in r:
            self.readers.setdefault(k, set()).add(oid)
        for k in w:
            self.last_w[k] = oid
            self.readers[k] = set()
        deps.discard(oid)
        self.ops.append(dict(kind=kind, eng=eng, fn=fn, deps=deps))
        return oid

    disabled = False

    def op(self, eng, fn, r=(), w=()):
        if self.disabled:
            return None
        return self._add('c', eng, fn, r, w)

    def dma(self, eng, out, in_, r=(), w=(), **kw):
        if self.disabled:
            return None
        return self._add('d', eng, (out, in_, kw), r, w)

    def emit(self):
        ops = self.ops
        need = [False] * len(ops)
        for i, o in enumerate(ops):
            for d in o['deps']:
                p = ops[d]
                if p['kind'] == 'c':
                    if p['eng'] == o['eng'] and o['kind'] == 'c' and p['eng'] == 'pe':
                        continue
                    need[d] = True
        cnt = {e: 0 for e in self.COMPUTE}
        tok = [None] * len(ops)
        seen = {}
        dcount = [0] * len(self.dsem)
        dk = {q: 0 for q in self.dq}
        nwaits = 0
        acts = {e: [] for e in self.h}
        for i, o in enumerate(ops):
            e = o['eng']
            wl = {}
            for d in o['deps']:
                p = ops[d]
                if p['kind'] == 'c' and p['eng'] == e and o['kind'] == 'c' and e == 'pe':
                    continue
                t = tok[d]
                if t is None:
                    continue
                ts, tv = t
                if tv > wl.get(id(ts), (ts, 0))[1]:
                    wl[id(ts)] = (ts, tv)
            if o['kind'] == 'd':
                j = self.dq[e][dk[e] % len(self.dq[e])]
                dk[e] += 1
                dsj = self.dsem[j]
                if dcount[j] > 0 and dcount[j] > wl.get(id(dsj), (dsj, 0))[1]:
                    wl[id(dsj)] = (dsj, dcount[j])
            for ws, wv in wl.values():
                key = (e, id(ws))
                if seen.get(key, 0) >= wv:
                    continue
                acts[e].append((lambda s_, v_: (lambda h: h.wait_ge(s_, v_)))(ws, wv))
                nwaits += 1
                seen[key] = wv
            if o['kind'] == 'c':
                if need[i]:
                    cnt[e] += 1
                    acts[e].append((lambda fn_, sm_: (lambda h: fn_(h).then_inc(sm_, 1)))(o['fn'], self.sem[e]))
                    tok[i] = (self.sem[e], cnt[e])
                else:
                    acts[e].append(o['fn'])
            else:
                out, in_, kw = o['fn']
                dcount[j] += 16
                acts[e].append((lambda o_, i_, k_, s_: (lambda h: h.dma_start(out=o_, in_=i_, **k_).then_inc(s_, 16)))(out, in_, kw, dsj))
                tok[i] = (dsj, dcount[j])
        for j, fs in enumerate(self.dsem):
            if dcount[j] > 0:
                acts['sp'].append((lambda s_, v_: (lambda h: h.wait_ge(s_, v_)))(fs, dcount[j]))
        with self.nc.Block() as block:
            @block.sync
            def _(h):
                for a in acts['sp']:
                    a(h)

            @block.tensor
            def _(h):
                for a in acts['pe']:
                    a(h)

            @block.scalar
            def _(h):
                for a in acts['act']:
                    a(h)

            @block.vector
            def _(h):
                for a in acts['dve']:
                    a(h)

            @block.gpsimd
            def _(h):
                for a in acts['pool']:
                    a(h)
        return dict(n_ops=len(ops), n_waits=nwaits, signals=dict(cnt))


def make_consts():
    c = np.zeros((128, NCST), np.float32)
    idx = np.arange(128)
    c[:, 0:128] = np.eye(128)
    c[:, 128:256] = (idx[:, None] // 64 == idx[None, :] // 64)
    same = (idx[:, None] // 64) == (idx[None, :] // 64)
    mstrict = same & (idx[:, None] < idx[None, :])
    mincl = same & (idx[:, None] <= idx[None, :])
    c[:, 256:384] = mstrict.T
    c[:, 384:512] = mstrict
    c[:, 512:640] = mincl
    c[:, 640:768] = idx[:, None] >= idx[None, :]
    c[:, 768:896] = idx[:, None] <= idx[None, :]
    c[:, 896:1024] = 0.0
    c[:, 1024:1152] = idx[:, None] <= idx[None, :]
    prot = np.zeros((128, 128), np.float32)
    for hb in (0, 64):
        for dd in range(32):
            prot[hb + dd + 32, hb + dd] = -1.0
            prot[hb + dd, hb + dd + 32] = 1.0
    c[:, 1152:1280] = prot
    rm = np.ones((128, 256), np.float32)
    rm[:, 0::64] = 0.0
    c[:, 1280:1536] = rm
    c[:, 1536:1664] = 1.0
    c[:, 1664:1728] = (idx[:, None] % 64) == np.arange(64)[None, :]
    return c


def rope_tables(pos):
    half = 32
    inv = (10000.0 ** (-np.arange(half, dtype=np.float32) / half)).astype(np.float32)
    ang = pos.astype(np.float32)[None, :] * inv[:, None]
    cos = np.cos(ang).astype(np.float32)
    sin = np.sin(ang).astype(np.float32)
    return np.tile(cos, (4, 1)), np.tile(sin, (4, 1))


def build(SEQ, NSS=16, dbg=(), upto=99, noconv=False):
    NG = SEQ // NT
    NTS = NSS * 4
    nc = bass.Bass("TRN2", target_bir_lowering=False)
    din = lambda name, shape, dt=F32: nc.dram_tensor(name, list(shape), dt, kind="ExternalInput").ap()
    dout = lambda name, shape, dt=F32: nc.dram_tensor(name, list(shape), dt, kind="ExternalOutput").ap()
    dscr = lambda name, shape, dt=BF16: nc.dram_tensor(name, list(shape), dt).ap()

    xp = din("xp", [SEQ, D])
    w_in = din("w_in", [2, D, IN_W]); w_brr = din("w_br_rwkv", [2, 512, D]); w_bra = din("w_br_attn", [2, 512, D])
    w_out = din("w_out", [2, D, D]); w_up = din("w_ff_up", [2, D, DFF]); w_dn = din("w_ff_down", [2, DFF, D])
    d_up = din("decay_up", [2, 64, 512]); i_up = din("iclr_up", [2, 64, 512]); g_up = din("gate_up", [2, 128, 512])
    pv_d = din("pv_in", [128, 2 * PVL]); cst_d = din("cst_in", [128, NCST])
    cos_d = din("cos_in", [128, SEQ]); sin_d = din("sin_in", [128, SEQ])

    xs = din("xs", [NSS, 4, D]); swkv_i = din("swkv_i", [2, NSS, 8, 64, 64]); sshift_i = din("sshift_i", [2, NSS, SHIFT_W])
    sck_i = din("sck_i", [2, NSS, 128, 2, 64]); scv_i = din("scv_i", [2, NSS, 128, 2, 64])
    coss_d = din("coss_in", [128, NT]); sins_d = din("sins_in", [128, NT]); tmask_d = din("tmask_in", [128, NT])
    y_s = dout("y_s", [NSS, 4, D]); so_wkv = dout("s_wkv", [2, NSS, 8, 64, 64]); so_shift = dout("s_shift", [2, NSS, SHIFT_W])
    so_ck = dout("s_ck", [2, NSS, 128, 2, 64]); so_cv = dout("s_cv", [2, NSS, 128, 2, 64])
    y_p = dout("y_p", [SEQ, D]); o_wkv = dout("p_wkv", [2, 8, 64, 64]); o_shift = dout("p_shift", [2, SHIFT_W])
    o_ck = dout("p_ck", [2, 128, 2, 64]); o_cv = dout("p_cv", [2, 128, 2, 64])
    dbg_out = {}

    wi_b = dscr("wi_b", [2, D, IN_W]); wbr_b = dscr("wbr_b", [2, 512, D]); wba_b = dscr("wba_b", [2, 512, D])
    wo_b = dscr("wo_b", [2, D, D]); wu_b = dscr("wu_b", [2, D, DFF]); wd_b = dscr("wd_b", [2, DFF, D])

    with ExitStack() as es:
        S = Sched(nc, es)
        T = lambda name, shape, dt=F32: es.enter_context(nc.sbuf_tensor(name, list(shape), dt))
        def MM(out, lhsT, rhs, start=True, stop=True, r=(), w=()):
            S.op('pe', lambda e: e.matmul(out, lhsT=lhsT, rhs=rhs, start=start, stop=stop), r=r, w=w)

        def TR(out, in_, ident, r=(), w=()):
            S.op('pe', lambda e: e.transpose(out, in_, ident), r=r, w=w)

        def TT(eng, out, in0, in1, op, r=(), w=()):
            S.op(eng, lambda e: e.tensor_tensor(out=out, in0=in0, in1=in1, op=op), r=r, w=w)

        def TS(eng, out, in0, s1, op0, s2=None, op1=None, r=(), w=()):
            if op1 is None:
                S.op(eng, lambda e: e.tensor_scalar(out=out, in0=in0, scalar1=s1, scalar2=None, op0=op0), r=r, w=w)
            else:
                S.op(eng, lambda e: e.tensor_scalar(out=out, in0=in0, scalar1=s1, scalar2=s2, op0=op0, op1=op1), r=r, w=w)

        def STT(eng, out, in0, scalar, in1, op0, op1, r=(), w=()):
            S.op(eng, lambda e: e.scalar_tensor_tensor(out=out, in0=in0, scalar=scalar, in1=in1, op0=op0, op1=op1), r=r, w=w)

        def ACT(out, in_, func, bias=None, scale=1.0, r=(), w=()):
            if bias is None:
                S.op('act', lambda e: e.activation(out=out, in_=in_, func=func, scale=scale), r=r, w=w)
            else:
                S.op('act', lambda e: e.activation(out=out, in_=in_, func=func, bias=bias, scale=scale), r=r, w=w)

        def CP(eng, out, in_, r=(), w=()):
            if eng == 'act':
                S.op('act', lambda e: e.copy(out=out, in_=in_), r=r, w=w)
            else:
                S.op(eng, lambda e: e.tensor_copy(out=out, in_=in_), r=r, w=w)

        def RCP(out, in_, r=(), w=()):
            S.op('dve', lambda e: e.reciprocal(out=out, in_=in_), r=r, w=w)

        def bc_mid(ap2d, n):
            return ap2d.unsqueeze(1).broadcast_to([ap2d.shape[0], n, ap2d.shape[1]])

        def dump(name, ap, keys, shape=None):
            if name not in dbg or name in dbg_out:
                return
            dbg_out[name] = 1
            shp = list(ap.shape)
            o = dout("dbg_" + name, shp, ap.dtype)
            full = o if len(shp) == 2 else o
            S.dma(QM, o[tuple(slice(None) for _ in shp)], ap, r=keys)

        cst = T("cst", [128, NCST])
        S.dma(QM, cst[:], cst_d[:, :], w=['cst'])
        identf = cst[:, 0:128]; blockones = cst[:, 128:256]; maskT = cst[:, 256:384]; mask12 = cst[:, 384:640]
        mAtt = cst[:, 640:896]; mAtt0 = cst[:, 896:1152]; resetm = cst[:, 1280:1536]; I2 = cst[:, 1664:1728]
        identb = T("identb", [128, 128], BF16); protb = T("protb", [128, 128], BF16); onesb = T("onesb", [128, 128], BF16)
        CP('pool', identb[:], cst[:, 0:128], r=['cst'], w=['identb'])
        CP('pool', protb[:], cst[:, 1152:1280], r=['cst'], w=['protb'])
        CP('pool', onesb[:], cst[:, 1536:1664], r=['cst'], w=['onesb'])
        pv = T("pv", [128, 2 * PVL])
        S.dma(QM, pv[:], pv_d[:, :], w=['pv'])
        pd = T("pd", [128, 2, 24])
        for l in range(2):
            b0 = l * PVL
            TS('dve', pd[:, l, 0:14], pv[:, b0:b0 + 14], -1.0, ALU.mult, 1.0, ALU.add, r=['pv'], w=[('pd', l)])
            TS('dve', pd[:, l, 14:18], pv[:, b0 + 18:b0 + 22], -1.0, ALU.mult, 1.0, ALU.add, r=['pv'], w=[('pd', l)])
            ACT(pd[:, l, 18:22], pv[:, b0 + 74:b0 + 78], AF.Exp, r=['pv'], w=[('pd', l)])
        P = lambda l, a, b: pv[:, l * PVL + a: l * PVL + b]
        lora = T("lora", [128, 2, 2, 512], BF16)
        for l in range(2):
            S.dma('pool', lora[0:64, l, 0, :], d_up[l], w=[('lora', l)])
            S.dma('pool', lora[64:128, l, 0, :], i_up[l], w=[('lora', l)])
            S.dma('pool', lora[:, l, 1, :], g_up[l], w=[('lora', l)])

        def conv(dst, src, l, rows, key):
            for k in range(rows // 128):
                S.dma('pool', dst[l, k * 128:(k + 1) * 128, :], src[l, k * 128:(k + 1) * 128, :], w=[(key, l, k)])
        convspec = dict(wi=(wi_b, w_in, D), wbr=(wbr_b, w_brr, 512), wba=(wba_b, w_bra, 512), wo=(wo_b, w_out, D),
                        wu=(wu_b, w_up, D), wd=(wd_b, w_dn, DFF))
        converted = set()

        def ensure_conv(kind, l):
            if (kind, l) in converted:
                return
            converted.add((kind, l))
            dst, src, rows = convspec[kind]
            conv(dst, src, l, rows, kind)

        NSLOT = 3
        ring = [T("ring%d" % i, [128, 4096], BF16) for i in range(NSLOT)]
        wk2 = T("wk2", [128, 8, 2, 128], BF16)

        def wsrc(kind, l, i):
            if kind == 'wi':
                return wi_b[l].rearrange("(k p) n -> p k n", p=128)[:, :, i * 512:(i + 1) * 512], [('wi', l, k) for k in range(8)], [8, 512]
            if kind == 'wbr':
                return wbr_b[l].rearrange("(k p) n -> p k n", p=128), [('wbr', l, k) for k in range(4)], [4, 1024]
            if kind == 'wba':
                return wba_b[l].rearrange("(k p) n -> p k n", p=128), [('wba', l, k) for k in range(4)], [4, 1024]
            if kind == 'wo':
                return wo_b[l].rearrange("(k p) n -> p k n", p=128)[:, :, i * 512:(i + 1) * 512], [('wo', l, k) for k in range(8)], [8, 512]
            if kind == 'wu':
                return wu_b[l].rearrange("(k p) n -> p k n", p=128)[:, :, i * 512:(i + 1) * 512], [('wu', l, k) for k in range(8)], [8, 512]
            if kind == 'wd':
                return wd_b[l].rearrange("(f p) n -> p f n", p=128)[:, :, i * 128:(i + 1) * 128], [('wd', l, k) for k in range(32)], [32, 128]
        layer_loads = ([('wi', i) for i in range(9)] + [('wbr', 0), ('wba', 0), ('wo', 0), ('wo', 1)] + [('wu', i) for i in range(8)] + [('wd', i) for i in range(8)])
        NSG = NSS // 2
        all_loads = [(kind, l, i) for g in range(NG + NSG) for l in range(2) for (kind, i) in layer_loads]
        wstate = dict(issued=0, used=0, done=set())

        def w_can_issue(n):
            return n < len(all_loads) and (n - NSLOT < 0 or (n - NSLOT) in wstate['done'])

        def w_issue():
            n = wstate['issued']
            kind, l, i = all_loads[n]
            ensure_conv(kind, l)
            src, keys, shp = wsrc(kind, l, i)
            slot = n % NSLOT
            dst = ring[slot][:].rearrange("p (a b) -> p a b", a=shp[0])
            S.dma('sp', dst, src, r=keys, w=[('ring', slot)])
            wstate['issued'] = n + 1

        def w_prefetch():
            while wstate['issued'] < min(wstate['used'] + NSLOT, len(all_loads)) and w_can_issue(wstate['issued']):
                w_issue()

        def w_get(kind, l, i):
            n = wstate['used']
            assert all_loads[n] == (kind, l, i), (all_loads[n], kind, l, i)
            wstate['used'] = n + 1
            while wstate['issued'] <= n:
                assert w_can_issue(wstate['issued']), ("ring slot still live", n)
                w_issue()
            w_prefetch()
            _, _, shp = wsrc(kind, l, i)
            slot = n % NSLOT
            return ring[slot][:].rearrange("p (a b) -> p a b", a=shp[0]), ('ring', slot), n

        def w_done(n):
            wstate['done'].add(n)
            w_prefetch()

        ps = es.enter_context(nc.psum_tensor("ps", [128, 3072], F32))
        pst = es.enter_context(nc.psum_tensor("pst", [128, 2048], BF16))

        def pk(c0, c1):
            return [('ps', b) for b in range(c0 // 512, (c1 - 1) // 512 + 1)]
        big = dict(i=0, banks=[4, 5])

        def bigslot():
            b = big['banks'][big['i'] % len(big['banks'])]
            big['i'] += 1
            c0 = b * 512
            return ps[:, c0:c0 + 256], [('ps', b)]

        xT = T("xT", [128, 8, NT]); xTb = T("xTb", [128, 8, NT], BF16)
        xio = T("xio", [128, TL, D])
        Zc = [T("Zc%d" % i, [128, NT + 1]) for i in range(2)]
        CARRY = T("CARRY", [128, 2, 14])
        rT = T("rT", [128, 4, NT]); kraw = T("kraw", [128, 4, NT]); vT = T("vT", [128, 4, NT])
        tw = T("tw", [128, NT], BF16); sgd = T("sgd", [128, NT], BF16)
        ACRC = T("ACRC", [128, 4, TL, 2, 128], BF16)
        bcb = T("bcb", [128, 4, NT], BF16); kcb = T("kcb", [128, 4, NT], BF16); asb = T("asb", [128, 4, NT], BF16)
        vb = T("vb", [128, 4, NT], BF16); rs = T("rs", [128, 4, NT]); bonus = T("bonus", [128, 4, NT], BF16)
        gg = T("gg", [128, 4, NT], BF16); GC = T("GC", [128, 4, NT // 64])
        NTM = 9
        Tm = [T("Tm%d" % i, [128, NT]) for i in range(NTM)]
        Tn = [T("Tn%d" % i, [128, NT]) for i in range(NTM)]
        X = T("X", [128, 8, 128], BF16); VV = T("VV", [128, 8, 128], BF16)
        BcT = T("BcT", [128, 8, 64], BF16); KcT = T("KcT", [128, 8, 64], BF16)
        SC1 = T("SC1", [128, 8, 256], BF16); SC2 = T("SC2", [128, 8, 256], BF16)
        Pm = [T("Pm%d" % i, [128, 8, 128], BF16) for i in range(2)]
        PTm = [T("PTm%d" % i, [128, 8, 128], BF16) for i in range(2)]
        RhatT = T("RhatT", [128, 4, 128]); McT = T("McT", [128, 4, 2, 64]); H = T("H", [128, 2, 4, 64]); Nc = T("Nc", [128, 4, 2, 64])
        YT = T("YT", [128, 4, NT])
        yf = T("yf", [128, 4, NT], BF16); qT = T("qT", [128, 4, NT], BF16)
        KT2 = T("KT2", [128, 2, 2, 128 + NT], BF16); Vtm = T("Vtm", [128, 2, 1 + TL, 128], BF16)
        pT = T("pT", [128, 8, 2, 128], BF16); YA = T("YA", [128, 4, NT], BF16)
        cosT = T("cosT", [128, NT]); sinT = T("sinT", [128, NT])
        qraw = [T("qraw%d" % i, [128, NT], BF16) for i in range(2)]
        kf = T("kf", [128, 2, 128]); vf = T("vf", [128, 128]); kfB = T("kfB", [128, 2, 128]); vfB = T("vfB", [128, 128])
        Gtmp = [T("Gtmp%d" % i, [128, NT], BF16) for i in range(2)]
        mixR = T("mixR", [128, 8, NT], BF16); mix = T("mix", [128, 8, NT], BF16)
        x1 = T("x1", [128, 8, NT])
        x1b = [T("x1b%d" % i, [128, NT], BF16) for i in range(2)]
        x1q = [T("x1q%d" % i, [128, NT], BF16) for i in range(2)]
        hT = T("hT", [128, 32, NT], BF16)
        dena = T("dena", [128, NT]); denb = T("denb", [128, NT])
        HB = xio[:, 1, 512:768].rearrange("p (c j) -> p c j", j=64)
        ostage = xio[:, 1, 768:1024]
        CARB = T("CARB", [128, 2, 14]); SHFB = T("SHFB", [128, 2, 14]); KTB = T("KTB", [128, 2, 128], BF16); VtmB = T("VtmB", [128, 128], BF16)
        tmask = T("tmask", [128, NT]); SHF = T("SHF", [128, 2, 14])
        Snat = xio[0:64, 0, 0:512]; ckd = xio[:, 1, 0:256].rearrange("p (a b c) -> p a b c", a=2, b=2)
        S.dma(QM, tmask[:], tmask_d[:, :], w=['tmask'])

        S.op('pool', lambda e: e.memset(H[:], 0.0), w=[('H', 0), ('H', 1)])
        S.op('pool', lambda e: e.memset(CARRY[:], 0.0), w=[('CARRY', 0), ('CARRY', 1)])
        S.op('pool', lambda e: e.memset(KT2[:], 0.0), w=[('KT2', 0), ('KT2', 1)])
        S.op('pool', lambda e: e.memset(Vtm[:], 0.0), w=[('Vtm', 0), ('Vtm', 1)])
        S.op('pool', lambda e: e.memset(VV[:], 0.0), w=['VV'])

        def layernorm(l, ga, gb_, tag):
            s1 = ps[:, 0:NT]; s2 = ps[:, 512:512 + NT]
            for k in range(8):
                j = k % 2
                CP('act', x1b[j][:], x1[:, k, :], r=[('x1', k)], w=[('x1b', j)])
                TT('dve', x1q[j][:], x1[:, k, :], x1[:, k, :], ALU.mult, r=[('x1', k)], w=[('x1q', j)])
                MM(s1, onesb[:], x1b[j][:], start=(k == 0), stop=(k == 7), r=['onesb', ('x1b', j)], w=pk(0, NT))
                MM(s2, onesb[:], x1q[j][:], start=(k == 0), stop=(k == 7), r=['onesb', ('x1q', j)], w=pk(512, 512 + NT))
            mean, msq, var, rstd = Tm[0], Tm[1], Tm[2], Tm[3]
            ACT(mean[:], s1, AF.Copy, scale=1.0 / D, r=pk(0, NT), w=[('Tm', 0)])
            TT('pool', msq[:], mean[:], mean[:], ALU.mult, r=[('Tm', 0)], w=[('Tm', 1)])
            STT('dve', var[:], s2, 1.0 / D, msq[:], ALU.mult, ALU.subtract, r=pk(512, 512 + NT) + [('Tm', 1)], w=[('Tm', 2)])
            ACT(var[:], var[:], AF.Sqrt, bias=epsln[:, 0:1], r=[('Tm', 2), 'eps'], w=[('Tm', 2)])
            RCP(rstd[:], var[:], r=[('Tm', 2)], w=[('Tm', 3)])
            for k in range(8):
                d = Tm[4 + (k % 2)]
                TT('pool', d[:], x1[:, k, :], mean[:], ALU.subtract, r=[('x1', k), ('Tm', 0)], w=[('Tm', 4 + k % 2)])
                TT('dve', d[:], d[:], rstd[:], ALU.mult, r=[('Tm', 4 + k % 2), ('Tm', 3)], w=[('Tm', 4 + k % 2)])
                TS('dve', xT[:, k, :], d[:], P(l, ga + k, ga + k + 1), ALU.mult, P(l, gb_ + k, gb_ + k + 1), ALU.add,
                   r=[('Tm', 4 + k % 2), 'pv'], w=[('xT', k)])
                CP('act', xTb[:, k, :], xT[:, k, :], r=[('xT', k)], w=[('xTb', k)])

        def emit_wkv(dst):
            for c in range(4):
                TR(ps[0:64, c * 128:(c + 1) * 128], H[:, l, c, :], identf, r=[('H', l), 'cst'], w=pk(0, 512))
            for hh in range(2):
                CP('act', ostage[0:64, 0:256], ps[0:64, hh * 256:(hh + 1) * 256], r=pk(0, 512), w=['xio'])
                S.dma(QM, dst[hh * 4:(hh + 1) * 4].rearrange("h v k -> v h k"), ostage[0:64, 0:256].rearrange("p (h k) -> p h k", k=64), r=['xio'])

        epsln = T("epsln", [128, 2])
        S.op('pool', lambda e: e.memset(epsln[:, 0:1], LN_EPS), w=['eps'])
        S.op('pool', lambda e: e.memset(epsln[:, 1:2], GN_EPS), w=['eps'])

        for gi in range(NG + NSG):
            samp = gi >= NG
            g = gi if not samp else -1
            q = 2 * (gi - NG)
            qb = q + 1
            t0g = g * NT
            if not samp:
                S.dma('pool', cosT[:], cos_d[:, t0g:t0g + NT], w=['cosT'])
                S.dma('pool', sinT[:], sin_d[:, t0g:t0g + NT], w=['sinT'])
            else:
                S.dma('pool', cosT[:], coss_d[:, :], w=['cosT'])
                S.dma('pool', sinT[:], sins_d[:, :], w=['sinT'])
            if upto < 1:
                S.disabled = True
            if not samp:
                S.dma('pool', xio[:], xp[t0g:t0g + NT, :].rearrange("(t p) d -> p t d", p=128), w=['xio'])
            else:
                S.op('pool', lambda e: e.memset(xio[:], 0.0), w=['xio'])
                S.dma('pool', xio[0:4, 0, :], xs[q], w=['xio'])
                S.dma('pool', xio[0:4, 1, :], xs[qb], w=['xio'])
            for kp in range(4):
                reg = ps[:, kp * 512:(kp + 1) * 512]
                for kk in range(2):
                    k = kp * 2 + kk
                    for t in range(TL):
                        TR(ps[:, kp * 512 + kk * 256 + t * 128: kp * 512 + kk * 256 + (t + 1) * 128],
                           xio[:, t, k * 128:(k + 1) * 128], identf, r=['xio', 'cst'], w=pk(kp * 512, kp * 512 + 512))
                CP('act', xT[:, 2 * kp:2 * kp + 2, :], reg.rearrange("p (a b) -> p a b", a=2), r=pk(kp * 512, kp * 512 + 512),
                   w=[('xT', 2 * kp), ('xT', 2 * kp + 1)])
                CP('dve', xTb[:, 2 * kp:2 * kp + 2, :], reg.rearrange("p (a b) -> p a b", a=2), r=pk(kp * 512, kp * 512 + 512),
                   w=[('xTb', 2 * kp), ('xTb', 2 * kp + 1)])
            for l in range(2):
                last = (g == NG - 1)
                if samp:
                    S.dma(QM, Snat.rearrange("v (h k) -> v h k", k=64), swkv_i[l, q].rearrange("h v k -> v h k"), w=['xio'])
                    for c in range(4):
                        TR(ps[:, c * 64:(c + 1) * 64], Snat[:, c * 128:(c + 1) * 128], identf[0:64, 0:64], r=['xio', 'cst'], w=pk(0, 256))
                    CP('dve', H[:, l, :, :], ps[:, 0:256].rearrange("p (c j) -> p c j", j=64), r=pk(0, 256), w=[('H', l)])
                    S.dma(QM, CARRY[:, l, :], sshift_i[l, q].rearrange("(c p) -> p c", p=128), w=[('CARRY', l)], allow_slow_non_contiguous=True)
                    for dup in range(2):
                        S.dma(QM, ckd[:, :, dup, :], sck_i[l, q], w=['xio'])
                    for kvh in range(2):
                        TR(ps[:, 512 + kvh * 128:512 + (kvh + 1) * 128], ckd[:, kvh, :, :].rearrange("p a b -> p (a b)"), identf, r=['xio', 'cst'], w=pk(512, 1024))
                    CP('act', KT2[:, l, :, 0:128], ps[:, 512:768].rearrange("p (a b) -> p a b", b=128), r=pk(512, 1024), w=[('KT2', l)])
                    S.dma('pool', Vtm[:, l, 0, :], scv_i[l, q].rearrange("t h d -> t (h d)"), w=[('Vtm', l)])
                    S.dma(QM, Snat.rearrange("v (h k) -> v h k", k=64), swkv_i[l, qb].rearrange("h v k -> v h k"), w=['xio'])
                    for c in range(4):
                        TR(ps[:, c * 64:(c + 1) * 64], Snat[:, c * 128:(c + 1) * 128], identf[0:64, 0:64], r=['xio', 'cst'], w=pk(0, 256))
                    CP('dve', HB, ps[:, 0:256].rearrange("p (c j) -> p c j", j=64), r=pk(0, 256), w=['xio', 'HB'])
                    S.dma(QM, CARB[:, l, :], sshift_i[l, qb].rearrange("(c p) -> p c", p=128), w=[('CARB', l)], allow_slow_non_contiguous=True)
                    for dup in range(2):
                        S.dma(QM, ckd[:, :, dup, :], sck_i[l, qb], w=['xio'])
                    for kvh in range(2):
                        TR(ps[:, 512 + kvh * 128:512 + (kvh + 1) * 128], ckd[:, kvh, :, :].rearrange("p a b -> p (a b)"), identf, r=['xio', 'cst'], w=pk(512, 1024))
                    CP('act', KTB[:], ps[:, 512:768].rearrange("p (a b) -> p a b", b=128), r=pk(512, 1024), w=['KTB'])
                    S.dma('pool', VtmB[:], scv_i[l, qb].rearrange("t h d -> t (h d)"), w=['VtmB'])
                allx = [('xTb', k) for k in range(8)]
                ensure_conv('wi', l)
                for kvh in range(2):
                    for dup in range(2):
                        S.dma('pool', wk2[:, :, kvh, dup * 64:(dup + 1) * 64],
                              wi_b[l].rearrange("(k p) n -> p k n", p=128)[:, :, 2304 + kvh * 64: 2304 + (kvh + 1) * 64],
                              r=[('wi', l, k) for k in range(8)], w=['wk2'])
                if upto < 2:
                    S.disabled = True
                big['banks'] = [4, 5, 2, 3]
                for blk in range(5):
                    W, wkey, wn = w_get('wi', l, blk)
                    for j in range(4):
                        c = blk * 4 + j
                        if c in (18, 19):
                            continue
                        po, pkey = bigslot()
                        for k in range(8):
                            MM(po, W[:, k, j * 128:(j + 1) * 128], xTb[:, k, :], start=(k == 0), stop=(k == 7),
                               r=[wkey, ('xTb', k)], w=pkey)
                        if c < 14:
                            z = Zc[c % 2]; zk = ('Zc', c % 2)
                            CP('act', z[:, 1:NT + 1], po, r=pkey, w=[zk])
                            CP('pool', z[:, 0:1], CARRY[:, l, c:c + 1], r=[('CARRY', l)], w=[zk])
                            tmp = Tm[c % 2]
                            TS('dve', tmp[:], z[:, 1:NT + 1], pd[:, l, c:c + 1], ALU.mult, r=[zk, ('pd', l)], w=[('Tm', c % 2)])
                            if c < 4:
                                dst, dk_ = rT[:, c, :], ('rT', c)
                            elif c < 8:
                                dst, dk_ = kraw[:, c - 4, :], ('kraw', c - 4)
                            elif c < 12:
                                dst, dk_ = vT[:, c - 8, :], ('vT', c - 8)
                            else:
                                dst, dk_ = Tm[2 + c % 2][:], ('Tm', 2 + c % 2)
                            if samp:
                                CP('pool', SHFB[:, l, c:c + 1], z[:, 132:133], r=[zk], w=[('SHFB', l)])
                                CP('pool', z[:, 128:129], CARB[:, l, c:c + 1], r=[zk, ('CARB', l), ('Tm', c % 2)], w=[zk])
                            STT('dve', dst, z[:, 0:NT], P(l, c, c + 1), tmp[:], ALU.mult, ALU.add,
                                r=[zk, 'pv', ('Tm', c % 2)], w=[dk_])
                            CP('pool', CARRY[:, l, c:c + 1], z[:, NT:NT + 1], r=[zk], w=[('CARRY', l)])
                            if samp:
                                CP('pool', SHF[:, l, c:c + 1], z[:, 4:5], r=[zk], w=[('SHF', l)])
                            if c == 12:
                                ACT(tw[0:64, :], dst[0:64, :], AF.Tanh, r=[dk_], w=['tw'])
                                CP('act', tw[64:128, :], dst[64:128, :], r=[dk_], w=['tw'])
                            if c == 13:
                                ACT(sgd[:], dst, AF.Sigmoid, r=[dk_], w=['sgd'])
                        else:
                            qi = c - 14
                            qr = qraw[qi % 2]; qk = ('qraw', qi % 2)
                            CP('act', qr[:], po, r=pkey, w=[qk])
                            p2, p2k = bigslot()
                            MM(p2, protb[:], qr[:], r=['protb', qk], w=p2k)
                            ta = Tm[4 + qi % 2]; tb_ = Tm[6 + qi % 2]
                            TT('dve', ta[:], p2, sinT[:], ALU.mult, r=p2k + ['sinT'], w=[('Tm', 4 + qi % 2)])
                            TT('pool', tb_[:], qr[:], cosT[:], ALU.mult, r=[qk, 'cosT'], w=[('Tm', 6 + qi % 2)])
                            TT('pool', qT[:, qi, :], ta[:], tb_[:], ALU.add, r=[('Tm', 4 + qi % 2), ('Tm', 6 + qi % 2)], w=[('qT', qi)])
                    if blk == 4:
                        for t in range(TL):
                            po, pkey = bigslot()
                            for k in range(8):
                                MM(po[:, 0:128], xTb[:, k, t * 128:(t + 1) * 128], W[:, k, 384:512], start=(k == 0), stop=(k == 7),
                                   r=[wkey, ('xTb', k)], w=pkey)
                            CP('act', Vtm[:, l, 1 + t, :], po[:, 0:128], r=pkey, w=[('Vtm', l)])
                            if (t == TL - 1 and last) or (samp and t == 0):
                                CP('dve', vf[:], po[:, 0:128], r=pkey, w=['vf'])
                            if samp and t == 1:
                                CP('dve', vfB[:], po[:, 0:128], r=pkey, w=['vfB'])
                    w_done(wn)
                for kvh in range(2):
                    po, pkey = bigslot()
                    for k in range(8):
                        MM(po, wk2[:, k, kvh, :], xTb[:, k, :], start=(k == 0), stop=(k == 7), r=['wk2', ('xTb', k)], w=pkey)
                    qr = qraw[kvh]; qk = ('qraw', kvh)
                    CP('act', qr[:], po, r=pkey, w=[qk])
                    p2, p2k = bigslot()
                    MM(p2, protb[:], qr[:], r=['protb', qk], w=p2k)
                    ta = Tm[4 + kvh]; tb_ = Tm[6 + kvh]
                    TT('dve', ta[:], p2, sinT[:], ALU.mult, r=p2k + ['sinT'], w=[('Tm', 4 + kvh)])
                    TT('pool', tb_[:], qr[:], cosT[:], ALU.mult, r=[qk, 'cosT'], w=[('Tm', 6 + kvh)])
                    TT('pool', KT2[:, l, kvh, 128:128 + NT], ta[:], tb_[:], ALU.add, r=[('Tm', 4 + kvh), ('Tm', 6 + kvh)], w=[('KT2', l)])
                    if last:
                        TT('pool', kf[:, kvh, :], ta[:, NT - 128:NT], tb_[:, NT - 128:NT], ALU.add,
                           r=[('Tm', 4 + kvh), ('Tm', 6 + kvh)], w=['kf'])
                    if samp:
                        TT('pool', kf[:, kvh, :], ta[:, 0:128], tb_[:, 0:128], ALU.add,
                           r=[('Tm', 4 + kvh), ('Tm', 6 + kvh)], w=['kf'])
                        TT('pool', kfB[:, kvh, :], ta[:, 128:256], tb_[:, 128:256], ALU.add,
                           r=[('Tm', 4 + kvh), ('Tm', 6 + kvh)], w=['kfB'])
                if upto < 3:
                    S.disabled = True
                big['banks'] = [4, 5]
                def stg(half):
                    wgn = None
                    for gi_ in range(half * 8, half * 8 + 8):
                        if gi_ % 4 == 0:
                            if wgn is not None:
                                w_done(wgn)
                            Wg, wgk, wgn = w_get('wi', l, 5 + gi_ // 4)
                        po, pkey = bigslot()
                        for k in range(8):
                            MM(po, Wg[:, k, (gi_ % 4) * 128:(gi_ % 4 + 1) * 128], xTb[:, k, :], start=(k == 0), stop=(k == 7), r=[wgk, ('xTb', k)], w=pkey)
                        ACT(hT[:, gi_, :], po, AF.Sigmoid, r=pkey, w=[('hT', gi_)])
                        yield
                    w_done(wgn)

                def st6(t):
                    tsl = slice(t * 128, (t + 1) * 128)
                    first_tile = (g == 0 and t == 0 and not samp)
                    for h in (0, 2, 4, 6, 1, 3, 5, 7):
                        c = h // 2; pb = 64 * (h % 2); kvh = h // 4
                        kprev_ = KTB[pb:pb + 64, kvh, :] if (samp and t == 1) else KT2[pb:pb + 64, l, kvh, t * 128:(t + 1) * 128]
                        MM(ps[:, h * 256:h * 256 + 128], kprev_, qT[pb:pb + 64, c, tsl],
                           r=[('KT2', l), ('qT', c), 'KTB'], w=pk(h * 256, h * 256 + 128))
                        yield
                        MM(ps[:, h * 256 + 128:h * 256 + 256], KT2[pb:pb + 64, l, kvh, (t + 1) * 128:(t + 2) * 128], qT[pb:pb + 64, c, tsl],
                           r=[('KT2', l), ('qT', c)], w=pk(h * 256 + 128, h * 256 + 256))
                        yield
                    for q4 in range(4):
                        ACT(pT[:, q4 * 2:q4 * 2 + 2, :, :].rearrange("p a b c -> p (a b c)"), ps[:, q4 * 512:(q4 + 1) * 512], AF.Exp, scale=0.125,
                            r=pk(q4 * 512, q4 * 512 + 512), w=[('pT', q4)])
                        yield
                    mm_ = (mAtt0 if first_tile else mAtt)
                    ptk = [('pT', q4) for q4 in range(4)]
                    TT('pool', pT[:].rearrange("p a b c -> p a (b c)"), pT[:].rearrange("p a b c -> p a (b c)"), bc_mid(mm_, 8), ALU.mult,
                       r=ptk + ['cst'], w=ptk)
                    yield
                    for h in range(8):
                        c = h // 2; pb = 64 * (h % 2); kvh = h // 4
                        oo = ps[pb:pb + 64, c * 128:(c + 1) * 128]
                        vprev_ = VtmB[:, kvh * 64:(kvh + 1) * 64] if (samp and t == 1) else Vtm[:, l, t, kvh * 64:(kvh + 1) * 64]
                        MM(oo, vprev_, pT[:, h, 0, :], start=True, stop=False, r=[('Vtm', l), 'VtmB'] + ptk, w=pk(0, 512))
                        yield
                        MM(oo, Vtm[:, l, t + 1, kvh * 64:(kvh + 1) * 64], pT[:, h, 1, :], start=False, stop=True, r=[('Vtm', l)] + ptk, w=pk(0, 512))
                        yield
                        do = ps[pb:pb + 64, 512 + c * 128:512 + (c + 1) * 128]
                        MM(do, onesb[:, 0:64], pT[:, h, 0, :], start=True, stop=False, r=['onesb'] + ptk, w=pk(512, 1024))
                        yield
                        MM(do, onesb[:, 0:64], pT[:, h, 1, :], start=False, stop=True, r=['onesb'] + ptk, w=pk(512, 1024))
                        yield
                    den = dena; den2 = denb
                    dv = lambda tl: tl[:].rearrange("p (a b) -> p a b", b=128)
                    for c in range(4):
                        tgt = (den if c < 2 else den2)[:, (c % 2) * 128:(c % 2 + 1) * 128]
                        TS('dve', tgt, ps[:, 512 + c * 128:512 + (c + 1) * 128], pd[:, l, 18 + c:19 + c], ALU.add,
                           r=pk(512, 1024) + [('pd', l)], w=[('den', 0 if c < 2 else 1)])
                        yield
                    RCP(den[:], den[:], r=[('den', 0)], w=[('den', 0)])
                    yield
                    RCP(den2[:], den2[:], r=[('den', 1)], w=[('den', 1)])
                    yield
                    TT('dve', YA[:, 0:2, tsl], ps[:, 0:256].rearrange("p (a b) -> p a b", b=128), dv(den), ALU.mult,
                       r=pk(0, 512) + [('den', 0)], w=[('YA', 0), ('YA', 1)])
                    yield
                    TT('dve', YA[:, 2:4, tsl], ps[:, 256:512].rearrange("p (a b) -> p a b", b=128), dv(den2), ALU.mult,
                       r=pk(0, 512) + [('den', 1)], w=[('YA', 2), ('YA', 3)])
                    yield
                def st2(c, TS_, TK_):
                    sg, ic, kk_, t4, bT_, gs, E3, E1, rkk = TS_[0], TS_[1], TS_[2], TS_[3], TS_[4], TS_[5], TS_[6], TS_[7], TS_[8]
                    K = lambda i: (TK_, i)
                    cs = slice(c * 128, (c + 1) * 128)
                    p1, p1k = bigslot()
                    MM(p1, lora[0:64, l, 0, cs], tw[0:64, :], r=[('lora', l), 'tw'], w=p1k)
                    ACT(sg[:], p1, AF.Sigmoid, bias=P(l, 34 + c, 35 + c), r=p1k + ['pv'], w=[K(0)])
                    yield
                    if samp:
                        TT('pool', sg[:], sg[:], tmask[:], ALU.mult, r=[K(0), 'tmask'], w=[K(0)])
                        yield
                    p2, p2k = bigslot()
                    MM(p2, lora[64:128, l, 0, cs], tw[64:128, :], r=[('lora', l), 'tw'], w=p2k)
                    ACT(ic[:], p2, AF.Sigmoid, bias=P(l, 38 + c, 39 + c), r=p2k + ['pv'], w=[K(1)])
                    yield
                    p3, p3k = bigslot()
                    MM(p3, lora[:, l, 1, cs], sgd[:], r=[('lora', l), 'sgd'], w=p3k)
                    CP('act', gg[:, c, :], p3, r=p3k, w=[('gg', c)])
                    yield
                    TS('dve', kk_[:], kraw[:, c, :], P(l, 14 + c, 15 + c), ALU.mult, r=[('kraw', c), 'pv'], w=[K(2)])
                    yield
                    TT('pool', t4[:], kk_[:], kk_[:], ALU.mult, r=[K(2)], w=[K(3)])
                    yield
                    p4, p4k = bigslot()
                    MM(p4, blockones, t4[:], r=['cst', K(3)], w=p4k)
                    TS('dve', t4[:], p4, 1e-24, ALU.max, r=p4k, w=[K(3)])
                    yield
                    ACT(t4[:], t4[:], AF.Sqrt, r=[K(3)], w=[K(3)])
                    yield
                    RCP(t4[:], t4[:], r=[K(3)], w=[K(3)])
                    yield
                    TT('pool', kk_[:], kk_[:], t4[:], ALU.mult, r=[K(2), K(3)], w=[K(2)])
                    yield
                    if samp:
                        TT('pool', kk_[:], kk_[:], tmask[:], ALU.mult, r=[K(2), 'tmask'], w=[K(2)])
                        yield
                    TT('pool', bT_[:], kk_[:], ic[:], ALU.mult, r=[K(2), K(1)], w=[K(4)])
                    yield
                    TS('dve', ic[:], ic[:], P(l, 18 + c, 19 + c), ALU.mult, pd[:, l, 14 + c:15 + c], ALU.add,
                       r=[K(1), 'pv', ('pd', l)], w=[K(1)])
                    yield
                    TT('pool', ic[:], kraw[:, c, :], ic[:], ALU.mult, r=[('kraw', c), K(1)], w=[K(1)])
                    yield
                    if samp:
                        TT('pool', ic[:], ic[:], tmask[:], ALU.mult, r=[K(1), 'tmask'], w=[K(1)])
                        yield
                    S.op('dve', (lambda o_, d0, d1: (lambda e: e.tensor_tensor_scan(out=o_, data0=d0, data1=d1, initial=0.0,
                                                                                    op0=ALU.mult, op1=ALU.add)))(gs[:], resetm, sg[:]),
                         r=['cst', K(0)], w=[K(5)])
                    yield
                    ACT(E3[:], gs[:], AF.Exp, scale=-DECAY_C, r=[K(5)], w=[K(6)])
                    yield
                    ACT(gs[:], gs[:], AF.Exp, scale=DECAY_C, r=[K(5)], w=[K(5)])
                    yield
                    ACT(sg[:], sg[:], AF.Exp, scale=DECAY_C, r=[K(0)], w=[K(0)])
                    yield
                    nch = NT // 64
                    e3v = E3[:].rearrange("p (a b) -> p a b", b=64)
                    i3v = gs[:].rearrange("p (a b) -> p a b", b=64)
                    CP('pool', GC[:, c, :], e3v[:, :, 63], r=[K(6)], w=[('GC', c)])
                    yield
                    TT('dve', E1[:].rearrange("p (a b) -> p a b", b=64), e3v, i3v[:, :, 63:64].broadcast_to([128, nch, 64]),
                       ALU.mult, r=[K(6), K(5)], w=[K(7)])
                    yield
                    TT('dve', i3v, i3v, e3v[:, :, 63:64].broadcast_to([128, nch, 64]), ALU.mult,
                       r=[K(5), K(6)], w=[K(5)])
                    yield
                    acv = ACRC[:, c, :, 0, :]
                    rcv = ACRC[:, c, :, 1, :]
                    v3 = lambda ap: ap.rearrange("p (a b) -> p a b", b=128)
                    TT('pool', rcv, v3(rT[:, c, :]), v3(E1[:]), ALU.mult, r=[('rT', c), K(7)], w=[('ACRC', c)])
                    yield
                    TT('pool', rs[:, c, :], rT[:, c, :], E3[:], ALU.mult, r=[('rT', c), K(6)], w=[('rs', c)])
                    yield
                    STT('dve', sg[:], kk_[:], -1.0, sg[:], ALU.mult, ALU.mult, r=[K(2), K(0)], w=[K(0)])
                    yield
                    TT('dve', acv, v3(sg[:]), v3(E1[:]), ALU.mult, r=[K(0), K(7)], w=[('ACRC', c)])
                    yield
                    TT('pool', asb[:, c, :], sg[:], E3[:], ALU.mult, r=[K(0), K(6)], w=[('asb', c)])
                    yield
                    TT('pool', bcb[:, c, :], bT_[:], gs[:], ALU.mult, r=[K(4), K(5)], w=[('bcb', c)])
                    yield
                    TT('dve', kcb[:, c, :], ic[:], gs[:], ALU.mult, r=[K(1), K(5)], w=[('kcb', c)])
                    yield
                    CP('act', vb[:, c, :], vT[:, c, :], r=[('vT', c)], w=[('vb', c)])
                    yield
                    TT('pool', rkk[:], rT[:, c, :], ic[:], ALU.mult, r=[('rT', c), K(1)], w=[K(8)])
                    yield
                    TS('dve', rkk[:], rkk[:], P(l, 22 + c, 23 + c), ALU.mult, r=[K(8), 'pv'], w=[K(8)])
                    yield
                    p5, p5k = bigslot()
                    MM(p5, blockones, rkk[:], r=['cst', K(8)], w=p5k)
                    TT('dve', bonus[:, c, :], p5, vT[:, c, :], ALU.mult, r=p5k + [('vT', c)], w=[('bonus', c)])
                    yield

                ntile6 = TL
                for pi_, pair_ in enumerate(((0, 1), (2, 3))):
                    gens_ = [st2(c, Tm if c % 2 == 0 else Tn, 'Tm' if c % 2 == 0 else 'Tn') for c in pair_]
                    if pi_ < ntile6 and upto >= 6:
                        gens_.append(st6(pi_))
                    gens_.append(stg(pi_))
                    for _ in zip_longest(*gens_):
                        pass
                dump("rT", rT[:], [('rT', c) for c in range(4)])
                dump("rs", rs[:], [('rs', c) for c in range(4)])
                dump("asb", asb[:], [('asb', c) for c in range(4)])
                dump("bcb", bcb[:], [('bcb', c) for c in range(4)])
                dump("kcb", kcb[:], [('kcb', c) for c in range(4)])
                dump("ACRC", ACRC[:].rearrange("p a b c d -> p (a b c d)"), [('ACRC', c) for c in range(4)])
                if upto < 3.05:
                    S.disabled = True
                allp = [('asb', c) for c in range(4)] + [('bcb', c) for c in range(4)] + [('kcb', c) for c in range(4)] + [('vb', c) for c in range(4)]
                for t in range(TL):
                    tsl = slice(t * 128, (t + 1) * 128)
                    for c in range(4):
                        MM(ps[:, c * 128:(c + 1) * 128], asb[:, c, tsl], identb[:], r=[('asb', c), 'identb'], w=pk(0, 512))
                        MM(ps[:, 512 + c * 128:512 + (c + 1) * 128], bcb[:, c, tsl], identb[:], r=[('bcb', c), 'identb'], w=pk(512, 1024))
                        MM(ps[:, 1024 + c * 128:1024 + (c + 1) * 128], kcb[:, c, tsl], identb[:], r=[('kcb', c), 'identb'], w=pk(1024, 1536))
                        MM(ps[:, 1536 + c * 128:1536 + (c + 1) * 128], vb[:, c, tsl], identb[:], r=[('vb', c), 'identb'], w=pk(1536, 2048))
                    h64 = lambda ap: ap.rearrange("p (h j) -> p h j", j=64)
                    CP('act', X[:, :, 0:64], h64(ps[:, 0:512]), r=pk(0, 512), w=[('X', 0), ('X', 1)])
                    CP('dve', BcT[:], h64(ps[:, 512:1024]), r=pk(512, 1024), w=['BcT'])
                    CP('act', KcT[:], h64(ps[:, 1024:1536]), r=pk(1024, 1536), w=['KcT'])
                    CP('dve', VV[:, :, 64:128], h64(ps[:, 1536:2048]), r=pk(1536, 2048), w=['VV'])
                    if upto < 3.1:
                        S.disabled = True
                    m12 = bc_mid(mask12, 4)
                    for hg in range(2):
                        for hi in (0, 2, 1, 3):
                            h = hg * 4 + hi; c = h // 2; pb = 64 * (h % 2)
                            rhs2 = ACRC[pb:pb + 64, c, t, :, :].rearrange("p a b -> p (a b)")
                            MM(ps[:, hi * 256:(hi + 1) * 256], bcb[pb:pb + 64, c, tsl], rhs2, r=[('bcb', c), ('ACRC', c)], w=pk(hi * 256, hi * 256 + 256))
                            MM(ps[:, 1024 + hi * 256:1024 + (hi + 1) * 256], kcb[pb:pb + 64, c, tsl], rhs2, r=[('kcb', c), ('ACRC', c)],
                               w=pk(1024 + hi * 256, 1024 + hi * 256 + 256))
                        m12h = bc_mid(mask12, 2)
                        for bq in range(2):
                            TT('dve', SC1[:, hg * 4 + 2 * bq:hg * 4 + 2 * bq + 2, :], ps[:, bq * 512:(bq + 1) * 512].rearrange("p (h j) -> p h j", j=256), m12h, ALU.mult,
                               r=pk(bq * 512, bq * 512 + 512) + ['cst'], w=[('SC1', hg)])
                            TT('dve', SC2[:, hg * 4 + 2 * bq:hg * 4 + 2 * bq + 2, :], ps[:, 1024 + bq * 512:1024 + (bq + 1) * 512].rearrange("p (h j) -> p h j", j=256), m12h, ALU.mult,
                               r=pk(1024 + bq * 512, 1024 + bq * 512 + 512) + ['cst'], w=[('SC2', hg)])
                    if upto < 3.2:
                        S.disabled = True
                    sck = [('SC1', 0), ('SC1', 1)]; sck2 = [('SC2', 0), ('SC2', 1)]
                    for h in (0, 2, 4, 6, 1, 3, 5, 7):
                        c = h // 2; pb = 64 * (h % 2)
                        MM(ps[:, h * 128:(h + 1) * 128], ACRC[pb:pb + 64, c, t, 0, :], bcb[pb:pb + 64, c, tsl], r=[('ACRC', c), ('bcb', c)],
                           w=pk(h * 128, h * 128 + 128))
                    for bq in range(2):
                        TT('dve', Pm[0][:, 4 * bq:4 * bq + 4, :], ps[:, bq * 512:(bq + 1) * 512].rearrange("p (h j) -> p h j", j=128), bc_mid(maskT, 4), ALU.mult,
                           r=pk(bq * 512, bq * 512 + 512) + ['cst'], w=[('Pm', 0, bq)])
                    for h in range(8):
                        MM(ps[:, 1024 + h * 64:1024 + (h + 1) * 64], SC2[:, h, 0:128], VV[:, h, 64:128], r=sck2 + ['VV'],
                           w=pk(1024 + h * 64, 1024 + h * 64 + 64))
                    CP('act', X[:, :, 64:128], h64(ps[:, 1024:1536]), r=pk(1024, 1536), w=[('X', 0), ('X', 1)])
                    if upto < 3.3:
                        S.disabled = True
                    for j in range(6):
                        a = j % 2; b = 1 - a
                        for bq in range(2):
                            hs_ = range(4 * bq, 4 * bq + 4)
                            ptv = (lambda h_: SC1[:, h_, 0:128]) if j == 0 else (lambda h_: PTm[a][:, h_, :])
                            ptk_ = sck if j == 0 else [('PTm', a, bq)]
                            for h in hs_:
                                MM(ps[:, h * 128:(h + 1) * 128], ptv(h), X[:, h, :], r=ptk_ + [('X', bq)], w=pk(h * 128, h * 128 + 128))
                            if j < 5:
                                for h in hs_:
                                    MM(ps[:, 1024 + h * 128:1024 + (h + 1) * 128], ptv(h), Pm[a][:, h, :], r=ptk_ + [('Pm', a, bq)],
                                       w=pk(1024 + h * 128, 1024 + h * 128 + 128))
                                for h in hs_:
                                    MM(ps[:, 2048 + h * 128:2048 + (h + 1) * 128], Pm[a][:, h, :], ptv(h), r=ptk_ + [('Pm', a, bq)],
                                       w=pk(2048 + h * 128, 2048 + h * 128 + 128))
                        for bq in range(2):
                            hs_ = slice(4 * bq, 4 * bq + 4)
                            TT('dve', X[:, hs_, :], ps[:, bq * 512:(bq + 1) * 512].rearrange("p (h j) -> p h j", j=128), X[:, hs_, :], ALU.add,
                               r=pk(bq * 512, bq * 512 + 512) + [('X', bq)], w=[('X', bq)])
                            if j < 5:
                                CP('act', Pm[b][:, hs_, :], ps[:, 1024 + bq * 512:1024 + (bq + 1) * 512].rearrange("p (h j) -> p h j", j=128),
                                   r=pk(1024 + bq * 512, 1536 + bq * 512), w=[('Pm', b, bq)])
                                CP('act' if bq == 0 else 'dve', PTm[b][:, hs_, :], ps[:, 2048 + bq * 512:2048 + (bq + 1) * 512].rearrange("p (h j) -> p h j", j=128),
                                   r=pk(2048 + bq * 512, 2560 + bq * 512), w=[('PTm', b, bq)])
                    if upto < 3.4:
                        S.disabled = True
                    for h in range(8):
                        c = h // 2; pb = 64 * (h % 2)
                        MM(ps[pb:pb + 64, c * 128:(c + 1) * 128], X[:, h, 0:64], SC1[:, h, 128:256], r=[('X', 0), ('X', 1)] + sck, w=pk(0, 512))
                    TT('dve', RhatT[:], ps[:, 0:512].rearrange("p (c j) -> p c j", j=128), rs[:, :, tsl], ALU.add,
                       r=pk(0, 512) + [('rs', c) for c in range(4)], w=['RhatT'])
                    if upto < 3.5:
                        S.disabled = True
                    mreg = [(512, 768), (1024, 1280)]; nreg = [(1536, 1792), (2048, 2304)]
                    for ch in range(2):
                        chs = slice(ch * 64, (ch + 1) * 64)
                        for h in range(8):
                            c = h // 2; pb = 64 * (h % 2)
                            MM(ps[pb:pb + 64, mreg[ch][0] + c * 64: mreg[ch][0] + (c + 1) * 64], X[chs, h, 0:64], BcT[chs, h, :],
                               r=[('X', 0), ('X', 1), 'BcT'], w=pk(*mreg[ch]))
                            no = ps[pb:pb + 64, nreg[ch][0] + c * 64: nreg[ch][0] + (c + 1) * 64]
                            MM(no, BcT[chs, h, :], X[chs, h, 64:128], start=True, stop=False, r=['BcT', ('X', 0), ('X', 1)], w=pk(*nreg[ch]))
                            MM(no, KcT[chs, h, :], VV[chs, h, 64:128], start=False, stop=True, r=['KcT', 'VV'], w=pk(*nreg[ch]))
                    for ch in range(2):
                        for c in range(4):
                            STT('dve', McT[:, c, ch, :], I2, GC[:, c, t * 2 + ch: t * 2 + ch + 1],
                                ps[:, mreg[ch][0] + c * 64: mreg[ch][0] + (c + 1) * 64], ALU.mult, ALU.add,
                                r=['cst', ('GC', c)] + pk(*mreg[ch]), w=['McT'])
                        CP('act', Nc[:, :, ch, :], ps[:, nreg[ch][0]:nreg[ch][1]].rearrange("p (c j) -> p c j", j=64), r=pk(*nreg[ch]), w=['Nc'])
                    if upto < 3.6:
                        S.disabled = True
                    if samp and t == 1:
                        emit_wkv(so_wkv[l, q])
                        CP('dve', H[:, l, :, :], HB, r=['xio', 'HB'], w=[('H', l)])
                    for ch in range(2):
                        if ch == 1 and upto < 3.95:
                            S.disabled = True
                        if ch == 0 and t == 1 and upto < 3.97:
                            S.disabled = True
                        chs = slice(ch * 64, (ch + 1) * 64)
                        yreg = (2560, 2816)
                        for h in range(8):
                            c = h // 2; pb = 64 * (h % 2)
                            yo = ps[pb:pb + 64, 2560 + c * 64:2560 + (c + 1) * 64]
                            MM(yo, X[:, h, 64:128], SC1[:, h, 128 + ch * 64:128 + (ch + 1) * 64], start=True, stop=False, r=[('X', 0), ('X', 1)] + sck, w=pk(*yreg))
                            MM(yo, VV[:, h, 64:128], SC2[:, h, 128 + ch * 64:128 + (ch + 1) * 64], start=False, stop=False, r=['VV'] + sck2, w=pk(*yreg))
                            MM(yo, H[pb:pb + 64, l, c, :], RhatT[pb:pb + 64, c, chs], start=False, stop=True, r=[('H', l), 'RhatT'], w=pk(*yreg))
                        if upto < 3.7:
                            S.disabled = True
                        for h in (0, 2, 4, 6, 1, 3, 5, 7):
                            c = h // 2; pb = 64 * (h % 2)
                            hb = 0 if pb == 0 else 512
                            MM(ps[pb:pb + 64, hb + c * 64:hb + (c + 1) * 64], McT[pb:pb + 64, c, ch, :], H[pb:pb + 64, l, c, :],
                               r=['McT', ('H', l)], w=pk(hb, hb + 256))
                        if upto < 3.8:
                            S.disabled = True
                        CP('act', YT[:, :, t * 128 + ch * 64: t * 128 + (ch + 1) * 64], ps[:, 2560:2816].rearrange("p (c j) -> p c j", j=64),
                           r=pk(*yreg), w=[('YT', c) for c in range(4)])
                        if upto < 3.9:
                            S.disabled = True
                        TT('dve', H[0:64, l, :, :], ps[0:64, 0:256].rearrange("p (c j) -> p c j", j=64), Nc[0:64, :, ch, :], ALU.add,
                           r=pk(0, 256) + ['Nc'], w=[('H', l)])
                        TT('dve', H[64:128, l, :, :], ps[64:128, 512:768].rearrange("p (c j) -> p c j", j=64), Nc[64:128, :, ch, :], ALU.add,
                           r=pk(512, 768) + ['Nc'], w=[('H', l)])
                dump("YT", YT[:], [('YT', c) for c in range(4)])
                if upto < 5:
                    S.disabled = True
                def st5(c):
                    K = lambda i: ('Tm', i)
                    d_, dq_, sd = Tm[0 + 3 * (c % 2)], Tm[1 + 3 * (c % 2)], Tm[2 + 3 * (c % 2)]
                    k0, k1, k2 = K(0 + 3 * (c % 2)), K(1 + 3 * (c % 2)), K(2 + 3 * (c % 2))
                    p1, p1k = bigslot()
                    MM(p1, blockones, YT[:, c, :], r=['cst', ('YT', c)], w=p1k)
                    STT('dve', d_[:], p1, -1.0 / 64, YT[:, c, :], ALU.mult, ALU.add, r=p1k + [('YT', c)], w=[k0])
                    yield
                    TT('pool', dq_[:], d_[:], d_[:], ALU.mult, r=[k0], w=[k1])
                    yield
                    p2, p2k = bigslot()
                    MM(p2, blockones, dq_[:], r=['cst', k1], w=p2k)
                    ACT(sd[:], p2, AF.Sqrt, bias=epsln[:, 1:2], scale=1.0 / 64, r=p2k + ['eps'], w=[k2])
                    yield
                    RCP(sd[:], sd[:], r=[k2], w=[k2])
                    yield
                    TT('dve', d_[:], d_[:], sd[:], ALU.mult, r=[k0, k2], w=[k0])
                    yield
                    TS('dve', d_[:], d_[:], P(l, 26 + c, 27 + c), ALU.mult, P(l, 30 + c, 31 + c), ALU.add, r=[k0, 'pv'], w=[k0])
                    yield
                    TT('pool', d_[:], d_[:], bonus[:, c, :], ALU.add, r=[k0, ('bonus', c)], w=[k0])
                    yield
                    TT('pool', yf[:, c, :], d_[:], gg[:, c, :], ALU.mult, r=[k0, ('gg', c)], w=[('yf', c)])
                    yield

                for pair_ in ((0, 1), (2, 3)):
                    gens_ = [st5(c) for c in pair_]
                    for _ in zip_longest(*gens_):
                        pass
                dump("yf", yf[:], [('yf', c) for c in range(4)])
                dump("YA", YA[:], [('YA', c) for c in range(4)])
                if upto < 7:
                    S.disabled = True
                big['banks'] = [4, 5, 2, 3]
                for br in range(2):
                    Wb, wbk, wbn = w_get('wbr' if br == 0 else 'wba', l, 0)
                    src_act = yf if br == 0 else YA
                    sk = 'yf' if br == 0 else 'YA'
                    for m in range(8):
                        gt = hT[:, br * 8 + m, :]; gk = ('hT', br * 8 + m)
                        p2, p2k = bigslot()
                        for c in range(4):
                            MM(p2, Wb[:, c, m * 128:(m + 1) * 128], src_act[:, c, :], start=(c == 0), stop=(c == 3), r=[wbk, (sk, c)], w=p2k)
                        if br == 0:
                            TT('dve', mixR[:, m, :], p2, gt, ALU.mult, r=p2k + [gk], w=[('mixR', m)])
                        else:
                            tm_ = Tm[m % 2]
                            TT('dve', tm_[:], p2, gt, ALU.mult, r=p2k + [gk], w=[('Tm', m % 2)])
                            TT('pool', mix[:, m, :], tm_[:], mixR[:, m, :], ALU.add, r=[('Tm', m % 2), ('mixR', m)], w=[('mix', m)])
                    w_done(wbn)
                for m in range(8):
                    if m % 4 == 0:
                        if m > 0:
                            w_done(won)
                        Wo, wok, won = w_get('wo', l, m // 4)
                    po, pkey = bigslot()
                    for k in range(8):
                        MM(po, Wo[:, k, (m % 4) * 128:(m % 4 + 1) * 128], mix[:, k, :], start=(k == 0), stop=(k == 7), r=[wok, ('mix', k)], w=pkey)
                    STT('dve', x1[:, m, :], xT[:, m, :], ALPHA, po, ALU.mult, ALU.add, r=[('xT', m)] + pkey, w=[('x1', m)])
                dump("mix", mix[:], [('mix', k) for k in range(8)])
                dump("x1pre", x1[:], [('x1', k) for k in range(8)])
                w_done(won)
                layernorm(l, 42, 50, 'ln1')
                dump("xln1", xT[:], [('xT', k) for k in range(8)])
                if upto < 8:
                    S.disabled = True
                for fb in range(8):
                    Wu, wuk, wun = w_get('wu', l, fb)
                    for j in range(4):
                        f = fb * 4 + j
                        po, pkey = bigslot()
                        for k in range(8):
                            MM(po, Wu[:, k, j * 128:(j + 1) * 128], xTb[:, k, :], start=(k == 0), stop=(k == 7), r=[wuk, ('xTb', k)], w=pkey)
                        gt = Gtmp[f % 2]; gk = ('Gtmp', f % 2)
                        ACT(gt[:], po, AF.Relu, r=pkey, w=[gk])
                        TT('dve', hT[:, f, :], gt[:], gt[:], ALU.mult, r=[gk], w=[('hT', f)])
                    w_done(wun)
                for m in range(8):
                    Wd, wdk, wdn = w_get('wd', l, m)
                    po, pkey = bigslot()
                    for f in range(32):
                        MM(po, Wd[:, f, :], hT[:, f, :], start=(f == 0), stop=(f == 31), r=[wdk, ('hT', f)], w=pkey)
                    STT('dve', x1[:, m, :], xT[:, m, :], ALPHA, po, ALU.mult, ALU.add, r=[('xT', m)] + pkey, w=[('x1', m)])
                    w_done(wdn)
                layernorm(l, 58, 66, 'ln2')
                if upto < 9:
                    S.disabled = True
                CP('pool', KT2[:, l, :, 0:128], KT2[:, l, :, NT:NT + 128], r=[('KT2', l)], w=[('KT2', l)])
                CP('pool', Vtm[:, l, 0, :], Vtm[:, l, TL, :], r=[('Vtm', l)], w=[('Vtm', l)])
                if samp:
                    emit_wkv(so_wkv[l, qb])
                    for (qq_, shf_, kf_, vf_) in ((q, SHF, kf, vf), (qb, SHFB, kfB, vfB)):
                        S.dma(QM, so_shift[l, qq_].rearrange("(c p) -> p c", p=128), shf_[:, l, :], r=[('SHF', l), ('SHFB', l)], allow_slow_non_contiguous=True)
                        for kvh in range(2):
                            TR(ps[:, 512 + kvh * 64:512 + (kvh + 1) * 64], kf_[0:64, kvh, :], identf[0:64, 0:64], r=['kf', 'kfB', 'cst'], w=pk(512, 1024))
                        CP('act', ostage[:, 0:128], ps[:, 512:640], r=pk(512, 1024), w=['xio'])
                        S.dma(QM, so_ck[l, qq_, 124:128].rearrange("t h d -> t (h d)"), ostage[0:4, 0:128], r=['xio'])
                        S.dma(QM, so_cv[l, qq_, 124:128].rearrange("t h d -> t (h d)"), vf_[0:4, :], r=['vf', 'vfB'])
                        S.dma(QM, so_ck[l, qq_, 0:124].rearrange("t h d -> t (h d)"), sck_i[l, qq_, 4:128].rearrange("t h d -> t (h d)"))
                        S.dma(QM, so_cv[l, qq_, 0:124].rearrange("t h d -> t (h d)"), scv_i[l, qq_, 4:128].rearrange("t h d -> t (h d)"))
                if last:
                    for c in range(4):
                        TR(ps[0:64, c * 128:(c + 1) * 128], H[:, l, c, :], identf, r=[('H', l), 'cst'], w=pk(0, 512))
                    CP('act', ostage[0:64, 0:512 // 2 * 0 + 256], ps[0:64, 0:256], r=pk(0, 512), w=['xio'])
                    S.dma(QM, o_wkv[l, 0:4].rearrange("h v k -> v h k"), ostage[0:64, 0:256].rearrange("p (h k) -> p h k", k=64), r=['xio'])
                    CP('act', ostage[0:64, 0:256], ps[0:64, 256:512], r=pk(0, 512), w=['xio'])
                    S.dma(QM, o_wkv[l, 4:8].rearrange("h v k -> v h k"), ostage[0:64, 0:256].rearrange("p (h k) -> p h k", k=64), r=['xio'])
                    S.dma(QM, o_shift[l].rearrange("(c p) -> p c", p=128), CARRY[:, l, :], r=[('CARRY', l)], allow_slow_non_contiguous=True)
                    for kvh in range(2):
                        TR(ps[:, 512 + kvh * 64:512 + (kvh + 1) * 64], kf[0:64, kvh, :], identf[0:64, 0:64], r=['kf', 'cst'], w=pk(512, 1024))
                    CP('act', ostage[:, 0:128], ps[:, 512:640], r=pk(512, 1024), w=['xio'])
                    S.dma(QM, o_ck[l].rearrange("t h d -> t (h d)"), ostage[:, 0:128], r=['xio'])
                    S.dma(QM, o_cv[l].rearrange("t h d -> t (h d)"), vf[:], r=['vf'])
            for t in range(TL):
                for kp in range(2):
                    for kk in range(4):
                        k = kp * 4 + kk
                        TR(ps[:, kp * 512 + kk * 128: kp * 512 + (kk + 1) * 128], xT[:, k, t * 128:(t + 1) * 128], identf,
                           r=[('xT', k), 'cst'], w=pk(kp * 512, kp * 512 + 512))
                    CP('act' if kp == 0 else 'dve', xio[:, t, kp * 512:(kp + 1) * 512], ps[:, kp * 512:(kp + 1) * 512],
                       r=pk(kp * 512, kp * 512 + 512), w=['xio'])
            if not samp:
                S.dma(QM, y_p[t0g:t0g + NT, :].rearrange("(t p) d -> p t d", p=128), xio[:], r=['xio'])
            else:
                S.dma(QM, y_s[q], xio[0:4, 0, :], r=['xio'])
                S.dma(QM, y_s[qb], xio[0:4, 1, :], r=['xio'])
        stats = S.emit()
    return nc, stats


def pack_pv(inp):
    pv = np.zeros((128, 2 * PVL), np.float32)
    for l in range(2):
        b = l * PVL
        pv[:, b:b + 14] = inp['mu_shift'][l].reshape(14, 128).T
        for off, name in ((14, 'k_k'), (18, 'k_a'), (26, 'lnx_g'), (30, 'lnx_b'), (34, 'decay_base'), (38, 'iclr_base')):
            pv[:, b + off:b + off + 4] = inp[name][l].reshape(4, 128).T
        pv[:, b + 22:b + 26] = inp['r_k'][l].reshape(512).reshape(4, 128).T
        for off, name in ((42, 'ln1_g'), (50, 'ln1_b'), (58, 'ln2_g'), (66, 'ln2_b')):
            pv[:, b + off:b + off + 8] = inp[name][l].reshape(8, 128).T
        pv[:, b + 74:b + 78] = np.repeat(inp['sinks'][l].reshape(4, 2), 64, axis=1).T
    return pv


_CACHE = {}


def host_inputs(inp, c, SEQ, NSS, consts):
    cst, cosT, sinT, coss, sins, tmask, pv = consts
    wnames = ['w_in', 'w_br_rwkv', 'w_br_attn', 'w_out', 'w_ff_up', 'w_ff_down', 'decay_up', 'iclr_up', 'gate_up']
    m = {k: inp[k] for k in wnames}
    sl = slice(c * NSS, (c + 1) * NSS)
    m.update(xp=inp['x_prompt'][(c * 2) // 8][:SEQ], pv_in=pv, cst_in=cst, cos_in=cosT, sin_in=sinT, coss_in=coss, sins_in=sins, tmask_in=tmask,
             xs=np.ascontiguousarray(inp['x_sample'][sl]), swkv_i=np.ascontiguousarray(inp['state_wkv'][:, sl]),
             sshift_i=np.ascontiguousarray(inp['state_shift'][:, sl]), sck_i=np.ascontiguousarray(inp['cache_k_win'][:, sl]),
             scv_i=np.ascontiguousarray(inp['cache_v_win'][:, sl]))
    return m


def host_consts(inp, SEQ, past_len=8192):
    cst = make_consts()
    cosT, sinT = rope_tables(np.arange(SEQ))
    coss, sins = rope_tables(past_len + (np.arange(NT) % 128))
    tmask = np.zeros((128, NT), np.float32)
    tmask[:, 0:4] = 1.0
    tmask[:, 128:132] = 1.0
    return cst, cosT, sinT, coss, sins, tmask, pack_pv(inp)


def kernel(**inputs):
    inp = {k: np.ascontiguousarray(np.asarray(v)) for k, v in inputs.items()}
    B, SEQ, _ = inp['x_prompt'].shape
    NSS = inp['x_sample'].shape[0] // 8
    if SEQ not in _CACHE:
        _CACHE[SEQ] = build(SEQ, NSS)
    nc, _ = _CACHE[SEQ]
    consts = host_consts(inp, SEQ)
    in_maps = [host_inputs(inp, c, SEQ, NSS, consts) for c in range(8)]
    res = run_bass_kernel_spmd(nc, in_maps, core_ids=list(range(8))).results
    y_p = np.stack([res[0]['y_p'], res[4]['y_p']])
    p_wkv = np.stack([res[0]['p_wkv'], res[4]['p_wkv']], 1)
    p_shift = np.stack([res[0]['p_shift'], res[4]['p_shift']], 1)
    p_ck = np.stack([res[0]['p_ck'], res[4]['p_ck']], 1)
    p_cv = np.stack([res[0]['p_cv'], res[4]['p_cv']], 1)
    y_s = np.concatenate([res[c]['y_s'] for c in range(8)], 0)
    s_wkv = np.concatenate([res[c]['s_wkv'] for c in range(8)], 1)
    s_shift = np.concatenate([res[c]['s_shift'] for c in range(8)], 1)
    s_ck = np.concatenate([res[c]['s_ck'] for c in range(8)], 1)
    s_cv = np.concatenate([res[c]['s_cv'] for c in range(8)], 1)
    return (y_p, y_s, p_wkv, p_shift, p_ck, p_cv, s_wkv, s_shift, s_ck, s_cv)
```
